# Optimizing a Trainium2 kernel written in Bass

```python
import math
import jax, jax.numpy as jnp
from jax import lax
import numpy as np

D_MODEL = 1024
BATCH = 1
SEQ = 16384
DEPTH = 4

CTX_LEN = 256
GRID_W = 64
EPS = 1e-6

A_HD = 64
A_HEADS = D_MODEL // 2 // A_HD * 2 // 2
A_W = A_HEADS * A_HD
A_LORA_W = 64
A_LORA_A = 64
A_SHIFT_COLS = 3 * A_W + 2 * A_LORA_W + 2 * A_LORA_A
A_GN_EPS = 64e-5
B_HD = 64
B_HEADS = D_MODEL // B_HD
B_W = B_HEADS * B_HD
B_GROUPS = 2
B_STATE = 128
B_XBC = B_W + 2 * B_GROUPS * B_STATE
CONV_K = 5
SSD_CHUNK = 128
C_HEADS = 4
C_HD = D_MODEL // C_HEADS
C_W = C_HEADS * C_HD
MLSTM_CHUNK = 128
D_HD = 64
D_HEADS = D_MODEL // D_HD
D_KV_HEADS = D_HEADS // 4
D_W = D_HEADS * D_HD
D_KV_W = D_KV_HEADS * D_HD
Q_BLOCK = 128
ROPE_THETA = 10000.0

AB_COLS = A_SHIFT_COLS + A_W + B_XBC + 2 * B_HEADS + B_W
AB_WIDTH = A_W + B_W
CD_COLS = 5 * C_W + 4 * C_HEADS + 2 * D_W + 2 * D_KV_W
CD_WIDTH = C_W + D_W
N_EVEN = (DEPTH + 1) // 2
N_ODD = DEPTH // 2

kernel_name = 'hybrid_rwkv7_ssd_mlstm_gqa_prefix'

F32 = jnp.float32


def rmsnorm(x, g):
    xf = x.astype(F32)
    y = xf * lax.rsqrt(jnp.mean(xf * xf, -1, keepdims=True) + EPS)
    return (y * g.astype(F32)).astype(x.dtype)


def split_cols(u, sizes):
    return jnp.split(u, np.cumsum(sizes)[:-1].tolist(), axis=-1)


def centred_shift(u):
    up = jnp.pad(u, ((0, 0), (1, 1), (0, 0)))
    return 0.5 * (up[:, :-2] + up[:, 2:])


def dwconv_centred(u, w, b):
    K, C = w.shape
    out = lax.conv_general_dilated(u, w[:, None, :].astype(u.dtype), window_strides=(1,),
                                   padding=[(K // 2, K // 2)], dimension_numbers=('NWC', 'WIO', 'NWC'),
                                   feature_group_count=C)
    return out + b.astype(u.dtype)


def rwkv7_bidir_scan(r, k_dir, v, decay, a, kk, S0):
    def dir_stack(u):
        return jnp.stack([u, jnp.flip(u, 1)], 0).transpose(2, 0, 1, 3, 4)

    def dir_split(u):
        u = jnp.moveaxis(u, 2, 0)
        return jnp.stack([u[0], jnp.flip(u[1], 1)], 0).transpose(2, 0, 1, 3, 4)

    def step(S, inp):
        r_t, k_t, v_t, w_t, a_t, kk_t = inp
        Skk = jnp.einsum('dbhvk,dbhk->dbhv', S, kk_t)
        S = S * w_t[..., None, :] - Skk[..., :, None] * (kk_t * a_t)[..., None, :] + v_t[..., :, None] * k_t[..., None, :]
        return S, jnp.einsum('dbhvk,dbhk->dbhv', S, r_t)

    xs = (dir_stack(r), dir_split(k_dir), dir_stack(v), dir_split(decay), dir_split(a), dir_stack(kk))
    S_fin, y = lax.scan(step, S0, xs)
    y = y[:, 0] + jnp.flip(y[:, 1], 0)
    return jnp.moveaxis(y, 0, 1), S_fin


def rwkv_branch(u_shift, u_gate, S0, p):
    Bsz, T = u_shift.shape[:2]
    u = (u_shift + p['mu'] * (centred_shift(u_shift) - u_shift)).astype(F32)
    r, k, v, wl, al = split_cols(u, [A_W, A_W, A_W, 2 * A_LORA_W, 2 * A_LORA_A])
    wl = wl.reshape(Bsz, T, 2, A_LORA_W)
    al = al.reshape(Bsz, T, 2, A_LORA_A)
    w_pre = p['w0'] + jnp.einsum('btdr,drc->btdc', jnp.tanh(wl), p['w2'])
    decay = jnp.exp(-jnp.exp(-jax.nn.softplus(-w_pre) - 0.5))
    a = jax.nn.sigmoid(p['a0'] + jnp.einsum('btdr,drc->btdc', al, p['a2']))

    def hs(t):
        return t.reshape(t.shape[:-1] + (A_HEADS, A_HD))

    kk = hs(k * p['k_k'])
    kk = kk / jnp.maximum(jnp.sqrt(jnp.sum(kk * kk, -1, keepdims=True)), 1e-12)
    k_dir = k[:, :, None] * (1.0 + (a - 1.0) * p['k_a'])
    y, S_fin = rwkv7_bidir_scan(hs(r), hs(k_dir), hs(v), hs(decay), hs(a), kk, S0)
    mu_ = jnp.mean(y, -1, keepdims=True)
    var = jnp.mean((y - mu_) ** 2, -1, keepdims=True)
    y = (y - mu_) * lax.rsqrt(var + A_GN_EPS) * hs(p['ln_w']) + hs(p['ln_b'])
    bonus = jnp.sum(hs(r) * hs(jnp.sum(k_dir, 2)) * p['r_k'], -1, keepdims=True)
    y = (y + bonus * hs(v)).reshape(Bsz, T, A_W)
    return y.astype(u_gate.dtype) * jax.nn.silu(u_gate), S_fin


def ssd_chunked(x, dt, A, Bm, Cm, S0):
    Bsz, T, H, P = x.shape
    G, N = Bm.shape[2:]
    E = H // G
    L = SSD_CHUNK
    nc = T // L
    x = x.reshape(Bsz, nc, L, G, E, P)
    dt = dt.reshape(Bsz, nc, L, G, E)
    Bm = Bm.reshape(Bsz, nc, L, G, N)
    Cm = Cm.reshape(Bsz, nc, L, G, N)
    Acs = jnp.cumsum(dt * A.reshape(G, E), axis=2)
    mask = jnp.tril(jnp.ones((L, L), bool))[:, :, None, None]
    seg = jnp.where(mask, Acs[:, :, :, None] - Acs[:, :, None, :], -jnp.inf)
    CB = jnp.einsum('bclgn,bcsgn->bclsg', Cm, Bm)
    Wm = CB[..., None] * jnp.exp(seg) * dt[:, :, None]
    y_diag = jnp.einsum('bclsge,bcsgep->bclgep', Wm, x)
    to_end = jnp.exp(Acs[:, :, -1:] - Acs) * dt
    states = jnp.einsum('bclgn,bclge,bclgep->bcgepn', Bm, to_end, x)
    chunk_decay = jnp.exp(Acs[:, :, -1])

    def pass_state(S, inp):
        st, dec = inp
        return S * dec[..., None, None] + st, S

    S_fin, S_in = lax.scan(pass_state, S0.reshape(Bsz, G, E, P, N),
                           (jnp.moveaxis(states, 1, 0), jnp.moveaxis(chunk_decay, 1, 0)))
    S_in = jnp.moveaxis(S_in, 0, 1)
    y_off = jnp.einsum('bclgn,bcgepn,bclge->bclgep', Cm, S_in, jnp.exp(Acs))
    return (y_diag + y_off).reshape(Bsz, T, H, P), S_fin.reshape(Bsz, H, P, N)


def ssd_bidir(x, dt, A, Bm, Cm, S0):
    def both(t):
        return jnp.stack([t, jnp.flip(t, 1)])
    dt2 = jnp.stack([dt[:, :, 0], jnp.flip(dt[:, :, 1], 1)])
    y2, S_fin = jax.vmap(ssd_chunked)(both(x), dt2, A, both(Bm), both(Cm), S0)
    return y2[0] + jnp.flip(y2[1], 1), S_fin


def mamba_branch(u_xbc, u_dt, u_z, S0, p):
    Bsz, T = u_xbc.shape[:2]
    xbc = jax.nn.silu(dwconv_centred(u_xbc, p['conv_w'], p['conv_b'])).astype(F32)
    xs, Bm, Cm = split_cols(xbc, [B_W, B_GROUPS * B_STATE, B_GROUPS * B_STATE])
    xs = xs.reshape(Bsz, T, B_HEADS, B_HD)
    Bm = Bm.reshape(Bsz, T, B_GROUPS, B_STATE)
    Cm = Cm.reshape(Bsz, T, B_GROUPS, B_STATE)
    dt = jax.nn.softplus(u_dt.astype(F32).reshape(Bsz, T, 2, B_HEADS) + p['dt_bias'])
    A = -jnp.exp(p['a_log'].astype(F32))
    y, S_fin = ssd_bidir(xs, dt, A, Bm, Cm, S0)
    y = y + p['d'][:, None] * xs
    y = y.reshape(Bsz, T, B_W) * jax.nn.silu(u_z.astype(F32))
    yg = y.reshape(Bsz, T, B_GROUPS, B_W // B_GROUPS)
    y = (yg * lax.rsqrt(jnp.mean(yg * yg, -1, keepdims=True) + EPS)).reshape(Bsz, T, B_W) * p['norm_w']
    return y.astype(u_z.dtype), S_fin


def mlstm_chunkwise(q, k, v, ig, lf, state0):
    Bsz, T, H, dh = q.shape
    L = MLSTM_CHUNK
    nc = T // L

    def to_chunks(t):
        return t.reshape(Bsz, nc, L, H, dh).transpose(1, 0, 3, 2, 4)

    def gate_chunks(t):
        return t.reshape(Bsz, nc, L, H).transpose(1, 0, 3, 2)

    causal = jnp.tril(jnp.ones((L, L), bool))

    def chunk_step(carry, inp):
        C_prev, n_prev, m_prev = carry
        qc, kc, vc, ic, fc = inp
        b = jnp.cumsum(fc, -1)
        logw = jnp.where(causal, b[..., :, None] - b[..., None, :] + ic[..., None, :], -jnp.inf)
        g = b + m_prev[..., None]
        m = jnp.maximum(g, jnp.max(logw, -1))
        s = jnp.einsum('bhld,bhsd->bhls', qc, kc) * jnp.exp(logw - m[..., None])
        wg = jnp.exp(g - m)
        num = jnp.einsum('bhls,bhsd->bhld', s, vc) + wg[..., None] * jnp.einsum('bhvd,bhld->bhlv', C_prev, qc)
        den = jnp.sum(s, -1) + wg * jnp.einsum('bhd,bhld->bhl', n_prev, qc)
        h = num / jnp.maximum(jnp.abs(den), jnp.exp(-m))[..., None]
        b_last = b[..., -1]
        logw_end = b_last[..., None] - b + ic
        m_new = jnp.maximum(b_last + m_prev, jnp.max(logw_end, -1))
        w_end = jnp.exp(logw_end - m_new[..., None])
        keep = jnp.exp(b_last + m_prev - m_new)
        C_new = keep[..., None, None] * C_prev + jnp.einsum('bhl,bhlv,bhld->bhvd', w_end, vc, kc)
        n_new = keep[..., None] * n_prev + jnp.einsum('bhl,bhld->bhd', w_end, kc)
        return (C_new, n_new, m_new), h

    state, h = lax.scan(chunk_step, state0, (to_chunks(q * dh ** -0.5), to_chunks(k), to_chunks(v),
                                             gate_chunks(ig), gate_chunks(lf)))
    return h.transpose(1, 0, 3, 2, 4).reshape(Bsz, T, H, dh), state


def mlstm_bidir(q, k, v, ig, lf, state0):
    def both(t):
        return jnp.stack([t, jnp.flip(t, 1)])
    ig2 = jnp.stack([ig[:, :, 0], jnp.flip(ig[:, :, 1], 1)])
    lf2 = jnp.stack([lf[:, :, 0], jnp.flip(lf[:, :, 1], 1)])
    h2, state = jax.vmap(mlstm_chunkwise)(both(q), both(k), both(v), ig2, lf2, state0)
    return h2[0] + jnp.flip(h2[1], 1), state


def mlstm_branch(u_qk, u_v, u_o, u_i, u_f, u_z, state0, p):
    Bsz, T = u_qk.shape[:2]
    qk = jax.nn.silu(dwconv_centred(u_qk, p['conv_w'], p['conv_b'])).astype(F32)
    q, k = split_cols(qk, [C_W, C_W])

    def hs(t):
        return t.reshape(Bsz, T, C_HEADS, C_HD)

    ig = u_i.astype(F32).reshape(Bsz, T, 2, C_HEADS) + p['i_bias']
    lf = jax.nn.log_sigmoid(u_f.astype(F32).reshape(Bsz, T, 2, C_HEADS) + p['f_bias'])
    h, state = mlstm_bidir(hs(q), hs(k), hs(u_v.astype(F32)), ig, lf, state0)
    h = h * lax.rsqrt(jnp.mean(h * h, -1, keepdims=True) + EPS) * p['norm_w'].reshape(C_HEADS, C_HD)
    h = h.reshape(Bsz, T, C_W) * jax.nn.sigmoid(u_o.astype(F32)) * jax.nn.silu(u_z.astype(F32))
    return h.astype(u_z.dtype), state


def rope_2d(x, row, col):
    half = x.shape[-1] // 2
    quarter = half // 2
    inv = ROPE_THETA ** (-jnp.arange(quarter, dtype=F32) / quarter)

    def rot(xa, pos):
        ang = pos.astype(F32)[:, None] * inv
        cos = jnp.cos(ang)[None, :, None, :]
        sin = jnp.sin(ang)[None, :, None, :]
        x1, x2 = xa[..., :quarter], xa[..., quarter:]
        return jnp.concatenate([x1 * cos - x2 * sin, x1 * sin + x2 * cos], -1)

    xf = x.astype(F32)
    return jnp.concatenate([rot(xf[..., :half], row), rot(xf[..., half:], col)], -1).astype(x.dtype)


def softmax_attend(q, k, v):
    s = jnp.einsum('bkgqd,bksd->bkgqs', q, k).astype(F32) * (q.shape[-1] ** -0.5)
    pr = jax.nn.softmax(s, axis=-1).astype(v.dtype)
    return jnp.einsum('bkgqs,bksd->bkgqd', pr, v)


def attn_branch(uq_l, uk_l, uv_l, ug_l, uq_c, uk_c, uv_c, ug_c, row, col, p, need_ctx):
    Bsz, T = uq_l.shape[:2]
    Cn = uq_c.shape[1]
    G = D_HEADS // D_KV_HEADS

    def qkv(uq, uk, uv):
        n = uq.shape[1]
        q = rmsnorm(uq.reshape(Bsz, n, D_HEADS, D_HD), p['q_norm'])
        k = rmsnorm(uk.reshape(Bsz, n, D_KV_HEADS, D_HD), p['k_norm'])
        return q, k, uv.reshape(Bsz, n, D_KV_HEADS, D_HD)

    q_l, k_l, v_l = qkv(uq_l, uk_l, uv_l)
    q_c, k_c, v_c = qkv(uq_c, uk_c, uv_c)
    q_l = rope_2d(q_l, row, col)
    k_l = rope_2d(k_l, row, col)

    def heads_first(t):
        return t.transpose(0, 2, 1, 3)

    K = heads_first(jnp.concatenate([k_l, k_c], 1))
    V = heads_first(jnp.concatenate([v_l, v_c], 1))
    qb = q_l.reshape(Bsz, T // Q_BLOCK, Q_BLOCK, D_KV_HEADS, G, D_HD).transpose(1, 0, 3, 4, 2, 5)
    o = lax.map(lambda blk: softmax_attend(blk, K, V), qb)
    y = o.transpose(1, 0, 4, 2, 3, 5).reshape(Bsz, T, D_W) * jax.nn.silu(ug_l)
    yc = None
    if need_ctx:
        qc = q_c.reshape(Bsz, Cn, D_KV_HEADS, G, D_HD).transpose(0, 2, 3, 1, 4)
        oc = softmax_attend(qc, heads_first(k_c), heads_first(v_c))
        yc = oc.transpose(0, 3, 1, 2, 4).reshape(Bsz, Cn, D_W) * jax.nn.silu(ug_c)
    return y, yc


def mixer_ab(u, uc, prk, pmb, need_ctx):
    sizes = [A_SHIFT_COLS, A_W, B_XBC, 2 * B_HEADS, B_W]
    rk_l, ga_l, xbc_l, dt_l, z_l = split_cols(u, sizes)
    rk_c, ga_c, xbc_c, dt_c, z_c = split_cols(uc, sizes)
    Bsz = u.shape[0]
    s_rk = jnp.zeros((2, Bsz, A_HEADS, A_HD, A_HD), F32)
    ya_c, s_rk = rwkv_branch(rk_c, ga_c, s_rk, prk)
    ya_l, _ = rwkv_branch(rk_l, ga_l, s_rk, prk)
    s_mb = jnp.zeros((2, Bsz, B_HEADS, B_HD, B_STATE), F32)
    yb_c, s_mb = mamba_branch(xbc_c, dt_c, z_c, s_mb, pmb)
    yb_l, _ = mamba_branch(xbc_l, dt_l, z_l, s_mb, pmb)
    y = jnp.concatenate([ya_l, yb_l], -1)
    yc = jnp.concatenate([ya_c, yb_c], -1) if need_ctx else None
    return y, yc


def mixer_cd(u, uc, pml, pat, row, col, need_ctx):
    sizes = [2 * C_W, C_W, C_W, 2 * C_HEADS, 2 * C_HEADS, C_W, D_W, D_KV_W, D_KV_W, D_W]
    qk_l, v_l, o_l, i_l, f_l, z_l, aq_l, ak_l, av_l, ag_l = split_cols(u, sizes)
    qk_c, v_c, o_c, i_c, f_c, z_c, aq_c, ak_c, av_c, ag_c = split_cols(uc, sizes)
    Bsz = u.shape[0]
    st = (jnp.zeros((2, Bsz, C_HEADS, C_HD, C_HD), F32), jnp.zeros((2, Bsz, C_HEADS, C_HD), F32),
          jnp.zeros((2, Bsz, C_HEADS), F32))
    yc_c, st = mlstm_branch(qk_c, v_c, o_c, i_c, f_c, z_c, st, pml)
    yc_l, _ = mlstm_branch(qk_l, v_l, o_l, i_l, f_l, z_l, st, pml)
    yd_l, yd_c = attn_branch(aq_l, ak_l, av_l, ag_l, aq_c, ak_c, av_c, ag_c, row, col, pat, need_ctx)
    y = jnp.concatenate([yc_l, yd_l], -1)
    yc = jnp.concatenate([yc_c, yd_c], -1) if need_ctx else None
    return y, yc


def setup_inputs(seed: int = 0) -> dict:
    key = jax.random.key(seed)
    ks = iter(jax.random.split(key, 48))
    D = D_MODEL

    def nrm(shape, s=1.0):
        return s * jax.random.normal(next(ks), shape, F32)

    dt0 = jnp.exp(jax.random.uniform(next(ks), (N_EVEN, 2, B_HEADS), F32, math.log(1e-3), math.log(1e-1)))
    return {
        'x': nrm((BATCH, SEQ, D)),
        'c': nrm((BATCH, D)),
        'ctx': nrm((BATCH, CTX_LEN, D)),
        'c_ctx': nrm((D,)),
        'norm_g': 1.0 + nrm((DEPTH, D), 0.05),
        'ada_w': nrm((DEPTH, D, 3 * D), 0.5 * D ** -0.5),
        'ada_b': nrm((DEPTH, 3 * D), 0.02),
        'norm_final': 1.0 + nrm((D,), 0.05),
        'ab_w_in': nrm((N_EVEN, D, AB_COLS), D ** -0.5),
        'ab_w_out': nrm((N_EVEN, AB_WIDTH, D), AB_WIDTH ** -0.5),
        'rk_mu': jax.random.uniform(next(ks), (N_EVEN, A_SHIFT_COLS), F32),
        'rk_w0': jnp.linspace(-6.0, -1.0, A_W, dtype=F32) + nrm((N_EVEN, 2, A_W), 0.1),
        'rk_w2': nrm((N_EVEN, 2, A_LORA_W, A_W), 0.1 * A_LORA_W ** -0.5),
        'rk_a0': nrm((N_EVEN, 2, A_W), 0.1),
        'rk_a2': nrm((N_EVEN, 2, A_LORA_A, A_W), 0.1 * A_LORA_A ** -0.5),
        'rk_k_k': 0.85 + nrm((N_EVEN, A_W), 0.05),
        'rk_k_a': 1.0 + nrm((N_EVEN, A_W), 0.05),
        'rk_r_k': nrm((N_EVEN, A_HEADS, A_HD), 0.1),
        'rk_ln_w': 1.0 + nrm((N_EVEN, A_W), 0.05),
        'rk_ln_b': nrm((N_EVEN, A_W), 0.02),
        'mb_conv_w': nrm((N_EVEN, CONV_K, B_XBC), CONV_K ** -0.5),
        'mb_conv_b': nrm((N_EVEN, B_XBC), 0.02),
        'mb_dt_bias': dt0 + jnp.log(-jnp.expm1(-dt0)),
        'mb_a_log': jnp.log(jax.random.uniform(next(ks), (N_EVEN, 2, B_HEADS), F32, 1.0, 16.0)),
        'mb_d': 1.0 + nrm((N_EVEN, B_HEADS), 0.05),
        'mb_norm_w': 1.0 + nrm((N_EVEN, B_W), 0.05),
        'cd_w_in': nrm((N_ODD, D, CD_COLS), D ** -0.5),
        'cd_w_out': nrm((N_ODD, CD_WIDTH, D), CD_WIDTH ** -0.5),
        'ml_conv_w': nrm((N_ODD, CONV_K, 2 * C_W), CONV_K ** -0.5),
        'ml_conv_b': nrm((N_ODD, 2 * C_W), 0.02),
        'ml_i_bias': nrm((N_ODD, 2, C_HEADS), 0.1),
        'ml_f_bias': jnp.linspace(3.0, 6.0, C_HEADS, dtype=F32) + nrm((N_ODD, 2, C_HEADS), 0.1),
        'ml_norm_w': 1.0 + nrm((N_ODD, C_W), 0.05),
        'at_q_norm': 1.0 + nrm((N_ODD, D_HD), 0.05),
        'at_k_norm': 1.0 + nrm((N_ODD, D_HD), 0.05),
    }


def reference(x, c, ctx, c_ctx, norm_g, ada_w, ada_b, norm_final, ab_w_in, ab_w_out, rk_mu, rk_w0, rk_w2,
              rk_a0, rk_a2, rk_k_k, rk_k_a, rk_r_k, rk_ln_w, rk_ln_b, mb_conv_w, mb_conv_b, mb_dt_bias,
              mb_a_log, mb_d, mb_norm_w, cd_w_in, cd_w_out, ml_conv_w, ml_conv_b, ml_i_bias, ml_f_bias,
              ml_norm_w, at_q_norm, at_k_norm):
    T = x.shape[1]
    rows = T // GRID_W
    row = jnp.repeat(jnp.arange(rows), GRID_W)
    col = jnp.arange(rows * GRID_W) % GRID_W
    cond = jax.nn.silu(c)
    cond_ctx = jax.nn.silu(c_ctx)
    for l in range(DEPTH):
        need_ctx = l < DEPTH - 1
        shift, scale, gate = jnp.split(cond @ ada_w[l] + ada_b[l], 3, -1)
        shift_c, scale_c, gate_c = jnp.split(cond_ctx @ ada_w[l] + ada_b[l], 3, -1)
        h = rmsnorm(x, norm_g[l]) * (1.0 + scale[:, None]) + shift[:, None]
        hc = rmsnorm(ctx, norm_g[l]) * (1.0 + scale_c) + shift_c
        j = l // 2
        if l % 2 == 0:
            prk = {'mu': rk_mu[j], 'w0': rk_w0[j], 'w2': rk_w2[j], 'a0': rk_a0[j], 'a2': rk_a2[j],
                   'k_k': rk_k_k[j], 'k_a': rk_k_a[j], 'r_k': rk_r_k[j], 'ln_w': rk_ln_w[j], 'ln_b': rk_ln_b[j]}
            pmb = {'conv_w': mb_conv_w[j], 'conv_b': mb_conv_b[j], 'dt_bias': mb_dt_bias[j],
                   'a_log': mb_a_log[j], 'd': mb_d[j], 'norm_w': mb_norm_w[j]}
            y, yc = mixer_ab(h @ ab_w_in[j], hc @ ab_w_in[j], prk, pmb, need_ctx)
            w_out = ab_w_out[j]
        else:
            pml = {'conv_w': ml_conv_w[j], 'conv_b': ml_conv_b[j], 'i_bias': ml_i_bias[j],
                   'f_bias': ml_f_bias[j], 'norm_w': ml_norm_w[j]}
            pat = {'q_norm': at_q_norm[j], 'k_norm': at_k_norm[j]}
            y, yc = mixer_cd(h @ cd_w_in[j], hc @ cd_w_in[j], pml, pat, row, col, need_ctx)
            w_out = cd_w_out[j]
        x = x + gate[:, None] * (y @ w_out)
        if need_ctx:
            ctx = ctx + gate_c * (yc @ w_out)
    return rmsnorm(x, norm_final)
```

```python
import contextlib
import os
import numpy as np
import concourse.bass as bass
import concourse.mybir as mybir
from concourse.bass_utils import run_bass_kernel_spmd

F32 = mybir.dt.float32
BF16 = mybir.dt.bfloat16
I32 = mybir.dt.int32
AF = mybir.ActivationFunctionType
ALU = mybir.AluOpType
AX = mybir.AxisListType

COMPUTE = ("pe", "dve", "act", "pool")
STREAMS = ("pe", "dve", "act", "pool", "sp")
DMA_K = 4


class Res:
    __slots__ = ("name", "w", "rs")

    def __init__(self, name):
        self.name = name
        self.w = None
        self.rs = []


class Ins:
    __slots__ = ("stream", "fn", "is_dma", "seq", "dn", "waits", "need_inc", "know")


class Prog:
    def __init__(self, nc):
        self.nc = nc
        self.ins = {s: [] for s in STREAMS}
        self.ndma = {s: 0 for s in STREAMS}
        self.know = {s: {e: -1 for e in COMPUTE} for s in STREAMS}
        self.kdma = {s: set() for s in STREAMS}
        self.n_res = 0
        self._bar = {}
        self._last = {}

    def res(self, name=None):
        self.n_res += 1
        return Res(name or f"r{self.n_res}")

    def _dep(self, I, D):
        s = I.stream
        if D is None or D is I:
            return
        if D.is_dma:
            key = (D.stream, D.dn)
            if key in self.kdma[s]:
                return
            self.kdma[s].add(key)
            I.waits.append(("dma", D.stream, D.dn % DMA_K, 16 * (D.dn // DMA_K + 1)))
        else:
            e = D.stream
            if self.know[s][e] >= D.seq:
                return
            if e == "pe" and s == "pe":
                return
            D.need_inc = True
            I.waits.append(("eng", e, D.seq))
            kn = self.know[s]
            kn[e] = D.seq
            for e2, v in D.know.items():
                if v > kn[e2]:
                    kn[e2] = v

    def _add(self, stream, fn, reads, writes, is_dma):
        I = Ins()
        I.stream = stream
        I.fn = fn
        I.is_dma = is_dma
        I.waits = []
        I.need_inc = False
        I.seq = None
        I.dn = None
        if is_dma:
            n = self.ndma[stream]
            self.ndma[stream] = n + 1
            I.dn = n
            if n >= DMA_K:
                key = (stream, n - DMA_K)
                if key not in self.kdma[stream]:
                    self.kdma[stream].add(key)
                    I.waits.append(("dma", stream, n % DMA_K, 16 * ((n - DMA_K) // DMA_K + 1)))
        bar = self._bar.pop(stream, None)
        if bar is not None:
            lastI, snap_d = bar
            for e, D in lastI.items():
                if D is not None:
                    self._dep(I, D)
            for s2, n2 in snap_d.items():
                for k in range(DMA_K):
                    cnt = len(range(k, n2, DMA_K))
                    if cnt:
                        I.waits.append(("dma", s2, k, 16 * cnt))
        for r in reads:
            self._dep(I, r.w)
        for r in writes:
            self._dep(I, r.w)
            for rd in r.rs:
                self._dep(I, rd)
        for r in reads:
            r.rs.append(I)
        for r in writes:
            r.w = I
            r.rs = []
        lst = self.ins[stream]
        if not is_dma:
            I.seq = self._nseq(stream)
            I.know = dict(self.know[stream])
        lst.append(I)
        if not is_dma:
            self._last[stream] = I
        return I

    def _nseq(self, stream):
        c = getattr(self, "_cnt", None)
        if c is None:
            c = self._cnt = {s: 0 for s in STREAMS}
        v = c[stream]
        c[stream] = v + 1
        return v

    def barrier(self):
        snap_e = {e: self._cnt_get(e) - 1 for e in COMPUTE}
        snap_d = {s: self.ndma[s] for s in STREAMS}
        lastI = {e: self._last.get(e) for e in COMPUTE}
        self._bar = {s: (lastI, dict(snap_d)) for s in STREAMS}

    def _cnt_get(self, e):
        c = getattr(self, "_cnt", None)
        return c[e] if c else 0

    def op(self, eng, fn, reads=(), writes=()):
        return self._add(eng, fn, reads, writes, False)

    def dma(self, stream, out, in_, reads=(), writes=()):
        return self._add(stream, lambda e: e.dma_start(out=out, in_=in_), reads, writes, True)

    def emit(self):
        nc = self.nc
        import contextlib
        with contextlib.ExitStack() as st:
            esem = {e: st.enter_context(nc.semaphore(f"s_{e}")) for e in COMPUTE}
            dsem = {}
            for s in STREAMS:
                if self.ndma[s] > 0:
                    for k in range(DMA_K):
                        dsem[(s, k)] = st.enter_context(nc.semaphore(f"d_{s}{k}"))
            inc_count = {e: 0 for e in COMPUTE}
            seq2cnt = {e: {} for e in COMPUTE}
            for e in COMPUTE:
                c = 0
                for I in self.ins[e]:
                    if I.is_dma:
                        continue
                    if I.need_inc:
                        c += 1
                        seq2cnt[e][I.seq] = c
            block = st.enter_context(nc.Block())

            def run(stream, eng):
                for I in self.ins[stream]:
                    for w in I.waits:
                        if w[0] == "dma":
                            eng.wait_ge(dsem[(w[1], w[2])], w[3])
                        else:
                            eng.wait_ge(esem[w[1]], seq2cnt[w[1]][w[2]])
                    r = I.fn(eng)
                    if I.is_dma:
                        r.then_inc(dsem[(stream, I.dn % DMA_K)], 16)
                    elif I.need_inc:
                        r.then_inc(esem[stream], 1)
                if stream == "sp":
                    for (s2, k), sem in dsem.items():
                        n = len(range(k, self.ndma[s2], DMA_K))
                        if n:
                            eng.wait_ge(sem, 16 * n)

            @block.tensor
            def _(eng):
                run("pe", eng)

            @block.vector
            def _(eng):
                run("dve", eng)

            @block.scalar
            def _(eng):
                run("act", eng)

            @block.gpsimd
            def _(eng):
                run("pool", eng)

            @block.sync
            def _(eng):
                run("sp", eng)


EPS = 1e-6
NCC = 12
NPP = 75
EM05 = float(np.exp(-0.5))
NO_POOL_OPS = bool(int(os.environ.get("NO_POOL_OPS", "1")))
NO_POOL_DMA = bool(int(os.environ.get("NO_POOL_DMA", "1")))


class Tl:
    __slots__ = ("t", "r")

    def __init__(self, t, r):
        self.t = t
        self.r = r

    def __getitem__(self, k):
        return self.t[k]


class Bld:
    def __init__(self):
        self.nc = bass.Bass("TRN2", target_bir_lowering=False)
        self.P = Prog(self.nc)
        self.stacks = [contextlib.ExitStack()]
        self.n = 0

    def push(self):
        self.stacks.append(contextlib.ExitStack())

    def pop(self):
        self.P.barrier()
        self.stacks.pop().close()

    def sb(self, shape, dt=F32, name=None):
        self.n += 1
        t = self.stacks[-1].enter_context(self.nc.sbuf_tensor(f"{name or 'sb'}_{self.n}", list(shape), dt))
        return Tl(t, self.P.res())

    def ps(self, shape, dt=F32, name=None):
        self.n += 1
        t = self.stacks[-1].enter_context(self.nc.psum_tensor(f"{name or 'ps'}_{self.n}", list(shape), dt))
        return Tl(t, self.P.res())

    def dram(self, name, shape, dt=F32, kind="Internal"):
        t = self.nc.dram_tensor(name, list(shape), dt, kind=kind)
        return Tl(t.ap(), self.P.res())

    def op(self, eng, fn, reads=(), writes=()):
        if eng == "pool" and NO_POOL_OPS:
            eng = "dve"
        return self.P.op(eng, fn, [x.r for x in reads], [x.r for x in writes])

    def dma(self, q, out, in_, reads=(), writes=()):
        if q == "pool" and NO_POOL_DMA:
            q = "sp"
        return self.P.dma(q, out, in_, [x.r for x in reads], [x.r for x in writes])

    def dbg(self, name, tl, ap, shape, dt=F32):
        if not getattr(self, "debug", False):
            return
        d = self.dram("dbg_" + name, shape, dt, kind="ExternalOutput")
        self.dma("sp", d[tuple(slice(None) for _ in shape)], ap, reads=[tl], writes=[d])

    def ts(self, eng, out, in0, s1, s2, op0, op1=None, reads=(), writes=()):
        if op1 is None:
            return self.op(eng, lambda e: e.tensor_scalar(out, in0, s1, None, op0), reads, writes)
        return self.op(eng, lambda e: e.tensor_scalar(out, in0, s1, s2, op0, op1), reads, writes)

    def tt(self, eng, out, in0, in1, op, reads=(), writes=()):
        return self.op(eng, lambda e: e.tensor_tensor(out, in0, in1, op), reads, writes)

    def stt(self, out, in0, sc, in1, op0, op1, reads=(), writes=()):
        return self.op("dve", lambda e: e.scalar_tensor_tensor(out, in0, sc, in1, op0, op1), reads, writes)

    def act(self, out, in_, func, bias=0.0, scale=1.0, reads=(), writes=(), accum=None):
        if accum is None:
            return self.op("act", lambda e: e.activation(out=out, in_=in_, func=func, bias=bias, scale=scale), reads, writes)
        return self.op("act", lambda e: e.activation(out=out, in_=in_, func=func, bias=bias, scale=scale, accum_out=accum), reads, writes)

    def mm(self, out, lhsT, rhs, start=True, stop=True, reads=(), writes=()):
        return self.op("pe", lambda e: e.matmul(out, lhsT=lhsT, rhs=rhs, start=start, stop=stop), reads, writes)

    def tr(self, out, in_, ident, reads=(), writes=()):
        return self.op("pe", lambda e: e.transpose(out, in_, ident), reads, writes)


def setup_mod(b, pp, adaw_d, nblk):
    sc = b.sb([128, 16], name="sc")
    b.act(sc[:], pp[:, 59:75], AF.Silu, reads=[pp], writes=[sc])
    modT = b.sb([128, nblk, 2], name="modT")
    b.push()
    aw = b.sb([128, 8, nblk * 128], name="aw")
    for k in range(8):
        b.dma("sp" if k % 2 == 0 else "pool", aw[:, k, :], adaw_d[k * 128:(k + 1) * 128, :], reads=[adaw_d], writes=[aw])
    pm = b.ps([128, nblk, 2], name="pm")
    for blk in range(nblk):
        for k in range(8):
            b.mm(pm[:, blk, :], aw[:, k, blk * 128:(blk + 1) * 128], sc[:, 2 * k:2 * k + 2], start=(k == 0), stop=(k == 7),
                 reads=[aw, sc], writes=[pm])
    for r in range(2):
        b.tt("dve", modT[:, :, r], pm[:, :, r], pp[:, 43:43 + nblk], ALU.add, reads=[pm, pp], writes=[modT])
    b.dbg("modT", modT, modT[:], [128, nblk, 2])
    b.pop()
    return modT


def phase_A(b, xseq, win_d, pp, modT, cm, groups, dests, ncc):
    nc = b.nc
    ident = cm[:, 0:128]
    b.push()
    b.dbg("modT2", modT, modT[:], [128, 16, 2])
    gm = b.sb([128, 8, 2], name="gm")
    b.ts("dve", gm[:], modT[:, 8:16, :], 1.0, None, ALU.add, reads=[modT], writes=[gm])
    for r in range(2):
        b.tt("dve", gm[:, :, r], gm[:, :, r], pp[:, 35:43], ALU.mult, reads=[gm, pp], writes=[gm])
    W = [b.sb([128, 8, ncc * 128], BF16, name=f"W{r}") for r in range(2)]
    sW = b.sb([128, ncc, 2], name="sW")
    b.push()
    stg = b.sb([128, 8, ncc * 128], name="stg")
    for k in range(8):
        b.dma("sp" if k % 2 == 0 else "pool", stg[:, k, :], win_d[k * 128:(k + 1) * 128, :], reads=[win_d], writes=[stg])
    psw = b.ps([128, ncc, 2], name="psw")
    for cc in range(ncc):
        for k in range(8):
            b.mm(psw[:, cc, :], stg[:, k, cc * 128:(cc + 1) * 128], modT[:, k, :], start=(k == 0), stop=(k == 7),
                 reads=[stg, modT], writes=[psw])
    b.op("act", lambda e: e.copy(out=sW[:], in_=psw[:]), reads=[psw], writes=[sW])
    for r in range(2):
        for k in range(8):
            eng = "dve" if k % 2 == 0 else "pool"
            b.ts(eng, W[r][:, k, :], stg[:, k, :], gm[:, k, r:r + 1], None, ALU.mult, reads=[stg, gm], writes=[W[r]])
    b.dbg("gm", gm, gm[:], [128, 8, 2])
    b.dbg("modT3", modT, modT[:], [128, 16, 2])
    b.dbg("sW", sW, sW[:], [128, ncc, 2])
    b.dbg("W0", W[0], W[0][:], [128, 8, ncc * 128], BF16)
    b.pop()
    xt = [b.sb([128, 1024], name=f"xt{i}") for i in range(3)]
    junk = b.sb([128, 1024], BF16, name="junk")
    ss = [b.sb([128, 1], name=f"ss{i}") for i in range(3)]
    rstd = [b.sb([128, 1], name=f"rstd{i}") for i in range(3)]
    xT = [b.sb([128, 8, 512], BF16, name=f"xT{i}") for i in range(2)]
    psT = [b.ps([128, 4, 128], name=f"psT{i}") for i in range(2)]
    pso = [b.ps([128, 512], name=f"pso{i}") for i in range(3)]
    ob = [b.sb([128, 512], name=f"ob{i}") for i in range(4)]
    ti = 0
    oi = 0
    for gi, (t0, ntl, r) in enumerate(groups):
        G = ntl * 128
        xg = xT[gi % 2]
        for j in range(ntl):
            x = xt[ti % 3]
            s_ = ss[ti % 3]
            rs_ = rstd[ti % 3]
            ti += 1
            b.dma("sp", x[:], xseq[t0 + j * 128:t0 + (j + 1) * 128, :], reads=[xseq], writes=[x])
            b.act(junk[:], x[:], AF.Square, reads=[x], writes=[junk, s_], accum=s_[:])
            b.act(s_[:], s_[:], AF.Sqrt, bias=EPS, scale=1.0 / 1024, reads=[s_], writes=[s_])
            b.op("dve", lambda e, o=rs_, i=s_: e.reciprocal(o[:], i[:]), reads=[s_], writes=[rs_])
            b.ts("dve", x[:], x[:], rs_[:, 0:1], None, ALU.mult, reads=[x, rs_], writes=[x])
            if gi == 1 and j == 0:
                b.dbg("rstd", rs_, rs_[:], [128, 1])
                b.dbg("xs", x, x[:], [128, 1024])
            for half in range(2):
                pt = psT[half]
                for q in range(4):
                    k = half * 4 + q
                    b.tr(pt[:, q, :], x[:, k * 128:(k + 1) * 128], ident, reads=[x, cm], writes=[pt])
                if half == 0:
                    b.op("act", lambda e, o=xg, p=pt, j=j: e.copy(out=o[:, 0:4, j * 128:(j + 1) * 128], in_=p[:]), reads=[pt], writes=[xg])
                else:
                    b.op("dve", lambda e, o=xg, p=pt, j=j: e.tensor_copy(out=o[:, 4:8, j * 128:(j + 1) * 128], in_=p[:]), reads=[pt], writes=[xg])
        if gi == 1:
            b.dbg("xT", xg, xg[:], [128, 8, 512], BF16)
        for cc in range(ncc):
            po = pso[oi % 3]
            o = ob[oi % 4]
            for k in range(8):
                b.mm(po[:, 0:G], W[r][:, k, cc * 128:(cc + 1) * 128], xg[:, k, 0:G], start=(k == 0), stop=(k == 7),
                     reads=[W[r], xg], writes=[po])
            if oi % 2 == 0:
                b.act(o[:, 0:G], po[:, 0:G], AF.Identity, bias=sW[:, cc, r:r + 1], reads=[po, sW], writes=[o])
            else:
                b.ts("dve", o[:, 0:G], po[:, 0:G], sW[:, cc, r:r + 1], None, ALU.add, reads=[po, sW], writes=[o])
            dst, roff = dests[cc]
            b.dma("pool" if oi % 2 == 0 else "sp", dst[roff:roff + 128, t0:t0 + G], o[:, 0:G], reads=[o], writes=[dst])
            oi += 1
    b.pop()


class _Stop(Exception):
    pass


def _stage(n):
    if float(os.environ.get("RW_STOP", "99")) <= n:
        raise _Stop()


def rwkv_phase(b, *a, **k):
    try:
        _rwkv_phase(b, *a, **k)
    except _Stop:
        b.pop()


def _rwkv_phase(b, U, pp, w2a2_d, cm, blocks, yA, bon, vg, TT, dbg=None):
    ident = cm[:, 0:128]
    su = cm[:, 128:256]
    sl = cm[:, 256:384]
    ui = cm[:, 384:512]
    bones = cm[:, 512:640]
    hind = cm[:, 640:642]
    b.push()
    w2a2 = b.sb([128, 256], name="w2a2")
    b.dma("sp", w2a2[:], w2a2_d[:, :], reads=[w2a2_d], writes=[w2a2])
    identb = b.sb([128, 128], BF16, name="identb")
    b.op("dve", lambda e: e.tensor_copy(out=identb[:], in_=ident), reads=[cm], writes=[identb])
    hmu = b.sb([128, 4], name="hmu")
    omm = b.sb([128, 4], name="omm")
    omka = b.sb([128, 1], name="omka")
    b.ts("dve", hmu[:], pp[:, 0:4], 0.5, None, ALU.mult, reads=[pp], writes=[hmu])
    b.ts("dve", omm[:], pp[:, 0:4], -1.0, 1.0, ALU.mult, ALU.add, reads=[pp], writes=[omm])
    b.ts("dve", omka[:], pp[:, 7:8], -1.0, 1.0, ALU.mult, ALU.add, reads=[pp], writes=[omka])
    WB = 1024
    m01 = b.sb([128, WB], name="m01")
    b.op("dve", lambda e: e.memset(m01[:], 1.0), writes=[m01])
    b.op("dve", lambda e: e.memset(m01[:].rearrange("p (n l) -> p n l", l=128)[:, :, 0:1], 0.0), writes=[m01])
    M = b.sb([128, 128], name="M")
    b.op("dve", lambda e: e.memset(M[:], 0.0), writes=[M])
    Mt = b.sb([128, 128], name="Mt")

    def fm(name, dt=F32):
        return b.sb([128, WB], dt, name=name)

    ub = [b.sb([128, WB + 2], name=f"ub{g}") for g in range(4)]
    mixed = [fm(f"mx{g}") for g in range(4)]
    tmp = fm("tmp")
    tmp2 = fm("tmp2")
    twl = fm("twl")
    sgw = fm("sgw")
    av = fm("av")
    cum = fm("cum")
    Pe = fm("Pe")
    Qe = fm("Qe")
    Pm = fm("Pm")
    kk = fm("kk")
    kd = fm("kd")
    vmb = fm("vmb")
    KKt = fm("KKt")
    Rt = fm("Rt")
    Kh = fm("Kh")
    Bh = fm("Bh")
    Khz = [fm(f"Khz{h}") for h in range(2)]
    Bhz = [fm(f"Bhz{h}") for h in range(2)]
    KKz = [fm(f"KKz{h}") for h in range(2)]
    wmid = b.sb([128, 8], name="wmid")
    dA = b.sb([128, 8], name="dA")
    bsb = b.sb([2, WB], name="bsb")
    psL = b.ps([128, 512], name="psL0")

    class _V:
        r = psL.r

        def __getitem__(self, k):
            return psL[:, :].rearrange("p (n l) -> p n l", l=128)[k]
    psLv = _V()
    psT = b.ps([128, 4, 128], name="psTr")
    psA = b.ps([128, 4, 128], name="psA")
    psB = b.ps([128, 4, 128], name="psB")
    psI1 = b.ps([128, 4, 128], name="psI1")
    psI2 = b.ps([128, 4, 128], name="psI2")
    psM = b.ps([128, 512], name="psM")
    psUY = b.ps([128, 2, 128], name="psUY")
    tok4 = b.sb([128, 5, 128], name="tok4")
    SA = b.sb([128, 4, 128], name="SA")
    SB_ = b.sb([128, 4, 128], name="SB")
    Xc = [b.sb([128, 2, 128], name=f"Xc{i}") for i in range(2)]
    XTc = [b.sb([128, 2, 128], name=f"XTc{i}") for i in range(2)]
    Gc = [b.sb([128, 2, 128], name=f"Gc{i}") for i in range(2)]
    KT = b.sb([128, 128], name="KT")
    AV = b.sb([128, 128], name="AV")
    X2 = b.sb([128, 128], name="X2")
    Un = b.sb([128, 128], name="Un")
    t1 = b.sb([128, 128], name="t1")
    Ysb = [b.sb([128, 128], name=f"Ysb{i}") for i in range(2)]
    rowoff = [0, 128, 256, 512]
    ci_glob = 0
    for (t0, nch, seg0, seg1) in blocks:
        Wd = nch * 128
        lo = t0 - 1 if t0 > seg0 else t0
        hi = t0 + Wd + 1 if t0 + Wd < seg1 else t0 + Wd
        for g in range(4):
            if lo == t0:
                b.op("dve", lambda e, u=ub[g]: e.memset(u[:, 0:1], 0.0), writes=[ub[g]])
            if hi == t0 + Wd:
                b.op("dve", lambda e, u=ub[g], Wd=Wd: e.memset(u[:, Wd + 1:Wd + 2], 0.0), writes=[ub[g]])
            b.dma("sp", ub[g][:, 1 - (t0 - lo):1 - (t0 - lo) + (hi - lo)],
                  U[rowoff[g]:rowoff[g] + 128, lo:hi], reads=[U], writes=[ub[g]])
            b.tt("dve", tmp[:, 0:Wd], ub[g][:, 0:Wd], ub[g][:, 2:Wd + 2], ALU.add, reads=[ub[g]], writes=[tmp])
            b.ts("dve", tmp2[:, 0:Wd], ub[g][:, 1:Wd + 1], omm[:, g:g + 1], None, ALU.mult, reads=[ub[g], omm], writes=[tmp2])
            b.stt(mixed[g][:, 0:Wd], tmp[:, 0:Wd], hmu[:, g:g + 1], tmp2[:, 0:Wd], ALU.mult, ALU.add, reads=[tmp, tmp2, hmu], writes=[mixed[g]])
        rm, km, vm, lm = mixed
        b.dma("sp", vg[0:128, t0:t0 + Wd], vm[:, 0:Wd], reads=[vm], writes=[vg])
        b.op("act", lambda e, Wd=Wd: e.copy(out=vmb[:, 0:Wd], in_=vm[:, 0:Wd]), reads=[vm], writes=[vmb])
        _stage(1)
        b.act(twl[:, 0:Wd], lm[:, 0:Wd], AF.Tanh, reads=[lm], writes=[twl])
        for pc in range(0, Wd, 512):
            pw = min(512, Wd - pc)
            b.mm(psL[:, 0:pw], w2a2[:, 0:128], twl[:, pc:pc + pw], reads=[w2a2, twl], writes=[psL])
            b.act(sgw[:, pc:pc + pw], psL[:, 0:pw], AF.Sigmoid, bias=pp[:, 4:5], reads=[psL, pp], writes=[sgw])
            b.mm(psL[:, 0:pw], w2a2[:, 128:256], lm[:, pc:pc + pw], reads=[w2a2, lm], writes=[psL])
            b.act(av[:, pc:pc + pw], psL[:, 0:pw], AF.Sigmoid, bias=pp[:, 5:6], reads=[psL, pp], writes=[av])
        b.ts("dve", sgw[:, 0:Wd], sgw[:, 0:Wd], -EM05, None, ALU.mult, reads=[sgw], writes=[sgw])
        b.op("dve", lambda e, Wd=Wd: e.tensor_tensor_scan(cum[:, 0:Wd], m01[:, 0:Wd], sgw[:, 0:Wd], 0.0, ALU.mult, ALU.add),
             reads=[m01, sgw], writes=[cum])
        c3 = cum[:, 0:Wd].rearrange("p (n l) -> p n l", l=128)
        cbar = c3[:, :, 63:64].to_broadcast([128, nch, 128])

        def v3(t, Wd=Wd):
            return t[:, 0:Wd].rearrange("p (n l) -> p n l", l=128)
        b.act(wmid[:, 0:nch], c3[:, :, 63], AF.Exp, reads=[cum], writes=[wmid])
        b.act(dA[:, 0:nch], c3[:, :, 127], AF.Exp, reads=[cum], writes=[dA])
        b.tt("dve", v3(tmp), c3, cbar, ALU.subtract, reads=[cum], writes=[tmp])
        b.tt("dve", tmp2[:, 0:Wd], tmp[:, 0:Wd], sgw[:, 0:Wd], ALU.subtract, reads=[tmp, sgw], writes=[tmp2])
        b.act(Pe[:, 0:Wd], tmp[:, 0:Wd], AF.Exp, reads=[tmp], writes=[Pe])
        b.act(Qe[:, 0:Wd], tmp[:, 0:Wd], AF.Exp, scale=-1.0, reads=[tmp], writes=[Qe])
        b.act(Pm[:, 0:Wd], tmp2[:, 0:Wd], AF.Exp, reads=[tmp2], writes=[Pm])
        _stage(2)
        b.ts("dve", kk[:, 0:Wd], km[:, 0:Wd], pp[:, 6:7], None, ALU.mult, reads=[km, pp], writes=[kk])
        b.tt("dve", tmp[:, 0:Wd], kk[:, 0:Wd], kk[:, 0:Wd], ALU.mult, reads=[kk], writes=[tmp])
        for pc in range(0, Wd, 512):
            pw = min(512, Wd - pc)
            b.mm(psL[:, 0:pw], bones, tmp[:, pc:pc + pw], reads=[cm, tmp], writes=[psL])
            b.act(tmp2[:, pc:pc + pw], psL[:, 0:pw], AF.Sqrt, reads=[psL], writes=[tmp2])
        b.ts("dve", tmp2[:, 0:Wd], tmp2[:, 0:Wd], 1e-12, None, ALU.max, reads=[tmp2], writes=[tmp2])
        b.op("dve", lambda e, Wd=Wd: e.reciprocal(tmp2[:, 0:Wd], tmp2[:, 0:Wd]), reads=[tmp2], writes=[tmp2])
        b.tt("dve", kk[:, 0:Wd], kk[:, 0:Wd], tmp2[:, 0:Wd], ALU.mult, reads=[kk, tmp2], writes=[kk])
        b.ts("dve", tmp[:, 0:Wd], av[:, 0:Wd], pp[:, 7:8], omka[:, 0:1], ALU.mult, ALU.add, reads=[av, pp, omka], writes=[tmp])
        b.tt("dve", kd[:, 0:Wd], km[:, 0:Wd], tmp[:, 0:Wd], ALU.mult, reads=[km, tmp], writes=[kd])
        b.stt(tmp[:, 0:Wd], rm[:, 0:Wd], pp[:, 8:9], kd[:, 0:Wd], ALU.mult, ALU.mult, reads=[rm, kd, pp], writes=[tmp])
        for pc in range(0, Wd, 512):
            pw = min(512, Wd - pc)
            b.mm(psL[0:2, 0:pw], hind, tmp[:, pc:pc + pw], reads=[cm, tmp], writes=[psL])
            b.op("act", lambda e, pc=pc, pw=pw: e.copy(out=bsb[:, pc:pc + pw], in_=psL[0:2, 0:pw]), reads=[psL], writes=[bsb])
        b.dma("sp", bon[:, t0:t0 + Wd], bsb[:, 0:Wd], reads=[bsb], writes=[bon])
        _stage(3)
        b.tt("dve", KKt[:, 0:Wd], kk[:, 0:Wd], Pm[:, 0:Wd], ALU.mult, reads=[kk, Pm], writes=[KKt])
        b.tt("dve", Rt[:, 0:Wd], rm[:, 0:Wd], Pe[:, 0:Wd], ALU.mult, reads=[rm, Pe], writes=[Rt])
        b.tt("dve", Kh[:, 0:Wd], kd[:, 0:Wd], Qe[:, 0:Wd], ALU.mult, reads=[kd, Qe], writes=[Kh])
        b.tt("dve", tmp[:, 0:Wd], kk[:, 0:Wd], av[:, 0:Wd], ALU.mult, reads=[kk, av], writes=[tmp])
        b.tt("dve", Bh[:, 0:Wd], tmp[:, 0:Wd], Qe[:, 0:Wd], ALU.mult, reads=[tmp, Qe], writes=[Bh])
        for h in range(2):
            b.ts("dve", Khz[h][:, 0:Wd], Kh[:, 0:Wd], hind[:, h:h + 1], None, ALU.mult, reads=[Kh, cm], writes=[Khz[h]])
            b.ts("dve", Bhz[h][:, 0:Wd], Bh[:, 0:Wd], hind[:, h:h + 1], None, ALU.mult, reads=[Bh, cm], writes=[Bhz[h]])
            b.ts("dve", KKz[h][:, 0:Wd], KKt[:, 0:Wd], hind[:, h:h + 1], None, ALU.mult, reads=[KKt, cm], writes=[KKz[h]])
        P3 = v3(Pe)
        _stage(4)
        for j in range(nch):
            c0 = j * 128
            cs = slice(c0, c0 + 128)
            for q, src in enumerate((Kh, Bh, vm, KKz[0])):
                b.tr(psT[:, q, :], src[:, cs], ident, reads=[src, cm], writes=[psT])
            b.tr(psM[:, 384:512], KKz[1][:, cs], ident, reads=[KKz[1], cm], writes=[psM])
            b.op("dve", lambda e: e.tensor_copy(out=tok4[:, 0:4, :], in_=psT[:]), reads=[psT], writes=[tok4])
            b.op("dve", lambda e: e.tensor_copy(out=tok4[:, 4, :], in_=psM[:, 384:512]), reads=[psM], writes=[tok4])
            if ci_glob == 0:
                b.dbg("tok4", tok4, tok4[:], [128, 5, 128])
                b.dbg("Kh", Kh, Kh[:, 0:128], [128, 128])
                b.dbg("KKt", KKt, KKt[:, 0:128], [128, 128])
                b.dbg("Bh", Bh, Bh[:, 0:128], [128, 128])
                b.dbg("Rt", Rt, Rt[:, 0:128], [128, 128])
            _stage(5)
            Kh_t, Bh_t, V_t = (tok4[:, q, :] for q in range(3))
            KKz_t = [tok4[:, 3, :], tok4[:, 4, :]]
            for h in range(2):
                b.mm(psA[:, 2 * h, :], Khz[h][:, cs], KKt[:, cs], reads=[Khz[h], KKt], writes=[psA])
                b.mm(psA[:, 2 * h + 1, :], Bhz[h][:, cs], KKt[:, cs], reads=[Bhz[h], KKt], writes=[psA])
                b.mm(psB[:, 2 * h, :], Khz[h][:, cs], Rt[:, cs], reads=[Khz[h], Rt], writes=[psB])
                b.mm(psB[:, 2 * h + 1, :], Bhz[h][:, cs], Rt[:, cs], reads=[Bhz[h], Rt], writes=[psB])
                b.mm(psL[:, 128 * h:128 * h + 128], KKz[h][:, cs], Bh[:, cs], reads=[Bh, KKz[h]], writes=[psL])
            b.tt("dve", SA[:], psA[:], su.unsqueeze(1).to_broadcast([128, 4, 128]), ALU.mult, reads=[psA, cm], writes=[SA])
            b.tt("dve", SB_[:], psB[:], ui.unsqueeze(1).to_broadcast([128, 4, 128]), ALU.mult, reads=[psB, cm], writes=[SB_])
            if ci_glob == 0:
                b.dbg("SA", SA, SA[:], [128, 4, 128])
                b.dbg("SB", SB_, SB_[:], [128, 4, 128])
            _stage(6)
            Xa, XTa, Ga = Xc[0], XTc[0], Gc[0]
            b.tt("dve", XTa[:], psL[:, 0:256].rearrange("p (n l) -> p n l", l=128), sl.unsqueeze(1).to_broadcast([128, 2, 128]), ALU.mult, reads=[psL, cm], writes=[XTa])
            b.op("dve", lambda e, Xa=Xa: e.tensor_copy(out=Xa[:], in_=SA[:, 1:4:2, :]), reads=[SA], writes=[Xa])
            b.tt("dve", Ga[:], ident.unsqueeze(1).to_broadcast([128, 2, 128]), SA[:, 1:4:2, :], ALU.subtract, reads=[SA, cm], writes=[Ga])
            _stage(6.1)
            cur = 0
            for lvl in range(1, 7):
                Xa, XTa, Ga = Xc[cur], XTc[cur], Gc[cur]
                Xn, XTn, Gn = Xc[1 - cur], XTc[1 - cur], Gc[1 - cur]
                pI = psA if lvl % 2 == 1 else psB
                for h in range(2):
                    if lvl < 6:
                        b.mm(pI[:, h, :], XTa[:, h, :], Xa[:, h, :], reads=[XTa, Xa], writes=[pI])
                    b.mm(pI[:, 2 + h, :], Xa[:, h, :], XTa[:, h, :], reads=[XTa, Xa], writes=[pI])
                _stage(6.2)
                if lvl < 6:
                    b.op("dve", lambda e, Xn=Xn, pI=pI: e.tensor_copy(out=Xn[:], in_=pI[:, 0:2, :]), reads=[pI], writes=[Xn])
                b.op("dve", lambda e, XTn=XTn, pI=pI: e.tensor_copy(out=XTn[:], in_=pI[:, 2:4, :]), reads=[pI], writes=[XTn])
                _stage(6.3)
                for h in range(2):
                    b.mm(psI2[:, h, :], XTn[:, h, :], Ga[:, h, :], reads=[XTn, Ga], writes=[psI2])
                b.tt("dve", Gn[:], Ga[:], psI2[:, 0:2, :], ALU.add, reads=[Ga, psI2], writes=[Gn])
                cur = 1 - cur
                _stage(6.4 + 0.01 * lvl)
            G = Gc[cur]
            if ci_glob == 0:
                b.dbg("G", G, G[:], [128, 2, 128])
            _stage(7)
            for h in range(2):
                hs = slice(64 * h, 64 * h + 64)
                b.mm(psI2[:, 2, :], KKz_t[h], G[:, h, :], start=(h == 0), stop=(h == 1), reads=[tok4, G], writes=[psI2])
                b.mm(psM[:, hs], SA[:, 2 * h, :], V_t[:, hs], reads=[SA, tok4], writes=[psM])
            b.op("dve", lambda e: e.tensor_copy(out=KT[:], in_=psI2[:, 2, :]), reads=[psI2], writes=[KT])
            b.op("dve", lambda e: e.tensor_copy(out=AV[:], in_=psM[:, 0:128]), reads=[psM], writes=[AV])
            for h in range(2):
                hs = slice(64 * h, 64 * h + 64)
                b.mm(psM[:, 128 + 64 * h:128 + 64 * h + 64], G[:, h, :], AV[:, hs], reads=[G, AV], writes=[psM])
            b.op("dve", lambda e: e.tensor_copy(out=X2[:], in_=psM[:, 128:256]), reads=[psM], writes=[X2])
            _stage(8)
            b.ts("dve", Mt[:], M[:], wmid[:, j:j + 1], None, ALU.mult, reads=[M, wmid], writes=[Mt])
            b.mm(psUY[:, 0, :], KT[:], Mt[:], reads=[KT, Mt], writes=[psUY])
            b.stt(Un[:], psUY[:, 0, :], -1.0, X2[:], ALU.mult, ALU.subtract, reads=[psUY, X2], writes=[Un])
            b.mm(psM[:, 256:384], Kh_t, V_t, start=True, stop=False, reads=[tok4], writes=[psM])
            b.mm(psM[:, 256:384], Bh_t, Un[:], start=False, stop=True, reads=[tok4, Un], writes=[psM])
            b.stt(t1[:], psM[:, 256:384], P3[:, j, 127:128], bones, ALU.mult, ALU.mult, reads=[psM, Pe, cm], writes=[t1])
            b.mm(psUY[:, 1, :], Rt[:, cs], Mt[:], start=True, stop=False, reads=[Rt, Mt], writes=[psUY])
            for h in range(2):
                hs = slice(64 * h, 64 * h + 64)
                b.mm(psUY[:, 1, hs], SB_[:, 2 * h, :], V_t[:, hs], start=False, stop=False, reads=[SB_, tok4], writes=[psUY])
                b.mm(psUY[:, 1, hs], SB_[:, 2 * h + 1, :], Un[:, hs], start=False, stop=(h == 1), reads=[SB_, Un], writes=[psUY])
            b.stt(M[:], M[:], dA[:, j:j + 1], t1[:], ALU.mult, ALU.add, reads=[M, dA, t1], writes=[M])
            Y = Ysb[ci_glob % 2]
            b.op("dve", lambda e, Y=Y: e.tensor_copy(out=Y[:], in_=psUY[:, 1, :]), reads=[psUY], writes=[Y])
            b.dma("sp", yA[t0 + c0:t0 + c0 + 128, :], Y[:], reads=[Y], writes=[yA])
            ci_glob += 1
    b.pop()


def mamba_phase(b, U, pp, cm, sel_d, blocks, yB, xsB, TT):
    ident = cm[:, 0:128]
    ui = cm[:, 384:512]
    b.push()
    sel = b.sb([128, 4, 128], name="sel")
    b.dma("sp", sel[:], sel_d[:, :].rearrange("p (h l) -> p h l", l=128), reads=[sel_d], writes=[sel])
    WB = 1024
    m01 = b.sb([128, WB], name="m01")
    b.op("dve", lambda e: e.memset(m01[:], 1.0), writes=[m01])
    b.op("dve", lambda e: e.memset(m01[:].rearrange("p (n l) -> p n l", l=128)[:, :, 0:1], 0.0), writes=[m01])
    Aneg = b.sb([4, 1], name="Aneg")
    b.act(Aneg[:], pp[0:4, 34:35], AF.Exp, reads=[pp], writes=[Aneg])
    b.ts("dve", Aneg[:], Aneg[:], -1.0, None, ALU.mult, reads=[Aneg], writes=[Aneg])
    ST = b.sb([128, 256], name="ST")
    b.op("dve", lambda e: e.memset(ST[:], 0.0), writes=[ST])
    ub = [b.sb([128, WB + 4], name=f"mub{q}") for q in range(4)]
    cv = [b.sb([128, WB], name=f"cv{q}") for q in range(4)]
    acc = b.sb([128, WB], name="acc")
    dtb = b.sb([128, WB], name="dtb")
    dta = b.sb([128, WB], name="dta")
    acs = b.sb([128, WB], name="acs")
    for t_ in (dtb, dta, acs):
        b.op("dve", lambda e, t_=t_: e.memset(t_[:], 0.0), writes=[t_])
    psT2 = b.ps([128, 2, 128], name="mpsT2")
    psT = b.ps([128, 4, 128], name="mpsT")
    psBC = b.ps([128, 4, 128], name="mpsBC")
    psCB = b.ps([128, 128], name="mpsCB")
    psY = b.ps([128, 256], name="mpsY")
    psO = b.ps([128, 256], name="mpsO")
    psS = b.ps([128, 256], name="mpsS")
    tok = b.sb([128, 3, 128], name="mtok")
    sm = b.sb([128, 8], name="msm")
    last = b.sb([128, 4], name="mlast")
    dd = b.sb([128, 4], name="mdd")
    te = b.sb([128, 4], name="mte")
    dec = b.sb([128, 4], name="mdec")
    eA = b.sb([128, 4], name="meA")
    seg = b.sb([128, 4, 128], name="mseg")
    Ee = b.sb([128, 4, 128], name="mE")
    CBm = b.sb([128, 128], name="mCBm")
    Wm = b.sb([128, 4, 128], name="mWm")
    xdt = b.sb([128, 256], name="mxdt")
    xw = b.sb([128, 256], name="mxw")
    ysb = b.sb([128, 256], name="mysb")
    yo = [b.sb([128, 256], name=f"myo{i}") for i in range(2)]
    xo = [b.sb([128, 256], name=f"mxo{i}") for i in range(2)]
    rowoff = [640, 768, 896, 1024]
    ci = 0
    for (t0, nch, seg0, seg1) in blocks:
        Wd = nch * 128
        lo = max(t0 - 2, seg0)
        hi = min(t0 + Wd + 2, seg1)
        for q in range(4):
            u = ub[q]
            if lo > t0 - 2:
                b.op("dve", lambda e, u=u: e.memset(u[:, 0:2], 0.0), writes=[u])
            if hi < t0 + Wd + 2:
                b.op("dve", lambda e, u=u, Wd=Wd: e.memset(u[:, Wd + 2:Wd + 4], 0.0), writes=[u])
            c_lo = lo - (t0 - 2)
            b.dma("sp", u[:, c_lo:c_lo + (hi - lo)], U[rowoff[q]:rowoff[q] + 128, lo:hi], reads=[U], writes=[u])
            wc = 9 + 5 * q
            b.ts("dve", acc[:, 0:Wd], u[:, 0:Wd], pp[:, wc:wc + 1], None, ALU.mult, reads=[u, pp], writes=[acc])
            for k in range(1, 5):
                b.stt(acc[:, 0:Wd], u[:, k:k + Wd], pp[:, wc + k:wc + k + 1], acc[:, 0:Wd], ALU.mult, ALU.add,
                      reads=[u, pp, acc], writes=[acc])
            b.act(cv[q][:, 0:Wd], acc[:, 0:Wd], AF.Silu, bias=pp[:, 29 + q:30 + q], reads=[acc, pp], writes=[cv[q]])
        b.dma("sp", dtb[0:4, 0:Wd], U[1408:1412, t0:t0 + Wd], reads=[U], writes=[dtb])
        b.act(dtb[0:4, 0:Wd], dtb[0:4, 0:Wd], AF.Exp, bias=pp[0:4, 33:34], reads=[dtb, pp], writes=[dtb])
        b.act(dtb[0:4, 0:Wd], dtb[0:4, 0:Wd], AF.Ln, bias=1.0, reads=[dtb], writes=[dtb])
        b.ts("dve", dta[0:4, 0:Wd], dtb[0:4, 0:Wd], Aneg[:, 0:1], None, ALU.mult, reads=[dtb, Aneg], writes=[dta])
        b.op("dve", lambda e, Wd=Wd: e.tensor_tensor_scan(acs[0:4, 0:Wd], m01[0:4, 0:Wd], dta[0:4, 0:Wd], 0.0, ALU.mult, ALU.add),
             reads=[m01, dta], writes=[acs])
        xs0, xs1, Bc, Cc = cv
        for j in range(nch):
            c0 = j * 128
            cs = slice(c0, c0 + 128)
            b.tr(psT[:, 0, :], xs0[:, cs], ident, reads=[xs0, cm], writes=[psT])
            b.tr(psT[:, 1, :], xs1[:, cs], ident, reads=[xs1, cm], writes=[psT])
            b.tr(psT[:, 2, :], Bc[:, cs], ident, reads=[Bc, cm], writes=[psT])
            b.tr(psT2[:, 0, :], acs[:, cs], ident, reads=[acs, cm], writes=[psT2])
            b.tr(psT2[:, 1, :], dtb[:, cs], ident, reads=[dtb, cm], writes=[psT2])
            b.op("dve", lambda e: e.tensor_copy(out=tok[:], in_=psT[:, 0:3, :]), reads=[psT], writes=[tok])
            b.op("dve", lambda e: e.tensor_copy(out=sm[:].rearrange("p (a q) -> p a q", q=4), in_=psT2[:, :, 0:4]), reads=[psT2], writes=[sm])
            x_tok = tok[:, 0:2, :]
            B_tok = tok[:, 2, :]
            for h in range(4):
                b.mm(psBC[:, h, :], sel[:, h, :], acs[:, cs], reads=[sel, acs], writes=[psBC])
            for h in range(4):
                b.ts("dve", seg[:, h, :], psBC[:, h, :], sm[:, h:h + 1], 0.0, ALU.subtract, ALU.min, reads=[psBC, sm], writes=[seg])
            b.op("dve", lambda e: e.tensor_copy(out=last[:], in_=psBC[:, :, 127]), reads=[psBC], writes=[last])
            b.act(Ee[:], seg[:], AF.Exp, reads=[seg], writes=[Ee])
            b.mm(psCB[:], Bc[:, cs], Cc[:, cs], reads=[Bc, Cc], writes=[psCB])
            b.tt("dve", CBm[:], psCB[:], ui, ALU.mult, reads=[psCB, cm], writes=[CBm])
            b.tt("dve", Wm[:], Ee[:], CBm[:].unsqueeze(1).to_broadcast([128, 4, 128]), ALU.mult, reads=[Ee, CBm], writes=[Wm])
            xt3 = tok[:, 0:2, :].rearrange("p a (h2 q) -> p (a h2) q", q=64)
            b.tt("dve", xdt[:].rearrange("p (h q) -> p h q", q=64), xt3, sm[:, 4:8].unsqueeze(2).to_broadcast([128, 4, 64]),
                 ALU.mult, reads=[tok, sm], writes=[xdt])
            for h in range(4):
                b.mm(psY[:, 64 * h:64 * h + 64], Wm[:, h, :], xdt[:, 64 * h:64 * h + 64], reads=[Wm, xdt], writes=[psY])
            b.mm(psO[:], Cc[:, cs], ST[:], reads=[Cc, ST], writes=[psO])
            b.act(eA[:], sm[:, 0:4], AF.Exp, reads=[sm], writes=[eA])
            b.op("dve", lambda e: e.tensor_copy(out=ysb[:], in_=psY[:]), reads=[psY], writes=[ysb])
            y = yo[ci % 2]
            b.tt("dve", y[:].rearrange("p (h q) -> p h q", q=64), psO[:].rearrange("p (h q) -> p h q", q=64),
                 eA[:].unsqueeze(2).to_broadcast([128, 4, 64]), ALU.mult, reads=[psO, eA], writes=[y])
            b.tt("dve", y[:], y[:], ysb[:], ALU.add, reads=[y, ysb], writes=[y])
            b.dma("sp", yB[t0 + c0:t0 + c0 + 128, :], y[:], reads=[y], writes=[yB])
            xout = xo[ci % 2]
            b.op("dve", lambda e, xout=xout: e.tensor_copy(out=xout[:].rearrange("p (a l) -> p a l", l=128), in_=tok[:, 0:2, :]),
                 reads=[tok], writes=[xout])
            b.dma("sp", xsB[t0 + c0:t0 + c0 + 128, :], xout[:], reads=[xout], writes=[xsB])
            b.tt("dve", dd[:], last[:], sm[:, 0:4], ALU.subtract, reads=[last, sm], writes=[dd])
            b.act(te[:], dd[:], AF.Exp, reads=[dd], writes=[te])
            b.act(dec[:], last[:], AF.Exp, reads=[last], writes=[dec])
            b.tt("dve", xw[:].rearrange("p (h q) -> p h q", q=64), xdt[:].rearrange("p (h q) -> p h q", q=64),
                 te[:].unsqueeze(2).to_broadcast([128, 4, 64]), ALU.mult, reads=[xdt, te], writes=[xw])
            b.mm(psS[:], B_tok, xw[:], reads=[tok, xw], writes=[psS])
            b.tt("dve", ST[:].rearrange("p (h q) -> p h q", q=64), ST[:].rearrange("p (h q) -> p h q", q=64),
                 dec[:].unsqueeze(2).to_broadcast([128, 4, 64]), ALU.mult, reads=[ST, dec], writes=[ST])
            b.tt("dve", ST[:], ST[:], psS[:], ALU.add, reads=[ST, psS], writes=[ST])
            ci += 1
    b.pop()


GN_EPS = 64e-5


def out_even(b, tiles, d, cm, final=False):
    ident = cm[:, 0:128]
    b.push()
    vb = b.sb([128, 5120], name="vb")
    for i in range(5):
        b.dma("sp", vb[:, i * 1024:(i + 1) * 1024], d["vecs"][0, i * 1024:(i + 1) * 1024].partition_broadcast(128),
              reads=[d["vecs"]], writes=[vb])
    lnw, lnb = vb[:, 0:512], vb[:, 512:1024]
    dvec, nw, adab = vb[:, 1024:2048], vb[:, 2048:3072], vb[:, 4096:5120]
    cvt = b.sb([128, 16], name="cvt")
    b.dma("sp", cvt[:], d["cv"][:, :], reads=[d["cv"]], writes=[cvt])
    sc = b.sb([128, 16], name="osc")
    b.act(sc[:], cvt[:], AF.Silu, reads=[cvt], writes=[sc])
    gate_b = [b.sb([128, 1024], name=f"gate{r}") for r in range(2)]
    Wg = [b.sb([128, 12, 1024], BF16, name=f"Wg{r}") for r in range(2)]
    b.push()
    aw = b.sb([128, 8, 1024], name="oaw")
    for k in range(8):
        b.dma("sp", aw[:, k, :], d["adawg"][k * 128:(k + 1) * 128, :], reads=[d["adawg"]], writes=[aw])
    scb = b.sb([128, 8, 128], name="scb")
    pg = b.ps([128, 512], name="pg")
    for r in range(2):
        b.op("dve", lambda e, r=r: e.tensor_copy(out=scb[:], in_=sc[:, r:16:2].unsqueeze(2).to_broadcast([128, 8, 128])),
             reads=[sc], writes=[scb])
        for half in range(2):
            hsl = slice(half * 512, half * 512 + 512)
            for k in range(8):
                b.mm(pg[:], scb[:, k, :], aw[:, k, hsl], start=(k == 0), stop=(k == 7), reads=[scb, aw], writes=[pg])
            b.tt("dve", gate_b[r][:, hsl], pg[:], adab[:, hsl], ALU.add, reads=[pg, vb], writes=[gate_b[r]])
    b.pop()
    b.push()
    stg = b.sb([128, 12, 1024], name="ostg")
    for k in range(12):
        b.dma("sp", stg[:, k, :], d["wout"][k * 128:(k + 1) * 128, :], reads=[d["wout"]], writes=[stg])
    for r in range(2):
        for k in range(12):
            b.tt("dve", Wg[r][:, k, :], stg[:, k, :], gate_b[r][:], ALU.mult, reads=[stg, gate_b[r]], writes=[Wg[r]])
    b.pop()
    yA = b.sb([128, 1024], name="oyA")
    bon = b.sb([128, 16], name="obon")
    v = b.sb([128, 512], name="ov")
    ga = b.sb([128, 512], name="oga")
    yB = b.sb([128, 2048], name="oyB")
    xsm = b.sb([128, 1024], name="oxsm")
    zz = b.sb([128, 1024], name="oz")
    xres = b.sb([128, 1024], name="oxres")
    ycat = b.sb([128, 1536], name="ycat")
    t5 = b.sb([128, 512], name="ot5")
    t10 = b.sb([128, 1024], name="ot10")
    st = b.sb([128, 64], name="ost")
    yT = b.sb([128, 12, 128], BF16, name="oyT")
    xo = b.sb([128, 1024], name="oxo")
    psT = [b.ps([128, 4, 128], name=f"opsT{i}") for i in range(3)]
    pso = [b.ps([128, 512], name=f"opso{i}") for i in range(2)]
    for (r0, r) in tiles:
        rs = slice(r0, r0 + 128)
        for tl, nm in ((yA, "yA"), (bon, "bon"), (v, "v"), (ga, "ga"), (yB, "yB"), (xsm, "xsm"), (zz, "z"), (xres, "xres")):
            b.dma("sp", tl[:], d[nm][rs, :], reads=[d[nm]], writes=[tl])
        y = ycat[:, 0:512]
        y3 = y.rearrange("p (h q) -> p h q", q=64)
        b.tt("dve", y, yA[:, 0:512], yA[:, 512:1024], ALU.add, reads=[yA], writes=[ycat])
        b.op("dve", lambda e: e.tensor_reduce(out=st[:, 0:8], in_=y3, axis=AX.X, op=ALU.add), reads=[ycat], writes=[st])
        b.tt("dve", t5[:], y, y, ALU.mult, reads=[ycat], writes=[t5])
        b.op("dve", lambda e: e.tensor_reduce(out=st[:, 8:16], in_=t5[:].rearrange("p (h q) -> p h q", q=64), axis=AX.X, op=ALU.add),
             reads=[t5], writes=[st])
        b.ts("dve", st[:, 0:8], st[:, 0:8], 1.0 / 64, None, ALU.mult, reads=[st], writes=[st])
        b.tt("dve", st[:, 16:24], st[:, 0:8], st[:, 0:8], ALU.mult, reads=[st], writes=[st])
        b.stt(st[:, 8:16], st[:, 8:16], 1.0 / 64, st[:, 16:24], ALU.mult, ALU.subtract, reads=[st], writes=[st])
        b.act(st[:, 8:16], st[:, 8:16], AF.Sqrt, bias=GN_EPS, reads=[st], writes=[st])
        b.op("dve", lambda e: e.reciprocal(st[:, 8:16], st[:, 8:16]), reads=[st], writes=[st])
        b.tt("dve", y3, y3, st[:, 0:8].unsqueeze(2).to_broadcast([128, 8, 64]), ALU.subtract, reads=[ycat, st], writes=[ycat])
        b.tt("dve", y3, y3, st[:, 8:16].unsqueeze(2).to_broadcast([128, 8, 64]), ALU.mult, reads=[ycat, st], writes=[ycat])
        b.tt("dve", y, y, lnw, ALU.mult, reads=[ycat, vb], writes=[ycat])
        b.tt("dve", y, y, lnb, ALU.add, reads=[ycat, vb], writes=[ycat])
        b.tt("dve", st[:, 24:32], bon[:, 0:8], bon[:, 8:16], ALU.add, reads=[bon], writes=[st])
        b.tt("dve", t5[:].rearrange("p (h q) -> p h q", q=64), v[:].rearrange("p (h q) -> p h q", q=64),
             st[:, 24:32].unsqueeze(2).to_broadcast([128, 8, 64]), ALU.mult, reads=[v, st], writes=[t5])
        b.tt("dve", y, y, t5[:], ALU.add, reads=[ycat, t5], writes=[ycat])
        b.act(ga[:], ga[:], AF.Silu, reads=[ga], writes=[ga])
        b.tt("dve", y, y, ga[:], ALU.mult, reads=[ycat, ga], writes=[ycat])
        yb = ycat[:, 512:1536]
        b.tt("dve", yb, yB[:, 0:1024], yB[:, 1024:2048], ALU.add, reads=[yB], writes=[ycat])
        b.tt("dve", t10[:], xsm[:], dvec, ALU.mult, reads=[xsm, vb], writes=[t10])
        b.tt("dve", yb, yb, t10[:], ALU.add, reads=[ycat, t10], writes=[ycat])
        b.act(zz[:], zz[:], AF.Silu, reads=[zz], writes=[zz])
        b.tt("dve", yb, yb, zz[:], ALU.mult, reads=[ycat, zz], writes=[ycat])
        b.tt("dve", t10[:], yb, yb, ALU.mult, reads=[ycat], writes=[t10])
        b.op("dve", lambda e: e.tensor_reduce(out=st[:, 32:34], in_=t10[:].rearrange("p (g q) -> p g q", q=512), axis=AX.X, op=ALU.add),
             reads=[t10], writes=[st])
        b.act(st[:, 32:34], st[:, 32:34], AF.Sqrt, bias=EPS, scale=1.0 / 512, reads=[st], writes=[st])
        b.op("dve", lambda e: e.reciprocal(st[:, 32:34], st[:, 32:34]), reads=[st], writes=[st])
        b.tt("dve", yb.rearrange("p (g q) -> p g q", q=512), yb.rearrange("p (g q) -> p g q", q=512),
             st[:, 32:34].unsqueeze(2).to_broadcast([128, 2, 512]), ALU.mult, reads=[ycat, st], writes=[ycat])
        b.tt("dve", yb, yb, nw, ALU.mult, reads=[ycat, vb], writes=[ycat])
        for k in range(12):
            pt = psT[k // 4]
            b.tr(pt[:, k % 4, :], ycat[:, k * 128:(k + 1) * 128], ident, reads=[ycat, cm], writes=[pt])
        for i3 in range(3):
            b.op("dve", lambda e, i3=i3: e.tensor_copy(out=yT[:, 4 * i3:4 * i3 + 4, :], in_=psT[i3][:]), reads=[psT[i3]], writes=[yT])
        for half in range(2):
            hsl = slice(half * 512, half * 512 + 512)
            po = pso[half]
            for k in range(12):
                b.mm(po[:], yT[:, k, :], Wg[r][:, k, hsl], start=(k == 0), stop=(k == 11), reads=[yT, Wg[r]], writes=[po])
            b.tt("dve", xo[:, hsl], po[:], xres[:, hsl], ALU.add, reads=[po, xres], writes=[xo])
        if final:
            b.act(t10[:], xo[:], AF.Square, reads=[xo], writes=[t10, st], accum=st[:, 40:41])
            b.act(st[:, 40:41], st[:, 40:41], AF.Sqrt, bias=EPS, scale=1.0 / 1024, reads=[st], writes=[st])
            b.op("dve", lambda e: e.reciprocal(st[:, 40:41], st[:, 40:41]), reads=[st], writes=[st])
            b.stt(xo[:], xo[:], st[:, 40:41], vb[:, 3072:4096], ALU.mult, ALU.mult, reads=[xo, st, vb], writes=[xo])
        b.dma("sp", d["xo"][rs, :], xo[:], reads=[xo], writes=[d["xo"]])
    b.pop()


C_ID, C_UI, C_BONES, C_SEL32, C_RQ, C_RK, C_E0, C_E1, C_SU, C_E64 = 0, 128, 256, 384, 512, 640, 768, 896, 1024, 1152


def mlstm_phase(b, U, pp, cmo, blocks, hC, TT):
    ident = cmo[:, C_ID:C_ID + 128]
    ui = cmo[:, C_UI:C_UI + 128]
    sel32 = cmo[:, C_SEL32:C_SEL32 + 128]
    b.push()
    WB = 1024
    m01 = b.sb([128, WB], name="lm01")
    b.op("dve", lambda e: e.memset(m01[:], 1.0), writes=[m01])
    b.op("dve", lambda e: e.memset(m01[:].rearrange("p (n l) -> p n l", l=128)[:, :, 0:1], 0.0), writes=[m01])
    CT1 = b.sb([128, 2, 257], name="CT1")
    b.op("dve", lambda e: e.memset(CT1[:], 0.0), writes=[CT1])
    ub = [b.sb([128, WB + 4], name=f"lub{q}") for q in range(4)]
    cv = [b.sb([128, WB], name=f"lcv{q}") for q in range(4)]
    vv = [b.sb([128, WB], name=f"lvv{q}") for q in range(2)]
    acc = b.sb([128, WB], name="lacc")
    gt = b.sb([128, WB], name="lgt")
    b.op("dve", lambda e: e.memset(gt[:], 0.0), writes=[gt])
    gt2 = b.sb([128, WB], name="lgt2")
    b.op("dve", lambda e: e.memset(gt2[:], 0.0), writes=[gt2])
    nfb = b.sb([128, 1], name="nfb")
    b.ts("dve", nfb[:], pp[:, 24:25], -1.0, None, ALU.mult, reads=[pp], writes=[nfb])
    psT = b.ps([128, 4, 128], name="lpsT")
    psG = b.ps([128, 3, 128], name="lpsG")
    psS = b.ps([128, 128], name="lpsS")
    psN = b.ps([128, 257], name="lpsN")
    psI = b.ps([128, 257], name="lpsI")
    psC = [b.ps([128, 257], name=f"lpsC{a}") for a in range(2)]
    ktok = b.sb([128, 256], name="lktok")
    v1 = b.sb([128, 257], name="lv1")
    b.op("dve", lambda e: e.memset(v1[:, 256:257], 1.0), writes=[v1])
    gtok = b.sb([128, 128], name="lgtok")
    sm = b.sb([128, 8], name="lsm")
    lw = b.sb([128, 128], name="llw")
    Ee = b.sb([128, 128], name="lE")
    WT = b.sb([128, 128], name="lWT")
    nsb = b.sb([128, 257], name="lnsb")
    tot = b.sb([128, 257], name="ltot")
    kw = b.sb([128, 256], name="lkw")
    ho = [b.sb([128, 256], name=f"lho{i}") for i in range(2)]
    rowoff = [0, 128, 256, 384]
    ci = 0
    for (t0, nch, seg0, seg1) in blocks:
        Wd = nch * 128
        lo = max(t0 - 2, seg0)
        hi = min(t0 + Wd + 2, seg1)
        for q in range(4):
            u = ub[q]
            if lo > t0 - 2:
                b.op("dve", lambda e, u=u: e.memset(u[:, 0:2], 0.0), writes=[u])
            if hi < t0 + Wd + 2:
                b.op("dve", lambda e, u=u, Wd=Wd: e.memset(u[:, Wd + 2:Wd + 4], 0.0), writes=[u])
            c_lo = lo - (t0 - 2)
            b.dma("sp", u[:, c_lo:c_lo + (hi - lo)], U[rowoff[q]:rowoff[q] + 128, lo:hi], reads=[U], writes=[u])
            wc = 5 * q
            b.ts("dve", acc[:, 0:Wd], u[:, 0:Wd], pp[:, wc:wc + 1], None, ALU.mult, reads=[u, pp], writes=[acc])
            for k in range(1, 5):
                b.stt(acc[:, 0:Wd], u[:, k:k + Wd], pp[:, wc + k:wc + k + 1], acc[:, 0:Wd], ALU.mult, ALU.add,
                      reads=[u, pp, acc], writes=[acc])
            b.act(cv[q][:, 0:Wd], acc[:, 0:Wd], AF.Silu, bias=pp[:, 20 + q:21 + q], reads=[acc, pp], writes=[cv[q]])
            if q < 2:
                b.ts("dve", cv[q][:, 0:Wd], cv[q][:, 0:Wd], 0.0625, None, ALU.mult, reads=[cv[q]], writes=[cv[q]])
        for a in range(2):
            b.dma("sp", vv[a][:, 0:Wd], U[512 + 128 * a:640 + 128 * a, t0:t0 + Wd], reads=[U], writes=[vv[a]])
        b.dma("sp", gt[0:1, 0:Wd], U[1408:1409, t0:t0 + Wd], reads=[U], writes=[gt])
        b.dma("sp", gt[32:33, 0:Wd], U[1440:1441, t0:t0 + Wd], reads=[U], writes=[gt])
        b.ts("dve", gt[0:1, 0:Wd], gt[0:1, 0:Wd], pp[0:1, 24:25], None, ALU.add, reads=[gt, pp], writes=[gt])
        b.act(gt[32:33, 0:Wd], gt[32:33, 0:Wd], AF.Exp, bias=nfb[32:33, 0:1], scale=-1.0, reads=[gt, nfb], writes=[gt])
        b.act(gt[32:33, 0:Wd], gt[32:33, 0:Wd], AF.Ln, bias=1.0, reads=[gt], writes=[gt])
        b.ts("dve", gt[32:33, 0:Wd], gt[32:33, 0:Wd], -1.0, None, ALU.mult, reads=[gt], writes=[gt])
        b.op("dve", lambda e, Wd=Wd: e.tensor_tensor_scan(gt2[32:33, 0:Wd], m01[32:33, 0:Wd], gt[32:33, 0:Wd], 0.0, ALU.mult, ALU.add),
             reads=[m01, gt], writes=[gt2])
        q0, q1, k0, k1 = cv
        qc = (q0, q1)
        kc = (k0, k1)
        for j in range(nch):
            c0 = j * 128
            cs = slice(c0, c0 + 128)
            b.tr(psT[:, 0, :], k0[:, cs], ident, reads=[k0, cmo], writes=[psT])
            b.tr(psT[:, 1, :], k1[:, cs], ident, reads=[k1, cmo], writes=[psT])
            b.tr(psT[:, 2, :], vv[0][:, cs], ident, reads=[vv[0], cmo], writes=[psT])
            b.tr(psT[:, 3, :], vv[1][:, cs], ident, reads=[vv[1], cmo], writes=[psT])
            b.tr(psG[:, 0, :], gt[:, cs], ident, reads=[gt, cmo], writes=[psG])
            b.tr(psG[:, 2, :], gt2[:, cs], ident, reads=[gt2, cmo], writes=[psG])
            b.mm(psG[:, 1, :], sel32, gt2[:, cs], reads=[cmo, gt2], writes=[psG])
            b.op("dve", lambda e: e.tensor_copy(out=ktok[:].rearrange("p (a l) -> p a l", l=128), in_=psT[:, 0:2, :]), reads=[psT], writes=[ktok])
            b.op("dve", lambda e: e.tensor_copy(out=v1[:, 0:256].rearrange("p (a l) -> p a l", l=128), in_=psT[:, 2:4, :]), reads=[psT], writes=[v1])
            b.op("dve", lambda e: e.tensor_copy(out=gtok[:, 0:1], in_=psG[:, 0, 0:1]), reads=[psG], writes=[gtok])
            b.op("dve", lambda e: e.tensor_copy(out=gtok[:, 32:33], in_=psG[:, 2, 32:33]), reads=[psG], writes=[gtok])
            b.tt("dve", sm[:, 0:1], gtok[:, 32:33], gtok[:, 0:1], ALU.subtract, reads=[gtok], writes=[sm])
            b.act(sm[:, 1:2], gtok[:, 32:33], AF.Exp, reads=[gtok], writes=[sm])
            b.op("dve", lambda e: e.tensor_copy(out=sm[:, 2:3], in_=psG[:, 1, 127:128]), reads=[psG], writes=[sm])
            b.ts("dve", lw[:], psG[:, 1, :], sm[:, 0:1], None, ALU.subtract, reads=[psG, sm], writes=[lw])
            b.act(Ee[:], lw[:], AF.Exp, reads=[lw], writes=[Ee])
            b.tt("dve", sm[:, 3:4], sm[:, 2:3], sm[:, 0:1], ALU.subtract, reads=[sm], writes=[sm])
            b.act(sm[:, 3:4], sm[:, 3:4], AF.Exp, reads=[sm], writes=[sm])
            b.act(sm[:, 4:5], sm[:, 2:3], AF.Exp, reads=[sm], writes=[sm])
            for a in range(2):
                b.mm(psS[:], kc[a][:, cs], qc[a][:, cs], start=(a == 0), stop=(a == 1), reads=[kc[a], qc[a]], writes=[psS])
            b.tt("dve", WT[:], psS[:], ui, ALU.mult, reads=[psS, cmo], writes=[WT])
            b.tt("dve", WT[:], WT[:], Ee[:], ALU.mult, reads=[WT, Ee], writes=[WT])
            b.mm(psN[:], WT[:], v1[:], reads=[WT, v1], writes=[psN])
            for a in range(2):
                b.mm(psI[:], qc[a][:, cs], CT1[:, a, :], start=(a == 0), stop=(a == 1), reads=[qc[a], CT1], writes=[psI])
            b.op("dve", lambda e: e.tensor_copy(out=nsb[:], in_=psN[:]), reads=[psN], writes=[nsb])
            b.stt(tot[:], psI[:], sm[:, 1:2], nsb[:], ALU.mult, ALU.add, reads=[psI, sm, nsb], writes=[tot])
            b.act(sm[:, 5:6], tot[:, 256:257], AF.Abs, reads=[tot], writes=[sm])
            b.ts("dve", sm[:, 5:6], sm[:, 5:6], 1.0, None, ALU.max, reads=[sm], writes=[sm])
            b.op("dve", lambda e: e.reciprocal(sm[:, 6:7], sm[:, 5:6]), reads=[sm], writes=[sm])
            h = ho[ci % 2]
            b.ts("dve", h[:], tot[:, 0:256], sm[:, 6:7], None, ALU.mult, reads=[tot, sm], writes=[h])
            b.dma("sp", hC[t0 + c0:t0 + c0 + 128, :], h[:], reads=[h], writes=[hC])
            b.ts("dve", kw[:], ktok[:], sm[:, 3:4], None, ALU.mult, reads=[ktok, sm], writes=[kw])
            for a in range(2):
                b.mm(psC[a][:], kw[:, 128 * a:128 * a + 128], v1[:], reads=[kw, v1], writes=[psC[a]])
                b.stt(CT1[:, a, :], CT1[:, a, :], sm[:, 4:5], psC[a][:], ALU.mult, ALU.add, reads=[CT1, sm, psC[a]], writes=[CT1])
            ci += 1
    b.pop()


def attn_phase(b, U, pp, cmo, tabs, TT, yD, need_ctx=True):
    ident = cmo[:, C_ID:C_ID + 128]
    bones = cmo[:, C_BONES:C_BONES + 128]
    Rq = cmo[:, C_RQ:C_RQ + 128]
    Rk = cmo[:, C_RK:C_RK + 128]
    E = [cmo[:, C_E0:C_E0 + 128], cmo[:, C_E1:C_E1 + 128]]
    e64 = cmo[:, C_E64:C_E64 + 64]
    cosq, sinq, cosk, sink = tabs
    NKT = TT // 128
    b.push()
    QT = b.sb([128, TT], BF16, name="QT")
    KTz = [b.sb([128, TT], BF16, name=f"KTz{h}") for h in range(2)]
    V1 = b.sb([128, NKT, 65], BF16, name="V1")
    b.op("dve", lambda e: e.memset(V1[:, :, 64:65], 1.0), writes=[V1])
    b.push()
    xq = b.sb([128, 512], name="axq")
    xk = b.sb([128, 512], name="axk")
    t1 = b.sb([128, 512], name="at1")
    t2 = b.sb([128, 512], name="at2")
    tc_ = b.sb([128, 512], name="atc")
    ts_ = b.sb([128, 512], name="ats")
    ps1 = b.ps([128, 512], name="aps1")
    ps2 = b.ps([128, 512], name="aps2")
    psV = b.ps([128, 4, 128], name="apsV")
    for p0 in range(0, TT, 512):
        pw = min(512, TT - p0)
        for which in range(2):
            x = xq if which == 0 else xk
            r0 = 1024 if which == 0 else 1152
            gcol = 25 + which
            ct, st_ = (cosq, sinq) if which == 0 else (cosk, sink)
            Rm = Rq if which == 0 else Rk
            b.dma("sp", x[:, 0:pw], U[r0:r0 + 128, p0:p0 + pw], reads=[U], writes=[x])
            b.dma("sp", tc_[:, 0:pw], ct[:, p0:p0 + pw], reads=[ct], writes=[tc_])
            b.dma("sp", ts_[:, 0:pw], st_[:, p0:p0 + pw], reads=[st_], writes=[ts_])
            b.tt("dve", t1[:, 0:pw], x[:, 0:pw], x[:, 0:pw], ALU.mult, reads=[x], writes=[t1])
            b.mm(ps1[:, 0:pw], bones, t1[:, 0:pw], reads=[cmo, t1], writes=[ps1])
            b.act(t1[:, 0:pw], ps1[:, 0:pw], AF.Sqrt, bias=EPS, scale=1.0 / 64, reads=[ps1], writes=[t1])
            b.op("dve", lambda e, pw=pw: e.reciprocal(t1[:, 0:pw], t1[:, 0:pw]), reads=[t1], writes=[t1])
            if which == 1:
                b.op("dve", lambda e, pw=pw: e.memset(t1[64:128, 0:pw], 1.0), writes=[t1])
            b.stt(t2[:, 0:pw], x[:, 0:pw], pp[:, gcol:gcol + 1], t1[:, 0:pw], ALU.mult, ALU.mult, reads=[x, pp, t1], writes=[t2])
            b.mm(ps2[:, 0:pw], Rm, t2[:, 0:pw], reads=[cmo, t2], writes=[ps2])
            b.tt("dve", t1[:, 0:pw], ps2[:, 0:pw], ts_[:, 0:pw], ALU.mult, reads=[ps2, ts_], writes=[t1])
            b.tt("dve", t2[:, 0:pw], t2[:, 0:pw], tc_[:, 0:pw], ALU.mult, reads=[t2, tc_], writes=[t2])
            if which == 0:
                b.stt(QT[:, p0:p0 + pw], t2[:, 0:pw], 1.0, t1[:, 0:pw], ALU.mult, ALU.add, reads=[t2, t1], writes=[QT])
                b.ts("dve", QT[:, p0:p0 + pw], QT[:, p0:p0 + pw], 0.125, None, ALU.mult, reads=[QT], writes=[QT])
            else:
                b.tt("dve", t2[:, 0:pw], t2[:, 0:pw], t1[:, 0:pw], ALU.add, reads=[t2, t1], writes=[t2])
                for h in range(2):
                    b.mm(ps1[:, 0:pw], E[h], t2[:, 0:pw], reads=[cmo, t2], writes=[ps1])
                    b.op("dve", lambda e, h=h, p0=p0, pw=pw: e.tensor_copy(out=KTz[h][:, p0:p0 + pw], in_=ps1[:, 0:pw]),
                         reads=[ps1], writes=[KTz[h]])
                nt = pw // 128
                for i in range(nt):
                    b.tr(psV[:, i, :], t2[:, i * 128:(i + 1) * 128], ident, reads=[t2, cmo], writes=[psV])
                b.op("dve", lambda e, p0=p0, nt=nt: e.tensor_copy(out=V1[:, p0 // 128:p0 // 128 + nt, 0:64], in_=psV[:, 0:nt, 64:128]),
                     reads=[psV], writes=[V1])
    b.pop()
    psS = [b.ps([128, 512], name=f"apsS{i}") for i in range(3)]
    psO = [b.ps([128, 512], name=f"apsO{h}") for h in range(2)]
    psD = b.ps([128, 512], name="apsD")
    PT = [b.sb([128, 512], BF16, name=f"aPT{i}") for i in range(3)]
    OT = b.sb([128, 512], name="aOT")
    b.op("dve", lambda e: e.memset(OT[:], 0.0), writes=[OT])
    rd = b.sb([64, 512], name="ard")
    yo = [b.sb([64, 512], name=f"ayo{i}") for i in range(2)]
    qblocks = []
    if need_ctx:
        qblocks.append((0, 256, 0, 2))
    t = 256
    while t < TT:
        w = min(512, TT - t)
        qblocks.append((t, w, 0, NKT))
        t += w
    si = 0
    oi = 0
    for (q0, qw, k_lo, k_hi) in qblocks:
        for kt in range(k_lo, k_hi):
            for h in range(2):
                ps = psS[si % 3]
                pt = PT[si % 3]
                si += 1
                b.mm(ps[:, 0:qw], KTz[h][:, kt * 128:(kt + 1) * 128], QT[:, q0:q0 + qw], reads=[KTz[h], QT], writes=[ps])
                b.act(pt[:, 0:qw], ps[:, 0:qw], AF.Exp, reads=[ps], writes=[pt])
                b.mm(psO[h][0:65, 0:qw], V1[:, kt, :], pt[:, 0:qw], start=(kt == k_lo), stop=(kt == k_hi - 1),
                     reads=[V1, pt], writes=[psO[h]])
        for h in range(2):
            b.op("dve", lambda e, h=h, qw=qw: e.tensor_copy(out=OT[0:65, 0:qw], in_=psO[h][0:65, 0:qw]), reads=[psO[h]], writes=[OT])
            b.mm(psD[0:64, 0:qw], e64, OT[:, 0:qw], reads=[cmo, OT], writes=[psD])
            b.op("dve", lambda e, qw=qw: e.reciprocal(rd[:, 0:qw], psD[0:64, 0:qw]), reads=[psD], writes=[rd])
            y = yo[oi % 2]
            oi += 1
            b.tt("dve", y[:, 0:qw], OT[0:64, 0:qw], rd[:, 0:qw], ALU.mult, reads=[OT, rd], writes=[y])
            b.dma("sp", yD[64 * h:64 * h + 64, q0:q0 + qw], y[:, 0:qw], reads=[y], writes=[yD])
    b.pop()


def out_odd(b, tiles, d, cm, final=False):
    ident = cm[:, 0:128]
    b.push()
    vb = b.sb([128, 5120], name="vb")
    for i in range(5):
        b.dma("sp", vb[:, i * 1024:(i + 1) * 1024], d["vecs"][0, i * 1024:(i + 1) * 1024].partition_broadcast(128),
              reads=[d["vecs"]], writes=[vb])
    nw, adab = vb[:, 0:1024], vb[:, 4096:5120]
    cvt = b.sb([128, 16], name="cvt")
    b.dma("sp", cvt[:], d["cv"][:, :], reads=[d["cv"]], writes=[cvt])
    sc = b.sb([128, 16], name="osc")
    b.act(sc[:], cvt[:], AF.Silu, reads=[cvt], writes=[sc])
    gate_b = [b.sb([128, 1024], name=f"gate{r}") for r in range(2)]
    Wg = [b.sb([128, 16, 1024], BF16, name=f"Wg{r}") for r in range(2)]
    b.push()
    aw = b.sb([128, 8, 1024], name="oaw")
    for k in range(8):
        b.dma("sp", aw[:, k, :], d["adawg"][k * 128:(k + 1) * 128, :], reads=[d["adawg"]], writes=[aw])
    scb = b.sb([128, 8, 128], name="scb")
    pg = b.ps([128, 512], name="pg")
    for r in range(2):
        b.op("dve", lambda e, r=r: e.tensor_copy(out=scb[:], in_=sc[:, r:16:2].unsqueeze(2).to_broadcast([128, 8, 128])),
             reads=[sc], writes=[scb])
        for half in range(2):
            hsl = slice(half * 512, half * 512 + 512)
            for k in range(8):
                b.mm(pg[:], scb[:, k, :], aw[:, k, hsl], start=(k == 0), stop=(k == 7), reads=[scb, aw], writes=[pg])
            b.tt("dve", gate_b[r][:, hsl], pg[:], adab[:, hsl], ALU.add, reads=[pg, vb], writes=[gate_b[r]])
    b.pop()
    for part in range(2):
        b.push()
        stg = b.sb([128, 8, 1024], name="ostg")
        for k in range(8):
            kk = part * 8 + k
            b.dma("sp", stg[:, k, :], d["wout"][kk * 128:(kk + 1) * 128, :], reads=[d["wout"]], writes=[stg])
        for r in range(2):
            for k in range(8):
                b.tt("dve", Wg[r][:, part * 8 + k, :], stg[:, k, :], gate_b[r][:], ALU.mult, reads=[stg, gate_b[r]], writes=[Wg[r]])
        b.pop()
    hC = b.sb([128, 2048], name="ohC")
    oo = b.sb([128, 1024], name="oo")
    zz = b.sb([128, 1024], name="oz")
    yD = b.sb([128, 1024], name="oyD")
    ag = b.sb([128, 1024], name="oag")
    xres = b.sb([128, 1024], name="oxres")
    ycat = b.sb([128, 2048], name="ycat")
    t10 = b.sb([128, 1024], name="ot10")
    st = b.sb([128, 64], name="ost")
    yT = b.sb([128, 16, 128], BF16, name="oyT")
    xo = b.sb([128, 1024], name="oxo")
    psT = [b.ps([128, 4, 128], name=f"opsT{i}") for i in range(4)]
    pso = [b.ps([128, 512], name=f"opso{i}") for i in range(2)]
    for (r0, r) in tiles:
        rs = slice(r0, r0 + 128)
        for tl, nm in ((hC, "hC"), (oo, "o"), (zz, "z"), (yD, "yD"), (ag, "ag"), (xres, "xres")):
            b.dma("sp", tl[:], d[nm][rs, :], reads=[d[nm]], writes=[tl])
        h = ycat[:, 0:1024]
        h3 = h.rearrange("p (g q) -> p g q", q=256)
        b.tt("dve", h, hC[:, 0:1024], hC[:, 1024:2048], ALU.add, reads=[hC], writes=[ycat])
        b.tt("dve", t10[:], h, h, ALU.mult, reads=[ycat], writes=[t10])
        b.op("dve", lambda e: e.tensor_reduce(out=st[:, 0:4], in_=t10[:].rearrange("p (g q) -> p g q", q=256), axis=AX.X, op=ALU.add),
             reads=[t10], writes=[st])
        b.act(st[:, 0:4], st[:, 0:4], AF.Sqrt, bias=EPS, scale=1.0 / 256, reads=[st], writes=[st])
        b.op("dve", lambda e: e.reciprocal(st[:, 0:4], st[:, 0:4]), reads=[st], writes=[st])
        b.tt("dve", h3, h3, st[:, 0:4].unsqueeze(2).to_broadcast([128, 4, 256]), ALU.mult, reads=[ycat, st], writes=[ycat])
        b.tt("dve", h, h, nw, ALU.mult, reads=[ycat, vb], writes=[ycat])
        b.act(oo[:], oo[:], AF.Sigmoid, reads=[oo], writes=[oo])
        b.act(zz[:], zz[:], AF.Silu, reads=[zz], writes=[zz])
        b.tt("dve", h, h, oo[:], ALU.mult, reads=[ycat, oo], writes=[ycat])
        b.tt("dve", h, h, zz[:], ALU.mult, reads=[ycat, zz], writes=[ycat])
        b.act(ag[:], ag[:], AF.Silu, reads=[ag], writes=[ag])
        b.tt("dve", ycat[:, 1024:2048], yD[:], ag[:], ALU.mult, reads=[yD, ag], writes=[ycat])
        for k in range(16):
            pt = psT[k // 4]
            b.tr(pt[:, k % 4, :], ycat[:, k * 128:(k + 1) * 128], ident, reads=[ycat, cm], writes=[pt])
        for i4 in range(4):
            b.op("dve", lambda e, i4=i4: e.tensor_copy(out=yT[:, 4 * i4:4 * i4 + 4, :], in_=psT[i4][:]), reads=[psT[i4]], writes=[yT])
        for half in range(2):
            hsl = slice(half * 512, half * 512 + 512)
            po = pso[half]
            for k in range(16):
                b.mm(po[:], yT[:, k, :], Wg[r][:, k, hsl], start=(k == 0), stop=(k == 15), reads=[yT, Wg[r]], writes=[po])
            b.tt("dve", xo[:, hsl], po[:], xres[:, hsl], ALU.add, reads=[po, xres], writes=[xo])
        if final:
            b.act(t10[:], xo[:], AF.Square, reads=[xo], writes=[t10, st], accum=st[:, 40:41])
            b.act(st[:, 40:41], st[:, 40:41], AF.Sqrt, bias=EPS, scale=1.0 / 1024, reads=[st], writes=[st])
            b.op("dve", lambda e: e.reciprocal(st[:, 40:41], st[:, 40:41]), reads=[st], writes=[st])
            b.stt(xo[:], xo[:], st[:, 40:41], vb[:, 3072:4096], ALU.mult, ALU.mult, reads=[xo, st, vb], writes=[xo])
        b.dma("sp", d["xo"][rs, :], xo[:], reads=[xo], writes=[d["xo"]])
    b.pop()


def consts_cm():
    cm = np.zeros((128, 642), np.float32)
    i = np.arange(128)
    cm[:, 0:128] = np.eye(128)
    cm[:, 128:256] = (i[:, None] < i[None, :])
    cm[:, 256:384] = (i[:, None] > i[None, :])
    cm[:, 384:512] = (i[:, None] <= i[None, :])
    cm[:, 512:640] = ((i[:, None] // 64) == (i[None, :] // 64))
    cm[:, 640] = (i < 64)
    cm[:, 641] = (i >= 64)
    return cm


def fm8(v):
    return np.ascontiguousarray(v.reshape(8, 128).T)


def even_core_inputs(core, j_layer, l, inp, xseq):
    d, jj = core // 4, core % 4
    W = inp['ab_w_in'][j_layer]
    o_r, o_k, o_v = 0, 512, 1024
    o_wl, o_al = 1536, 1664
    o_ga = 1792
    o_xbc = 2304
    o_dt = o_xbc + 1536
    o_z = o_dt + 32
    hc = slice(128 * jj, 128 * jj + 128)
    g = jj // 2
    cols = []
    cols.append(np.arange(o_r, o_r + 512)[hc])
    cols.append(np.arange(o_k, o_k + 512)[hc])
    cols.append(np.arange(o_v, o_v + 512)[hc])
    cols.append(np.arange(o_ga, o_ga + 512)[hc])
    cols.append(np.concatenate([np.arange(o_wl + 64 * d, o_wl + 64 * d + 64), np.arange(o_al + 64 * d, o_al + 64 * d + 64)]))
    xs_cols = np.arange(o_xbc, o_xbc + 1024)[256 * jj:256 * jj + 256]
    cols.append(xs_cols[:128])
    cols.append(xs_cols[128:])
    cols.append(np.arange(o_xbc + 1024 + 128 * g, o_xbc + 1024 + 128 * g + 128))
    cols.append(np.arange(o_xbc + 1280 + 128 * g, o_xbc + 1280 + 128 * g + 128))
    z_cols = np.arange(o_z, o_z + 1024)[256 * jj:256 * jj + 256]
    cols.append(z_cols[:128])
    cols.append(z_cols[128:])
    win = np.zeros((1024, NCC * 128), np.float32)
    for cc, c in enumerate(cols):
        win[:, cc * 128:cc * 128 + len(c)] = W[:, c]
    dt_cols = np.arange(o_dt + 16 * d + 4 * jj, o_dt + 16 * d + 4 * jj + 4)
    win[:, 11 * 128:11 * 128 + 4] = W[:, dt_cols]
    pp = np.zeros((128, NPP), np.float32)
    mu = inp['rk_mu'][j_layer]
    pp[:, 0] = mu[o_r:o_r + 512][hc]
    pp[:, 1] = mu[o_k:o_k + 512][hc]
    pp[:, 2] = mu[o_v:o_v + 512][hc]
    pp[0:64, 3] = mu[o_wl + 64 * d:o_wl + 64 * d + 64]
    pp[64:128, 3] = mu[o_al + 64 * d:o_al + 64 * d + 64]
    pp[:, 4] = inp['rk_w0'][j_layer, d][hc]
    pp[:, 5] = inp['rk_a0'][j_layer, d][hc]
    pp[:, 6] = inp['rk_k_k'][j_layer][hc]
    pp[:, 7] = inp['rk_k_a'][j_layer][hc]
    pp[:, 8] = inp['rk_r_k'][j_layer].reshape(512)[hc]
    cw = inp['mb_conv_w'][j_layer]
    cb = inp['mb_conv_b'][j_layer]
    if d == 1:
        cw = cw[::-1]
    xbc_rel = [xs_cols[:128] - o_xbc, xs_cols[128:] - o_xbc, cols[7] - o_xbc, cols[8] - o_xbc]
    for q, rel in enumerate(xbc_rel):
        pp[:, 9 + 5 * q:14 + 5 * q] = cw[:, rel].T
        pp[:, 29 + q] = cb[rel]
    pp[0:4, 33] = inp['mb_dt_bias'][j_layer, d, 4 * jj:4 * jj + 4]
    pp[0:4, 34] = inp['mb_a_log'][j_layer, d, 4 * jj:4 * jj + 4]
    pp[:, 35:43] = fm8(inp['norm_g'][l])
    pp[:, 43:51] = fm8(inp['ada_b'][l][0:1024])
    pp[:, 51:59] = fm8(inp['ada_b'][l][1024:2048])
    cv = np.stack([fm8(inp['c'][0]), fm8(inp['c_ctx'])], -1)
    pp[:, 59:75] = cv.reshape(128, 16)
    w2a2 = np.zeros((128, 256), np.float32)
    w2a2[0:64, 0:128] = inp['rk_w2'][j_layer, d][:, hc]
    w2a2[64:128, 128:256] = inp['rk_a2'][j_layer, d][:, hc]
    return {
        'xseq': xseq, 'pp': pp, 'win': win,
        'adaw': np.ascontiguousarray(inp['ada_w'][l][:, 0:2048]),
        'w2a2': w2a2, 'cm': consts_cm(), 'sel': np.concatenate([np.kron(np.eye(4, dtype=np.float32), np.ones((1, 128), np.float32)), np.zeros((124, 512), np.float32)], 0),
    }


NCMO = 128 * 9 + 64


def consts_cmo():
    i = np.arange(128)
    c = np.zeros((128, NCMO), np.float32)
    c[:, 0:128] = np.eye(128)
    c[:, 128:256] = (i[:, None] <= i[None, :])
    c[:, 256:384] = ((i[:, None] // 64) == (i[None, :] // 64))
    c[32, 384:512] = 1.0
    R = np.zeros((128, 128), np.float32)
    for p in range(128):
        q = p % 32
        if q < 16:
            R[p + 16, p] = -1.0
        else:
            R[p - 16, p] = 1.0
    c[:, 512:640] = R
    Rk = R.copy()
    Rk[64:, :] = 0
    Rk[:, 64:] = 0
    c[:, 640:768] = Rk
    E0 = np.zeros((128, 128), np.float32)
    E1 = np.zeros((128, 128), np.float32)
    for dd in range(64):
        E0[dd, dd] = 1.0
        E1[dd, 64 + dd] = 1.0
    c[:, 768:896] = E0
    c[:, 896:1024] = E1
    c[:, 1024:1152] = (i[:, None] < i[None, :])
    c[64, 1152:1216] = 1.0
    return c


def rope_tables(TT, T, d, grid_w=64, theta=10000.0):
    idx = np.arange(T)
    t = idx if d == 0 else T - 1 - idx
    row = (t // grid_w).astype(np.float64)
    col = (t % grid_w).astype(np.float64)
    inv = theta ** (-np.arange(16, dtype=np.float64) / 16)
    cos = np.ones((128, TT), np.float64)
    sin = np.zeros((128, TT), np.float64)
    for p in range(128):
        pp_ = p % 64
        pos = row if pp_ < 32 else col
        ang = pos * inv[pp_ % 16]
        cos[p, TT - T:] = np.cos(ang)
        sin[p, TT - T:] = np.sin(ang)
    cosk, sink = cos.copy(), sin.copy()
    cosk[64:] = 1.0
    sink[64:] = 0.0
    return cos.astype(np.float32), sin.astype(np.float32), cosk.astype(np.float32), sink.astype(np.float32)


def odd_core_inputs(core, j, l, inp, xseq, T):
    d, jj = core // 4, core % 4
    c = core
    W = inp['cd_w_in'][j]
    TT = xseq.shape[0]
    cols = [np.arange(256 * jj, 256 * jj + 128), np.arange(256 * jj + 128, 256 * jj + 256),
            np.arange(1024 + 256 * jj, 1024 + 256 * jj + 128), np.arange(1024 + 256 * jj + 128, 1024 + 256 * jj + 256),
            np.arange(2048 + 256 * jj, 2048 + 256 * jj + 128), np.arange(2048 + 256 * jj + 128, 2048 + 256 * jj + 256)]
    oz = 3072 if d == 0 else 4112
    cols += [np.arange(oz + 256 * jj, oz + 256 * jj + 128), np.arange(oz + 256 * jj + 128, oz + 256 * jj + 256)]
    cols.append(np.arange(5136 + 128 * c, 5136 + 128 * c + 128))
    cols.append(np.concatenate([np.arange(6160 + 64 * (c // 2), 6160 + 64 * (c // 2) + 64),
                                np.arange(6416 + 64 * (c // 2), 6416 + 64 * (c // 2) + 64)]))
    cols.append(np.arange(6672 + 128 * c, 6672 + 128 * c + 128))
    win = np.zeros((1024, NCC * 128), np.float32)
    for cc, cl in enumerate(cols):
        win[:, cc * 128:cc * 128 + len(cl)] = W[:, cl]
    win[:, 11 * 128 + 0] = W[:, 4096 + 4 * d + jj]
    win[:, 11 * 128 + 32] = W[:, 4104 + 4 * d + jj]
    pp = np.zeros((128, NPP), np.float32)
    cw = inp['ml_conv_w'][j]
    cb = inp['ml_conv_b'][j]
    if d == 1:
        cw = cw[::-1]
    for q in range(4):
        pp[:, 5 * q:5 * q + 5] = cw[:, cols[q]].T
        pp[:, 20 + q] = cb[cols[q]]
    pp[0, 24] = inp['ml_i_bias'][j, d, jj]
    pp[32, 24] = inp['ml_f_bias'][j, d, jj]
    pp[:, 25] = np.tile(inp['at_q_norm'][j], 2)
    pp[0:64, 26] = inp['at_k_norm'][j]
    pp[64:, 26] = 1.0
    pp[:, 35:43] = fm8(inp['norm_g'][l])
    pp[:, 43:51] = fm8(inp['ada_b'][l][0:1024])
    pp[:, 51:59] = fm8(inp['ada_b'][l][1024:2048])
    cv = np.stack([fm8(inp['c'][0]), fm8(inp['c_ctx'])], -1)
    pp[:, 59:75] = cv.reshape(128, 16)
    cq, sq, ck, sk = rope_tables(TT, T, d)
    return {'xseq': xseq, 'pp': pp, 'win': win, 'adaw': np.ascontiguousarray(inp['ada_w'][l][:, 0:2048]),
            'cmo': consts_cmo(), 'cosq': cq, 'sinq': sq, 'cosk': ck, 'sink': sk}


def assemble_even_out_inputs(inp, outs, j, l, xs, ctx):
    def unflip(a):
        return np.concatenate([a[:256][::-1], a[256:][::-1]], 0)
    R = ctx.shape[0] + xs.shape[0]
    yA = np.zeros((R, 1024), np.float32); bon = np.zeros((R, 16), np.float32)
    v = np.zeros((R, 512), np.float32); ga = np.zeros((R, 512), np.float32)
    yB = np.zeros((R, 2048), np.float32); xsm = np.zeros((R, 1024), np.float32); z = np.zeros((R, 1024), np.float32)
    for core in range(8):
        d, jj = core // 4, core % 4
        o = outs[core]
        f = (lambda a: a) if d == 0 else unflip
        yA[:, 512 * d + 128 * jj:512 * d + 128 * jj + 128] = f(o["yA"])
        bon[:, 8 * d + 2 * jj:8 * d + 2 * jj + 2] = f(np.ascontiguousarray(o["bon"].T))
        yB[:, 1024 * d + 256 * jj:1024 * d + 256 * jj + 256] = f(o["yB"])
        if d == 0:
            v[:, 128 * jj:128 * jj + 128] = o["vg"][0:128].T
            ga[:, 128 * jj:128 * jj + 128] = o["vg"][128:256].T
            xsm[:, 256 * jj:256 * jj + 256] = o["xsB"]
            z[:, 256 * jj:256 * jj + 256] = o["zB"].T
    vecs = np.zeros((1, 5120), np.float32)
    vecs[0, 0:512] = inp['rk_ln_w'][j]; vecs[0, 512:1024] = inp['rk_ln_b'][j]
    vecs[0, 1024:2048] = np.repeat(inp['mb_d'][j], 64); vecs[0, 2048:3072] = inp['mb_norm_w'][j]
    vecs[0, 4096:5120] = inp['ada_b'][l][2048:3072]
    cv = np.stack([fm8(inp['c'][0]), fm8(inp['c_ctx'])], -1).reshape(128, 16)
    return {"yA": yA, "bon": bon, "v": v, "ga": ga, "yB": yB, "xsm": xsm, "z": z,
            "xres": np.ascontiguousarray(np.concatenate([ctx, xs], 0)),
            "wout": np.ascontiguousarray(inp['ab_w_out'][j]), "adawg": np.ascontiguousarray(inp['ada_w'][l][:, 2048:3072]),
            "vecs": vecs, "cv": np.ascontiguousarray(cv), "cm": consts_cm()}


def unflip(a):
    return np.concatenate([a[:256][::-1], a[256:][::-1]], 0)

def assemble_odd_out_inputs(inp, outs, j, l, xs, ctx):
    R = ctx.shape[0] + xs.shape[0]
    hC = np.zeros((R, 2048), np.float32); o = np.zeros((R, 1024), np.float32); z = np.zeros((R, 1024), np.float32)
    yD = np.zeros((R, 1024), np.float32); ag = np.zeros((R, 1024), np.float32)
    for core in range(8):
        d, jj = core // 4, core % 4
        oc = outs[core]
        f = (lambda a: a) if d == 0 else unflip
        hC[:, 1024 * d + 256 * jj:1024 * d + 256 * jj + 256] = f(oc["hC"])
        ozt = f(np.ascontiguousarray(oc["oz"].T))
        if d == 0:
            o[:, 256 * jj:256 * jj + 256] = ozt
        else:
            z[:, 256 * jj:256 * jj + 256] = ozt
        yD[:, 128 * core:128 * core + 128] = f(np.ascontiguousarray(oc["yD"].T))
        ag[:, 128 * core:128 * core + 128] = f(np.ascontiguousarray(oc["agT"].T))
    vecs = np.zeros((1, 5120), np.float32)
    vecs[0, 0:1024] = inp['ml_norm_w'][j]
    vecs[0, 4096:5120] = inp['ada_b'][l][2048:3072]
    cv = np.stack([fm8(inp['c'][0]), fm8(inp['c_ctx'])], -1).reshape(128, 16)
    return {"hC": hC, "o": o, "z": z, "yD": yD, "ag": ag, "xres": np.ascontiguousarray(np.concatenate([ctx, xs], 0)),
            "wout": np.ascontiguousarray(inp['cd_w_out'][j]), "adawg": np.ascontiguousarray(inp['ada_w'][l][:, 2048:3072]),
            "vecs": vecs, "cv": np.ascontiguousarray(cv), "cm": consts_cm()}


T_SEQ = 16384
CTX = 256
TT_ALL = T_SEQ + CTX


def _groups_blocks(TT):
    groups = [(0, 2, 1)]
    t = CTX
    while t < TT:
        n = min(4, (TT - t) // 128)
        groups.append((t, n, 0))
        t += n * 128
    blocks = [(0, 2, 0, CTX)]
    t = CTX
    while t < TT:
        n = min(8, (TT - t) // 128)
        blocks.append((t, n, CTX, TT))
        t += n * 128
    return groups, blocks


def build_mix_even():
    b = Bld()
    TT = TT_ALL
    xseq = b.dram("xseq", [TT, 1024], kind="ExternalInput")
    pp_d = b.dram("pp", [128, NPP], kind="ExternalInput")
    win_d = b.dram("win", [1024, NCC * 128], kind="ExternalInput")
    adaw_d = b.dram("adaw", [1024, 2048], kind="ExternalInput")
    w2a2_d = b.dram("w2a2", [128, 256], kind="ExternalInput")
    cm_d = b.dram("cm", [128, 642], kind="ExternalInput")
    sel_d = b.dram("sel", [128, 512], kind="ExternalInput")
    U = b.dram("U", [NCC * 128, TT], kind="Internal")
    yA = b.dram("yA", [TT, 128], kind="ExternalOutput")
    bon = b.dram("bon", [2, TT], kind="ExternalOutput")
    vg = b.dram("vg", [256, TT], kind="ExternalOutput")
    zB = b.dram("zB", [256, TT], kind="ExternalOutput")
    yB = b.dram("yB", [TT, 256], kind="ExternalOutput")
    xsB = b.dram("xsB", [TT, 256], kind="ExternalOutput")
    pp = b.sb([128, NPP], name="pp")
    cm = b.sb([128, 642], name="cm")
    b.dma("sp", pp[:], pp_d[:, :], reads=[pp_d], writes=[pp])
    b.dma("sp", cm[:], cm_d[:, :], reads=[cm_d], writes=[cm])
    modT = setup_mod(b, pp, adaw_d, 16)
    groups, blocks = _groups_blocks(TT)
    dests = {cc: (U, cc * 128) for cc in range(NCC)}
    dests[3] = (vg, 128)
    dests[9] = (zB, 0)
    dests[10] = (zB, 128)
    phase_A(b, xseq, win_d, pp, modT, cm, groups, dests, NCC)
    b.P.barrier()
    rwkv_phase(b, U, pp, w2a2_d, cm, blocks, yA, bon, vg, TT)
    mamba_phase(b, U, pp, cm, sel_d, blocks, yB, xsB, TT)
    b.P.emit()
    b.stacks[0].close()
    return b.nc


def build_mix_odd():
    b = Bld()
    TT = TT_ALL
    xseq = b.dram("xseq", [TT, 1024], kind="ExternalInput")
    pp_d = b.dram("pp", [128, NPP], kind="ExternalInput")
    win_d = b.dram("win", [1024, NCC * 128], kind="ExternalInput")
    adaw_d = b.dram("adaw", [1024, 2048], kind="ExternalInput")
    cmo_d = b.dram("cmo", [128, NCMO], kind="ExternalInput")
    tabs = [b.dram(n, [128, TT], kind="ExternalInput") for n in ("cosq", "sinq", "cosk", "sink")]
    U = b.dram("U", [NCC * 128, TT], kind="Internal")
    hC = b.dram("hC", [TT, 256], kind="ExternalOutput")
    oz = b.dram("oz", [256, TT], kind="ExternalOutput")
    yD = b.dram("yD", [128, TT], kind="ExternalOutput")
    agT = b.dram("agT", [128, TT], kind="ExternalOutput")
    pp = b.sb([128, NPP], name="pp")
    cmo = b.sb([128, NCMO], name="cmo")
    b.dma("sp", pp[:], pp_d[:, :], reads=[pp_d], writes=[pp])
    b.dma("sp", cmo[:], cmo_d[:, :], reads=[cmo_d], writes=[cmo])
    modT = setup_mod(b, pp, adaw_d, 16)
    groups, blocks = _groups_blocks(TT)
    dests = {cc: (U, cc * 128) for cc in range(NCC)}
    dests[6] = (oz, 0)
    dests[7] = (oz, 128)
    dests[10] = (agT, 0)
    phase_A(b, xseq, win_d, pp, modT, cmo, groups, dests, NCC)
    b.P.barrier()
    mlstm_phase(b, U, pp, cmo, blocks, hC, TT)
    attn_phase(b, U, pp, cmo, tabs, TT, yD, need_ctx=True)
    b.P.emit()
    b.stacks[0].close()
    return b.nc


def build_out(kind, R, tiles, final):
    b = Bld()
    if kind == "even":
        shapes = {"yA": [R, 1024], "bon": [R, 16], "v": [R, 512], "ga": [R, 512], "yB": [R, 2048], "xsm": [R, 1024],
                  "z": [R, 1024], "xres": [R, 1024], "wout": [1536, 1024], "adawg": [1024, 1024], "vecs": [1, 5120], "cv": [128, 16]}
    else:
        shapes = {"hC": [R, 2048], "o": [R, 1024], "z": [R, 1024], "yD": [R, 1024], "ag": [R, 1024], "xres": [R, 1024],
                  "wout": [2048, 1024], "adawg": [1024, 1024], "vecs": [1, 5120], "cv": [128, 16]}
    d = {k: b.dram(k, s, kind="ExternalInput") for k, s in shapes.items()}
    d["xo"] = b.dram("xo", [R, 1024], kind="ExternalOutput")
    cm_d = b.dram("cm", [128, 642], kind="ExternalInput")
    cm = b.sb([128, 642], name="cm")
    b.dma("sp", cm[:], cm_d[:, :], reads=[cm_d], writes=[cm])
    if kind == "even":
        out_even(b, tiles, d, cm, final=final)
    else:
        out_odd(b, tiles, d, cm, final=final)
    b.P.emit()
    b.stacks[0].close()
    return b.nc


def kernel(**inp):
    inp = {k: np.asarray(v) for k, v in inp.items()}
    xs = np.ascontiguousarray(inp['x'][0])
    ctx = np.ascontiguousarray(inp['ctx'][0])
    sh = T_SEQ // 8
    tiles = [(0, 1), (128, 1)] + [(CTX + 128 * i, 0) for i in range(sh // 128)]
    cores = list(range(8))
    for l in range(4):
        j = l // 2
        even = (l % 2 == 0)
        fwd = np.ascontiguousarray(np.concatenate([ctx, xs], 0))
        bwd = np.ascontiguousarray(np.concatenate([ctx[::-1], xs[::-1]], 0))
        if even:
            maps = [even_core_inputs(c, j, l, inp, fwd if c < 4 else bwd) for c in cores]
            res = run_bass_kernel_spmd(build_mix_even(), maps, core_ids=cores)
            full = assemble_even_out_inputs(inp, res.results, j, l, xs, ctx)
            row_keys = ("yA", "bon", "v", "ga", "yB", "xsm", "z", "xres")
        else:
            maps = [odd_core_inputs(c, j, l, inp, fwd if c < 4 else bwd, T_SEQ) for c in cores]
            res = run_bass_kernel_spmd(build_mix_odd(), maps, core_ids=cores)
            full = assemble_odd_out_inputs(inp, res.results, j, l, xs, ctx)
            row_keys = ("hC", "o", "z", "yD", "ag", "xres")
        del res, maps
        final = (l == 3)
        full["vecs"][0, 3072:4096] = inp['norm_final']
        maps2 = []
        for c in cores:
            rows = np.concatenate([np.arange(CTX), CTX + c * sh + np.arange(sh)])
            m = {k: np.ascontiguousarray(full[k][rows]) for k in row_keys}
            for k in ("wout", "adawg", "vecs", "cv", "cm"):
                m[k] = full[k]
            maps2.append(m)
        del full
        res2 = run_bass_kernel_spmd(build_out("even" if even else "odd", CTX + sh, tiles, final), maps2, core_ids=cores)
        outs = [np.asarray(r['xo'], dtype=np.float32) for r in res2.results]
        xs = np.ascontiguousarray(np.concatenate([o[CTX:] for o in outs], 0))
        ctx = np.ascontiguousarray(outs[0][:CTX])
        del res2, maps2, outs
    return xs[None]
```

```python
import contextlib
import os
import numpy as np
import concourse.bass as bass
import concourse.mybir as mybir
from concourse.bass_utils import run_bass_kernel_spmd

F32 = mybir.dt.float32
BF16 = mybir.dt.bfloat16
I32 = mybir.dt.int32
AF = mybir.ActivationFunctionType
ALU = mybir.AluOpType
AX = mybir.AxisListType

COMPUTE = ("pe", "dve", "act", "pool")
STREAMS = ("pe", "dve", "act", "pool", "sp")
DMA_K = 4


class Res:
    __slots__ = ("name", "w", "rs")

    def __init__(self, name):
        self.name = name
        self.w = None
        self.rs = []


class Ins:
    __slots__ = ("stream", "fn", "is_dma", "seq", "dn", "waits", "need_inc", "know")


class Prog:
    def __init__(self, nc):
        self.nc = nc
        self.ins = {s: [] for s in STREAMS}
        self.ndma = {s: 0 for s in STREAMS}
        self.know = {s: {e: -1 for e in COMPUTE} for s in STREAMS}
        self.kdma = {s: set() for s in STREAMS}
        self.n_res = 0
        self._bar = {}
        self._last = {}

    def res(self, name=None):
        self.n_res += 1
        return Res(name or f"r{self.n_res}")

    def _dep(self, I, D):
        s = I.stream
        if D is None or D is I:
            return
        if D.is_dma:
            key = (D.stream, D.dn)
            if key in self.kdma[s]:
                return
            self.kdma[s].add(key)
            I.waits.append(("dma", D.stream, D.dn % DMA_K, 16 * (D.dn // DMA_K + 1)))
        else:
            e = D.stream
            if self.know[s][e] >= D.seq:
                return
            if e == "pe" and s == "pe":
                return
            D.need_inc = True
            I.waits.append(("eng", e, D.seq))
            kn = self.know[s]
            kn[e] = D.seq
            for e2, v in D.know.items():
                if v > kn[e2]:
                    kn[e2] = v

    def _add(self, stream, fn, reads, writes, is_dma):
        I = Ins()
        I.stream = stream
        I.fn = fn
        I.is_dma = is_dma
        I.waits = []
        I.need_inc = False
        I.seq = None
        I.dn = None
        if is_dma:
            n = self.ndma[stream]
            self.ndma[stream] = n + 1
            I.dn = n
            if n >= DMA_K:
                key = (stream, n - DMA_K)
                if key not in self.kdma[stream]:
                    self.kdma[stream].add(key)
                    I.waits.append(("dma", stream, n % DMA_K, 16 * ((n - DMA_K) // DMA_K + 1)))
        bar = self._bar.pop(stream, None)
        if bar is not None:
            lastI, snap_d = bar
            for e, D in lastI.items():
                if D is not None:
                    self._dep(I, D)
            for s2, n2 in snap_d.items():
                for k in range(DMA_K):
                    cnt = len(range(k, n2, DMA_K))
                    if cnt:
                        I.waits.append(("dma", s2, k, 16 * cnt))
        for r in reads:
            self._dep(I, r.w)
        for r in writes:
            self._dep(I, r.w)
            for rd in r.rs:
                self._dep(I, rd)
        for r in reads:
            r.rs.append(I)
        for r in writes:
            r.w = I
            r.rs = []
        lst = self.ins[stream]
        if not is_dma:
            I.seq = self._nseq(stream)
            I.know = dict(self.know[stream])
        lst.append(I)
        if not is_dma:
            self._last[stream] = I
        return I

    def _nseq(self, stream):
        c = getattr(self, "_cnt", None)
        if c is None:
            c = self._cnt = {s: 0 for s in STREAMS}
        v = c[stream]
        c[stream] = v + 1
        return v

    def barrier(self):
        snap_e = {e: self._cnt_get(e) - 1 for e in COMPUTE}
        snap_d = {s: self.ndma[s] for s in STREAMS}
        lastI = {e: self._last.get(e) for e in COMPUTE}
        self._bar = {s: (lastI, dict(snap_d)) for s in STREAMS}

    def _cnt_get(self, e):
        c = getattr(self, "_cnt", None)
        return c[e] if c else 0

    def op(self, eng, fn, reads=(), writes=()):
        return self._add(eng, fn, reads, writes, False)

    def dma(self, stream, out, in_, reads=(), writes=()):
        return self._add(stream, lambda e: e.dma_start(out=out, in_=in_), reads, writes, True)

    def emit(self):
        nc = self.nc
        import contextlib
        with contextlib.ExitStack() as st:
            esem = {e: st.enter_context(nc.semaphore(f"s_{e}")) for e in COMPUTE}
            dsem = {}
            for s in STREAMS:
                if self.ndma[s] > 0:
                    for k in range(DMA_K):
                        dsem[(s, k)] = st.enter_context(nc.semaphore(f"d_{s}{k}"))
            inc_count = {e: 0 for e in COMPUTE}
            seq2cnt = {e: {} for e in COMPUTE}
            for e in COMPUTE:
                c = 0
                for I in self.ins[e]:
                    if I.is_dma:
                        continue
                    if I.need_inc:
                        c += 1
                        seq2cnt[e][I.seq] = c
            block = st.enter_context(nc.Block())

            def run(stream, eng):
                for I in self.ins[stream]:
                    for w in I.waits:
                        if w[0] == "dma":
                            eng.wait_ge(dsem[(w[1], w[2])], w[3])
                        else:
                            eng.wait_ge(esem[w[1]], seq2cnt[w[1]][w[2]])
                    r = I.fn(eng)
                    if I.is_dma:
                        r.then_inc(dsem[(stream, I.dn % DMA_K)], 16)
                    elif I.need_inc:
                        r.then_inc(esem[stream], 1)
                if stream == "sp":
                    for (s2, k), sem in dsem.items():
                        n = len(range(k, self.ndma[s2], DMA_K))
                        if n:
                            eng.wait_ge(sem, 16 * n)

            @block.tensor
            def _(eng):
                run("pe", eng)

            @block.vector
            def _(eng):
                run("dve", eng)

            @block.scalar
            def _(eng):
                run("act", eng)

            @block.gpsimd
            def _(eng):
                run("pool", eng)

            @block.sync
            def _(eng):
                run("sp", eng)


EPS = 1e-6
NCC = 12
NPP = 75
EM05 = float(np.exp(-0.5))
NO_POOL_OPS = bool(int(os.environ.get("NO_POOL_OPS", "1")))
NO_POOL_DMA = bool(int(os.environ.get("NO_POOL_DMA", "1")))


class Tl:
    __slots__ = ("t", "r")

    def __init__(self, t, r):
        self.t = t
        self.r = r

    def __getitem__(self, k):
        return self.t[k]


class Bld:
    def __init__(self):
        self.nc = bass.Bass("TRN2", target_bir_lowering=False)
        self.P = Prog(self.nc)
        self.stacks = [contextlib.ExitStack()]
        self.n = 0

    def push(self):
        self.stacks.append(contextlib.ExitStack())

    def pop(self):
        self.P.barrier()
        self.stacks.pop().close()

    def sb(self, shape, dt=F32, name=None):
        self.n += 1
        t = self.stacks[-1].enter_context(self.nc.sbuf_tensor(f"{name or 'sb'}_{self.n}", list(shape), dt))
        return Tl(t, self.P.res())

    def ps(self, shape, dt=F32, name=None):
        self.n += 1
        t = self.stacks[-1].enter_context(self.nc.psum_tensor(f"{name or 'ps'}_{self.n}", list(shape), dt))
        return Tl(t, self.P.res())

    def dram(self, name, shape, dt=F32, kind="Internal"):
        t = self.nc.dram_tensor(name, list(shape), dt, kind=kind)
        return Tl(t.ap(), self.P.res())

    def op(self, eng, fn, reads=(), writes=()):
        if eng == "pool" and NO_POOL_OPS:
            eng = "dve"
        return self.P.op(eng, fn, [x.r for x in reads], [x.r for x in writes])

    def dma(self, q, out, in_, reads=(), writes=()):
        if q == "pool" and NO_POOL_DMA:
            q = "sp"
        return self.P.dma(q, out, in_, [x.r for x in reads], [x.r for x in writes])

    def dbg(self, name, tl, ap, shape, dt=F32):
        if not getattr(self, "debug", False):
            return
        d = self.dram("dbg_" + name, shape, dt, kind="ExternalOutput")
        self.dma("sp", d[tuple(slice(None) for _ in shape)], ap, reads=[tl], writes=[d])

    def ts(self, eng, out, in0, s1, s2, op0, op1=None, reads=(), writes=()):
        if op1 is None:
            return self.op(eng, lambda e: e.tensor_scalar(out, in0, s1, None, op0), reads, writes)
        return self.op(eng, lambda e: e.tensor_scalar(out, in0, s1, s2, op0, op1), reads, writes)

    def tt(self, eng, out, in0, in1, op, reads=(), writes=()):
        return self.op(eng, lambda e: e.tensor_tensor(out, in0, in1, op), reads, writes)

    def stt(self, out, in0, sc, in1, op0, op1, reads=(), writes=()):
        return self.op("dve", lambda e: e.scalar_tensor_tensor(out, in0, sc, in1, op0, op1), reads, writes)

    def act(self, out, in_, func, bias=0.0, scale=1.0, reads=(), writes=(), accum=None):
        if accum is None:
            return self.op("act", lambda e: e.activation(out=out, in_=in_, func=func, bias=bias, scale=scale), reads, writes)
        return self.op("act", lambda e: e.activation(out=out, in_=in_, func=func, bias=bias, scale=scale, accum_out=accum), reads, writes)

    def mm(self, out, lhsT, rhs, start=True, stop=True, reads=(), writes=()):
        return self.op("pe", lambda e: e.matmul(out, lhsT=lhsT, rhs=rhs, start=start, stop=stop), reads, writes)

    def tr(self, out, in_, ident, reads=(), writes=()):
        return self.op("pe", lambda e: e.transpose(out, in_, ident), reads, writes)


def setup_mod(b, pp, adaw_d, nblk):
    sc = b.sb([128, 16], name="sc")
    b.act(sc[:], pp[:, 59:75], AF.Silu, reads=[pp], writes=[sc])
    modT = b.sb([128, nblk, 2], name="modT")
    b.push()
    aw = b.sb([128, 8, nblk * 128], name="aw")
    for k in range(8):
        b.dma("sp" if k % 2 == 0 else "pool", aw[:, k, :], adaw_d[k * 128:(k + 1) * 128, :], reads=[adaw_d], writes=[aw])
    pm = b.ps([128, nblk, 2], name="pm")
    for blk in range(nblk):
        for k in range(8):
            b.mm(pm[:, blk, :], aw[:, k, blk * 128:(blk + 1) * 128], sc[:, 2 * k:2 * k + 2], start=(k == 0), stop=(k == 7),
                 reads=[aw, sc], writes=[pm])
    for r in range(2):
        b.tt("dve", modT[:, :, r], pm[:, :, r], pp[:, 43:43 + nblk], ALU.add, reads=[pm, pp], writes=[modT])
    b.dbg("modT", modT, modT[:], [128, nblk, 2])
    b.pop()
    return modT


def phase_A(b, xseq, win_d, pp, modT, cm, groups, dests, ncc):
    nc = b.nc
    ident = cm[:, 0:128]
    b.push()
    b.dbg("modT2", modT, modT[:], [128, 16, 2])
    gm = b.sb([128, 8, 2], name="gm")
    b.ts("dve", gm[:], modT[:, 8:16, :], 1.0, None, ALU.add, reads=[modT], writes=[gm])
    for r in range(2):
        b.tt("dve", gm[:, :, r], gm[:, :, r], pp[:, 35:43], ALU.mult, reads=[gm, pp], writes=[gm])
    W = [b.sb([128, 8, ncc * 128], BF16, name=f"W{r}") for r in range(2)]
    sW = b.sb([128, ncc, 2], name="sW")
    b.push()
    stg = b.sb([128, 8, ncc * 128], name="stg")
    for k in range(8):
        b.dma("sp" if k % 2 == 0 else "pool", stg[:, k, :], win_d[k * 128:(k + 1) * 128, :], reads=[win_d], writes=[stg])
    psw = b.ps([128, ncc, 2], name="psw")
    for cc in range(ncc):
        for k in range(8):
            b.mm(psw[:, cc, :], stg[:, k, cc * 128:(cc + 1) * 128], modT[:, k, :], start=(k == 0), stop=(k == 7),
                 reads=[stg, modT], writes=[psw])
    b.op("act", lambda e: e.copy(out=sW[:], in_=psw[:]), reads=[psw], writes=[sW])
    for r in range(2):
        for k in range(8):
            eng = "dve" if k % 2 == 0 else "pool"
            b.ts(eng, W[r][:, k, :], stg[:, k, :], gm[:, k, r:r + 1], None, ALU.mult, reads=[stg, gm], writes=[W[r]])
    b.dbg("gm", gm, gm[:], [128, 8, 2])
    b.dbg("modT3", modT, modT[:], [128, 16, 2])
    b.dbg("sW", sW, sW[:], [128, ncc, 2])
    b.dbg("W0", W[0], W[0][:], [128, 8, ncc * 128], BF16)
    b.pop()
    xt = [b.sb([128, 1024], name=f"xt{i}") for i in range(3)]
    junk = b.sb([128, 1024], BF16, name="junk")
    ss = [b.sb([128, 1], name=f"ss{i}") for i in range(3)]
    rstd = [b.sb([128, 1], name=f"rstd{i}") for i in range(3)]
    xT = [b.sb([128, 8, 512], BF16, name=f"xT{i}") for i in range(2)]
    psT = [b.ps([128, 4, 128], name=f"psT{i}") for i in range(2)]
    pso = [b.ps([128, 512], name=f"pso{i}") for i in range(3)]
    ob = [b.sb([128, 512], name=f"ob{i}") for i in range(4)]
    ti = 0
    oi = 0
    for gi, (t0, ntl, r) in enumerate(groups):
        G = ntl * 128
        xg = xT[gi % 2]
        for j in range(ntl):
            x = xt[ti % 3]
            s_ = ss[ti % 3]
            rs_ = rstd[ti % 3]
            ti += 1
            b.dma("sp", x[:], xseq[t0 + j * 128:t0 + (j + 1) * 128, :], reads=[xseq], writes=[x])
            b.act(junk[:], x[:], AF.Square, reads=[x], writes=[junk, s_], accum=s_[:])
            b.act(s_[:], s_[:], AF.Sqrt, bias=EPS, scale=1.0 / 1024, reads=[s_], writes=[s_])
            b.op("dve", lambda e, o=rs_, i=s_: e.reciprocal(o[:], i[:]), reads=[s_], writes=[rs_])
            b.ts("dve", x[:], x[:], rs_[:, 0:1], None, ALU.mult, reads=[x, rs_], writes=[x])
            if gi == 1 and j == 0:
                b.dbg("rstd", rs_, rs_[:], [128, 1])
                b.dbg("xs", x, x[:], [128, 1024])
            for half in range(2):
                pt = psT[half]
                for q in range(4):
                    k = half * 4 + q
                    b.tr(pt[:, q, :], x[:, k * 128:(k + 1) * 128], ident, reads=[x, cm], writes=[pt])
                if half == 0:
                    b.op("act", lambda e, o=xg, p=pt, j=j: e.copy(out=o[:, 0:4, j * 128:(j + 1) * 128], in_=p[:]), reads=[pt], writes=[xg])
                else:
                    b.op("dve", lambda e, o=xg, p=pt, j=j: e.tensor_copy(out=o[:, 4:8, j * 128:(j + 1) * 128], in_=p[:]), reads=[pt], writes=[xg])
        if gi == 1:
            b.dbg("xT", xg, xg[:], [128, 8, 512], BF16)
        for cc in range(ncc):
            po = pso[oi % 3]
            o = ob[oi % 4]
            for k in range(8):
                b.mm(po[:, 0:G], W[r][:, k, cc * 128:(cc + 1) * 128], xg[:, k, 0:G], start=(k == 0), stop=(k == 7),
                     reads=[W[r], xg], writes=[po])
            if oi % 2 == 0:
                b.act(o[:, 0:G], po[:, 0:G], AF.Identity, bias=sW[:, cc, r:r + 1], reads=[po, sW], writes=[o])
            else:
                b.ts("dve", o[:, 0:G], po[:, 0:G], sW[:, cc, r:r + 1], None, ALU.add, reads=[po, sW], writes=[o])
            dst, roff = dests[cc]
            b.dma("pool" if oi % 2 == 0 else "sp", dst[roff:roff + 128, t0:t0 + G], o[:, 0:G], reads=[o], writes=[dst])
            oi += 1
    b.pop()


class _Stop(Exception):
    pass


def _stage(n):
    if float(os.environ.get("RW_STOP", "99")) <= n:
        raise _Stop()


def rwkv_phase(b, *a, **k):
    try:
        _rwkv_phase(b, *a, **k)
    except _Stop:
        b.pop()


def _rwkv_phase(b, U, pp, w2a2_d, cm, blocks, yA, bon, vg, TT, dbg=None):
    ident = cm[:, 0:128]
    su = cm[:, 128:256]
    sl = cm[:, 256:384]
    ui = cm[:, 384:512]
    bones = cm[:, 512:640]
    hind = cm[:, 640:642]
    b.push()
    w2a2 = b.sb([128, 256], name="w2a2")
    b.dma("sp", w2a2[:], w2a2_d[:, :], reads=[w2a2_d], writes=[w2a2])
    identb = b.sb([128, 128], BF16, name="identb")
    b.op("dve", lambda e: e.tensor_copy(out=identb[:], in_=ident), reads=[cm], writes=[identb])
    hmu = b.sb([128, 4], name="hmu")
    omm = b.sb([128, 4], name="omm")
    omka = b.sb([128, 1], name="omka")
    b.ts("dve", hmu[:], pp[:, 0:4], 0.5, None, ALU.mult, reads=[pp], writes=[hmu])
    b.ts("dve", omm[:], pp[:, 0:4], -1.0, 1.0, ALU.mult, ALU.add, reads=[pp], writes=[omm])
    b.ts("dve", omka[:], pp[:, 7:8], -1.0, 1.0, ALU.mult, ALU.add, reads=[pp], writes=[omka])
    WB = 1024
    m01 = b.sb([128, WB], name="m01")
    b.op("dve", lambda e: e.memset(m01[:], 1.0), writes=[m01])
    b.op("dve", lambda e: e.memset(m01[:].rearrange("p (n l) -> p n l", l=128)[:, :, 0:1], 0.0), writes=[m01])
    M = b.sb([128, 128], name="M")
    b.op("dve", lambda e: e.memset(M[:], 0.0), writes=[M])
    Mt = b.sb([128, 128], name="Mt")

    def fm(name, dt=F32):
        return b.sb([128, WB], dt, name=name)

    ub = [b.sb([128, WB + 2], name=f"ub{g}") for g in range(4)]
    mixed = [fm(f"mx{g}") for g in range(4)]
    tmp = fm("tmp")
    tmp2 = fm("tmp2")
    twl = fm("twl")
    sgw = fm("sgw")
    av = fm("av")
    cum = fm("cum")
    Pe = fm("Pe")
    Qe = fm("Qe")
    Pm = fm("Pm")
    kk = fm("kk")
    kd = fm("kd")
    vmb = fm("vmb")
    KKt = fm("KKt")
    Rt = fm("Rt")
    Kh = fm("Kh")
    Bh = fm("Bh")
    Khz = [fm(f"Khz{h}") for h in range(2)]
    Bhz = [fm(f"Bhz{h}") for h in range(2)]
    KKz = [fm(f"KKz{h}") for h in range(2)]
    wmid = b.sb([128, 8], name="wmid")
    dA = b.sb([128, 8], name="dA")
    bsb = b.sb([2, WB], name="bsb")
    psL = b.ps([128, 512], name="psL0")

    class _V:
        r = psL.r

        def __getitem__(self, k):
            return psL[:, :].rearrange("p (n l) -> p n l", l=128)[k]
    psLv = _V()
    psT = b.ps([128, 4, 128], name="psTr")
    psA = b.ps([128, 4, 128], name="psA")
    psB = b.ps([128, 4, 128], name="psB")
    psI1 = b.ps([128, 4, 128], name="psI1")
    psI2 = b.ps([128, 4, 128], name="psI2")
    psM = b.ps([128, 512], name="psM")
    psUY = b.ps([128, 2, 128], name="psUY")
    tok4 = b.sb([128, 5, 128], name="tok4")
    SA = b.sb([128, 4, 128], name="SA")
    SB_ = b.sb([128, 4, 128], name="SB")
    Xc = [b.sb([128, 2, 128], name=f"Xc{i}") for i in range(2)]
    XTc = [b.sb([128, 2, 128], name=f"XTc{i}") for i in range(2)]
    Gc = [b.sb([128, 2, 128], name=f"Gc{i}") for i in range(2)]
    KT = b.sb([128, 128], name="KT")
    AV = b.sb([128, 128], name="AV")
    X2 = b.sb([128, 128], name="X2")
    Un = b.sb([128, 128], name="Un")
    t1 = b.sb([128, 128], name="t1")
    Ysb = [b.sb([128, 128], name=f"Ysb{i}") for i in range(2)]
    rowoff = [0, 128, 256, 512]
    ci_glob = 0
    for (t0, nch, seg0, seg1) in blocks:
        Wd = nch * 128
        lo = t0 - 1 if t0 > seg0 else t0
        hi = t0 + Wd + 1 if t0 + Wd < seg1 else t0 + Wd
        for g in range(4):
            if lo == t0:
                b.op("dve", lambda e, u=ub[g]: e.memset(u[:, 0:1], 0.0), writes=[ub[g]])
            if hi == t0 + Wd:
                b.op("dve", lambda e, u=ub[g], Wd=Wd: e.memset(u[:, Wd + 1:Wd + 2], 0.0), writes=[ub[g]])
            b.dma("sp", ub[g][:, 1 - (t0 - lo):1 - (t0 - lo) + (hi - lo)],
                  U[rowoff[g]:rowoff[g] + 128, lo:hi], reads=[U], writes=[ub[g]])
            b.tt("dve", tmp[:, 0:Wd], ub[g][:, 0:Wd], ub[g][:, 2:Wd + 2], ALU.add, reads=[ub[g]], writes=[tmp])
            b.ts("dve", tmp2[:, 0:Wd], ub[g][:, 1:Wd + 1], omm[:, g:g + 1], None, ALU.mult, reads=[ub[g], omm], writes=[tmp2])
            b.stt(mixed[g][:, 0:Wd], tmp[:, 0:Wd], hmu[:, g:g + 1], tmp2[:, 0:Wd], ALU.mult, ALU.add, reads=[tmp, tmp2, hmu], writes=[mixed[g]])
        rm, km, vm, lm = mixed
        b.dma("sp", vg[0:128, t0:t0 + Wd], vm[:, 0:Wd], reads=[vm], writes=[vg])
        b.op("act", lambda e, Wd=Wd: e.copy(out=vmb[:, 0:Wd], in_=vm[:, 0:Wd]), reads=[vm], writes=[vmb])
        _stage(1)
        b.act(twl[:, 0:Wd], lm[:, 0:Wd], AF.Tanh, reads=[lm], writes=[twl])
        for pc in range(0, Wd, 512):
            pw = min(512, Wd - pc)
            b.mm(psL[:, 0:pw], w2a2[:, 0:128], twl[:, pc:pc + pw], reads=[w2a2, twl], writes=[psL])
            b.act(sgw[:, pc:pc + pw], psL[:, 0:pw], AF.Sigmoid, bias=pp[:, 4:5], reads=[psL, pp], writes=[sgw])
            b.mm(psL[:, 0:pw], w2a2[:, 128:256], lm[:, pc:pc + pw], reads=[w2a2, lm], writes=[psL])
            b.act(av[:, pc:pc + pw], psL[:, 0:pw], AF.Sigmoid, bias=pp[:, 5:6], reads=[psL, pp], writes=[av])
        b.ts("dve", sgw[:, 0:Wd], sgw[:, 0:Wd], -EM05, None, ALU.mult, reads=[sgw], writes=[sgw])
        b.op("dve", lambda e, Wd=Wd: e.tensor_tensor_scan(cum[:, 0:Wd], m01[:, 0:Wd], sgw[:, 0:Wd], 0.0, ALU.mult, ALU.add),
             reads=[m01, sgw], writes=[cum])
        c3 = cum[:, 0:Wd].rearrange("p (n l) -> p n l", l=128)
        cbar = c3[:, :, 63:64].to_broadcast([128, nch, 128])

        def v3(t, Wd=Wd):
            return t[:, 0:Wd].rearrange("p (n l) -> p n l", l=128)
        b.act(wmid[:, 0:nch], c3[:, :, 63], AF.Exp, reads=[cum], writes=[wmid])
        b.act(dA[:, 0:nch], c3[:, :, 127], AF.Exp, reads=[cum], writes=[dA])
        b.tt("dve", v3(tmp), c3, cbar, ALU.subtract, reads=[cum], writes=[tmp])
        b.tt("dve", tmp2[:, 0:Wd], tmp[:, 0:Wd], sgw[:, 0:Wd], ALU.subtract, reads=[tmp, sgw], writes=[tmp2])
        b.act(Pe[:, 0:Wd], tmp[:, 0:Wd], AF.Exp, reads=[tmp], writes=[Pe])
        b.act(Qe[:, 0:Wd], tmp[:, 0:Wd], AF.Exp, scale=-1.0, reads=[tmp], writes=[Qe])
        b.act(Pm[:, 0:Wd], tmp2[:, 0:Wd], AF.Exp, reads=[tmp2], writes=[Pm])
        _stage(2)
        b.ts("dve", kk[:, 0:Wd], km[:, 0:Wd], pp[:, 6:7], None, ALU.mult, reads=[km, pp], writes=[kk])
        b.tt("dve", tmp[:, 0:Wd], kk[:, 0:Wd], kk[:, 0:Wd], ALU.mult, reads=[kk], writes=[tmp])
        for pc in range(0, Wd, 512):
            pw = min(512, Wd - pc)
            b.mm(psL[:, 0:pw], bones, tmp[:, pc:pc + pw], reads=[cm, tmp], writes=[psL])
            b.act(tmp2[:, pc:pc + pw], psL[:, 0:pw], AF.Sqrt, reads=[psL], writes=[tmp2])
        b.ts("dve", tmp2[:, 0:Wd], tmp2[:, 0:Wd], 1e-12, None, ALU.max, reads=[tmp2], writes=[tmp2])
        b.op("dve", lambda e, Wd=Wd: e.reciprocal(tmp2[:, 0:Wd], tmp2[:, 0:Wd]), reads=[tmp2], writes=[tmp2])
        b.tt("dve", kk[:, 0:Wd], kk[:, 0:Wd], tmp2[:, 0:Wd], ALU.mult, reads=[kk, tmp2], writes=[kk])
        b.ts("dve", tmp[:, 0:Wd], av[:, 0:Wd], pp[:, 7:8], omka[:, 0:1], ALU.mult, ALU.add, reads=[av, pp, omka], writes=[tmp])
        b.tt("dve", kd[:, 0:Wd], km[:, 0:Wd], tmp[:, 0:Wd], ALU.mult, reads=[km, tmp], writes=[kd])
        b.stt(tmp[:, 0:Wd], rm[:, 0:Wd], pp[:, 8:9], kd[:, 0:Wd], ALU.mult, ALU.mult, reads=[rm, kd, pp], writes=[tmp])
        for pc in range(0, Wd, 512):
            pw = min(512, Wd - pc)
            b.mm(psL[0:2, 0:pw], hind, tmp[:, pc:pc + pw], reads=[cm, tmp], writes=[psL])
            b.op("act", lambda e, pc=pc, pw=pw: e.copy(out=bsb[:, pc:pc + pw], in_=psL[0:2, 0:pw]), reads=[psL], writes=[bsb])
        b.dma("sp", bon[:, t0:t0 + Wd], bsb[:, 0:Wd], reads=[bsb], writes=[bon])
        _stage(3)
        b.tt("dve", KKt[:, 0:Wd], kk[:, 0:Wd], Pm[:, 0:Wd], ALU.mult, reads=[kk, Pm], writes=[KKt])
        b.tt("dve", Rt[:, 0:Wd], rm[:, 0:Wd], Pe[:, 0:Wd], ALU.mult, reads=[rm, Pe], writes=[Rt])
        b.tt("dve", Kh[:, 0:Wd], kd[:, 0:Wd], Qe[:, 0:Wd], ALU.mult, reads=[kd, Qe], writes=[Kh])
        b.tt("dve", tmp[:, 0:Wd], kk[:, 0:Wd], av[:, 0:Wd], ALU.mult, reads=[kk, av], writes=[tmp])
        b.tt("dve", Bh[:, 0:Wd], tmp[:, 0:Wd], Qe[:, 0:Wd], ALU.mult, reads=[tmp, Qe], writes=[Bh])
        for h in range(2):
            b.ts("dve", Khz[h][:, 0:Wd], Kh[:, 0:Wd], hind[:, h:h + 1], None, ALU.mult, reads=[Kh, cm], writes=[Khz[h]])
            b.ts("dve", Bhz[h][:, 0:Wd], Bh[:, 0:Wd], hind[:, h:h + 1], None, ALU.mult, reads=[Bh, cm], writes=[Bhz[h]])
            b.ts("dve", KKz[h][:, 0:Wd], KKt[:, 0:Wd], hind[:, h:h + 1], None, ALU.mult, reads=[KKt, cm], writes=[KKz[h]])
        P3 = v3(Pe)
        _stage(4)
        for j in range(nch):
            c0 = j * 128
            cs = slice(c0, c0 + 128)
            for q, src in enumerate((Kh, Bh, vm, KKz[0])):
                b.tr(psT[:, q, :], src[:, cs], ident, reads=[src, cm], writes=[psT])
            b.tr(psM[:, 384:512], KKz[1][:, cs], ident, reads=[KKz[1], cm], writes=[psM])
            b.op("dve", lambda e: e.tensor_copy(out=tok4[:, 0:4, :], in_=psT[:]), reads=[psT], writes=[tok4])
            b.op("dve", lambda e: e.tensor_copy(out=tok4[:, 4, :], in_=psM[:, 384:512]), reads=[psM], writes=[tok4])
            if ci_glob == 0:
                b.dbg("tok4", tok4, tok4[:], [128, 5, 128])
                b.dbg("Kh", Kh, Kh[:, 0:128], [128, 128])
                b.dbg("KKt", KKt, KKt[:, 0:128], [128, 128])
                b.dbg("Bh", Bh, Bh[:, 0:128], [128, 128])
                b.dbg("Rt", Rt, Rt[:, 0:128], [128, 128])
            _stage(5)
            Kh_t, Bh_t, V_t = (tok4[:, q, :] for q in range(3))
            KKz_t = [tok4[:, 3, :], tok4[:, 4, :]]
            for h in range(2):
                b.mm(psA[:, 2 * h, :], Khz[h][:, cs], KKt[:, cs], reads=[Khz[h], KKt], writes=[psA])
                b.mm(psA[:, 2 * h + 1, :], Bhz[h][:, cs], KKt[:, cs], reads=[Bhz[h], KKt], writes=[psA])
                b.mm(psB[:, 2 * h, :], Khz[h][:, cs], Rt[:, cs], reads=[Khz[h], Rt], writes=[psB])
                b.mm(psB[:, 2 * h + 1, :], Bhz[h][:, cs], Rt[:, cs], reads=[Bhz[h], Rt], writes=[psB])
                b.mm(psL[:, 128 * h:128 * h + 128], KKz[h][:, cs], Bh[:, cs], reads=[Bh, KKz[h]], writes=[psL])
            b.tt("dve", SA[:], psA[:], su.unsqueeze(1).to_broadcast([128, 4, 128]), ALU.mult, reads=[psA, cm], writes=[SA])
            b.tt("dve", SB_[:], psB[:], ui.unsqueeze(1).to_broadcast([128, 4, 128]), ALU.mult, reads=[psB, cm], writes=[SB_])
            if ci_glob == 0:
                b.dbg("SA", SA, SA[:], [128, 4, 128])
                b.dbg("SB", SB_, SB_[:], [128, 4, 128])
            _stage(6)
            Xa, XTa, Ga = Xc[0], XTc[0], Gc[0]
            b.tt("dve", XTa[:], psL[:, 0:256].rearrange("p (n l) -> p n l", l=128), sl.unsqueeze(1).to_broadcast([128, 2, 128]), ALU.mult, reads=[psL, cm], writes=[XTa])
            b.op("dve", lambda e, Xa=Xa: e.tensor_copy(out=Xa[:], in_=SA[:, 1:4:2, :]), reads=[SA], writes=[Xa])
            b.tt("dve", Ga[:], ident.unsqueeze(1).to_broadcast([128, 2, 128]), SA[:, 1:4:2, :], ALU.subtract, reads=[SA, cm], writes=[Ga])
            _stage(6.1)
            cur = 0
            for lvl in range(1, 7):
                Xa, XTa, Ga = Xc[cur], XTc[cur], Gc[cur]
                Xn, XTn, Gn = Xc[1 - cur], XTc[1 - cur], Gc[1 - cur]
                pI = psA if lvl % 2 == 1 else psB
                for h in range(2):
                    if lvl < 6:
                        b.mm(pI[:, h, :], XTa[:, h, :], Xa[:, h, :], reads=[XTa, Xa], writes=[pI])
                    b.mm(pI[:, 2 + h, :], Xa[:, h, :], XTa[:, h, :], reads=[XTa, Xa], writes=[pI])
                _stage(6.2)
                if lvl < 6:
                    b.op("dve", lambda e, Xn=Xn, pI=pI: e.tensor_copy(out=Xn[:], in_=pI[:, 0:2, :]), reads=[pI], writes=[Xn])
                b.op("dve", lambda e, XTn=XTn, pI=pI: e.tensor_copy(out=XTn[:], in_=pI[:, 2:4, :]), reads=[pI], writes=[XTn])
                _stage(6.3)
                for h in range(2):
                    b.mm(psI2[:, h, :], XTn[:, h, :], Ga[:, h, :], reads=[XTn, Ga], writes=[psI2])
                b.tt("dve", Gn[:], Ga[:], psI2[:, 0:2, :], ALU.add, reads=[Ga, psI2], writes=[Gn])
                cur = 1 - cur
                _stage(6.4 + 0.01 * lvl)
            G = Gc[cur]
            if ci_glob == 0:
                b.dbg("G", G, G[:], [128, 2, 128])
            _stage(7)
            for h in range(2):
                hs = slice(64 * h, 64 * h + 64)
                b.mm(psI2[:, 2, :], KKz_t[h], G[:, h, :], start=(h == 0), stop=(h == 1), reads=[tok4, G], writes=[psI2])
                b.mm(psM[:, hs], SA[:, 2 * h, :], V_t[:, hs], reads=[SA, tok4], writes=[psM])
            b.op("dve", lambda e: e.tensor_copy(out=KT[:], in_=psI2[:, 2, :]), reads=[psI2], writes=[KT])
            b.op("dve", lambda e: e.tensor_copy(out=AV[:], in_=psM[:, 0:128]), reads=[psM], writes=[AV])
            for h in range(2):
                hs = slice(64 * h, 64 * h + 64)
                b.mm(psM[:, 128 + 64 * h:128 + 64 * h + 64], G[:, h, :], AV[:, hs], reads=[G, AV], writes=[psM])
            b.op("dve", lambda e: e.tensor_copy(out=X2[:], in_=psM[:, 128:256]), reads=[psM], writes=[X2])
            _stage(8)
            b.ts("dve", Mt[:], M[:], wmid[:, j:j + 1], None, ALU.mult, reads=[M, wmid], writes=[Mt])
            b.mm(psUY[:, 0, :], KT[:], Mt[:], reads=[KT, Mt], writes=[psUY])
            b.stt(Un[:], psUY[:, 0, :], -1.0, X2[:], ALU.mult, ALU.subtract, reads=[psUY, X2], writes=[Un])
            b.mm(psM[:, 256:384], Kh_t, V_t, start=True, stop=False, reads=[tok4], writes=[psM])
            b.mm(psM[:, 256:384], Bh_t, Un[:], start=False, stop=True, reads=[tok4, Un], writes=[psM])
            b.stt(t1[:], psM[:, 256:384], P3[:, j, 127:128], bones, ALU.mult, ALU.mult, reads=[psM, Pe, cm], writes=[t1])
            b.mm(psUY[:, 1, :], Rt[:, cs], Mt[:], start=True, stop=False, reads=[Rt, Mt], writes=[psUY])
            for h in range(2):
                hs = slice(64 * h, 64 * h + 64)
                b.mm(psUY[:, 1, hs], SB_[:, 2 * h, :], V_t[:, hs], start=False, stop=False, reads=[SB_, tok4], writes=[psUY])
                b.mm(psUY[:, 1, hs], SB_[:, 2 * h + 1, :], Un[:, hs], start=False, stop=(h == 1), reads=[SB_, Un], writes=[psUY])
            b.stt(M[:], M[:], dA[:, j:j + 1], t1[:], ALU.mult, ALU.add, reads=[M, dA, t1], writes=[M])
            Y = Ysb[ci_glob % 2]
            b.op("dve", lambda e, Y=Y: e.tensor_copy(out=Y[:], in_=psUY[:, 1, :]), reads=[psUY], writes=[Y])
            b.dma("sp", yA[t0 + c0:t0 + c0 + 128, :], Y[:], reads=[Y], writes=[yA])
            ci_glob += 1
    b.pop()


def mamba_phase(b, U, pp, cm, sel_d, blocks, yB, xsB, TT):
    ident = cm[:, 0:128]
    ui = cm[:, 384:512]
    b.push()
    sel = b.sb([128, 4, 128], name="sel")
    b.dma("sp", sel[:], sel_d[:, :].rearrange("p (h l) -> p h l", l=128), reads=[sel_d], writes=[sel])
    WB = 1024
    m01 = b.sb([128, WB], name="m01")
    b.op("dve", lambda e: e.memset(m01[:], 1.0), writes=[m01])
    b.op("dve", lambda e: e.memset(m01[:].rearrange("p (n l) -> p n l", l=128)[:, :, 0:1], 0.0), writes=[m01])
    Aneg = b.sb([4, 1], name="Aneg")
    b.act(Aneg[:], pp[0:4, 34:35], AF.Exp, reads=[pp], writes=[Aneg])
    b.ts("dve", Aneg[:], Aneg[:], -1.0, None, ALU.mult, reads=[Aneg], writes=[Aneg])
    ST = b.sb([128, 256], name="ST")
    b.op("dve", lambda e: e.memset(ST[:], 0.0), writes=[ST])
    ub = [b.sb([128, WB + 4], name=f"mub{q}") for q in range(4)]
    cv = [b.sb([128, WB], name=f"cv{q}") for q in range(4)]
    acc = b.sb([128, WB], name="acc")
    dtb = b.sb([128, WB], name="dtb")
    dta = b.sb([128, WB], name="dta")
    acs = b.sb([128, WB], name="acs")
    for t_ in (dtb, dta, acs):
        b.op("dve", lambda e, t_=t_: e.memset(t_[:], 0.0), writes=[t_])
    psT2 = b.ps([128, 2, 128], name="mpsT2")
    psT = b.ps([128, 4, 128], name="mpsT")
    psBC = b.ps([128, 4, 128], name="mpsBC")
    psCB = b.ps([128, 128], name="mpsCB")
    psY = b.ps([128, 256], name="mpsY")
    psO = b.ps([128, 256], name="mpsO")
    psS = b.ps([128, 256], name="mpsS")
    tok = b.sb([128, 3, 128], name="mtok")
    sm = b.sb([128, 8], name="msm")
    last = b.sb([128, 4], name="mlast")
    dd = b.sb([128, 4], name="mdd")
    te = b.sb([128, 4], name="mte")
    dec = b.sb([128, 4], name="mdec")
    eA = b.sb([128, 4], name="meA")
    seg = b.sb([128, 4, 128], name="mseg")
    Ee = b.sb([128, 4, 128], name="mE")
    CBm = b.sb([128, 128], name="mCBm")
    Wm = b.sb([128, 4, 128], name="mWm")
    xdt = b.sb([128, 256], name="mxdt")
    xw = b.sb([128, 256], name="mxw")
    ysb = b.sb([128, 256], name="mysb")
    yo = [b.sb([128, 256], name=f"myo{i}") for i in range(2)]
    xo = [b.sb([128, 256], name=f"mxo{i}") for i in range(2)]
    rowoff = [640, 768, 896, 1024]
    ci = 0
    for (t0, nch, seg0, seg1) in blocks:
        Wd = nch * 128
        lo = max(t0 - 2, seg0)
        hi = min(t0 + Wd + 2, seg1)
        for q in range(4):
            u = ub[q]
            if lo > t0 - 2:
                b.op("dve", lambda e, u=u: e.memset(u[:, 0:2], 0.0), writes=[u])
            if hi < t0 + Wd + 2:
                b.op("dve", lambda e, u=u, Wd=Wd: e.memset(u[:, Wd + 2:Wd + 4], 0.0), writes=[u])
            c_lo = lo - (t0 - 2)
            b.dma("sp", u[:, c_lo:c_lo + (hi - lo)], U[rowoff[q]:rowoff[q] + 128, lo:hi], reads=[U], writes=[u])
            wc = 9 + 5 * q
            b.ts("dve", acc[:, 0:Wd], u[:, 0:Wd], pp[:, wc:wc + 1], None, ALU.mult, reads=[u, pp], writes=[acc])
            for k in range(1, 5):
                b.stt(acc[:, 0:Wd], u[:, k:k + Wd], pp[:, wc + k:wc + k + 1], acc[:, 0:Wd], ALU.mult, ALU.add,
                      reads=[u, pp, acc], writes=[acc])
            b.act(cv[q][:, 0:Wd], acc[:, 0:Wd], AF.Silu, bias=pp[:, 29 + q:30 + q], reads=[acc, pp], writes=[cv[q]])
        b.dma("sp", dtb[0:4, 0:Wd], U[1408:1412, t0:t0 + Wd], reads=[U], writes=[dtb])
        b.act(dtb[0:4, 0:Wd], dtb[0:4, 0:Wd], AF.Exp, bias=pp[0:4, 33:34], reads=[dtb, pp], writes=[dtb])
        b.act(dtb[0:4, 0:Wd], dtb[0:4, 0:Wd], AF.Ln, bias=1.0, reads=[dtb], writes=[dtb])
        b.ts("dve", dta[0:4, 0:Wd], dtb[0:4, 0:Wd], Aneg[:, 0:1], None, ALU.mult, reads=[dtb, Aneg], writes=[dta])
        b.op("dve", lambda e, Wd=Wd: e.tensor_tensor_scan(acs[0:4, 0:Wd], m01[0:4, 0:Wd], dta[0:4, 0:Wd], 0.0, ALU.mult, ALU.add),
             reads=[m01, dta], writes=[acs])
        xs0, xs1, Bc, Cc = cv
        for j in range(nch):
            c0 = j * 128
            cs = slice(c0, c0 + 128)
            b.tr(psT[:, 0, :], xs0[:, cs], ident, reads=[xs0, cm], writes=[psT])
            b.tr(psT[:, 1, :], xs1[:, cs], ident, reads=[xs1, cm], writes=[psT])
            b.tr(psT[:, 2, :], Bc[:, cs], ident, reads=[Bc, cm], writes=[psT])
            b.tr(psT2[:, 0, :], acs[:, cs], ident, reads=[acs, cm], writes=[psT2])
            b.tr(psT2[:, 1, :], dtb[:, cs], ident, reads=[dtb, cm], writes=[psT2])
            b.op("dve", lambda e: e.tensor_copy(out=tok[:], in_=psT[:, 0:3, :]), reads=[psT], writes=[tok])
            b.op("dve", lambda e: e.tensor_copy(out=sm[:].rearrange("p (a q) -> p a q", q=4), in_=psT2[:, :, 0:4]), reads=[psT2], writes=[sm])
            x_tok = tok[:, 0:2, :]
            B_tok = tok[:, 2, :]
            for h in range(4):
                b.mm(psBC[:, h, :], sel[:, h, :], acs[:, cs], reads=[sel, acs], writes=[psBC])
            for h in range(4):
                b.ts("dve", seg[:, h, :], psBC[:, h, :], sm[:, h:h + 1], 0.0, ALU.subtract, ALU.min, reads=[psBC, sm], writes=[seg])
            b.op("dve", lambda e: e.tensor_copy(out=last[:], in_=psBC[:, :, 127]), reads=[psBC], writes=[last])
            b.act(Ee[:], seg[:], AF.Exp, reads=[seg], writes=[Ee])
            b.mm(psCB[:], Bc[:, cs], Cc[:, cs], reads=[Bc, Cc], writes=[psCB])
            b.tt("dve", CBm[:], psCB[:], ui, ALU.mult, reads=[psCB, cm], writes=[CBm])
            b.tt("dve", Wm[:], Ee[:], CBm[:].unsqueeze(1).to_broadcast([128, 4, 128]), ALU.mult, reads=[Ee, CBm], writes=[Wm])
            xt3 = tok[:, 0:2, :].rearrange("p a (h2 q) -> p (a h2) q", q=64)
            b.tt("dve", xdt[:].rearrange("p (h q) -> p h q", q=64), xt3, sm[:, 4:8].unsqueeze(2).to_broadcast([128, 4, 64]),
                 ALU.mult, reads=[tok, sm], writes=[xdt])
            for h in range(4):
                b.mm(psY[:, 64 * h:64 * h + 64], Wm[:, h, :], xdt[:, 64 * h:64 * h + 64], reads=[Wm, xdt], writes=[psY])
            b.mm(psO[:], Cc[:, cs], ST[:], reads=[Cc, ST], writes=[psO])
            b.act(eA[:], sm[:, 0:4], AF.Exp, reads=[sm], writes=[eA])
            b.op("dve", lambda e: e.tensor_copy(out=ysb[:], in_=psY[:]), reads=[psY], writes=[ysb])
            y = yo[ci % 2]
            b.tt("dve", y[:].rearrange("p (h q) -> p h q", q=64), psO[:].rearrange("p (h q) -> p h q", q=64),
                 eA[:].unsqueeze(2).to_broadcast([128, 4, 64]), ALU.mult, reads=[psO, eA], writes=[y])
            b.tt("dve", y[:], y[:], ysb[:], ALU.add, reads=[y, ysb], writes=[y])
            b.dma("sp", yB[t0 + c0:t0 + c0 + 128, :], y[:], reads=[y], writes=[yB])
            xout = xo[ci % 2]
            b.op("dve", lambda e, xout=xout: e.tensor_copy(out=xout[:].rearrange("p (a l) -> p a l", l=128), in_=tok[:, 0:2, :]),
                 reads=[tok], writes=[xout])
            b.dma("sp", xsB[t0 + c0:t0 + c0 + 128, :], xout[:], reads=[xout], writes=[xsB])
            b.tt("dve", dd[:], last[:], sm[:, 0:4], ALU.subtract, reads=[last, sm], writes=[dd])
            b.act(te[:], dd[:], AF.Exp, reads=[dd], writes=[te])
            b.act(dec[:], last[:], AF.Exp, reads=[last], writes=[dec])
            b.tt("dve", xw[:].rearrange("p (h q) -> p h q", q=64), xdt[:].rearrange("p (h q) -> p h q", q=64),
                 te[:].unsqueeze(2).to_broadcast([128, 4, 64]), ALU.mult, reads=[xdt, te], writes=[xw])
            b.mm(psS[:], B_tok, xw[:], reads=[tok, xw], writes=[psS])
            b.tt("dve", ST[:].rearrange("p (h q) -> p h q", q=64), ST[:].rearrange("p (h q) -> p h q", q=64),
                 dec[:].unsqueeze(2).to_broadcast([128, 4, 64]), ALU.mult, reads=[ST, dec], writes=[ST])
            b.tt("dve", ST[:], ST[:], psS[:], ALU.add, reads=[ST, psS], writes=[ST])
            ci += 1
    b.pop()


GN_EPS = 64e-5


def out_even(b, tiles, d, cm, final=False):
    ident = cm[:, 0:128]
    b.push()
    vb = b.sb([128, 5120], name="vb")
    for i in range(5):
        b.dma("sp", vb[:, i * 1024:(i + 1) * 1024], d["vecs"][0, i * 1024:(i + 1) * 1024].partition_broadcast(128),
              reads=[d["vecs"]], writes=[vb])
    lnw, lnb = vb[:, 0:512], vb[:, 512:1024]
    dvec, nw, adab = vb[:, 1024:2048], vb[:, 2048:3072], vb[:, 4096:5120]
    cvt = b.sb([128, 16], name="cvt")
    b.dma("sp", cvt[:], d["cv"][:, :], reads=[d["cv"]], writes=[cvt])
    sc = b.sb([128, 16], name="osc")
    b.act(sc[:], cvt[:], AF.Silu, reads=[cvt], writes=[sc])
    gate_b = [b.sb([128, 1024], name=f"gate{r}") for r in range(2)]
    Wg = [b.sb([128, 12, 1024], BF16, name=f"Wg{r}") for r in range(2)]
    b.push()
    aw = b.sb([128, 8, 1024], name="oaw")
    for k in range(8):
        b.dma("sp", aw[:, k, :], d["adawg"][k * 128:(k + 1) * 128, :], reads=[d["adawg"]], writes=[aw])
    scb = b.sb([128, 8, 128], name="scb")
    pg = b.ps([128, 512], name="pg")
    for r in range(2):
        b.op("dve", lambda e, r=r: e.tensor_copy(out=scb[:], in_=sc[:, r:16:2].unsqueeze(2).to_broadcast([128, 8, 128])),
             reads=[sc], writes=[scb])
        for half in range(2):
            hsl = slice(half * 512, half * 512 + 512)
            for k in range(8):
                b.mm(pg[:], scb[:, k, :], aw[:, k, hsl], start=(k == 0), stop=(k == 7), reads=[scb, aw], writes=[pg])
            b.tt("dve", gate_b[r][:, hsl], pg[:], adab[:, hsl], ALU.add, reads=[pg, vb], writes=[gate_b[r]])
    b.pop()
    b.push()
    stg = b.sb([128, 12, 1024], name="ostg")
    for k in range(12):
        b.dma("sp", stg[:, k, :], d["wout"][k * 128:(k + 1) * 128, :], reads=[d["wout"]], writes=[stg])
    for r in range(2):
        for k in range(12):
            b.tt("dve", Wg[r][:, k, :], stg[:, k, :], gate_b[r][:], ALU.mult, reads=[stg, gate_b[r]], writes=[Wg[r]])
    b.pop()
    yA = b.sb([128, 1024], name="oyA")
    bon = b.sb([128, 16], name="obon")
    v = b.sb([128, 512], name="ov")
    ga = b.sb([128, 512], name="oga")
    yB = b.sb([128, 2048], name="oyB")
    xsm = b.sb([128, 1024], name="oxsm")
    zz = b.sb([128, 1024], name="oz")
    xres = b.sb([128, 1024], name="oxres")
    ycat = b.sb([128, 1536], name="ycat")
    t5 = b.sb([128, 512], name="ot5")
    t10 = b.sb([128, 1024], name="ot10")
    st = b.sb([128, 64], name="ost")
    yT = b.sb([128, 12, 128], BF16, name="oyT")
    xo = b.sb([128, 1024], name="oxo")
    psT = [b.ps([128, 4, 128], name=f"opsT{i}") for i in range(3)]
    pso = [b.ps([128, 512], name=f"opso{i}") for i in range(2)]
    for (r0, r) in tiles:
        rs = slice(r0, r0 + 128)
        for tl, nm in ((yA, "yA"), (bon, "bon"), (v, "v"), (ga, "ga"), (yB, "yB"), (xsm, "xsm"), (zz, "z"), (xres, "xres")):
            b.dma("sp", tl[:], d[nm][rs, :], reads=[d[nm]], writes=[tl])
        y = ycat[:, 0:512]
        y3 = y.rearrange("p (h q) -> p h q", q=64)
        b.tt("dve", y, yA[:, 0:512], yA[:, 512:1024], ALU.add, reads=[yA], writes=[ycat])
        b.op("dve", lambda e: e.tensor_reduce(out=st[:, 0:8], in_=y3, axis=AX.X, op=ALU.add), reads=[ycat], writes=[st])
        b.tt("dve", t5[:], y, y, ALU.mult, reads=[ycat], writes=[t5])
        b.op("dve", lambda e: e.tensor_reduce(out=st[:, 8:16], in_=t5[:].rearrange("p (h q) -> p h q", q=64), axis=AX.X, op=ALU.add),
             reads=[t5], writes=[st])
        b.ts("dve", st[:, 0:8], st[:, 0:8], 1.0 / 64, None, ALU.mult, reads=[st], writes=[st])
        b.tt("dve", st[:, 16:24], st[:, 0:8], st[:, 0:8], ALU.mult, reads=[st], writes=[st])
        b.stt(st[:, 8:16], st[:, 8:16], 1.0 / 64, st[:, 16:24], ALU.mult, ALU.subtract, reads=[st], writes=[st])
        b.act(st[:, 8:16], st[:, 8:16], AF.Sqrt, bias=GN_EPS, reads=[st], writes=[st])
        b.op("dve", lambda e: e.reciprocal(st[:, 8:16], st[:, 8:16]), reads=[st], writes=[st])
        b.tt("dve", y3, y3, st[:, 0:8].unsqueeze(2).to_broadcast([128, 8, 64]), ALU.subtract, reads=[ycat, st], writes=[ycat])
        b.tt("dve", y3, y3, st[:, 8:16].unsqueeze(2).to_broadcast([128, 8, 64]), ALU.mult, reads=[ycat, st], writes=[ycat])
        b.tt("dve", y, y, lnw, ALU.mult, reads=[ycat, vb], writes=[ycat])
        b.tt("dve", y, y, lnb, ALU.add, reads=[ycat, vb], writes=[ycat])
        b.tt("dve", st[:, 24:32], bon[:, 0:8], bon[:, 8:16], ALU.add, reads=[bon], writes=[st])
        b.tt("dve", t5[:].rearrange("p (h q) -> p h q", q=64), v[:].rearrange("p (h q) -> p h q", q=64),
             st[:, 24:32].unsqueeze(2).to_broadcast([128, 8, 64]), ALU.mult, reads=[v, st], writes=[t5])
        b.tt("dve", y, y, t5[:], ALU.add, reads=[ycat, t5], writes=[ycat])
        b.act(ga[:], ga[:], AF.Silu, reads=[ga], writes=[ga])
        b.tt("dve", y, y, ga[:], ALU.mult, reads=[ycat, ga], writes=[ycat])
        yb = ycat[:, 512:1536]
        b.tt("dve", yb, yB[:, 0:1024], yB[:, 1024:2048], ALU.add, reads=[yB], writes=[ycat])
        b.tt("dve", t10[:], xsm[:], dvec, ALU.mult, reads=[xsm, vb], writes=[t10])
        b.tt("dve", yb, yb, t10[:], ALU.add, reads=[ycat, t10], writes=[ycat])
        b.act(zz[:], zz[:], AF.Silu, reads=[zz], writes=[zz])
        b.tt("dve", yb, yb, zz[:], ALU.mult, reads=[ycat, zz], writes=[ycat])
        b.tt("dve", t10[:], yb, yb, ALU.mult, reads=[ycat], writes=[t10])
        b.op("dve", lambda e: e.tensor_reduce(out=st[:, 32:34], in_=t10[:].rearrange("p (g q) -> p g q", q=512), axis=AX.X, op=ALU.add),
             reads=[t10], writes=[st])
        b.act(st[:, 32:34], st[:, 32:34], AF.Sqrt, bias=EPS, scale=1.0 / 512, reads=[st], writes=[st])
        b.op("dve", lambda e: e.reciprocal(st[:, 32:34], st[:, 32:34]), reads=[st], writes=[st])
        b.tt("dve", yb.rearrange("p (g q) -> p g q", q=512), yb.rearrange("p (g q) -> p g q", q=512),
             st[:, 32:34].unsqueeze(2).to_broadcast([128, 2, 512]), ALU.mult, reads=[ycat, st], writes=[ycat])
        b.tt("dve", yb, yb, nw, ALU.mult, reads=[ycat, vb], writes=[ycat])
        for k in range(12):
            pt = psT[k // 4]
            b.tr(pt[:, k % 4, :], ycat[:, k * 128:(k + 1) * 128], ident, reads=[ycat, cm], writes=[pt])
        for i3 in range(3):
            b.op("dve", lambda e, i3=i3: e.tensor_copy(out=yT[:, 4 * i3:4 * i3 + 4, :], in_=psT[i3][:]), reads=[psT[i3]], writes=[yT])
        for half in range(2):
            hsl = slice(half * 512, half * 512 + 512)
            po = pso[half]
            for k in range(12):
                b.mm(po[:], yT[:, k, :], Wg[r][:, k, hsl], start=(k == 0), stop=(k == 11), reads=[yT, Wg[r]], writes=[po])
            b.tt("dve", xo[:, hsl], po[:], xres[:, hsl], ALU.add, reads=[po, xres], writes=[xo])
        if final:
            b.act(t10[:], xo[:], AF.Square, reads=[xo], writes=[t10, st], accum=st[:, 40:41])
            b.act(st[:, 40:41], st[:, 40:41], AF.Sqrt, bias=EPS, scale=1.0 / 1024, reads=[st], writes=[st])
            b.op("dve", lambda e: e.reciprocal(st[:, 40:41], st[:, 40:41]), reads=[st], writes=[st])
            b.stt(xo[:], xo[:], st[:, 40:41], vb[:, 3072:4096], ALU.mult, ALU.mult, reads=[xo, st, vb], writes=[xo])
        b.dma("sp", d["xo"][rs, :], xo[:], reads=[xo], writes=[d["xo"]])
    b.pop()


C_ID, C_UI, C_BONES, C_SEL32, C_RQ, C_RK, C_E0, C_E1, C_SU, C_E64 = 0, 128, 256, 384, 512, 640, 768, 896, 1024, 1152


def mlstm_phase(b, U, pp, cmo, blocks, hC, TT):
    ident = cmo[:, C_ID:C_ID + 128]
    ui = cmo[:, C_UI:C_UI + 128]
    sel32 = cmo[:, C_SEL32:C_SEL32 + 128]
    b.push()
    WB = 1024
    m01 = b.sb([128, WB], name="lm01")
    b.op("dve", lambda e: e.memset(m01[:], 1.0), writes=[m01])
    b.op("dve", lambda e: e.memset(m01[:].rearrange("p (n l) -> p n l", l=128)[:, :, 0:1], 0.0), writes=[m01])
    CT1 = b.sb([128, 2, 257], name="CT1")
    b.op("dve", lambda e: e.memset(CT1[:], 0.0), writes=[CT1])
    ub = [b.sb([128, WB + 4], name=f"lub{q}") for q in range(4)]
    cv = [b.sb([128, WB], name=f"lcv{q}") for q in range(4)]
    vv = [b.sb([128, WB], name=f"lvv{q}") for q in range(2)]
    acc = b.sb([128, WB], name="lacc")
    gt = b.sb([128, WB], name="lgt")
    b.op("dve", lambda e: e.memset(gt[:], 0.0), writes=[gt])
    gt2 = b.sb([128, WB], name="lgt2")
    b.op("dve", lambda e: e.memset(gt2[:], 0.0), writes=[gt2])
    nfb = b.sb([128, 1], name="nfb")
    b.ts("dve", nfb[:], pp[:, 24:25], -1.0, None, ALU.mult, reads=[pp], writes=[nfb])
    psT = b.ps([128, 4, 128], name="lpsT")
    psG = b.ps([128, 3, 128], name="lpsG")
    psS = b.ps([128, 128], name="lpsS")
    psN = b.ps([128, 257], name="lpsN")
    psI = b.ps([128, 257], name="lpsI")
    psC = [b.ps([128, 257], name=f"lpsC{a}") for a in range(2)]
    ktok = b.sb([128, 256], name="lktok")
    v1 = b.sb([128, 257], name="lv1")
    b.op("dve", lambda e: e.memset(v1[:, 256:257], 1.0), writes=[v1])
    gtok = b.sb([128, 128], name="lgtok")
    sm = b.sb([128, 8], name="lsm")
    lw = b.sb([128, 128], name="llw")
    Ee = b.sb([128, 128], name="lE")
    WT = b.sb([128, 128], name="lWT")
    nsb = b.sb([128, 257], name="lnsb")
    tot = b.sb([128, 257], name="ltot")
    kw = b.sb([128, 256], name="lkw")
    ho = [b.sb([128, 256], name=f"lho{i}") for i in range(2)]
    rowoff = [0, 128, 256, 384]
    ci = 0
    for (t0, nch, seg0, seg1) in blocks:
        Wd = nch * 128
        lo = max(t0 - 2, seg0)
        hi = min(t0 + Wd + 2, seg1)
        for q in range(4):
            u = ub[q]
            if lo > t0 - 2:
                b.op("dve", lambda e, u=u: e.memset(u[:, 0:2], 0.0), writes=[u])
            if hi < t0 + Wd + 2:
                b.op("dve", lambda e, u=u, Wd=Wd: e.memset(u[:, Wd + 2:Wd + 4], 0.0), writes=[u])
            c_lo = lo - (t0 - 2)
            b.dma("sp", u[:, c_lo:c_lo + (hi - lo)], U[rowoff[q]:rowoff[q] + 128, lo:hi], reads=[U], writes=[u])
            wc = 5 * q
            b.ts("dve", acc[:, 0:Wd], u[:, 0:Wd], pp[:, wc:wc + 1], None, ALU.mult, reads=[u, pp], writes=[acc])
            for k in range(1, 5):
                b.stt(acc[:, 0:Wd], u[:, k:k + Wd], pp[:, wc + k:wc + k + 1], acc[:, 0:Wd], ALU.mult, ALU.add,
                      reads=[u, pp, acc], writes=[acc])
            b.act(cv[q][:, 0:Wd], acc[:, 0:Wd], AF.Silu, bias=pp[:, 20 + q:21 + q], reads=[acc, pp], writes=[cv[q]])
            if q < 2:
                b.ts("dve", cv[q][:, 0:Wd], cv[q][:, 0:Wd], 0.0625, None, ALU.mult, reads=[cv[q]], writes=[cv[q]])
        for a in range(2):
            b.dma("sp", vv[a][:, 0:Wd], U[512 + 128 * a:640 + 128 * a, t0:t0 + Wd], reads=[U], writes=[vv[a]])
        b.dma("sp", gt[0:1, 0:Wd], U[1408:1409, t0:t0 + Wd], reads=[U], writes=[gt])
        b.dma("sp", gt[32:33, 0:Wd], U[1440:1441, t0:t0 + Wd], reads=[U], writes=[gt])
        b.ts("dve", gt[0:1, 0:Wd], gt[0:1, 0:Wd], pp[0:1, 24:25], None, ALU.add, reads=[gt, pp], writes=[gt])
        b.act(gt[32:33, 0:Wd], gt[32:33, 0:Wd], AF.Exp, bias=nfb[32:33, 0:1], scale=-1.0, reads=[gt, nfb], writes=[gt])
        b.act(gt[32:33, 0:Wd], gt[32:33, 0:Wd], AF.Ln, bias=1.0, reads=[gt], writes=[gt])
        b.ts("dve", gt[32:33, 0:Wd], gt[32:33, 0:Wd], -1.0, None, ALU.mult, reads=[gt], writes=[gt])
        b.op("dve", lambda e, Wd=Wd: e.tensor_tensor_scan(gt2[32:33, 0:Wd], m01[32:33, 0:Wd], gt[32:33, 0:Wd], 0.0, ALU.mult, ALU.add),
             reads=[m01, gt], writes=[gt2])
        q0, q1, k0, k1 = cv
        qc = (q0, q1)
        kc = (k0, k1)
        for j in range(nch):
            c0 = j * 128
            cs = slice(c0, c0 + 128)
            b.tr(psT[:, 0, :], k0[:, cs], ident, reads=[k0, cmo], writes=[psT])
            b.tr(psT[:, 1, :], k1[:, cs], ident, reads=[k1, cmo], writes=[psT])
            b.tr(psT[:, 2, :], vv[0][:, cs], ident, reads=[vv[0], cmo], writes=[psT])
            b.tr(psT[:, 3, :], vv[1][:, cs], ident, reads=[vv[1], cmo], writes=[psT])
            b.tr(psG[:, 0, :], gt[:, cs], ident, reads=[gt, cmo], writes=[psG])
            b.tr(psG[:, 2, :], gt2[:, cs], ident, reads=[gt2, cmo], writes=[psG])
            b.mm(psG[:, 1, :], sel32, gt2[:, cs], reads=[cmo, gt2], writes=[psG])
            b.op("dve", lambda e: e.tensor_copy(out=ktok[:].rearrange("p (a l) -> p a l", l=128), in_=psT[:, 0:2, :]), reads=[psT], writes=[ktok])
            b.op("dve", lambda e: e.tensor_copy(out=v1[:, 0:256].rearrange("p (a l) -> p a l", l=128), in_=psT[:, 2:4, :]), reads=[psT], writes=[v1])
            b.op("dve", lambda e: e.tensor_copy(out=gtok[:, 0:1], in_=psG[:, 0, 0:1]), reads=[psG], writes=[gtok])
            b.op("dve", lambda e: e.tensor_copy(out=gtok[:, 32:33], in_=psG[:, 2, 32:33]), reads=[psG], writes=[gtok])
            b.tt("dve", sm[:, 0:1], gtok[:, 32:33], gtok[:, 0:1], ALU.subtract, reads=[gtok], writes=[sm])
            b.act(sm[:, 1:2], gtok[:, 32:33], AF.Exp, reads=[gtok], writes=[sm])
            b.op("dve", lambda e: e.tensor_copy(out=sm[:, 2:3], in_=psG[:, 1, 127:128]), reads=[psG], writes=[sm])
            b.ts("dve", lw[:], psG[:, 1, :], sm[:, 0:1], None, ALU.subtract, reads=[psG, sm], writes=[lw])
            b.act(Ee[:], lw[:], AF.Exp, reads=[lw], writes=[Ee])
            b.tt("dve", sm[:, 3:4], sm[:, 2:3], sm[:, 0:1], ALU.subtract, reads=[sm], writes=[sm])
            b.act(sm[:, 3:4], sm[:, 3:4], AF.Exp, reads=[sm], writes=[sm])
            b.act(sm[:, 4:5], sm[:, 2:3], AF.Exp, reads=[sm], writes=[sm])
            for a in range(2):
                b.mm(psS[:], kc[a][:, cs], qc[a][:, cs], start=(a == 0), stop=(a == 1), reads=[kc[a], qc[a]], writes=[psS])
            b.tt("dve", WT[:], psS[:], ui, ALU.mult, reads=[psS, cmo], writes=[WT])
            b.tt("dve", WT[:], WT[:], Ee[:], ALU.mult, reads=[WT, Ee], writes=[WT])
            b.mm(psN[:], WT[:], v1[:], reads=[WT, v1], writes=[psN])
            for a in range(2):
                b.mm(psI[:], qc[a][:, cs], CT1[:, a, :], start=(a == 0), stop=(a == 1), reads=[qc[a], CT1], writes=[psI])
            b.op("dve", lambda e: e.tensor_copy(out=nsb[:], in_=psN[:]), reads=[psN], writes=[nsb])
            b.stt(tot[:], psI[:], sm[:, 1:2], nsb[:], ALU.mult, ALU.add, reads=[psI, sm, nsb], writes=[tot])
            b.act(sm[:, 5:6], tot[:, 256:257], AF.Abs, reads=[tot], writes=[sm])
            b.ts("dve", sm[:, 5:6], sm[:, 5:6], 1.0, None, ALU.max, reads=[sm], writes=[sm])
            b.op("dve", lambda e: e.reciprocal(sm[:, 6:7], sm[:, 5:6]), reads=[sm], writes=[sm])
            h = ho[ci % 2]
            b.ts("dve", h[:], tot[:, 0:256], sm[:, 6:7], None, ALU.mult, reads=[tot, sm], writes=[h])
            b.dma("sp", hC[t0 + c0:t0 + c0 + 128, :], h[:], reads=[h], writes=[hC])
            b.ts("dve", kw[:], ktok[:], sm[:, 3:4], None, ALU.mult, reads=[ktok, sm], writes=[kw])
            for a in range(2):
                b.mm(psC[a][:], kw[:, 128 * a:128 * a + 128], v1[:], reads=[kw, v1], writes=[psC[a]])
                b.stt(CT1[:, a, :], CT1[:, a, :], sm[:, 4:5], psC[a][:], ALU.mult, ALU.add, reads=[CT1, sm, psC[a]], writes=[CT1])
            ci += 1
    b.pop()


def attn_phase(b, U, pp, cmo, tabs, TT, yD, need_ctx=True):
    ident = cmo[:, C_ID:C_ID + 128]
    bones = cmo[:, C_BONES:C_BONES + 128]
    Rq = cmo[:, C_RQ:C_RQ + 128]
    Rk = cmo[:, C_RK:C_RK + 128]
    E = [cmo[:, C_E0:C_E0 + 128], cmo[:, C_E1:C_E1 + 128]]
    e64 = cmo[:, C_E64:C_E64 + 64]
    cosq, sinq, cosk, sink = tabs
    NKT = TT // 128
    b.push()
    QT = b.sb([128, TT], BF16, name="QT")
    KTz = [b.sb([128, TT], BF16, name=f"KTz{h}") for h in range(2)]
    V1 = b.sb([128, NKT, 65], BF16, name="V1")
    b.op("dve", lambda e: e.memset(V1[:, :, 64:65], 1.0), writes=[V1])
    b.push()
    xq = b.sb([128, 512], name="axq")
    xk = b.sb([128, 512], name="axk")
    t1 = b.sb([128, 512], name="at1")
    t2 = b.sb([128, 512], name="at2")
    tc_ = b.sb([128, 512], name="atc")
    ts_ = b.sb([128, 512], name="ats")
    ps1 = b.ps([128, 512], name="aps1")
    ps2 = b.ps([128, 512], name="aps2")
    psV = b.ps([128, 4, 128], name="apsV")
    for p0 in range(0, TT, 512):
        pw = min(512, TT - p0)
        for which in range(2):
            x = xq if which == 0 else xk
            r0 = 1024 if which == 0 else 1152
            gcol = 25 + which
            ct, st_ = (cosq, sinq) if which == 0 else (cosk, sink)
            Rm = Rq if which == 0 else Rk
            b.dma("sp", x[:, 0:pw], U[r0:r0 + 128, p0:p0 + pw], reads=[U], writes=[x])
            b.dma("sp", tc_[:, 0:pw], ct[:, p0:p0 + pw], reads=[ct], writes=[tc_])
            b.dma("sp", ts_[:, 0:pw], st_[:, p0:p0 + pw], reads=[st_], writes=[ts_])
            b.tt("dve", t1[:, 0:pw], x[:, 0:pw], x[:, 0:pw], ALU.mult, reads=[x], writes=[t1])
            b.mm(ps1[:, 0:pw], bones, t1[:, 0:pw], reads=[cmo, t1], writes=[ps1])
            b.act(t1[:, 0:pw], ps1[:, 0:pw], AF.Sqrt, bias=EPS, scale=1.0 / 64, reads=[ps1], writes=[t1])
            b.op("dve", lambda e, pw=pw: e.reciprocal(t1[:, 0:pw], t1[:, 0:pw]), reads=[t1], writes=[t1])
            if which == 1:
                b.op("dve", lambda e, pw=pw: e.memset(t1[64:128, 0:pw], 1.0), writes=[t1])
            b.stt(t2[:, 0:pw], x[:, 0:pw], pp[:, gcol:gcol + 1], t1[:, 0:pw], ALU.mult, ALU.mult, reads=[x, pp, t1], writes=[t2])
            b.mm(ps2[:, 0:pw], Rm, t2[:, 0:pw], reads=[cmo, t2], writes=[ps2])
            b.tt("dve", t1[:, 0:pw], ps2[:, 0:pw], ts_[:, 0:pw], ALU.mult, reads=[ps2, ts_], writes=[t1])
            b.tt("dve", t2[:, 0:pw], t2[:, 0:pw], tc_[:, 0:pw], ALU.mult, reads=[t2, tc_], writes=[t2])
            if which == 0:
                b.stt(QT[:, p0:p0 + pw], t2[:, 0:pw], 1.0, t1[:, 0:pw], ALU.mult, ALU.add, reads=[t2, t1], writes=[QT])
                b.ts("dve", QT[:, p0:p0 + pw], QT[:, p0:p0 + pw], 0.125, None, ALU.mult, reads=[QT], writes=[QT])
            else:
                b.tt("dve", t2[:, 0:pw], t2[:, 0:pw], t1[:, 0:pw], ALU.add, reads=[t2, t1], writes=[t2])
                for h in range(2):
                    b.mm(ps1[:, 0:pw], E[h], t2[:, 0:pw], reads=[cmo, t2], writes=[ps1])
                    b.op("dve", lambda e, h=h, p0=p0, pw=pw: e.tensor_copy(out=KTz[h][:, p0:p0 + pw], in_=ps1[:, 0:pw]),
                         reads=[ps1], writes=[KTz[h]])
                nt = pw // 128
                for i in range(nt):
                    b.tr(psV[:, i, :], t2[:, i * 128:(i + 1) * 128], ident, reads=[t2, cmo], writes=[psV])
                b.op("dve", lambda e, p0=p0, nt=nt: e.tensor_copy(out=V1[:, p0 // 128:p0 // 128 + nt, 0:64], in_=psV[:, 0:nt, 64:128]),
                     reads=[psV], writes=[V1])
    b.pop()
    psS = [b.ps([128, 512], name=f"apsS{i}") for i in range(3)]
    psO = [b.ps([128, 512], name=f"apsO{h}") for h in range(2)]
    psD = b.ps([128, 512], name="apsD")
    PT = [b.sb([128, 512], BF16, name=f"aPT{i}") for i in range(3)]
    OT = b.sb([128, 512], name="aOT")
    b.op("dve", lambda e: e.memset(OT[:], 0.0), writes=[OT])
    rd = b.sb([64, 512], name="ard")
    yo = [b.sb([64, 512], name=f"ayo{i}") for i in range(2)]
    qblocks = []
    if need_ctx:
        qblocks.append((0, 256, 0, 2))
    t = 256
    while t < TT:
        w = min(512, TT - t)
        qblocks.append((t, w, 0, NKT))
        t += w
    steps = []
    for (q0, qw, k_lo, k_hi) in qblocks:
        for kt in range(k_lo, k_hi):
            for h in range(2):
                steps.append((q0, qw, kt, h, kt == k_lo, kt == k_hi - 1))
    oi = 0

    def issue_S(i):
        q0, qw, kt, h, first, last = steps[i]
        b.mm(psS[i % 3][:, 0:qw], KTz[h][:, kt * 128:(kt + 1) * 128], QT[:, q0:q0 + qw], reads=[KTz[h], QT], writes=[psS[i % 3]])

    LOOK = 2
    for i in range(min(LOOK, len(steps))):
        issue_S(i)
    for i, (q0, qw, kt, h, first, last) in enumerate(steps):
        ps = psS[i % 3]
        pt = PT[i % 3]
        b.act(pt[:, 0:qw], ps[:, 0:qw], AF.Exp, reads=[ps], writes=[pt])
        if i + LOOK < len(steps):
            issue_S(i + LOOK)
        b.mm(psO[h][0:65, 0:qw], V1[:, kt, :], pt[:, 0:qw], start=first, stop=last, reads=[V1, pt], writes=[psO[h]])
        if last:
            b.op("dve", lambda e, h=h, qw=qw: e.tensor_copy(out=OT[0:65, 0:qw], in_=psO[h][0:65, 0:qw]), reads=[psO[h]], writes=[OT])
            b.mm(psD[0:64, 0:qw], e64, OT[:, 0:qw], reads=[cmo, OT], writes=[psD])
            b.op("dve", lambda e, qw=qw: e.reciprocal(rd[:, 0:qw], psD[0:64, 0:qw]), reads=[psD], writes=[rd])
            y = yo[oi % 2]
            oi += 1
            b.tt("dve", y[:, 0:qw], OT[0:64, 0:qw], rd[:, 0:qw], ALU.mult, reads=[OT, rd], writes=[y])
            b.dma("sp", yD[64 * h:64 * h + 64, q0:q0 + qw], y[:, 0:qw], reads=[y], writes=[yD])
    b.pop()


def out_odd(b, tiles, d, cm, final=False):
    ident = cm[:, 0:128]
    b.push()
    vb = b.sb([128, 5120], name="vb")
    for i in range(5):
        b.dma("sp", vb[:, i * 1024:(i + 1) * 1024], d["vecs"][0, i * 1024:(i + 1) * 1024].partition_broadcast(128),
              reads=[d["vecs"]], writes=[vb])
    nw, adab = vb[:, 0:1024], vb[:, 4096:5120]
    cvt = b.sb([128, 16], name="cvt")
    b.dma("sp", cvt[:], d["cv"][:, :], reads=[d["cv"]], writes=[cvt])
    sc = b.sb([128, 16], name="osc")
    b.act(sc[:], cvt[:], AF.Silu, reads=[cvt], writes=[sc])
    gate_b = [b.sb([128, 1024], name=f"gate{r}") for r in range(2)]
    Wg = [b.sb([128, 16, 1024], BF16, name=f"Wg{r}") for r in range(2)]
    b.push()
    aw = b.sb([128, 8, 1024], name="oaw")
    for k in range(8):
        b.dma("sp", aw[:, k, :], d["adawg"][k * 128:(k + 1) * 128, :], reads=[d["adawg"]], writes=[aw])
    scb = b.sb([128, 8, 128], name="scb")
    pg = b.ps([128, 512], name="pg")
    for r in range(2):
        b.op("dve", lambda e, r=r: e.tensor_copy(out=scb[:], in_=sc[:, r:16:2].unsqueeze(2).to_broadcast([128, 8, 128])),
             reads=[sc], writes=[scb])
        for half in range(2):
            hsl = slice(half * 512, half * 512 + 512)
            for k in range(8):
                b.mm(pg[:], scb[:, k, :], aw[:, k, hsl], start=(k == 0), stop=(k == 7), reads=[scb, aw], writes=[pg])
            b.tt("dve", gate_b[r][:, hsl], pg[:], adab[:, hsl], ALU.add, reads=[pg, vb], writes=[gate_b[r]])
    b.pop()
    for part in range(2):
        b.push()
        stg = b.sb([128, 8, 1024], name="ostg")
        for k in range(8):
            kk = part * 8 + k
            b.dma("sp", stg[:, k, :], d["wout"][kk * 128:(kk + 1) * 128, :], reads=[d["wout"]], writes=[stg])
        for r in range(2):
            for k in range(8):
                b.tt("dve", Wg[r][:, part * 8 + k, :], stg[:, k, :], gate_b[r][:], ALU.mult, reads=[stg, gate_b[r]], writes=[Wg[r]])
        b.pop()
    hC = b.sb([128, 2048], name="ohC")
    oo = b.sb([128, 1024], name="oo")
    zz = b.sb([128, 1024], name="oz")
    yD = b.sb([128, 1024], name="oyD")
    ag = b.sb([128, 1024], name="oag")
    xres = b.sb([128, 1024], name="oxres")
    ycat = b.sb([128, 2048], name="ycat")
    t10 = b.sb([128, 1024], name="ot10")
    st = b.sb([128, 64], name="ost")
    yT = b.sb([128, 16, 128], BF16, name="oyT")
    xo = b.sb([128, 1024], name="oxo")
    psT = [b.ps([128, 4, 128], name=f"opsT{i}") for i in range(4)]
    pso = [b.ps([128, 512], name=f"opso{i}") for i in range(2)]
    for (r0, r) in tiles:
        rs = slice(r0, r0 + 128)
        for tl, nm in ((hC, "hC"), (oo, "o"), (zz, "z"), (yD, "yD"), (ag, "ag"), (xres, "xres")):
            b.dma("sp", tl[:], d[nm][rs, :], reads=[d[nm]], writes=[tl])
        h = ycat[:, 0:1024]
        h3 = h.rearrange("p (g q) -> p g q", q=256)
        b.tt("dve", h, hC[:, 0:1024], hC[:, 1024:2048], ALU.add, reads=[hC], writes=[ycat])
        b.tt("dve", t10[:], h, h, ALU.mult, reads=[ycat], writes=[t10])
        b.op("dve", lambda e: e.tensor_reduce(out=st[:, 0:4], in_=t10[:].rearrange("p (g q) -> p g q", q=256), axis=AX.X, op=ALU.add),
             reads=[t10], writes=[st])
        b.act(st[:, 0:4], st[:, 0:4], AF.Sqrt, bias=EPS, scale=1.0 / 256, reads=[st], writes=[st])
        b.op("dve", lambda e: e.reciprocal(st[:, 0:4], st[:, 0:4]), reads=[st], writes=[st])
        b.tt("dve", h3, h3, st[:, 0:4].unsqueeze(2).to_broadcast([128, 4, 256]), ALU.mult, reads=[ycat, st], writes=[ycat])
        b.tt("dve", h, h, nw, ALU.mult, reads=[ycat, vb], writes=[ycat])
        b.act(oo[:], oo[:], AF.Sigmoid, reads=[oo], writes=[oo])
        b.act(zz[:], zz[:], AF.Silu, reads=[zz], writes=[zz])
        b.tt("dve", h, h, oo[:], ALU.mult, reads=[ycat, oo], writes=[ycat])
        b.tt("dve", h, h, zz[:], ALU.mult, reads=[ycat, zz], writes=[ycat])
        b.act(ag[:], ag[:], AF.Silu, reads=[ag], writes=[ag])
        b.tt("dve", ycat[:, 1024:2048], yD[:], ag[:], ALU.mult, reads=[yD, ag], writes=[ycat])
        for k in range(16):
            pt = psT[k // 4]
            b.tr(pt[:, k % 4, :], ycat[:, k * 128:(k + 1) * 128], ident, reads=[ycat, cm], writes=[pt])
        for i4 in range(4):
            b.op("dve", lambda e, i4=i4: e.tensor_copy(out=yT[:, 4 * i4:4 * i4 + 4, :], in_=psT[i4][:]), reads=[psT[i4]], writes=[yT])
        for half in range(2):
            hsl = slice(half * 512, half * 512 + 512)
            po = pso[half]
            for k in range(16):
                b.mm(po[:], yT[:, k, :], Wg[r][:, k, hsl], start=(k == 0), stop=(k == 15), reads=[yT, Wg[r]], writes=[po])
            b.tt("dve", xo[:, hsl], po[:], xres[:, hsl], ALU.add, reads=[po, xres], writes=[xo])
        if final:
            b.act(t10[:], xo[:], AF.Square, reads=[xo], writes=[t10, st], accum=st[:, 40:41])
            b.act(st[:, 40:41], st[:, 40:41], AF.Sqrt, bias=EPS, scale=1.0 / 1024, reads=[st], writes=[st])
            b.op("dve", lambda e: e.reciprocal(st[:, 40:41], st[:, 40:41]), reads=[st], writes=[st])
            b.stt(xo[:], xo[:], st[:, 40:41], vb[:, 3072:4096], ALU.mult, ALU.mult, reads=[xo, st, vb], writes=[xo])
        b.dma("sp", d["xo"][rs, :], xo[:], reads=[xo], writes=[d["xo"]])
    b.pop()


def consts_cm():
    cm = np.zeros((128, 642), np.float32)
    i = np.arange(128)
    cm[:, 0:128] = np.eye(128)
    cm[:, 128:256] = (i[:, None] < i[None, :])
    cm[:, 256:384] = (i[:, None] > i[None, :])
    cm[:, 384:512] = (i[:, None] <= i[None, :])
    cm[:, 512:640] = ((i[:, None] // 64) == (i[None, :] // 64))
    cm[:, 640] = (i < 64)
    cm[:, 641] = (i >= 64)
    return cm


def fm8(v):
    return np.ascontiguousarray(v.reshape(8, 128).T)


def even_core_inputs(core, j_layer, l, inp, xseq):
    d, jj = core // 4, core % 4
    W = inp['ab_w_in'][j_layer]
    o_r, o_k, o_v = 0, 512, 1024
    o_wl, o_al = 1536, 1664
    o_ga = 1792
    o_xbc = 2304
    o_dt = o_xbc + 1536
    o_z = o_dt + 32
    hc = slice(128 * jj, 128 * jj + 128)
    g = jj // 2
    cols = []
    cols.append(np.arange(o_r, o_r + 512)[hc])
    cols.append(np.arange(o_k, o_k + 512)[hc])
    cols.append(np.arange(o_v, o_v + 512)[hc])
    cols.append(np.arange(o_ga, o_ga + 512)[hc])
    cols.append(np.concatenate([np.arange(o_wl + 64 * d, o_wl + 64 * d + 64), np.arange(o_al + 64 * d, o_al + 64 * d + 64)]))
    xs_cols = np.arange(o_xbc, o_xbc + 1024)[256 * jj:256 * jj + 256]
    cols.append(xs_cols[:128])
    cols.append(xs_cols[128:])
    cols.append(np.arange(o_xbc + 1024 + 128 * g, o_xbc + 1024 + 128 * g + 128))
    cols.append(np.arange(o_xbc + 1280 + 128 * g, o_xbc + 1280 + 128 * g + 128))
    z_cols = np.arange(o_z, o_z + 1024)[256 * jj:256 * jj + 256]
    cols.append(z_cols[:128])
    cols.append(z_cols[128:])
    win = np.zeros((1024, NCC * 128), np.float32)
    for cc, c in enumerate(cols):
        win[:, cc * 128:cc * 128 + len(c)] = W[:, c]
    dt_cols = np.arange(o_dt + 16 * d + 4 * jj, o_dt + 16 * d + 4 * jj + 4)
    win[:, 11 * 128:11 * 128 + 4] = W[:, dt_cols]
    pp = np.zeros((128, NPP), np.float32)
    mu = inp['rk_mu'][j_layer]
    pp[:, 0] = mu[o_r:o_r + 512][hc]
    pp[:, 1] = mu[o_k:o_k + 512][hc]
    pp[:, 2] = mu[o_v:o_v + 512][hc]
    pp[0:64, 3] = mu[o_wl + 64 * d:o_wl + 64 * d + 64]
    pp[64:128, 3] = mu[o_al + 64 * d:o_al + 64 * d + 64]
    pp[:, 4] = inp['rk_w0'][j_layer, d][hc]
    pp[:, 5] = inp['rk_a0'][j_layer, d][hc]
    pp[:, 6] = inp['rk_k_k'][j_layer][hc]
    pp[:, 7] = inp['rk_k_a'][j_layer][hc]
    pp[:, 8] = inp['rk_r_k'][j_layer].reshape(512)[hc]
    cw = inp['mb_conv_w'][j_layer]
    cb = inp['mb_conv_b'][j_layer]
    if d == 1:
        cw = cw[::-1]
    xbc_rel = [xs_cols[:128] - o_xbc, xs_cols[128:] - o_xbc, cols[7] - o_xbc, cols[8] - o_xbc]
    for q, rel in enumerate(xbc_rel):
        pp[:, 9 + 5 * q:14 + 5 * q] = cw[:, rel].T
        pp[:, 29 + q] = cb[rel]
    pp[0:4, 33] = inp['mb_dt_bias'][j_layer, d, 4 * jj:4 * jj + 4]
    pp[0:4, 34] = inp['mb_a_log'][j_layer, d, 4 * jj:4 * jj + 4]
    pp[:, 35:43] = fm8(inp['norm_g'][l])
    pp[:, 43:51] = fm8(inp['ada_b'][l][0:1024])
    pp[:, 51:59] = fm8(inp['ada_b'][l][1024:2048])
    cv = np.stack([fm8(inp['c'][0]), fm8(inp['c_ctx'])], -1)
    pp[:, 59:75] = cv.reshape(128, 16)
    w2a2 = np.zeros((128, 256), np.float32)
    w2a2[0:64, 0:128] = inp['rk_w2'][j_layer, d][:, hc]
    w2a2[64:128, 128:256] = inp['rk_a2'][j_layer, d][:, hc]
    return {
        'xseq': xseq, 'pp': pp, 'win': win,
        'adaw': np.ascontiguousarray(inp['ada_w'][l][:, 0:2048]),
        'w2a2': w2a2, 'cm': consts_cm(), 'sel': np.concatenate([np.kron(np.eye(4, dtype=np.float32), np.ones((1, 128), np.float32)), np.zeros((124, 512), np.float32)], 0),
    }


NCMO = 128 * 9 + 64


def consts_cmo():
    i = np.arange(128)
    c = np.zeros((128, NCMO), np.float32)
    c[:, 0:128] = np.eye(128)
    c[:, 128:256] = (i[:, None] <= i[None, :])
    c[:, 256:384] = ((i[:, None] // 64) == (i[None, :] // 64))
    c[32, 384:512] = 1.0
    R = np.zeros((128, 128), np.float32)
    for p in range(128):
        q = p % 32
        if q < 16:
            R[p + 16, p] = -1.0
        else:
            R[p - 16, p] = 1.0
    c[:, 512:640] = R
    Rk = R.copy()
    Rk[64:, :] = 0
    Rk[:, 64:] = 0
    c[:, 640:768] = Rk
    E0 = np.zeros((128, 128), np.float32)
    E1 = np.zeros((128, 128), np.float32)
    for dd in range(64):
        E0[dd, dd] = 1.0
        E1[dd, 64 + dd] = 1.0
    c[:, 768:896] = E0
    c[:, 896:1024] = E1
    c[:, 1024:1152] = (i[:, None] < i[None, :])
    c[64, 1152:1216] = 1.0
    return c


def rope_tables(TT, T, d, grid_w=64, theta=10000.0):
    idx = np.arange(T)
    t = idx if d == 0 else T - 1 - idx
    row = (t // grid_w).astype(np.float64)
    col = (t % grid_w).astype(np.float64)
    inv = theta ** (-np.arange(16, dtype=np.float64) / 16)
    cos = np.ones((128, TT), np.float64)
    sin = np.zeros((128, TT), np.float64)
    for p in range(128):
        pp_ = p % 64
        pos = row if pp_ < 32 else col
        ang = pos * inv[pp_ % 16]
        cos[p, TT - T:] = np.cos(ang)
        sin[p, TT - T:] = np.sin(ang)
    cosk, sink = cos.copy(), sin.copy()
    cosk[64:] = 1.0
    sink[64:] = 0.0
    return cos.astype(np.float32), sin.astype(np.float32), cosk.astype(np.float32), sink.astype(np.float32)


def odd_core_inputs(core, j, l, inp, xseq, T):
    d, jj = core // 4, core % 4
    c = core
    W = inp['cd_w_in'][j]
    TT = xseq.shape[0]
    cols = [np.arange(256 * jj, 256 * jj + 128), np.arange(256 * jj + 128, 256 * jj + 256),
            np.arange(1024 + 256 * jj, 1024 + 256 * jj + 128), np.arange(1024 + 256 * jj + 128, 1024 + 256 * jj + 256),
            np.arange(2048 + 256 * jj, 2048 + 256 * jj + 128), np.arange(2048 + 256 * jj + 128, 2048 + 256 * jj + 256)]
    oz = 3072 if d == 0 else 4112
    cols += [np.arange(oz + 256 * jj, oz + 256 * jj + 128), np.arange(oz + 256 * jj + 128, oz + 256 * jj + 256)]
    cols.append(np.arange(5136 + 128 * c, 5136 + 128 * c + 128))
    cols.append(np.concatenate([np.arange(6160 + 64 * (c // 2), 6160 + 64 * (c // 2) + 64),
                                np.arange(6416 + 64 * (c // 2), 6416 + 64 * (c // 2) + 64)]))
    cols.append(np.arange(6672 + 128 * c, 6672 + 128 * c + 128))
    win = np.zeros((1024, NCC * 128), np.float32)
    for cc, cl in enumerate(cols):
        win[:, cc * 128:cc * 128 + len(cl)] = W[:, cl]
    win[:, 11 * 128 + 0] = W[:, 4096 + 4 * d + jj]
    win[:, 11 * 128 + 32] = W[:, 4104 + 4 * d + jj]
    pp = np.zeros((128, NPP), np.float32)
    cw = inp['ml_conv_w'][j]
    cb = inp['ml_conv_b'][j]
    if d == 1:
        cw = cw[::-1]
    for q in range(4):
        pp[:, 5 * q:5 * q + 5] = cw[:, cols[q]].T
        pp[:, 20 + q] = cb[cols[q]]
    pp[0, 24] = inp['ml_i_bias'][j, d, jj]
    pp[32, 24] = inp['ml_f_bias'][j, d, jj]
    pp[:, 25] = np.tile(inp['at_q_norm'][j], 2)
    pp[0:64, 26] = inp['at_k_norm'][j]
    pp[64:, 26] = 1.0
    pp[:, 35:43] = fm8(inp['norm_g'][l])
    pp[:, 43:51] = fm8(inp['ada_b'][l][0:1024])
    pp[:, 51:59] = fm8(inp['ada_b'][l][1024:2048])
    cv = np.stack([fm8(inp['c'][0]), fm8(inp['c_ctx'])], -1)
    pp[:, 59:75] = cv.reshape(128, 16)
    cq, sq, ck, sk = rope_tables(TT, T, d)
    return {'xseq': xseq, 'pp': pp, 'win': win, 'adaw': np.ascontiguousarray(inp['ada_w'][l][:, 0:2048]),
            'cmo': consts_cmo(), 'cosq': cq, 'sinq': sq, 'cosk': ck, 'sink': sk}


def assemble_even_out_inputs(inp, outs, j, l, xs, ctx):
    def unflip(a):
        return np.concatenate([a[:256][::-1], a[256:][::-1]], 0)
    R = ctx.shape[0] + xs.shape[0]
    yA = np.zeros((R, 1024), np.float32); bon = np.zeros((R, 16), np.float32)
    v = np.zeros((R, 512), np.float32); ga = np.zeros((R, 512), np.float32)
    yB = np.zeros((R, 2048), np.float32); xsm = np.zeros((R, 1024), np.float32); z = np.zeros((R, 1024), np.float32)
    for core in range(8):
        d, jj = core // 4, core % 4
        o = outs[core]
        f = (lambda a: a) if d == 0 else unflip
        yA[:, 512 * d + 128 * jj:512 * d + 128 * jj + 128] = f(o["yA"])
        bon[:, 8 * d + 2 * jj:8 * d + 2 * jj + 2] = f(np.ascontiguousarray(o["bon"].T))
        yB[:, 1024 * d + 256 * jj:1024 * d + 256 * jj + 256] = f(o["yB"])
        if d == 0:
            v[:, 128 * jj:128 * jj + 128] = o["vg"][0:128].T
            ga[:, 128 * jj:128 * jj + 128] = o["vg"][128:256].T
            xsm[:, 256 * jj:256 * jj + 256] = o["xsB"]
            z[:, 256 * jj:256 * jj + 256] = o["zB"].T
    vecs = np.zeros((1, 5120), np.float32)
    vecs[0, 0:512] = inp['rk_ln_w'][j]; vecs[0, 512:1024] = inp['rk_ln_b'][j]
    vecs[0, 1024:2048] = np.repeat(inp['mb_d'][j], 64); vecs[0, 2048:3072] = inp['mb_norm_w'][j]
    vecs[0, 4096:5120] = inp['ada_b'][l][2048:3072]
    cv = np.stack([fm8(inp['c'][0]), fm8(inp['c_ctx'])], -1).reshape(128, 16)
    return {"yA": yA, "bon": bon, "v": v, "ga": ga, "yB": yB, "xsm": xsm, "z": z,
            "xres": np.ascontiguousarray(np.concatenate([ctx, xs], 0)),
            "wout": np.ascontiguousarray(inp['ab_w_out'][j]), "adawg": np.ascontiguousarray(inp['ada_w'][l][:, 2048:3072]),
            "vecs": vecs, "cv": np.ascontiguousarray(cv), "cm": consts_cm()}


def unflip(a):
    return np.concatenate([a[:256][::-1], a[256:][::-1]], 0)

def assemble_odd_out_inputs(inp, outs, j, l, xs, ctx):
    R = ctx.shape[0] + xs.shape[0]
    hC = np.zeros((R, 2048), np.float32); o = np.zeros((R, 1024), np.float32); z = np.zeros((R, 1024), np.float32)
    yD = np.zeros((R, 1024), np.float32); ag = np.zeros((R, 1024), np.float32)
    for core in range(8):
        d, jj = core // 4, core % 4
        oc = outs[core]
        f = (lambda a: a) if d == 0 else unflip
        hC[:, 1024 * d + 256 * jj:1024 * d + 256 * jj + 256] = f(oc["hC"])
        ozt = f(np.ascontiguousarray(oc["oz"].T))
        if d == 0:
            o[:, 256 * jj:256 * jj + 256] = ozt
        else:
            z[:, 256 * jj:256 * jj + 256] = ozt
        yD[:, 128 * core:128 * core + 128] = f(np.ascontiguousarray(oc["yD"].T))
        ag[:, 128 * core:128 * core + 128] = f(np.ascontiguousarray(oc["agT"].T))
    vecs = np.zeros((1, 5120), np.float32)
    vecs[0, 0:1024] = inp['ml_norm_w'][j]
    vecs[0, 4096:5120] = inp['ada_b'][l][2048:3072]
    cv = np.stack([fm8(inp['c'][0]), fm8(inp['c_ctx'])], -1).reshape(128, 16)
    return {"hC": hC, "o": o, "z": z, "yD": yD, "ag": ag, "xres": np.ascontiguousarray(np.concatenate([ctx, xs], 0)),
            "wout": np.ascontiguousarray(inp['cd_w_out'][j]), "adawg": np.ascontiguousarray(inp['ada_w'][l][:, 2048:3072]),
            "vecs": vecs, "cv": np.ascontiguousarray(cv), "cm": consts_cm()}


T_SEQ = 16384
CTX = 256
TT_ALL = T_SEQ + CTX


def _groups_blocks(TT):
    groups = [(0, 2, 1)]
    t = CTX
    while t < TT:
        n = min(4, (TT - t) // 128)
        groups.append((t, n, 0))
        t += n * 128
    blocks = [(0, 2, 0, CTX)]
    t = CTX
    while t < TT:
        n = min(8, (TT - t) // 128)
        blocks.append((t, n, CTX, TT))
        t += n * 128
    return groups, blocks


def build_mix_even():
    b = Bld()
    TT = TT_ALL
    xseq = b.dram("xseq", [TT, 1024], kind="ExternalInput")
    pp_d = b.dram("pp", [128, NPP], kind="ExternalInput")
    win_d = b.dram("win", [1024, NCC * 128], kind="ExternalInput")
    adaw_d = b.dram("adaw", [1024, 2048], kind="ExternalInput")
    w2a2_d = b.dram("w2a2", [128, 256], kind="ExternalInput")
    cm_d = b.dram("cm", [128, 642], kind="ExternalInput")
    sel_d = b.dram("sel", [128, 512], kind="ExternalInput")
    U = b.dram("U", [NCC * 128, TT], kind="Internal")
    yA = b.dram("yA", [TT, 128], kind="ExternalOutput")
    bon = b.dram("bon", [2, TT], kind="ExternalOutput")
    vg = b.dram("vg", [256, TT], kind="ExternalOutput")
    zB = b.dram("zB", [256, TT], kind="ExternalOutput")
    yB = b.dram("yB", [TT, 256], kind="ExternalOutput")
    xsB = b.dram("xsB", [TT, 256], kind="ExternalOutput")
    pp = b.sb([128, NPP], name="pp")
    cm = b.sb([128, 642], name="cm")
    b.dma("sp", pp[:], pp_d[:, :], reads=[pp_d], writes=[pp])
    b.dma("sp", cm[:], cm_d[:, :], reads=[cm_d], writes=[cm])
    modT = setup_mod(b, pp, adaw_d, 16)
    groups, blocks = _groups_blocks(TT)
    dests = {cc: (U, cc * 128) for cc in range(NCC)}
    dests[3] = (vg, 128)
    dests[9] = (zB, 0)
    dests[10] = (zB, 128)
    phase_A(b, xseq, win_d, pp, modT, cm, groups, dests, NCC)
    b.P.barrier()
    rwkv_phase(b, U, pp, w2a2_d, cm, blocks, yA, bon, vg, TT)
    mamba_phase(b, U, pp, cm, sel_d, blocks, yB, xsB, TT)
    b.P.emit()
    b.stacks[0].close()
    return b.nc


def build_mix_odd():
    b = Bld()
    TT = TT_ALL
    xseq = b.dram("xseq", [TT, 1024], kind="ExternalInput")
    pp_d = b.dram("pp", [128, NPP], kind="ExternalInput")
    win_d = b.dram("win", [1024, NCC * 128], kind="ExternalInput")
    adaw_d = b.dram("adaw", [1024, 2048], kind="ExternalInput")
    cmo_d = b.dram("cmo", [128, NCMO], kind="ExternalInput")
    tabs = [b.dram(n, [128, TT], kind="ExternalInput") for n in ("cosq", "sinq", "cosk", "sink")]
    U = b.dram("U", [NCC * 128, TT], kind="Internal")
    hC = b.dram("hC", [TT, 256], kind="ExternalOutput")
    oz = b.dram("oz", [256, TT], kind="ExternalOutput")
    yD = b.dram("yD", [128, TT], kind="ExternalOutput")
    agT = b.dram("agT", [128, TT], kind="ExternalOutput")
    pp = b.sb([128, NPP], name="pp")
    cmo = b.sb([128, NCMO], name="cmo")
    b.dma("sp", pp[:], pp_d[:, :], reads=[pp_d], writes=[pp])
    b.dma("sp", cmo[:], cmo_d[:, :], reads=[cmo_d], writes=[cmo])
    modT = setup_mod(b, pp, adaw_d, 16)
    groups, blocks = _groups_blocks(TT)
    dests = {cc: (U, cc * 128) for cc in range(NCC)}
    dests[6] = (oz, 0)
    dests[7] = (oz, 128)
    dests[10] = (agT, 0)
    phase_A(b, xseq, win_d, pp, modT, cmo, groups, dests, NCC)
    b.P.barrier()
    mlstm_phase(b, U, pp, cmo, blocks, hC, TT)
    attn_phase(b, U, pp, cmo, tabs, TT, yD, need_ctx=True)
    b.P.emit()
    b.stacks[0].close()
    return b.nc


def build_out(kind, R, tiles, final):
    b = Bld()
    if kind == "even":
        shapes = {"yA": [R, 1024], "bon": [R, 16], "v": [R, 512], "ga": [R, 512], "yB": [R, 2048], "xsm": [R, 1024],
                  "z": [R, 1024], "xres": [R, 1024], "wout": [1536, 1024], "adawg": [1024, 1024], "vecs": [1, 5120], "cv": [128, 16]}
    else:
        shapes = {"hC": [R, 2048], "o": [R, 1024], "z": [R, 1024], "yD": [R, 1024], "ag": [R, 1024], "xres": [R, 1024],
                  "wout": [2048, 1024], "adawg": [1024, 1024], "vecs": [1, 5120], "cv": [128, 16]}
    d = {k: b.dram(k, s, kind="ExternalInput") for k, s in shapes.items()}
    d["xo"] = b.dram("xo", [R, 1024], kind="ExternalOutput")
    cm_d = b.dram("cm", [128, 642], kind="ExternalInput")
    cm = b.sb([128, 642], name="cm")
    b.dma("sp", cm[:], cm_d[:, :], reads=[cm_d], writes=[cm])
    if kind == "even":
        out_even(b, tiles, d, cm, final=final)
    else:
        out_odd(b, tiles, d, cm, final=final)
    b.P.emit()
    b.stacks[0].close()
    return b.nc


def kernel(**inp):
    inp = {k: np.asarray(v) for k, v in inp.items()}
    xs = np.ascontiguousarray(inp['x'][0])
    ctx = np.ascontiguousarray(inp['ctx'][0])
    sh = T_SEQ // 8
    tiles = [(0, 1), (128, 1)] + [(CTX + 128 * i, 0) for i in range(sh // 128)]
    cores = list(range(8))
    for l in range(4):
        j = l // 2
        even = (l % 2 == 0)
        fwd = np.ascontiguousarray(np.concatenate([ctx, xs], 0))
        bwd = np.ascontiguousarray(np.concatenate([ctx[::-1], xs[::-1]], 0))
        if even:
            maps = [even_core_inputs(c, j, l, inp, fwd if c < 4 else bwd) for c in cores]
            res = run_bass_kernel_spmd(build_mix_even(), maps, core_ids=cores)
            full = assemble_even_out_inputs(inp, res.results, j, l, xs, ctx)
            row_keys = ("yA", "bon", "v", "ga", "yB", "xsm", "z", "xres")
        else:
            maps = [odd_core_inputs(c, j, l, inp, fwd if c < 4 else bwd, T_SEQ) for c in cores]
            res = run_bass_kernel_spmd(build_mix_odd(), maps, core_ids=cores)
            full = assemble_odd_out_inputs(inp, res.results, j, l, xs, ctx)
            row_keys = ("hC", "o", "z", "yD", "ag", "xres")
        del res, maps
        final = (l == 3)
        full["vecs"][0, 3072:4096] = inp['norm_final']
        maps2 = []
        for c in cores:
            rows = np.concatenate([np.arange(CTX), CTX + c * sh + np.arange(sh)])
            m = {k: np.ascontiguousarray(full[k][rows]) for k in row_keys}
            for k in ("wout", "adawg", "vecs", "cv", "cm"):
                m[k] = full[k]
            maps2.append(m)
        del full
        res2 = run_bass_kernel_spmd(build_out("even" if even else "odd", CTX + sh, tiles, final), maps2, core_ids=cores)
        outs = [np.asarray(r['xo'], dtype=np.float32) for r in res2.results]
        xs = np.ascontiguousarray(np.concatenate([o[CTX:] for o in outs], 0))
        ctx = np.ascontiguousarray(outs[0][:CTX])
        del res2, maps2, outs
    return xs[None]
```

```python
import contextlib
import os
import numpy as np
import concourse.bass as bass
import concourse.mybir as mybir
from concourse.bass_utils import run_bass_kernel_spmd

F32 = mybir.dt.float32
BF16 = mybir.dt.bfloat16
I32 = mybir.dt.int32
AF = mybir.ActivationFunctionType
ALU = mybir.AluOpType
AX = mybir.AxisListType

COMPUTE = ("pe", "dve", "act", "pool")
STREAMS = ("pe", "dve", "act", "pool", "sp")
DMA_K = 4


class Res:
    __slots__ = ("name", "w", "rs")

    def __init__(self, name):
        self.name = name
        self.w = None
        self.rs = []


class Ins:
    __slots__ = ("stream", "fn", "is_dma", "seq", "dn", "waits", "need_inc", "know")


class Prog:
    def __init__(self, nc):
        self.nc = nc
        self.ins = {s: [] for s in STREAMS}
        self.ndma = {s: 0 for s in STREAMS}
        self.know = {s: {e: -1 for e in COMPUTE} for s in STREAMS}
        self.kdma = {s: set() for s in STREAMS}
        self.n_res = 0
        self._bar = {}
        self._last = {}

    def res(self, name=None):
        self.n_res += 1
        return Res(name or f"r{self.n_res}")

    def _dep(self, I, D):
        s = I.stream
        if D is None or D is I:
            return
        if D.is_dma:
            key = (D.stream, D.dn)
            if key in self.kdma[s]:
                return
            self.kdma[s].add(key)
            I.waits.append(("dma", D.stream, D.dn % DMA_K, 16 * (D.dn // DMA_K + 1)))
        else:
            e = D.stream
            if self.know[s][e] >= D.seq:
                return
            if e == "pe" and s == "pe":
                return
            D.need_inc = True
            I.waits.append(("eng", e, D.seq))
            kn = self.know[s]
            kn[e] = D.seq
            for e2, v in D.know.items():
                if v > kn[e2]:
                    kn[e2] = v

    def _add(self, stream, fn, reads, writes, is_dma):
        I = Ins()
        I.stream = stream
        I.fn = fn
        I.is_dma = is_dma
        I.waits = []
        I.need_inc = False
        I.seq = None
        I.dn = None
        if is_dma:
            n = self.ndma[stream]
            self.ndma[stream] = n + 1
            I.dn = n
            if n >= DMA_K:
                key = (stream, n - DMA_K)
                if key not in self.kdma[stream]:
                    self.kdma[stream].add(key)
                    I.waits.append(("dma", stream, n % DMA_K, 16 * ((n - DMA_K) // DMA_K + 1)))
        bar = self._bar.pop(stream, None)
        if bar is not None:
            lastI, snap_d = bar
            for e, D in lastI.items():
                if D is not None:
                    self._dep(I, D)
            for s2, n2 in snap_d.items():
                for k in range(DMA_K):
                    cnt = len(range(k, n2, DMA_K))
                    if cnt:
                        I.waits.append(("dma", s2, k, 16 * cnt))
        for r in reads:
            self._dep(I, r.w)
        for r in writes:
            self._dep(I, r.w)
            for rd in r.rs:
                self._dep(I, rd)
        for r in reads:
            r.rs.append(I)
        for r in writes:
            r.w = I
            r.rs = []
        lst = self.ins[stream]
        if not is_dma:
            I.seq = self._nseq(stream)
            I.know = dict(self.know[stream])
        lst.append(I)
        if not is_dma:
            self._last[stream] = I
        return I

    def _nseq(self, stream):
        c = getattr(self, "_cnt", None)
        if c is None:
            c = self._cnt = {s: 0 for s in STREAMS}
        v = c[stream]
        c[stream] = v + 1
        return v

    def barrier(self):
        snap_e = {e: self._cnt_get(e) - 1 for e in COMPUTE}
        snap_d = {s: self.ndma[s] for s in STREAMS}
        lastI = {e: self._last.get(e) for e in COMPUTE}
        self._bar = {s: (lastI, dict(snap_d)) for s in STREAMS}

    def _cnt_get(self, e):
        c = getattr(self, "_cnt", None)
        return c[e] if c else 0

    def op(self, eng, fn, reads=(), writes=()):
        return self._add(eng, fn, reads, writes, False)

    def dma(self, stream, out, in_, reads=(), writes=()):
        return self._add(stream, lambda e: e.dma_start(out=out, in_=in_), reads, writes, True)

    def emit(self):
        nc = self.nc
        import contextlib
        with contextlib.ExitStack() as st:
            esem = {e: st.enter_context(nc.semaphore(f"s_{e}")) for e in COMPUTE}
            dsem = {}
            for s in STREAMS:
                if self.ndma[s] > 0:
                    for k in range(DMA_K):
                        dsem[(s, k)] = st.enter_context(nc.semaphore(f"d_{s}{k}"))
            inc_count = {e: 0 for e in COMPUTE}
            seq2cnt = {e: {} for e in COMPUTE}
            for e in COMPUTE:
                c = 0
                for I in self.ins[e]:
                    if I.is_dma:
                        continue
                    if I.need_inc:
                        c += 1
                        seq2cnt[e][I.seq] = c
            block = st.enter_context(nc.Block())

            def run(stream, eng):
                for I in self.ins[stream]:
                    for w in I.waits:
                        if w[0] == "dma":
                            eng.wait_ge(dsem[(w[1], w[2])], w[3])
                        else:
                            eng.wait_ge(esem[w[1]], seq2cnt[w[1]][w[2]])
                    r = I.fn(eng)
                    if I.is_dma:
                        r.then_inc(dsem[(stream, I.dn % DMA_K)], 16)
                    elif I.need_inc:
                        r.then_inc(esem[stream], 1)
                if stream == "sp":
                    for (s2, k), sem in dsem.items():
                        n = len(range(k, self.ndma[s2], DMA_K))
                        if n:
                            eng.wait_ge(sem, 16 * n)

            @block.tensor
            def _(eng):
                run("pe", eng)

            @block.vector
            def _(eng):
                run("dve", eng)

            @block.scalar
            def _(eng):
                run("act", eng)

            @block.gpsimd
            def _(eng):
                run("pool", eng)

            @block.sync
            def _(eng):
                run("sp", eng)


EPS = 1e-6
NCC = 12
NPP = 75
EM05 = float(np.exp(-0.5))
NO_POOL_OPS = bool(int(os.environ.get("NO_POOL_OPS", "1")))
NO_POOL_DMA = bool(int(os.environ.get("NO_POOL_DMA", "1")))


class Tl:
    __slots__ = ("t", "r")

    def __init__(self, t, r):
        self.t = t
        self.r = r

    def __getitem__(self, k):
        return self.t[k]


class Bld:
    def __init__(self):
        self.nc = bass.Bass("TRN2", target_bir_lowering=False)
        self.P = Prog(self.nc)
        self.stacks = [contextlib.ExitStack()]
        self.n = 0

    def push(self):
        self.stacks.append(contextlib.ExitStack())

    def pop(self):
        self.P.barrier()
        self.stacks.pop().close()

    def sb(self, shape, dt=F32, name=None):
        self.n += 1
        t = self.stacks[-1].enter_context(self.nc.sbuf_tensor(f"{name or 'sb'}_{self.n}", list(shape), dt))
        return Tl(t, self.P.res())

    def ps(self, shape, dt=F32, name=None):
        self.n += 1
        t = self.stacks[-1].enter_context(self.nc.psum_tensor(f"{name or 'ps'}_{self.n}", list(shape), dt))
        return Tl(t, self.P.res())

    def dram(self, name, shape, dt=F32, kind="Internal"):
        t = self.nc.dram_tensor(name, list(shape), dt, kind=kind)
        return Tl(t.ap(), self.P.res())

    def op(self, eng, fn, reads=(), writes=()):
        if eng == "pool" and NO_POOL_OPS:
            eng = "dve"
        return self.P.op(eng, fn, [x.r for x in reads], [x.r for x in writes])

    def dma(self, q, out, in_, reads=(), writes=()):
        if q == "pool" and NO_POOL_DMA:
            q = "sp"
        return self.P.dma(q, out, in_, [x.r for x in reads], [x.r for x in writes])

    def dbg(self, name, tl, ap, shape, dt=F32):
        if not getattr(self, "debug", False):
            return
        d = self.dram("dbg_" + name, shape, dt, kind="ExternalOutput")
        self.dma("sp", d[tuple(slice(None) for _ in shape)], ap, reads=[tl], writes=[d])

    def ts(self, eng, out, in0, s1, s2, op0, op1=None, reads=(), writes=()):
        if op1 is None:
            return self.op(eng, lambda e: e.tensor_scalar(out, in0, s1, None, op0), reads, writes)
        return self.op(eng, lambda e: e.tensor_scalar(out, in0, s1, s2, op0, op1), reads, writes)

    def tt(self, eng, out, in0, in1, op, reads=(), writes=()):
        return self.op(eng, lambda e: e.tensor_tensor(out, in0, in1, op), reads, writes)

    def stt(self, out, in0, sc, in1, op0, op1, reads=(), writes=()):
        return self.op("dve", lambda e: e.scalar_tensor_tensor(out, in0, sc, in1, op0, op1), reads, writes)

    def act(self, out, in_, func, bias=0.0, scale=1.0, reads=(), writes=(), accum=None):
        if accum is None:
            return self.op("act", lambda e: e.activation(out=out, in_=in_, func=func, bias=bias, scale=scale), reads, writes)
        return self.op("act", lambda e: e.activation(out=out, in_=in_, func=func, bias=bias, scale=scale, accum_out=accum), reads, writes)

    def mm(self, out, lhsT, rhs, start=True, stop=True, reads=(), writes=()):
        return self.op("pe", lambda e: e.matmul(out, lhsT=lhsT, rhs=rhs, start=start, stop=stop), reads, writes)

    def tr(self, out, in_, ident, reads=(), writes=()):
        return self.op("pe", lambda e: e.transpose(out, in_, ident), reads, writes)


def setup_mod(b, pp, adaw_d, nblk):
    sc = b.sb([128, 16], name="sc")
    b.act(sc[:], pp[:, 59:75], AF.Silu, reads=[pp], writes=[sc])
    modT = b.sb([128, nblk, 2], name="modT")
    b.push()
    aw = b.sb([128, 8, nblk * 128], name="aw")
    for k in range(8):
        b.dma("sp" if k % 2 == 0 else "pool", aw[:, k, :], adaw_d[k * 128:(k + 1) * 128, :], reads=[adaw_d], writes=[aw])
    pm = b.ps([128, nblk, 2], name="pm")
    for blk in range(nblk):
        for k in range(8):
            b.mm(pm[:, blk, :], aw[:, k, blk * 128:(blk + 1) * 128], sc[:, 2 * k:2 * k + 2], start=(k == 0), stop=(k == 7),
                 reads=[aw, sc], writes=[pm])
    for r in range(2):
        b.tt("dve", modT[:, :, r], pm[:, :, r], pp[:, 43:43 + nblk], ALU.add, reads=[pm, pp], writes=[modT])
    b.dbg("modT", modT, modT[:], [128, nblk, 2])
    b.pop()
    return modT


def phase_A(b, xseq, win_d, pp, modT, cm, groups, dests, ncc):
    nc = b.nc
    ident = cm[:, 0:128]
    b.push()
    b.dbg("modT2", modT, modT[:], [128, 16, 2])
    gm = b.sb([128, 8, 2], name="gm")
    b.ts("dve", gm[:], modT[:, 8:16, :], 1.0, None, ALU.add, reads=[modT], writes=[gm])
    for r in range(2):
        b.tt("dve", gm[:, :, r], gm[:, :, r], pp[:, 35:43], ALU.mult, reads=[gm, pp], writes=[gm])
    W = [b.sb([128, 8, ncc * 128], BF16, name=f"W{r}") for r in range(2)]
    sW = b.sb([128, ncc, 2], name="sW")
    b.push()
    stg = b.sb([128, 8, ncc * 128], name="stg")
    for k in range(8):
        b.dma("sp" if k % 2 == 0 else "pool", stg[:, k, :], win_d[k * 128:(k + 1) * 128, :], reads=[win_d], writes=[stg])
    psw = b.ps([128, ncc, 2], name="psw")
    for cc in range(ncc):
        for k in range(8):
            b.mm(psw[:, cc, :], stg[:, k, cc * 128:(cc + 1) * 128], modT[:, k, :], start=(k == 0), stop=(k == 7),
                 reads=[stg, modT], writes=[psw])
    b.op("act", lambda e: e.copy(out=sW[:], in_=psw[:]), reads=[psw], writes=[sW])
    for r in range(2):
        for k in range(8):
            eng = "dve" if k % 2 == 0 else "pool"
            b.ts(eng, W[r][:, k, :], stg[:, k, :], gm[:, k, r:r + 1], None, ALU.mult, reads=[stg, gm], writes=[W[r]])
    b.dbg("gm", gm, gm[:], [128, 8, 2])
    b.dbg("modT3", modT, modT[:], [128, 16, 2])
    b.dbg("sW", sW, sW[:], [128, ncc, 2])
    b.dbg("W0", W[0], W[0][:], [128, 8, ncc * 128], BF16)
    b.pop()
    NXB = 8
    xt = [b.sb([128, 1024], name=f"xt{i}") for i in range(NXB)]
    junk = b.sb([128, 1024], BF16, name="junk")
    ss = [b.sb([128, 1], name=f"ss{i}") for i in range(3)]
    rstd = [b.sb([128, 1], name=f"rstd{i}") for i in range(3)]
    xT = [b.sb([128, 8, 512], BF16, name=f"xT{i}") for i in range(2)]
    psT = [b.ps([128, 4, 128], name=f"psT{i}") for i in range(2)]
    pso = [b.ps([128, 512], name=f"pso{i}") for i in range(3)]
    ob = [b.sb([128, 512], name=f"ob{i}") for i in range(4)]
    tile_buf = {}
    nload = [0]

    def issue_loads(gi_):
        t0_, ntl_, _r = groups[gi_]
        for j_ in range(ntl_):
            x_ = xt[nload[0] % NXB]
            nload[0] += 1
            tile_buf[(gi_, j_)] = x_
            b.dma("sp", x_[:], xseq[t0_ + j_ * 128:t0_ + (j_ + 1) * 128, :], reads=[xseq], writes=[x_])

    issue_loads(0)
    ti = 0
    oi = 0
    for gi, (t0, ntl, r) in enumerate(groups):
        G = ntl * 128
        xg = xT[gi % 2]
        if gi + 1 < len(groups):
            issue_loads(gi + 1)
        for j in range(ntl):
            x = tile_buf.pop((gi, j))
            s_ = ss[ti % 3]
            rs_ = rstd[ti % 3]
            ti += 1
            b.act(junk[:], x[:], AF.Square, reads=[x], writes=[junk, s_], accum=s_[:])
            b.act(s_[:], s_[:], AF.Sqrt, bias=EPS, scale=1.0 / 1024, reads=[s_], writes=[s_])
            b.op("dve", lambda e, o=rs_, i=s_: e.reciprocal(o[:], i[:]), reads=[s_], writes=[rs_])
            b.ts("dve", x[:], x[:], rs_[:, 0:1], None, ALU.mult, reads=[x, rs_], writes=[x])
            for half in range(2):
                pt = psT[half]
                for q in range(4):
                    k = half * 4 + q
                    b.tr(pt[:, q, :], x[:, k * 128:(k + 1) * 128], ident, reads=[x, cm], writes=[pt])
                if half == 0:
                    b.op("act", lambda e, o=xg, p=pt, j=j: e.copy(out=o[:, 0:4, j * 128:(j + 1) * 128], in_=p[:]), reads=[pt], writes=[xg])
                else:
                    b.op("dve", lambda e, o=xg, p=pt, j=j: e.tensor_copy(out=o[:, 4:8, j * 128:(j + 1) * 128], in_=p[:]), reads=[pt], writes=[xg])
        if gi == 1:
            b.dbg("xT", xg, xg[:], [128, 8, 512], BF16)
        for cc in range(ncc):
            po = pso[oi % 3]
            o = ob[oi % 4]
            for k in range(8):
                b.mm(po[:, 0:G], W[r][:, k, cc * 128:(cc + 1) * 128], xg[:, k, 0:G], start=(k == 0), stop=(k == 7),
                     reads=[W[r], xg], writes=[po])
            if oi % 2 == 0:
                b.act(o[:, 0:G], po[:, 0:G], AF.Identity, bias=sW[:, cc, r:r + 1], reads=[po, sW], writes=[o])
            else:
                b.ts("dve", o[:, 0:G], po[:, 0:G], sW[:, cc, r:r + 1], None, ALU.add, reads=[po, sW], writes=[o])
            dst, roff = dests[cc]
            b.dma("pool" if oi % 2 == 0 else "sp", dst[roff:roff + 128, t0:t0 + G], o[:, 0:G], reads=[o], writes=[dst])
            oi += 1
    b.pop()


class _Stop(Exception):
    pass


def _stage(n):
    if float(os.environ.get("RW_STOP", "99")) <= n:
        raise _Stop()


def rwkv_phase(b, *a, **k):
    try:
        _rwkv_phase(b, *a, **k)
    except _Stop:
        b.pop()


def _rwkv_phase(b, U, pp, w2a2_d, cm, blocks, yA, bon, vg, TT, dbg=None):
    ident = cm[:, 0:128]
    su = cm[:, 128:256]
    sl = cm[:, 256:384]
    ui = cm[:, 384:512]
    bones = cm[:, 512:640]
    hind = cm[:, 640:642]
    b.push()
    w2a2 = b.sb([128, 256], name="w2a2")
    b.dma("sp", w2a2[:], w2a2_d[:, :], reads=[w2a2_d], writes=[w2a2])
    identb = b.sb([128, 128], BF16, name="identb")
    b.op("dve", lambda e: e.tensor_copy(out=identb[:], in_=ident), reads=[cm], writes=[identb])
    hmu = b.sb([128, 4], name="hmu")
    omm = b.sb([128, 4], name="omm")
    omka = b.sb([128, 1], name="omka")
    b.ts("dve", hmu[:], pp[:, 0:4], 0.5, None, ALU.mult, reads=[pp], writes=[hmu])
    b.ts("dve", omm[:], pp[:, 0:4], -1.0, 1.0, ALU.mult, ALU.add, reads=[pp], writes=[omm])
    b.ts("dve", omka[:], pp[:, 7:8], -1.0, 1.0, ALU.mult, ALU.add, reads=[pp], writes=[omka])
    WB = 1024
    m01 = b.sb([128, WB], name="m01")
    b.op("dve", lambda e: e.memset(m01[:], 1.0), writes=[m01])
    b.op("dve", lambda e: e.memset(m01[:].rearrange("p (n l) -> p n l", l=128)[:, :, 0:1], 0.0), writes=[m01])
    M = b.sb([128, 128], name="M")
    b.op("dve", lambda e: e.memset(M[:], 0.0), writes=[M])
    Mt = b.sb([128, 128], name="Mt")

    def fm(name, dt=F32):
        return b.sb([128, WB], dt, name=name)

    ub = [b.sb([128, WB + 2], name=f"ub{g}") for g in range(4)]
    mixed = [fm(f"mx{g}") for g in range(4)]
    tmp = fm("tmp")
    tmp2 = fm("tmp2")
    twl = fm("twl")
    sgw = fm("sgw")
    av = fm("av")
    cum = fm("cum")
    Pe = fm("Pe")
    Qe = fm("Qe")
    Pm = fm("Pm")
    kk = fm("kk")
    kd = fm("kd")
    vmb = fm("vmb")
    KKt = fm("KKt")
    Rt = fm("Rt")
    Kh = fm("Kh")
    Bh = fm("Bh")
    Khz = [fm(f"Khz{h}") for h in range(2)]
    Bhz = [fm(f"Bhz{h}") for h in range(2)]
    KKz = [fm(f"KKz{h}") for h in range(2)]
    wmid = b.sb([128, 8], name="wmid")
    dA = b.sb([128, 8], name="dA")
    bsb = b.sb([2, WB], name="bsb")
    psL = b.ps([128, 512], name="psL0")

    class _V:
        r = psL.r

        def __getitem__(self, k):
            return psL[:, :].rearrange("p (n l) -> p n l", l=128)[k]
    psLv = _V()
    psT = b.ps([128, 4, 128], name="psTr")
    psA = b.ps([128, 4, 128], name="psA")
    psB = b.ps([128, 4, 128], name="psB")
    psI1 = b.ps([128, 4, 128], name="psI1")
    psI2 = b.ps([128, 4, 128], name="psI2")
    psM = b.ps([128, 512], name="psM")
    psUY = b.ps([128, 2, 128], name="psUY")
    tok4 = b.sb([128, 5, 128], name="tok4")
    SA = b.sb([128, 4, 128], name="SA")
    SB_ = b.sb([128, 4, 128], name="SB")
    Xc = [b.sb([128, 2, 128], name=f"Xc{i}") for i in range(2)]
    XTc = [b.sb([128, 2, 128], name=f"XTc{i}") for i in range(2)]
    Gc = [b.sb([128, 2, 128], name=f"Gc{i}") for i in range(2)]
    KT = b.sb([128, 128], name="KT")
    AV = b.sb([128, 128], name="AV")
    X2 = b.sb([128, 128], name="X2")
    Un = b.sb([128, 128], name="Un")
    t1 = b.sb([128, 128], name="t1")
    Ysb = [b.sb([128, 128], name=f"Ysb{i}") for i in range(2)]
    rowoff = [0, 128, 256, 512]
    ci_glob = 0
    for (t0, nch, seg0, seg1) in blocks:
        Wd = nch * 128
        lo = t0 - 1 if t0 > seg0 else t0
        hi = t0 + Wd + 1 if t0 + Wd < seg1 else t0 + Wd
        for g in range(4):
            if lo == t0:
                b.op("dve", lambda e, u=ub[g]: e.memset(u[:, 0:1], 0.0), writes=[ub[g]])
            if hi == t0 + Wd:
                b.op("dve", lambda e, u=ub[g], Wd=Wd: e.memset(u[:, Wd + 1:Wd + 2], 0.0), writes=[ub[g]])
            b.dma("sp", ub[g][:, 1 - (t0 - lo):1 - (t0 - lo) + (hi - lo)],
                  U[rowoff[g]:rowoff[g] + 128, lo:hi], reads=[U], writes=[ub[g]])
            b.tt("dve", tmp[:, 0:Wd], ub[g][:, 0:Wd], ub[g][:, 2:Wd + 2], ALU.add, reads=[ub[g]], writes=[tmp])
            b.ts("dve", tmp2[:, 0:Wd], ub[g][:, 1:Wd + 1], omm[:, g:g + 1], None, ALU.mult, reads=[ub[g], omm], writes=[tmp2])
            b.stt(mixed[g][:, 0:Wd], tmp[:, 0:Wd], hmu[:, g:g + 1], tmp2[:, 0:Wd], ALU.mult, ALU.add, reads=[tmp, tmp2, hmu], writes=[mixed[g]])
        rm, km, vm, lm = mixed
        b.dma("sp", vg[0:128, t0:t0 + Wd], vm[:, 0:Wd], reads=[vm], writes=[vg])
        b.op("act", lambda e, Wd=Wd: e.copy(out=vmb[:, 0:Wd], in_=vm[:, 0:Wd]), reads=[vm], writes=[vmb])
        _stage(1)
        b.act(twl[:, 0:Wd], lm[:, 0:Wd], AF.Tanh, reads=[lm], writes=[twl])
        for pc in range(0, Wd, 512):
            pw = min(512, Wd - pc)
            b.mm(psL[:, 0:pw], w2a2[:, 0:128], twl[:, pc:pc + pw], reads=[w2a2, twl], writes=[psL])
            b.act(sgw[:, pc:pc + pw], psL[:, 0:pw], AF.Sigmoid, bias=pp[:, 4:5], reads=[psL, pp], writes=[sgw])
            b.mm(psL[:, 0:pw], w2a2[:, 128:256], lm[:, pc:pc + pw], reads=[w2a2, lm], writes=[psL])
            b.act(av[:, pc:pc + pw], psL[:, 0:pw], AF.Sigmoid, bias=pp[:, 5:6], reads=[psL, pp], writes=[av])
        b.ts("dve", sgw[:, 0:Wd], sgw[:, 0:Wd], -EM05, None, ALU.mult, reads=[sgw], writes=[sgw])
        b.op("dve", lambda e, Wd=Wd: e.tensor_tensor_scan(cum[:, 0:Wd], m01[:, 0:Wd], sgw[:, 0:Wd], 0.0, ALU.mult, ALU.add),
             reads=[m01, sgw], writes=[cum])
        c3 = cum[:, 0:Wd].rearrange("p (n l) -> p n l", l=128)
        cbar = c3[:, :, 63:64].to_broadcast([128, nch, 128])

        def v3(t, Wd=Wd):
            return t[:, 0:Wd].rearrange("p (n l) -> p n l", l=128)
        b.act(wmid[:, 0:nch], c3[:, :, 63], AF.Exp, reads=[cum], writes=[wmid])
        b.act(dA[:, 0:nch], c3[:, :, 127], AF.Exp, reads=[cum], writes=[dA])
        b.tt("dve", v3(tmp), c3, cbar, ALU.subtract, reads=[cum], writes=[tmp])
        b.tt("dve", tmp2[:, 0:Wd], tmp[:, 0:Wd], sgw[:, 0:Wd], ALU.subtract, reads=[tmp, sgw], writes=[tmp2])
        b.act(Pe[:, 0:Wd], tmp[:, 0:Wd], AF.Exp, reads=[tmp], writes=[Pe])
        b.act(Qe[:, 0:Wd], tmp[:, 0:Wd], AF.Exp, scale=-1.0, reads=[tmp], writes=[Qe])
        b.act(Pm[:, 0:Wd], tmp2[:, 0:Wd], AF.Exp, reads=[tmp2], writes=[Pm])
        _stage(2)
        b.ts("dve", kk[:, 0:Wd], km[:, 0:Wd], pp[:, 6:7], None, ALU.mult, reads=[km, pp], writes=[kk])
        b.tt("dve", tmp[:, 0:Wd], kk[:, 0:Wd], kk[:, 0:Wd], ALU.mult, reads=[kk], writes=[tmp])
        for pc in range(0, Wd, 512):
            pw = min(512, Wd - pc)
            b.mm(psL[:, 0:pw], bones, tmp[:, pc:pc + pw], reads=[cm, tmp], writes=[psL])
            b.act(tmp2[:, pc:pc + pw], psL[:, 0:pw], AF.Sqrt, reads=[psL], writes=[tmp2])
        b.ts("dve", tmp2[:, 0:Wd], tmp2[:, 0:Wd], 1e-12, None, ALU.max, reads=[tmp2], writes=[tmp2])
        b.op("dve", lambda e, Wd=Wd: e.reciprocal(tmp2[:, 0:Wd], tmp2[:, 0:Wd]), reads=[tmp2], writes=[tmp2])
        b.tt("dve", kk[:, 0:Wd], kk[:, 0:Wd], tmp2[:, 0:Wd], ALU.mult, reads=[kk, tmp2], writes=[kk])
        b.ts("dve", tmp[:, 0:Wd], av[:, 0:Wd], pp[:, 7:8], omka[:, 0:1], ALU.mult, ALU.add, reads=[av, pp, omka], writes=[tmp])
        b.tt("dve", kd[:, 0:Wd], km[:, 0:Wd], tmp[:, 0:Wd], ALU.mult, reads=[km, tmp], writes=[kd])
        b.stt(tmp[:, 0:Wd], rm[:, 0:Wd], pp[:, 8:9], kd[:, 0:Wd], ALU.mult, ALU.mult, reads=[rm, kd, pp], writes=[tmp])
        for pc in range(0, Wd, 512):
            pw = min(512, Wd - pc)
            b.mm(psL[0:2, 0:pw], hind, tmp[:, pc:pc + pw], reads=[cm, tmp], writes=[psL])
            b.op("act", lambda e, pc=pc, pw=pw: e.copy(out=bsb[:, pc:pc + pw], in_=psL[0:2, 0:pw]), reads=[psL], writes=[bsb])
        b.dma("sp", bon[:, t0:t0 + Wd], bsb[:, 0:Wd], reads=[bsb], writes=[bon])
        _stage(3)
        b.tt("dve", KKt[:, 0:Wd], kk[:, 0:Wd], Pm[:, 0:Wd], ALU.mult, reads=[kk, Pm], writes=[KKt])
        b.tt("dve", Rt[:, 0:Wd], rm[:, 0:Wd], Pe[:, 0:Wd], ALU.mult, reads=[rm, Pe], writes=[Rt])
        b.tt("dve", Kh[:, 0:Wd], kd[:, 0:Wd], Qe[:, 0:Wd], ALU.mult, reads=[kd, Qe], writes=[Kh])
        b.tt("dve", tmp[:, 0:Wd], kk[:, 0:Wd], av[:, 0:Wd], ALU.mult, reads=[kk, av], writes=[tmp])
        b.tt("dve", Bh[:, 0:Wd], tmp[:, 0:Wd], Qe[:, 0:Wd], ALU.mult, reads=[tmp, Qe], writes=[Bh])
        for h in range(2):
            b.ts("dve", Khz[h][:, 0:Wd], Kh[:, 0:Wd], hind[:, h:h + 1], None, ALU.mult, reads=[Kh, cm], writes=[Khz[h]])
            b.ts("dve", Bhz[h][:, 0:Wd], Bh[:, 0:Wd], hind[:, h:h + 1], None, ALU.mult, reads=[Bh, cm], writes=[Bhz[h]])
            b.ts("dve", KKz[h][:, 0:Wd], KKt[:, 0:Wd], hind[:, h:h + 1], None, ALU.mult, reads=[KKt, cm], writes=[KKz[h]])
        P3 = v3(Pe)
        _stage(4)
        for j in range(nch):
            c0 = j * 128
            cs = slice(c0, c0 + 128)
            for q, src in enumerate((Kh, Bh, vm, KKz[0])):
                b.tr(psT[:, q, :], src[:, cs], ident, reads=[src, cm], writes=[psT])
            b.tr(psM[:, 384:512], KKz[1][:, cs], ident, reads=[KKz[1], cm], writes=[psM])
            b.op("dve", lambda e: e.tensor_copy(out=tok4[:, 0:4, :], in_=psT[:]), reads=[psT], writes=[tok4])
            b.op("dve", lambda e: e.tensor_copy(out=tok4[:, 4, :], in_=psM[:, 384:512]), reads=[psM], writes=[tok4])
            if ci_glob == 0:
                b.dbg("tok4", tok4, tok4[:], [128, 5, 128])
                b.dbg("Kh", Kh, Kh[:, 0:128], [128, 128])
                b.dbg("KKt", KKt, KKt[:, 0:128], [128, 128])
                b.dbg("Bh", Bh, Bh[:, 0:128], [128, 128])
                b.dbg("Rt", Rt, Rt[:, 0:128], [128, 128])
            _stage(5)
            Kh_t, Bh_t, V_t = (tok4[:, q, :] for q in range(3))
            KKz_t = [tok4[:, 3, :], tok4[:, 4, :]]
            for h in range(2):
                b.mm(psA[:, 2 * h, :], Khz[h][:, cs], KKt[:, cs], reads=[Khz[h], KKt], writes=[psA])
                b.mm(psA[:, 2 * h + 1, :], Bhz[h][:, cs], KKt[:, cs], reads=[Bhz[h], KKt], writes=[psA])
                b.mm(psB[:, 2 * h, :], Khz[h][:, cs], Rt[:, cs], reads=[Khz[h], Rt], writes=[psB])
                b.mm(psB[:, 2 * h + 1, :], Bhz[h][:, cs], Rt[:, cs], reads=[Bhz[h], Rt], writes=[psB])
                b.mm(psL[:, 128 * h:128 * h + 128], KKz[h][:, cs], Bh[:, cs], reads=[Bh, KKz[h]], writes=[psL])
            b.tt("dve", SA[:], psA[:], su.unsqueeze(1).to_broadcast([128, 4, 128]), ALU.mult, reads=[psA, cm], writes=[SA])
            b.tt("dve", SB_[:], psB[:], ui.unsqueeze(1).to_broadcast([128, 4, 128]), ALU.mult, reads=[psB, cm], writes=[SB_])
            if ci_glob == 0:
                b.dbg("SA", SA, SA[:], [128, 4, 128])
                b.dbg("SB", SB_, SB_[:], [128, 4, 128])
            _stage(6)
            Xa, XTa, Ga = Xc[0], XTc[0], Gc[0]
            b.tt("dve", XTa[:], psL[:, 0:256].rearrange("p (n l) -> p n l", l=128), sl.unsqueeze(1).to_broadcast([128, 2, 128]), ALU.mult, reads=[psL, cm], writes=[XTa])
            b.op("dve", lambda e, Xa=Xa: e.tensor_copy(out=Xa[:], in_=SA[:, 1:4:2, :]), reads=[SA], writes=[Xa])
            b.tt("dve", Ga[:], ident.unsqueeze(1).to_broadcast([128, 2, 128]), SA[:, 1:4:2, :], ALU.subtract, reads=[SA, cm], writes=[Ga])
            _stage(6.1)
            cur = 0
            for lvl in range(1, 7):
                Xa, XTa, Ga = Xc[cur], XTc[cur], Gc[cur]
                Xn, XTn, Gn = Xc[1 - cur], XTc[1 - cur], Gc[1 - cur]
                pI = psA if lvl % 2 == 1 else psB
                for h in range(2):
                    if lvl < 6:
                        b.mm(pI[:, h, :], XTa[:, h, :], Xa[:, h, :], reads=[XTa, Xa], writes=[pI])
                    b.mm(pI[:, 2 + h, :], Xa[:, h, :], XTa[:, h, :], reads=[XTa, Xa], writes=[pI])
                _stage(6.2)
                if lvl < 6:
                    b.op("dve", lambda e, Xn=Xn, pI=pI: e.tensor_copy(out=Xn[:], in_=pI[:, 0:2, :]), reads=[pI], writes=[Xn])
                b.op("dve", lambda e, XTn=XTn, pI=pI: e.tensor_copy(out=XTn[:], in_=pI[:, 2:4, :]), reads=[pI], writes=[XTn])
                _stage(6.3)
                for h in range(2):
                    b.mm(psI2[:, h, :], XTn[:, h, :], Ga[:, h, :], reads=[XTn, Ga], writes=[psI2])
                b.tt("dve", Gn[:], Ga[:], psI2[:, 0:2, :], ALU.add, reads=[Ga, psI2], writes=[Gn])
                cur = 1 - cur
                _stage(6.4 + 0.01 * lvl)
            G = Gc[cur]
            if ci_glob == 0:
                b.dbg("G", G, G[:], [128, 2, 128])
            _stage(7)
            for h in range(2):
                hs = slice(64 * h, 64 * h + 64)
                b.mm(psI2[:, 2, :], KKz_t[h], G[:, h, :], start=(h == 0), stop=(h == 1), reads=[tok4, G], writes=[psI2])
                b.mm(psM[:, hs], SA[:, 2 * h, :], V_t[:, hs], reads=[SA, tok4], writes=[psM])
            b.op("dve", lambda e: e.tensor_copy(out=KT[:], in_=psI2[:, 2, :]), reads=[psI2], writes=[KT])
            b.op("dve", lambda e: e.tensor_copy(out=AV[:], in_=psM[:, 0:128]), reads=[psM], writes=[AV])
            for h in range(2):
                hs = slice(64 * h, 64 * h + 64)
                b.mm(psM[:, 128 + 64 * h:128 + 64 * h + 64], G[:, h, :], AV[:, hs], reads=[G, AV], writes=[psM])
            b.op("dve", lambda e: e.tensor_copy(out=X2[:], in_=psM[:, 128:256]), reads=[psM], writes=[X2])
            _stage(8)
            b.ts("dve", Mt[:], M[:], wmid[:, j:j + 1], None, ALU.mult, reads=[M, wmid], writes=[Mt])
            b.mm(psUY[:, 0, :], KT[:], Mt[:], reads=[KT, Mt], writes=[psUY])
            b.stt(Un[:], psUY[:, 0, :], -1.0, X2[:], ALU.mult, ALU.subtract, reads=[psUY, X2], writes=[Un])
            b.mm(psM[:, 256:384], Kh_t, V_t, start=True, stop=False, reads=[tok4], writes=[psM])
            b.mm(psM[:, 256:384], Bh_t, Un[:], start=False, stop=True, reads=[tok4, Un], writes=[psM])
            b.stt(t1[:], psM[:, 256:384], P3[:, j, 127:128], bones, ALU.mult, ALU.mult, reads=[psM, Pe, cm], writes=[t1])
            b.mm(psUY[:, 1, :], Rt[:, cs], Mt[:], start=True, stop=False, reads=[Rt, Mt], writes=[psUY])
            for h in range(2):
                hs = slice(64 * h, 64 * h + 64)
                b.mm(psUY[:, 1, hs], SB_[:, 2 * h, :], V_t[:, hs], start=False, stop=False, reads=[SB_, tok4], writes=[psUY])
                b.mm(psUY[:, 1, hs], SB_[:, 2 * h + 1, :], Un[:, hs], start=False, stop=(h == 1), reads=[SB_, Un], writes=[psUY])
            b.stt(M[:], M[:], dA[:, j:j + 1], t1[:], ALU.mult, ALU.add, reads=[M, dA, t1], writes=[M])
            Y = Ysb[ci_glob % 2]
            b.op("dve", lambda e, Y=Y: e.tensor_copy(out=Y[:], in_=psUY[:, 1, :]), reads=[psUY], writes=[Y])
            b.dma("sp", yA[t0 + c0:t0 + c0 + 128, :], Y[:], reads=[Y], writes=[yA])
            ci_glob += 1
    b.pop()


def mamba_phase(b, U, pp, cm, sel_d, blocks, yB, xsB, TT):
    ident = cm[:, 0:128]
    ui = cm[:, 384:512]
    b.push()
    sel = b.sb([128, 4, 128], name="sel")
    b.dma("sp", sel[:], sel_d[:, :].rearrange("p (h l) -> p h l", l=128), reads=[sel_d], writes=[sel])
    WB = 1024
    m01 = b.sb([128, WB], name="m01")
    b.op("dve", lambda e: e.memset(m01[:], 1.0), writes=[m01])
    b.op("dve", lambda e: e.memset(m01[:].rearrange("p (n l) -> p n l", l=128)[:, :, 0:1], 0.0), writes=[m01])
    Aneg = b.sb([4, 1], name="Aneg")
    b.act(Aneg[:], pp[0:4, 34:35], AF.Exp, reads=[pp], writes=[Aneg])
    b.ts("dve", Aneg[:], Aneg[:], -1.0, None, ALU.mult, reads=[Aneg], writes=[Aneg])
    ST = b.sb([128, 256], name="ST")
    b.op("dve", lambda e: e.memset(ST[:], 0.0), writes=[ST])
    ub = [b.sb([128, WB + 4], name=f"mub{q}") for q in range(4)]
    cv = [b.sb([128, WB], name=f"cv{q}") for q in range(4)]
    acc = b.sb([128, WB], name="acc")
    dtb = b.sb([128, WB], name="dtb")
    dta = b.sb([128, WB], name="dta")
    acs = b.sb([128, WB], name="acs")
    for t_ in (dtb, dta, acs):
        b.op("dve", lambda e, t_=t_: e.memset(t_[:], 0.0), writes=[t_])
    psT2 = b.ps([128, 2, 128], name="mpsT2")
    psT = b.ps([128, 4, 128], name="mpsT")
    psBC = b.ps([128, 4, 128], name="mpsBC")
    psCB = b.ps([128, 128], name="mpsCB")
    psY = b.ps([128, 256], name="mpsY")
    psO = b.ps([128, 256], name="mpsO")
    psS = b.ps([128, 256], name="mpsS")
    tok = b.sb([128, 3, 128], name="mtok")
    sm = b.sb([128, 8], name="msm")
    last = b.sb([128, 4], name="mlast")
    dd = b.sb([128, 4], name="mdd")
    te = b.sb([128, 4], name="mte")
    dec = b.sb([128, 4], name="mdec")
    eA = b.sb([128, 4], name="meA")
    seg = b.sb([128, 4, 128], name="mseg")
    Ee = b.sb([128, 4, 128], name="mE")
    CBm = b.sb([128, 128], name="mCBm")
    Wm = b.sb([128, 4, 128], name="mWm")
    xdt = b.sb([128, 256], name="mxdt")
    xw = b.sb([128, 256], name="mxw")
    ysb = b.sb([128, 256], name="mysb")
    yo = [b.sb([128, 256], name=f"myo{i}") for i in range(2)]
    xo = [b.sb([128, 256], name=f"mxo{i}") for i in range(2)]
    rowoff = [640, 768, 896, 1024]
    ci = 0
    for (t0, nch, seg0, seg1) in blocks:
        Wd = nch * 128
        lo = max(t0 - 2, seg0)
        hi = min(t0 + Wd + 2, seg1)
        for q in range(4):
            u = ub[q]
            if lo > t0 - 2:
                b.op("dve", lambda e, u=u: e.memset(u[:, 0:2], 0.0), writes=[u])
            if hi < t0 + Wd + 2:
                b.op("dve", lambda e, u=u, Wd=Wd: e.memset(u[:, Wd + 2:Wd + 4], 0.0), writes=[u])
            c_lo = lo - (t0 - 2)
            b.dma("sp", u[:, c_lo:c_lo + (hi - lo)], U[rowoff[q]:rowoff[q] + 128, lo:hi], reads=[U], writes=[u])
            wc = 9 + 5 * q
            b.ts("dve", acc[:, 0:Wd], u[:, 0:Wd], pp[:, wc:wc + 1], None, ALU.mult, reads=[u, pp], writes=[acc])
            for k in range(1, 5):
                b.stt(acc[:, 0:Wd], u[:, k:k + Wd], pp[:, wc + k:wc + k + 1], acc[:, 0:Wd], ALU.mult, ALU.add,
                      reads=[u, pp, acc], writes=[acc])
            b.act(cv[q][:, 0:Wd], acc[:, 0:Wd], AF.Silu, bias=pp[:, 29 + q:30 + q], reads=[acc, pp], writes=[cv[q]])
        b.dma("sp", dtb[0:4, 0:Wd], U[1408:1412, t0:t0 + Wd], reads=[U], writes=[dtb])
        b.act(dtb[0:4, 0:Wd], dtb[0:4, 0:Wd], AF.Exp, bias=pp[0:4, 33:34], reads=[dtb, pp], writes=[dtb])
        b.act(dtb[0:4, 0:Wd], dtb[0:4, 0:Wd], AF.Ln, bias=1.0, reads=[dtb], writes=[dtb])
        b.ts("dve", dta[0:4, 0:Wd], dtb[0:4, 0:Wd], Aneg[:, 0:1], None, ALU.mult, reads=[dtb, Aneg], writes=[dta])
        b.op("dve", lambda e, Wd=Wd: e.tensor_tensor_scan(acs[0:4, 0:Wd], m01[0:4, 0:Wd], dta[0:4, 0:Wd], 0.0, ALU.mult, ALU.add),
             reads=[m01, dta], writes=[acs])
        xs0, xs1, Bc, Cc = cv
        for j in range(nch):
            c0 = j * 128
            cs = slice(c0, c0 + 128)
            b.tr(psT[:, 0, :], xs0[:, cs], ident, reads=[xs0, cm], writes=[psT])
            b.tr(psT[:, 1, :], xs1[:, cs], ident, reads=[xs1, cm], writes=[psT])
            b.tr(psT[:, 2, :], Bc[:, cs], ident, reads=[Bc, cm], writes=[psT])
            b.tr(psT2[:, 0, :], acs[:, cs], ident, reads=[acs, cm], writes=[psT2])
            b.tr(psT2[:, 1, :], dtb[:, cs], ident, reads=[dtb, cm], writes=[psT2])
            b.op("dve", lambda e: e.tensor_copy(out=tok[:], in_=psT[:, 0:3, :]), reads=[psT], writes=[tok])
            b.op("dve", lambda e: e.tensor_copy(out=sm[:].rearrange("p (a q) -> p a q", q=4), in_=psT2[:, :, 0:4]), reads=[psT2], writes=[sm])
            x_tok = tok[:, 0:2, :]
            B_tok = tok[:, 2, :]
            for h in range(4):
                b.mm(psBC[:, h, :], sel[:, h, :], acs[:, cs], reads=[sel, acs], writes=[psBC])
            for h in range(4):
                b.ts("dve", seg[:, h, :], psBC[:, h, :], sm[:, h:h + 1], 0.0, ALU.subtract, ALU.min, reads=[psBC, sm], writes=[seg])
            b.op("dve", lambda e: e.tensor_copy(out=last[:], in_=psBC[:, :, 127]), reads=[psBC], writes=[last])
            b.act(Ee[:], seg[:], AF.Exp, reads=[seg], writes=[Ee])
            b.mm(psCB[:], Bc[:, cs], Cc[:, cs], reads=[Bc, Cc], writes=[psCB])
            b.tt("dve", CBm[:], psCB[:], ui, ALU.mult, reads=[psCB, cm], writes=[CBm])
            b.tt("dve", Wm[:], Ee[:], CBm[:].unsqueeze(1).to_broadcast([128, 4, 128]), ALU.mult, reads=[Ee, CBm], writes=[Wm])
            xt3 = tok[:, 0:2, :].rearrange("p a (h2 q) -> p (a h2) q", q=64)
            b.tt("dve", xdt[:].rearrange("p (h q) -> p h q", q=64), xt3, sm[:, 4:8].unsqueeze(2).to_broadcast([128, 4, 64]),
                 ALU.mult, reads=[tok, sm], writes=[xdt])
            for h in range(4):
                b.mm(psY[:, 64 * h:64 * h + 64], Wm[:, h, :], xdt[:, 64 * h:64 * h + 64], reads=[Wm, xdt], writes=[psY])
            b.mm(psO[:], Cc[:, cs], ST[:], reads=[Cc, ST], writes=[psO])
            b.act(eA[:], sm[:, 0:4], AF.Exp, reads=[sm], writes=[eA])
            b.op("dve", lambda e: e.tensor_copy(out=ysb[:], in_=psY[:]), reads=[psY], writes=[ysb])
            y = yo[ci % 2]
            b.tt("dve", y[:].rearrange("p (h q) -> p h q", q=64), psO[:].rearrange("p (h q) -> p h q", q=64),
                 eA[:].unsqueeze(2).to_broadcast([128, 4, 64]), ALU.mult, reads=[psO, eA], writes=[y])
            b.tt("dve", y[:], y[:], ysb[:], ALU.add, reads=[y, ysb], writes=[y])
            b.dma("sp", yB[t0 + c0:t0 + c0 + 128, :], y[:], reads=[y], writes=[yB])
            xout = xo[ci % 2]
            b.op("dve", lambda e, xout=xout: e.tensor_copy(out=xout[:].rearrange("p (a l) -> p a l", l=128), in_=tok[:, 0:2, :]),
                 reads=[tok], writes=[xout])
            b.dma("sp", xsB[t0 + c0:t0 + c0 + 128, :], xout[:], reads=[xout], writes=[xsB])
            b.tt("dve", dd[:], last[:], sm[:, 0:4], ALU.subtract, reads=[last, sm], writes=[dd])
            b.act(te[:], dd[:], AF.Exp, reads=[dd], writes=[te])
            b.act(dec[:], last[:], AF.Exp, reads=[last], writes=[dec])
            b.tt("dve", xw[:].rearrange("p (h q) -> p h q", q=64), xdt[:].rearrange("p (h q) -> p h q", q=64),
                 te[:].unsqueeze(2).to_broadcast([128, 4, 64]), ALU.mult, reads=[xdt, te], writes=[xw])
            b.mm(psS[:], B_tok, xw[:], reads=[tok, xw], writes=[psS])
            b.tt("dve", ST[:].rearrange("p (h q) -> p h q", q=64), ST[:].rearrange("p (h q) -> p h q", q=64),
                 dec[:].unsqueeze(2).to_broadcast([128, 4, 64]), ALU.mult, reads=[ST, dec], writes=[ST])
            b.tt("dve", ST[:], ST[:], psS[:], ALU.add, reads=[ST, psS], writes=[ST])
            ci += 1
    b.pop()


GN_EPS = 64e-5


def out_even(b, tiles, d, cm, final=False):
    ident = cm[:, 0:128]
    b.push()
    vb = b.sb([128, 5120], name="vb")
    for i in range(5):
        b.dma("sp", vb[:, i * 1024:(i + 1) * 1024], d["vecs"][0, i * 1024:(i + 1) * 1024].partition_broadcast(128),
              reads=[d["vecs"]], writes=[vb])
    lnw, lnb = vb[:, 0:512], vb[:, 512:1024]
    dvec, nw, adab = vb[:, 1024:2048], vb[:, 2048:3072], vb[:, 4096:5120]
    cvt = b.sb([128, 16], name="cvt")
    b.dma("sp", cvt[:], d["cv"][:, :], reads=[d["cv"]], writes=[cvt])
    sc = b.sb([128, 16], name="osc")
    b.act(sc[:], cvt[:], AF.Silu, reads=[cvt], writes=[sc])
    gate_b = [b.sb([128, 1024], name=f"gate{r}") for r in range(2)]
    Wg = [b.sb([128, 12, 1024], BF16, name=f"Wg{r}") for r in range(2)]
    b.push()
    aw = b.sb([128, 8, 1024], name="oaw")
    for k in range(8):
        b.dma("sp", aw[:, k, :], d["adawg"][k * 128:(k + 1) * 128, :], reads=[d["adawg"]], writes=[aw])
    scb = b.sb([128, 8, 128], name="scb")
    pg = b.ps([128, 512], name="pg")
    for r in range(2):
        b.op("dve", lambda e, r=r: e.tensor_copy(out=scb[:], in_=sc[:, r:16:2].unsqueeze(2).to_broadcast([128, 8, 128])),
             reads=[sc], writes=[scb])
        for half in range(2):
            hsl = slice(half * 512, half * 512 + 512)
            for k in range(8):
                b.mm(pg[:], scb[:, k, :], aw[:, k, hsl], start=(k == 0), stop=(k == 7), reads=[scb, aw], writes=[pg])
            b.tt("dve", gate_b[r][:, hsl], pg[:], adab[:, hsl], ALU.add, reads=[pg, vb], writes=[gate_b[r]])
    b.pop()
    b.push()
    stg = b.sb([128, 12, 1024], name="ostg")
    for k in range(12):
        b.dma("sp", stg[:, k, :], d["wout"][k * 128:(k + 1) * 128, :], reads=[d["wout"]], writes=[stg])
    for r in range(2):
        for k in range(12):
            b.tt("dve", Wg[r][:, k, :], stg[:, k, :], gate_b[r][:], ALU.mult, reads=[stg, gate_b[r]], writes=[Wg[r]])
    b.pop()
    yA = b.sb([128, 1024], name="oyA")
    bon = b.sb([128, 16], name="obon")
    v = b.sb([128, 512], name="ov")
    ga = b.sb([128, 512], name="oga")
    yB = b.sb([128, 2048], name="oyB")
    xsm = b.sb([128, 1024], name="oxsm")
    zz = b.sb([128, 1024], name="oz")
    xres = b.sb([128, 1024], name="oxres")
    ycat = b.sb([128, 1536], name="ycat")
    t5 = b.sb([128, 512], name="ot5")
    t10 = b.sb([128, 1024], name="ot10")
    st = b.sb([128, 64], name="ost")
    yT = b.sb([128, 12, 128], BF16, name="oyT")
    xo = b.sb([128, 1024], name="oxo")
    psT = [b.ps([128, 4, 128], name=f"opsT{i}") for i in range(3)]
    pso = [b.ps([128, 512], name=f"opso{i}") for i in range(2)]
    for (r0, r) in tiles:
        rs = slice(r0, r0 + 128)
        for tl, nm in ((yA, "yA"), (bon, "bon"), (v, "v"), (ga, "ga"), (yB, "yB"), (xsm, "xsm"), (zz, "z"), (xres, "xres")):
            b.dma("sp", tl[:], d[nm][rs, :], reads=[d[nm]], writes=[tl])
        y = ycat[:, 0:512]
        y3 = y.rearrange("p (h q) -> p h q", q=64)
        b.tt("dve", y, yA[:, 0:512], yA[:, 512:1024], ALU.add, reads=[yA], writes=[ycat])
        b.op("dve", lambda e: e.tensor_reduce(out=st[:, 0:8], in_=y3, axis=AX.X, op=ALU.add), reads=[ycat], writes=[st])
        b.tt("dve", t5[:], y, y, ALU.mult, reads=[ycat], writes=[t5])
        b.op("dve", lambda e: e.tensor_reduce(out=st[:, 8:16], in_=t5[:].rearrange("p (h q) -> p h q", q=64), axis=AX.X, op=ALU.add),
             reads=[t5], writes=[st])
        b.ts("dve", st[:, 0:8], st[:, 0:8], 1.0 / 64, None, ALU.mult, reads=[st], writes=[st])
        b.tt("dve", st[:, 16:24], st[:, 0:8], st[:, 0:8], ALU.mult, reads=[st], writes=[st])
        b.stt(st[:, 8:16], st[:, 8:16], 1.0 / 64, st[:, 16:24], ALU.mult, ALU.subtract, reads=[st], writes=[st])
        b.act(st[:, 8:16], st[:, 8:16], AF.Sqrt, bias=GN_EPS, reads=[st], writes=[st])
        b.op("dve", lambda e: e.reciprocal(st[:, 8:16], st[:, 8:16]), reads=[st], writes=[st])
        b.tt("dve", y3, y3, st[:, 0:8].unsqueeze(2).to_broadcast([128, 8, 64]), ALU.subtract, reads=[ycat, st], writes=[ycat])
        b.tt("dve", y3, y3, st[:, 8:16].unsqueeze(2).to_broadcast([128, 8, 64]), ALU.mult, reads=[ycat, st], writes=[ycat])
        b.tt("dve", y, y, lnw, ALU.mult, reads=[ycat, vb], writes=[ycat])
        b.tt("dve", y, y, lnb, ALU.add, reads=[ycat, vb], writes=[ycat])
        b.tt("dve", st[:, 24:32], bon[:, 0:8], bon[:, 8:16], ALU.add, reads=[bon], writes=[st])
        b.tt("dve", t5[:].rearrange("p (h q) -> p h q", q=64), v[:].rearrange("p (h q) -> p h q", q=64),
             st[:, 24:32].unsqueeze(2).to_broadcast([128, 8, 64]), ALU.mult, reads=[v, st], writes=[t5])
        b.tt("dve", y, y, t5[:], ALU.add, reads=[ycat, t5], writes=[ycat])
        b.act(ga[:], ga[:], AF.Silu, reads=[ga], writes=[ga])
        b.tt("dve", y, y, ga[:], ALU.mult, reads=[ycat, ga], writes=[ycat])
        yb = ycat[:, 512:1536]
        b.tt("dve", yb, yB[:, 0:1024], yB[:, 1024:2048], ALU.add, reads=[yB], writes=[ycat])
        b.tt("dve", t10[:], xsm[:], dvec, ALU.mult, reads=[xsm, vb], writes=[t10])
        b.tt("dve", yb, yb, t10[:], ALU.add, reads=[ycat, t10], writes=[ycat])
        b.act(zz[:], zz[:], AF.Silu, reads=[zz], writes=[zz])
        b.tt("dve", yb, yb, zz[:], ALU.mult, reads=[ycat, zz], writes=[ycat])
        b.tt("dve", t10[:], yb, yb, ALU.mult, reads=[ycat], writes=[t10])
        b.op("dve", lambda e: e.tensor_reduce(out=st[:, 32:34], in_=t10[:].rearrange("p (g q) -> p g q", q=512), axis=AX.X, op=ALU.add),
             reads=[t10], writes=[st])
        b.act(st[:, 32:34], st[:, 32:34], AF.Sqrt, bias=EPS, scale=1.0 / 512, reads=[st], writes=[st])
        b.op("dve", lambda e: e.reciprocal(st[:, 32:34], st[:, 32:34]), reads=[st], writes=[st])
        b.tt("dve", yb.rearrange("p (g q) -> p g q", q=512), yb.rearrange("p (g q) -> p g q", q=512),
             st[:, 32:34].unsqueeze(2).to_broadcast([128, 2, 512]), ALU.mult, reads=[ycat, st], writes=[ycat])
        b.tt("dve", yb, yb, nw, ALU.mult, reads=[ycat, vb], writes=[ycat])
        for k in range(12):
            pt = psT[k // 4]
            b.tr(pt[:, k % 4, :], ycat[:, k * 128:(k + 1) * 128], ident, reads=[ycat, cm], writes=[pt])
        for i3 in range(3):
            b.op("dve", lambda e, i3=i3: e.tensor_copy(out=yT[:, 4 * i3:4 * i3 + 4, :], in_=psT[i3][:]), reads=[psT[i3]], writes=[yT])
        for half in range(2):
            hsl = slice(half * 512, half * 512 + 512)
            po = pso[half]
            for k in range(12):
                b.mm(po[:], yT[:, k, :], Wg[r][:, k, hsl], start=(k == 0), stop=(k == 11), reads=[yT, Wg[r]], writes=[po])
            b.tt("dve", xo[:, hsl], po[:], xres[:, hsl], ALU.add, reads=[po, xres], writes=[xo])
        if final:
            b.act(t10[:], xo[:], AF.Square, reads=[xo], writes=[t10, st], accum=st[:, 40:41])
            b.act(st[:, 40:41], st[:, 40:41], AF.Sqrt, bias=EPS, scale=1.0 / 1024, reads=[st], writes=[st])
            b.op("dve", lambda e: e.reciprocal(st[:, 40:41], st[:, 40:41]), reads=[st], writes=[st])
            b.stt(xo[:], xo[:], st[:, 40:41], vb[:, 3072:4096], ALU.mult, ALU.mult, reads=[xo, st, vb], writes=[xo])
        b.dma("sp", d["xo"][rs, :], xo[:], reads=[xo], writes=[d["xo"]])
    b.pop()


C_ID, C_UI, C_BONES, C_SEL32, C_RQ, C_RK, C_E0, C_E1, C_SU, C_E64 = 0, 128, 256, 384, 512, 640, 768, 896, 1024, 1152


def mlstm_phase(b, U, pp, cmo, blocks, hC, TT):
    ident = cmo[:, C_ID:C_ID + 128]
    ui = cmo[:, C_UI:C_UI + 128]
    sel32 = cmo[:, C_SEL32:C_SEL32 + 128]
    b.push()
    WB = 1024
    m01 = b.sb([128, WB], name="lm01")
    b.op("dve", lambda e: e.memset(m01[:], 1.0), writes=[m01])
    b.op("dve", lambda e: e.memset(m01[:].rearrange("p (n l) -> p n l", l=128)[:, :, 0:1], 0.0), writes=[m01])
    CT1 = b.sb([128, 2, 257], name="CT1")
    b.op("dve", lambda e: e.memset(CT1[:], 0.0), writes=[CT1])
    ub = [b.sb([128, WB + 4], name=f"lub{q}") for q in range(4)]
    cv = [b.sb([128, WB], name=f"lcv{q}") for q in range(4)]
    vv = [b.sb([128, WB], name=f"lvv{q}") for q in range(2)]
    acc = b.sb([128, WB], name="lacc")
    gt = b.sb([128, WB], name="lgt")
    b.op("dve", lambda e: e.memset(gt[:], 0.0), writes=[gt])
    gt2 = b.sb([128, WB], name="lgt2")
    b.op("dve", lambda e: e.memset(gt2[:], 0.0), writes=[gt2])
    nfb = b.sb([128, 1], name="nfb")
    b.ts("dve", nfb[:], pp[:, 24:25], -1.0, None, ALU.mult, reads=[pp], writes=[nfb])
    psT = b.ps([128, 4, 128], name="lpsT")
    psG = b.ps([128, 3, 128], name="lpsG")
    psS = b.ps([128, 128], name="lpsS")
    psN = b.ps([128, 257], name="lpsN")
    psI = b.ps([128, 257], name="lpsI")
    psC = [b.ps([128, 257], name=f"lpsC{a}") for a in range(2)]
    ktok = b.sb([128, 256], name="lktok")
    v1 = b.sb([128, 257], name="lv1")
    b.op("dve", lambda e: e.memset(v1[:, 256:257], 1.0), writes=[v1])
    gtok = b.sb([128, 128], name="lgtok")
    sm = b.sb([128, 8], name="lsm")
    lw = b.sb([128, 128], name="llw")
    Ee = b.sb([128, 128], name="lE")
    WT = b.sb([128, 128], name="lWT")
    nsb = b.sb([128, 257], name="lnsb")
    tot = b.sb([128, 257], name="ltot")
    kw = b.sb([128, 256], name="lkw")
    ho = [b.sb([128, 256], name=f"lho{i}") for i in range(2)]
    rowoff = [0, 128, 256, 384]
    ci = 0
    for (t0, nch, seg0, seg1) in blocks:
        Wd = nch * 128
        lo = max(t0 - 2, seg0)
        hi = min(t0 + Wd + 2, seg1)
        for q in range(4):
            u = ub[q]
            if lo > t0 - 2:
                b.op("dve", lambda e, u=u: e.memset(u[:, 0:2], 0.0), writes=[u])
            if hi < t0 + Wd + 2:
                b.op("dve", lambda e, u=u, Wd=Wd: e.memset(u[:, Wd + 2:Wd + 4], 0.0), writes=[u])
            c_lo = lo - (t0 - 2)
            b.dma("sp", u[:, c_lo:c_lo + (hi - lo)], U[rowoff[q]:rowoff[q] + 128, lo:hi], reads=[U], writes=[u])
            wc = 5 * q
            b.ts("dve", acc[:, 0:Wd], u[:, 0:Wd], pp[:, wc:wc + 1], None, ALU.mult, reads=[u, pp], writes=[acc])
            for k in range(1, 5):
                b.stt(acc[:, 0:Wd], u[:, k:k + Wd], pp[:, wc + k:wc + k + 1], acc[:, 0:Wd], ALU.mult, ALU.add,
                      reads=[u, pp, acc], writes=[acc])
            b.act(cv[q][:, 0:Wd], acc[:, 0:Wd], AF.Silu, bias=pp[:, 20 + q:21 + q], reads=[acc, pp], writes=[cv[q]])
            if q < 2:
                b.ts("dve", cv[q][:, 0:Wd], cv[q][:, 0:Wd], 0.0625, None, ALU.mult, reads=[cv[q]], writes=[cv[q]])
        for a in range(2):
            b.dma("sp", vv[a][:, 0:Wd], U[512 + 128 * a:640 + 128 * a, t0:t0 + Wd], reads=[U], writes=[vv[a]])
        b.dma("sp", gt[0:1, 0:Wd], U[1408:1409, t0:t0 + Wd], reads=[U], writes=[gt])
        b.dma("sp", gt[32:33, 0:Wd], U[1440:1441, t0:t0 + Wd], reads=[U], writes=[gt])
        b.ts("dve", gt[0:1, 0:Wd], gt[0:1, 0:Wd], pp[0:1, 24:25], None, ALU.add, reads=[gt, pp], writes=[gt])
        b.act(gt[32:33, 0:Wd], gt[32:33, 0:Wd], AF.Exp, bias=nfb[32:33, 0:1], scale=-1.0, reads=[gt, nfb], writes=[gt])
        b.act(gt[32:33, 0:Wd], gt[32:33, 0:Wd], AF.Ln, bias=1.0, reads=[gt], writes=[gt])
        b.ts("dve", gt[32:33, 0:Wd], gt[32:33, 0:Wd], -1.0, None, ALU.mult, reads=[gt], writes=[gt])
        b.op("dve", lambda e, Wd=Wd: e.tensor_tensor_scan(gt2[32:33, 0:Wd], m01[32:33, 0:Wd], gt[32:33, 0:Wd], 0.0, ALU.mult, ALU.add),
             reads=[m01, gt], writes=[gt2])
        q0, q1, k0, k1 = cv
        qc = (q0, q1)
        kc = (k0, k1)
        for j in range(nch):
            c0 = j * 128
            cs = slice(c0, c0 + 128)
            b.tr(psT[:, 0, :], k0[:, cs], ident, reads=[k0, cmo], writes=[psT])
            b.tr(psT[:, 1, :], k1[:, cs], ident, reads=[k1, cmo], writes=[psT])
            b.tr(psT[:, 2, :], vv[0][:, cs], ident, reads=[vv[0], cmo], writes=[psT])
            b.tr(psT[:, 3, :], vv[1][:, cs], ident, reads=[vv[1], cmo], writes=[psT])
            b.tr(psG[:, 0, :], gt[:, cs], ident, reads=[gt, cmo], writes=[psG])
            b.tr(psG[:, 2, :], gt2[:, cs], ident, reads=[gt2, cmo], writes=[psG])
            b.mm(psG[:, 1, :], sel32, gt2[:, cs], reads=[cmo, gt2], writes=[psG])
            b.op("dve", lambda e: e.tensor_copy(out=ktok[:].rearrange("p (a l) -> p a l", l=128), in_=psT[:, 0:2, :]), reads=[psT], writes=[ktok])
            b.op("dve", lambda e: e.tensor_copy(out=v1[:, 0:256].rearrange("p (a l) -> p a l", l=128), in_=psT[:, 2:4, :]), reads=[psT], writes=[v1])
            b.op("dve", lambda e: e.tensor_copy(out=gtok[:, 0:1], in_=psG[:, 0, 0:1]), reads=[psG], writes=[gtok])
            b.op("dve", lambda e: e.tensor_copy(out=gtok[:, 32:33], in_=psG[:, 2, 32:33]), reads=[psG], writes=[gtok])
            b.tt("dve", sm[:, 0:1], gtok[:, 32:33], gtok[:, 0:1], ALU.subtract, reads=[gtok], writes=[sm])
            b.act(sm[:, 1:2], gtok[:, 32:33], AF.Exp, reads=[gtok], writes=[sm])
            b.op("dve", lambda e: e.tensor_copy(out=sm[:, 2:3], in_=psG[:, 1, 127:128]), reads=[psG], writes=[sm])
            b.ts("dve", lw[:], psG[:, 1, :], sm[:, 0:1], None, ALU.subtract, reads=[psG, sm], writes=[lw])
            b.act(Ee[:], lw[:], AF.Exp, reads=[lw], writes=[Ee])
            b.tt("dve", sm[:, 3:4], sm[:, 2:3], sm[:, 0:1], ALU.subtract, reads=[sm], writes=[sm])
            b.act(sm[:, 3:4], sm[:, 3:4], AF.Exp, reads=[sm], writes=[sm])
            b.act(sm[:, 4:5], sm[:, 2:3], AF.Exp, reads=[sm], writes=[sm])
            for a in range(2):
                b.mm(psS[:], kc[a][:, cs], qc[a][:, cs], start=(a == 0), stop=(a == 1), reads=[kc[a], qc[a]], writes=[psS])
            b.tt("dve", WT[:], psS[:], ui, ALU.mult, reads=[psS, cmo], writes=[WT])
            b.tt("dve", WT[:], WT[:], Ee[:], ALU.mult, reads=[WT, Ee], writes=[WT])
            b.mm(psN[:], WT[:], v1[:], reads=[WT, v1], writes=[psN])
            for a in range(2):
                b.mm(psI[:], qc[a][:, cs], CT1[:, a, :], start=(a == 0), stop=(a == 1), reads=[qc[a], CT1], writes=[psI])
            b.op("dve", lambda e: e.tensor_copy(out=nsb[:], in_=psN[:]), reads=[psN], writes=[nsb])
            b.stt(tot[:], psI[:], sm[:, 1:2], nsb[:], ALU.mult, ALU.add, reads=[psI, sm, nsb], writes=[tot])
            b.act(sm[:, 5:6], tot[:, 256:257], AF.Abs, reads=[tot], writes=[sm])
            b.ts("dve", sm[:, 5:6], sm[:, 5:6], 1.0, None, ALU.max, reads=[sm], writes=[sm])
            b.op("dve", lambda e: e.reciprocal(sm[:, 6:7], sm[:, 5:6]), reads=[sm], writes=[sm])
            h = ho[ci % 2]
            b.ts("dve", h[:], tot[:, 0:256], sm[:, 6:7], None, ALU.mult, reads=[tot, sm], writes=[h])
            b.dma("sp", hC[t0 + c0:t0 + c0 + 128, :], h[:], reads=[h], writes=[hC])
            b.ts("dve", kw[:], ktok[:], sm[:, 3:4], None, ALU.mult, reads=[ktok, sm], writes=[kw])
            for a in range(2):
                b.mm(psC[a][:], kw[:, 128 * a:128 * a + 128], v1[:], reads=[kw, v1], writes=[psC[a]])
                b.stt(CT1[:, a, :], CT1[:, a, :], sm[:, 4:5], psC[a][:], ALU.mult, ALU.add, reads=[CT1, sm, psC[a]], writes=[CT1])
            ci += 1
    b.pop()


def attn_phase(b, U, pp, cmo, tabs, TT, yD, need_ctx=True):
    ident = cmo[:, C_ID:C_ID + 128]
    bones = cmo[:, C_BONES:C_BONES + 128]
    Rq = cmo[:, C_RQ:C_RQ + 128]
    Rk = cmo[:, C_RK:C_RK + 128]
    E = [cmo[:, C_E0:C_E0 + 128], cmo[:, C_E1:C_E1 + 128]]
    e64 = cmo[:, C_E64:C_E64 + 64]
    cosq, sinq, cosk, sink = tabs
    NKT = TT // 128
    b.push()
    QT = b.sb([128, TT], BF16, name="QT")
    KTz = [b.sb([128, TT], BF16, name=f"KTz{h}") for h in range(2)]
    V1 = b.sb([128, NKT, 65], BF16, name="V1")
    b.op("dve", lambda e: e.memset(V1[:, :, 64:65], 1.0), writes=[V1])
    b.push()
    xq = b.sb([128, 512], name="axq")
    xk = b.sb([128, 512], name="axk")
    t1 = b.sb([128, 512], name="at1")
    t2 = b.sb([128, 512], name="at2")
    tc_ = b.sb([128, 512], name="atc")
    ts_ = b.sb([128, 512], name="ats")
    ps1 = b.ps([128, 512], name="aps1")
    ps2 = b.ps([128, 512], name="aps2")
    psV = b.ps([128, 4, 128], name="apsV")
    for p0 in range(0, TT, 512):
        pw = min(512, TT - p0)
        for which in range(2):
            x = xq if which == 0 else xk
            r0 = 1024 if which == 0 else 1152
            gcol = 25 + which
            ct, st_ = (cosq, sinq) if which == 0 else (cosk, sink)
            Rm = Rq if which == 0 else Rk
            b.dma("sp", x[:, 0:pw], U[r0:r0 + 128, p0:p0 + pw], reads=[U], writes=[x])
            b.dma("sp", tc_[:, 0:pw], ct[:, p0:p0 + pw], reads=[ct], writes=[tc_])
            b.dma("sp", ts_[:, 0:pw], st_[:, p0:p0 + pw], reads=[st_], writes=[ts_])
            b.tt("dve", t1[:, 0:pw], x[:, 0:pw], x[:, 0:pw], ALU.mult, reads=[x], writes=[t1])
            b.mm(ps1[:, 0:pw], bones, t1[:, 0:pw], reads=[cmo, t1], writes=[ps1])
            b.act(t1[:, 0:pw], ps1[:, 0:pw], AF.Sqrt, bias=EPS, scale=1.0 / 64, reads=[ps1], writes=[t1])
            b.op("dve", lambda e, pw=pw: e.reciprocal(t1[:, 0:pw], t1[:, 0:pw]), reads=[t1], writes=[t1])
            if which == 1:
                b.op("dve", lambda e, pw=pw: e.memset(t1[64:128, 0:pw], 1.0), writes=[t1])
            b.stt(t2[:, 0:pw], x[:, 0:pw], pp[:, gcol:gcol + 1], t1[:, 0:pw], ALU.mult, ALU.mult, reads=[x, pp, t1], writes=[t2])
            b.mm(ps2[:, 0:pw], Rm, t2[:, 0:pw], reads=[cmo, t2], writes=[ps2])
            b.tt("dve", t1[:, 0:pw], ps2[:, 0:pw], ts_[:, 0:pw], ALU.mult, reads=[ps2, ts_], writes=[t1])
            b.tt("dve", t2[:, 0:pw], t2[:, 0:pw], tc_[:, 0:pw], ALU.mult, reads=[t2, tc_], writes=[t2])
            if which == 0:
                b.stt(QT[:, p0:p0 + pw], t2[:, 0:pw], 1.0, t1[:, 0:pw], ALU.mult, ALU.add, reads=[t2, t1], writes=[QT])
                b.ts("dve", QT[:, p0:p0 + pw], QT[:, p0:p0 + pw], 0.125, None, ALU.mult, reads=[QT], writes=[QT])
            else:
                b.tt("dve", t2[:, 0:pw], t2[:, 0:pw], t1[:, 0:pw], ALU.add, reads=[t2, t1], writes=[t2])
                for h in range(2):
                    b.mm(ps1[:, 0:pw], E[h], t2[:, 0:pw], reads=[cmo, t2], writes=[ps1])
                    b.op("dve", lambda e, h=h, p0=p0, pw=pw: e.tensor_copy(out=KTz[h][:, p0:p0 + pw], in_=ps1[:, 0:pw]),
                         reads=[ps1], writes=[KTz[h]])
                nt = pw // 128
                for i in range(nt):
                    b.tr(psV[:, i, :], t2[:, i * 128:(i + 1) * 128], ident, reads=[t2, cmo], writes=[psV])
                b.op("dve", lambda e, p0=p0, nt=nt: e.tensor_copy(out=V1[:, p0 // 128:p0 // 128 + nt, 0:64], in_=psV[:, 0:nt, 64:128]),
                     reads=[psV], writes=[V1])
    b.pop()
    psS = [b.ps([128, 512], name=f"apsS{i}") for i in range(3)]
    psO = [b.ps([128, 512], name=f"apsO{h}") for h in range(2)]
    psD = b.ps([128, 512], name="apsD")
    PT = [b.sb([128, 512], BF16, name=f"aPT{i}") for i in range(3)]
    OT = b.sb([128, 512], name="aOT")
    b.op("dve", lambda e: e.memset(OT[:], 0.0), writes=[OT])
    rd = b.sb([64, 512], name="ard")
    yo = [b.sb([64, 512], name=f"ayo{i}") for i in range(2)]
    qblocks = []
    if need_ctx:
        qblocks.append((0, 256, 0, 2))
    t = 256
    while t < TT:
        w = min(512, TT - t)
        qblocks.append((t, w, 0, NKT))
        t += w
    steps = []
    for (q0, qw, k_lo, k_hi) in qblocks:
        for kt in range(k_lo, k_hi):
            for h in range(2):
                steps.append((q0, qw, kt, h, kt == k_lo, kt == k_hi - 1))
    oi = 0

    def issue_S(i):
        q0, qw, kt, h, first, last = steps[i]
        b.mm(psS[i % 3][:, 0:qw], KTz[h][:, kt * 128:(kt + 1) * 128], QT[:, q0:q0 + qw], reads=[KTz[h], QT], writes=[psS[i % 3]])

    LOOK = 2
    for i in range(min(LOOK, len(steps))):
        issue_S(i)
    for i, (q0, qw, kt, h, first, last) in enumerate(steps):
        ps = psS[i % 3]
        pt = PT[i % 3]
        b.act(pt[:, 0:qw], ps[:, 0:qw], AF.Exp, reads=[ps], writes=[pt])
        if i + LOOK < len(steps):
            issue_S(i + LOOK)
        b.mm(psO[h][0:65, 0:qw], V1[:, kt, :], pt[:, 0:qw], start=first, stop=last, reads=[V1, pt], writes=[psO[h]])
        if last:
            b.op("dve", lambda e, h=h, qw=qw: e.tensor_copy(out=OT[0:65, 0:qw], in_=psO[h][0:65, 0:qw]), reads=[psO[h]], writes=[OT])
            b.mm(psD[0:64, 0:qw], e64, OT[:, 0:qw], reads=[cmo, OT], writes=[psD])
            b.op("dve", lambda e, qw=qw: e.reciprocal(rd[:, 0:qw], psD[0:64, 0:qw]), reads=[psD], writes=[rd])
            y = yo[oi % 2]
            oi += 1
            b.tt("dve", y[:, 0:qw], OT[0:64, 0:qw], rd[:, 0:qw], ALU.mult, reads=[OT, rd], writes=[y])
            b.dma("sp", yD[64 * h:64 * h + 64, q0:q0 + qw], y[:, 0:qw], reads=[y], writes=[yD])
    b.pop()


def out_odd(b, tiles, d, cm, final=False):
    ident = cm[:, 0:128]
    b.push()
    vb = b.sb([128, 5120], name="vb")
    for i in range(5):
        b.dma("sp", vb[:, i * 1024:(i + 1) * 1024], d["vecs"][0, i * 1024:(i + 1) * 1024].partition_broadcast(128),
              reads=[d["vecs"]], writes=[vb])
    nw, adab = vb[:, 0:1024], vb[:, 4096:5120]
    cvt = b.sb([128, 16], name="cvt")
    b.dma("sp", cvt[:], d["cv"][:, :], reads=[d["cv"]], writes=[cvt])
    sc = b.sb([128, 16], name="osc")
    b.act(sc[:], cvt[:], AF.Silu, reads=[cvt], writes=[sc])
    gate_b = [b.sb([128, 1024], name=f"gate{r}") for r in range(2)]
    Wg = [b.sb([128, 16, 1024], BF16, name=f"Wg{r}") for r in range(2)]
    b.push()
    aw = b.sb([128, 8, 1024], name="oaw")
    for k in range(8):
        b.dma("sp", aw[:, k, :], d["adawg"][k * 128:(k + 1) * 128, :], reads=[d["adawg"]], writes=[aw])
    scb = b.sb([128, 8, 128], name="scb")
    pg = b.ps([128, 512], name="pg")
    for r in range(2):
        b.op("dve", lambda e, r=r: e.tensor_copy(out=scb[:], in_=sc[:, r:16:2].unsqueeze(2).to_broadcast([128, 8, 128])),
             reads=[sc], writes=[scb])
        for half in range(2):
            hsl = slice(half * 512, half * 512 + 512)
            for k in range(8):
                b.mm(pg[:], scb[:, k, :], aw[:, k, hsl], start=(k == 0), stop=(k == 7), reads=[scb, aw], writes=[pg])
            b.tt("dve", gate_b[r][:, hsl], pg[:], adab[:, hsl], ALU.add, reads=[pg, vb], writes=[gate_b[r]])
    b.pop()
    for part in range(2):
        b.push()
        stg = b.sb([128, 8, 1024], name="ostg")
        for k in range(8):
            kk = part * 8 + k
            b.dma("sp", stg[:, k, :], d["wout"][kk * 128:(kk + 1) * 128, :], reads=[d["wout"]], writes=[stg])
        for r in range(2):
            for k in range(8):
                b.tt("dve", Wg[r][:, part * 8 + k, :], stg[:, k, :], gate_b[r][:], ALU.mult, reads=[stg, gate_b[r]], writes=[Wg[r]])
        b.pop()
    hC = b.sb([128, 2048], name="ohC")
    oo = b.sb([128, 1024], name="oo")
    zz = b.sb([128, 1024], name="oz")
    yD = b.sb([128, 1024], name="oyD")
    ag = b.sb([128, 1024], name="oag")
    xres = b.sb([128, 1024], name="oxres")
    ycat = b.sb([128, 2048], name="ycat")
    t10 = b.sb([128, 1024], name="ot10")
    st = b.sb([128, 64], name="ost")
    yT = b.sb([128, 16, 128], BF16, name="oyT")
    xo = b.sb([128, 1024], name="oxo")
    psT = [b.ps([128, 4, 128], name=f"opsT{i}") for i in range(4)]
    pso = [b.ps([128, 512], name=f"opso{i}") for i in range(2)]
    for (r0, r) in tiles:
        rs = slice(r0, r0 + 128)
        for tl, nm in ((hC, "hC"), (oo, "o"), (zz, "z"), (yD, "yD"), (ag, "ag"), (xres, "xres")):
            b.dma("sp", tl[:], d[nm][rs, :], reads=[d[nm]], writes=[tl])
        h = ycat[:, 0:1024]
        h3 = h.rearrange("p (g q) -> p g q", q=256)
        b.tt("dve", h, hC[:, 0:1024], hC[:, 1024:2048], ALU.add, reads=[hC], writes=[ycat])
        b.tt("dve", t10[:], h, h, ALU.mult, reads=[ycat], writes=[t10])
        b.op("dve", lambda e: e.tensor_reduce(out=st[:, 0:4], in_=t10[:].rearrange("p (g q) -> p g q", q=256), axis=AX.X, op=ALU.add),
             reads=[t10], writes=[st])
        b.act(st[:, 0:4], st[:, 0:4], AF.Sqrt, bias=EPS, scale=1.0 / 256, reads=[st], writes=[st])
        b.op("dve", lambda e: e.reciprocal(st[:, 0:4], st[:, 0:4]), reads=[st], writes=[st])
        b.tt("dve", h3, h3, st[:, 0:4].unsqueeze(2).to_broadcast([128, 4, 256]), ALU.mult, reads=[ycat, st], writes=[ycat])
        b.tt("dve", h, h, nw, ALU.mult, reads=[ycat, vb], writes=[ycat])
        b.act(oo[:], oo[:], AF.Sigmoid, reads=[oo], writes=[oo])
        b.act(zz[:], zz[:], AF.Silu, reads=[zz], writes=[zz])
        b.tt("dve", h, h, oo[:], ALU.mult, reads=[ycat, oo], writes=[ycat])
        b.tt("dve", h, h, zz[:], ALU.mult, reads=[ycat, zz], writes=[ycat])
        b.act(ag[:], ag[:], AF.Silu, reads=[ag], writes=[ag])
        b.tt("dve", ycat[:, 1024:2048], yD[:], ag[:], ALU.mult, reads=[yD, ag], writes=[ycat])
        for k in range(16):
            pt = psT[k // 4]
            b.tr(pt[:, k % 4, :], ycat[:, k * 128:(k + 1) * 128], ident, reads=[ycat, cm], writes=[pt])
        for i4 in range(4):
            b.op("dve", lambda e, i4=i4: e.tensor_copy(out=yT[:, 4 * i4:4 * i4 + 4, :], in_=psT[i4][:]), reads=[psT[i4]], writes=[yT])
        for half in range(2):
            hsl = slice(half * 512, half * 512 + 512)
            po = pso[half]
            for k in range(16):
                b.mm(po[:], yT[:, k, :], Wg[r][:, k, hsl], start=(k == 0), stop=(k == 15), reads=[yT, Wg[r]], writes=[po])
            b.tt("dve", xo[:, hsl], po[:], xres[:, hsl], ALU.add, reads=[po, xres], writes=[xo])
        if final:
            b.act(t10[:], xo[:], AF.Square, reads=[xo], writes=[t10, st], accum=st[:, 40:41])
            b.act(st[:, 40:41], st[:, 40:41], AF.Sqrt, bias=EPS, scale=1.0 / 1024, reads=[st], writes=[st])
            b.op("dve", lambda e: e.reciprocal(st[:, 40:41], st[:, 40:41]), reads=[st], writes=[st])
            b.stt(xo[:], xo[:], st[:, 40:41], vb[:, 3072:4096], ALU.mult, ALU.mult, reads=[xo, st, vb], writes=[xo])
        b.dma("sp", d["xo"][rs, :], xo[:], reads=[xo], writes=[d["xo"]])
    b.pop()


def consts_cm():
    cm = np.zeros((128, 642), np.float32)
    i = np.arange(128)
    cm[:, 0:128] = np.eye(128)
    cm[:, 128:256] = (i[:, None] < i[None, :])
    cm[:, 256:384] = (i[:, None] > i[None, :])
    cm[:, 384:512] = (i[:, None] <= i[None, :])
    cm[:, 512:640] = ((i[:, None] // 64) == (i[None, :] // 64))
    cm[:, 640] = (i < 64)
    cm[:, 641] = (i >= 64)
    return cm


def fm8(v):
    return np.ascontiguousarray(v.reshape(8, 128).T)


def even_core_inputs(core, j_layer, l, inp, xseq):
    d, jj = core // 4, core % 4
    W = inp['ab_w_in'][j_layer]
    o_r, o_k, o_v = 0, 512, 1024
    o_wl, o_al = 1536, 1664
    o_ga = 1792
    o_xbc = 2304
    o_dt = o_xbc + 1536
    o_z = o_dt + 32
    hc = slice(128 * jj, 128 * jj + 128)
    g = jj // 2
    cols = []
    cols.append(np.arange(o_r, o_r + 512)[hc])
    cols.append(np.arange(o_k, o_k + 512)[hc])
    cols.append(np.arange(o_v, o_v + 512)[hc])
    cols.append(np.arange(o_ga, o_ga + 512)[hc])
    cols.append(np.concatenate([np.arange(o_wl + 64 * d, o_wl + 64 * d + 64), np.arange(o_al + 64 * d, o_al + 64 * d + 64)]))
    xs_cols = np.arange(o_xbc, o_xbc + 1024)[256 * jj:256 * jj + 256]
    cols.append(xs_cols[:128])
    cols.append(xs_cols[128:])
    cols.append(np.arange(o_xbc + 1024 + 128 * g, o_xbc + 1024 + 128 * g + 128))
    cols.append(np.arange(o_xbc + 1280 + 128 * g, o_xbc + 1280 + 128 * g + 128))
    z_cols = np.arange(o_z, o_z + 1024)[256 * jj:256 * jj + 256]
    cols.append(z_cols[:128])
    cols.append(z_cols[128:])
    win = np.zeros((1024, NCC * 128), np.float32)
    for cc, c in enumerate(cols):
        win[:, cc * 128:cc * 128 + len(c)] = W[:, c]
    dt_cols = np.arange(o_dt + 16 * d + 4 * jj, o_dt + 16 * d + 4 * jj + 4)
    win[:, 11 * 128:11 * 128 + 4] = W[:, dt_cols]
    pp = np.zeros((128, NPP), np.float32)
    mu = inp['rk_mu'][j_layer]
    pp[:, 0] = mu[o_r:o_r + 512][hc]
    pp[:, 1] = mu[o_k:o_k + 512][hc]
    pp[:, 2] = mu[o_v:o_v + 512][hc]
    pp[0:64, 3] = mu[o_wl + 64 * d:o_wl + 64 * d + 64]
    pp[64:128, 3] = mu[o_al + 64 * d:o_al + 64 * d + 64]
    pp[:, 4] = inp['rk_w0'][j_layer, d][hc]
    pp[:, 5] = inp['rk_a0'][j_layer, d][hc]
    pp[:, 6] = inp['rk_k_k'][j_layer][hc]
    pp[:, 7] = inp['rk_k_a'][j_layer][hc]
    pp[:, 8] = inp['rk_r_k'][j_layer].reshape(512)[hc]
    cw = inp['mb_conv_w'][j_layer]
    cb = inp['mb_conv_b'][j_layer]
    if d == 1:
        cw = cw[::-1]
    xbc_rel = [xs_cols[:128] - o_xbc, xs_cols[128:] - o_xbc, cols[7] - o_xbc, cols[8] - o_xbc]
    for q, rel in enumerate(xbc_rel):
        pp[:, 9 + 5 * q:14 + 5 * q] = cw[:, rel].T
        pp[:, 29 + q] = cb[rel]
    pp[0:4, 33] = inp['mb_dt_bias'][j_layer, d, 4 * jj:4 * jj + 4]
    pp[0:4, 34] = inp['mb_a_log'][j_layer, d, 4 * jj:4 * jj + 4]
    pp[:, 35:43] = fm8(inp['norm_g'][l])
    pp[:, 43:51] = fm8(inp['ada_b'][l][0:1024])
    pp[:, 51:59] = fm8(inp['ada_b'][l][1024:2048])
    cv = np.stack([fm8(inp['c'][0]), fm8(inp['c_ctx'])], -1)
    pp[:, 59:75] = cv.reshape(128, 16)
    w2a2 = np.zeros((128, 256), np.float32)
    w2a2[0:64, 0:128] = inp['rk_w2'][j_layer, d][:, hc]
    w2a2[64:128, 128:256] = inp['rk_a2'][j_layer, d][:, hc]
    return {
        'xseq': xseq, 'pp': pp, 'win': win,
        'adaw': np.ascontiguousarray(inp['ada_w'][l][:, 0:2048]),
        'w2a2': w2a2, 'cm': consts_cm(), 'sel': np.concatenate([np.kron(np.eye(4, dtype=np.float32), np.ones((1, 128), np.float32)), np.zeros((124, 512), np.float32)], 0),
    }


NCMO = 128 * 9 + 64


def consts_cmo():
    i = np.arange(128)
    c = np.zeros((128, NCMO), np.float32)
    c[:, 0:128] = np.eye(128)
    c[:, 128:256] = (i[:, None] <= i[None, :])
    c[:, 256:384] = ((i[:, None] // 64) == (i[None, :] // 64))
    c[32, 384:512] = 1.0
    R = np.zeros((128, 128), np.float32)
    for p in range(128):
        q = p % 32
        if q < 16:
            R[p + 16, p] = -1.0
        else:
            R[p - 16, p] = 1.0
    c[:, 512:640] = R
    Rk = R.copy()
    Rk[64:, :] = 0
    Rk[:, 64:] = 0
    c[:, 640:768] = Rk
    E0 = np.zeros((128, 128), np.float32)
    E1 = np.zeros((128, 128), np.float32)
    for dd in range(64):
        E0[dd, dd] = 1.0
        E1[dd, 64 + dd] = 1.0
    c[:, 768:896] = E0
    c[:, 896:1024] = E1
    c[:, 1024:1152] = (i[:, None] < i[None, :])
    c[64, 1152:1216] = 1.0
    return c


def rope_tables(TT, T, d, grid_w=64, theta=10000.0):
    idx = np.arange(T)
    t = idx if d == 0 else T - 1 - idx
    row = (t // grid_w).astype(np.float64)
    col = (t % grid_w).astype(np.float64)
    inv = theta ** (-np.arange(16, dtype=np.float64) / 16)
    cos = np.ones((128, TT), np.float64)
    sin = np.zeros((128, TT), np.float64)
    for p in range(128):
        pp_ = p % 64
        pos = row if pp_ < 32 else col
        ang = pos * inv[pp_ % 16]
        cos[p, TT - T:] = np.cos(ang)
        sin[p, TT - T:] = np.sin(ang)
    cosk, sink = cos.copy(), sin.copy()
    cosk[64:] = 1.0
    sink[64:] = 0.0
    return cos.astype(np.float32), sin.astype(np.float32), cosk.astype(np.float32), sink.astype(np.float32)


def odd_core_inputs(core, j, l, inp, xseq, T):
    d, jj = core // 4, core % 4
    c = core
    W = inp['cd_w_in'][j]
    TT = xseq.shape[0]
    cols = [np.arange(256 * jj, 256 * jj + 128), np.arange(256 * jj + 128, 256 * jj + 256),
            np.arange(1024 + 256 * jj, 1024 + 256 * jj + 128), np.arange(1024 + 256 * jj + 128, 1024 + 256 * jj + 256),
            np.arange(2048 + 256 * jj, 2048 + 256 * jj + 128), np.arange(2048 + 256 * jj + 128, 2048 + 256 * jj + 256)]
    oz = 3072 if d == 0 else 4112
    cols += [np.arange(oz + 256 * jj, oz + 256 * jj + 128), np.arange(oz + 256 * jj + 128, oz + 256 * jj + 256)]
    cols.append(np.arange(5136 + 128 * c, 5136 + 128 * c + 128))
    cols.append(np.concatenate([np.arange(6160 + 64 * (c // 2), 6160 + 64 * (c // 2) + 64),
                                np.arange(6416 + 64 * (c // 2), 6416 + 64 * (c // 2) + 64)]))
    cols.append(np.arange(6672 + 128 * c, 6672 + 128 * c + 128))
    win = np.zeros((1024, NCC * 128), np.float32)
    for cc, cl in enumerate(cols):
        win[:, cc * 128:cc * 128 + len(cl)] = W[:, cl]
    win[:, 11 * 128 + 0] = W[:, 4096 + 4 * d + jj]
    win[:, 11 * 128 + 32] = W[:, 4104 + 4 * d + jj]
    pp = np.zeros((128, NPP), np.float32)
    cw = inp['ml_conv_w'][j]
    cb = inp['ml_conv_b'][j]
    if d == 1:
        cw = cw[::-1]
    for q in range(4):
        pp[:, 5 * q:5 * q + 5] = cw[:, cols[q]].T
        pp[:, 20 + q] = cb[cols[q]]
    pp[0, 24] = inp['ml_i_bias'][j, d, jj]
    pp[32, 24] = inp['ml_f_bias'][j, d, jj]
    pp[:, 25] = np.tile(inp['at_q_norm'][j], 2)
    pp[0:64, 26] = inp['at_k_norm'][j]
    pp[64:, 26] = 1.0
    pp[:, 35:43] = fm8(inp['norm_g'][l])
    pp[:, 43:51] = fm8(inp['ada_b'][l][0:1024])
    pp[:, 51:59] = fm8(inp['ada_b'][l][1024:2048])
    cv = np.stack([fm8(inp['c'][0]), fm8(inp['c_ctx'])], -1)
    pp[:, 59:75] = cv.reshape(128, 16)
    cq, sq, ck, sk = rope_tables(TT, T, d)
    return {'xseq': xseq, 'pp': pp, 'win': win, 'adaw': np.ascontiguousarray(inp['ada_w'][l][:, 0:2048]),
            'cmo': consts_cmo(), 'cosq': cq, 'sinq': sq, 'cosk': ck, 'sink': sk}


def assemble_even_out_inputs(inp, outs, j, l, xs, ctx):
    def unflip(a):
        return np.concatenate([a[:256][::-1], a[256:][::-1]], 0)
    R = ctx.shape[0] + xs.shape[0]
    yA = np.zeros((R, 1024), np.float32); bon = np.zeros((R, 16), np.float32)
    v = np.zeros((R, 512), np.float32); ga = np.zeros((R, 512), np.float32)
    yB = np.zeros((R, 2048), np.float32); xsm = np.zeros((R, 1024), np.float32); z = np.zeros((R, 1024), np.float32)
    for core in range(8):
        d, jj = core // 4, core % 4
        o = outs[core]
        f = (lambda a: a) if d == 0 else unflip
        yA[:, 512 * d + 128 * jj:512 * d + 128 * jj + 128] = f(o["yA"])
        bon[:, 8 * d + 2 * jj:8 * d + 2 * jj + 2] = f(np.ascontiguousarray(o["bon"].T))
        yB[:, 1024 * d + 256 * jj:1024 * d + 256 * jj + 256] = f(o["yB"])
        if d == 0:
            v[:, 128 * jj:128 * jj + 128] = o["vg"][0:128].T
            ga[:, 128 * jj:128 * jj + 128] = o["vg"][128:256].T
            xsm[:, 256 * jj:256 * jj + 256] = o["xsB"]
            z[:, 256 * jj:256 * jj + 256] = o["zB"].T
    vecs = np.zeros((1, 5120), np.float32)
    vecs[0, 0:512] = inp['rk_ln_w'][j]; vecs[0, 512:1024] = inp['rk_ln_b'][j]
    vecs[0, 1024:2048] = np.repeat(inp['mb_d'][j], 64); vecs[0, 2048:3072] = inp['mb_norm_w'][j]
    vecs[0, 4096:5120] = inp['ada_b'][l][2048:3072]
    cv = np.stack([fm8(inp['c'][0]), fm8(inp['c_ctx'])], -1).reshape(128, 16)
    return {"yA": yA, "bon": bon, "v": v, "ga": ga, "yB": yB, "xsm": xsm, "z": z,
            "xres": np.ascontiguousarray(np.concatenate([ctx, xs], 0)),
            "wout": np.ascontiguousarray(inp['ab_w_out'][j]), "adawg": np.ascontiguousarray(inp['ada_w'][l][:, 2048:3072]),
            "vecs": vecs, "cv": np.ascontiguousarray(cv), "cm": consts_cm()}


def unflip(a):
    return np.concatenate([a[:256][::-1], a[256:][::-1]], 0)

def assemble_odd_out_inputs(inp, outs, j, l, xs, ctx):
    R = ctx.shape[0] + xs.shape[0]
    hC = np.zeros((R, 2048), np.float32); o = np.zeros((R, 1024), np.float32); z = np.zeros((R, 1024), np.float32)
    yD = np.zeros((R, 1024), np.float32); ag = np.zeros((R, 1024), np.float32)
    for core in range(8):
        d, jj = core // 4, core % 4
        oc = outs[core]
        f = (lambda a: a) if d == 0 else unflip
        hC[:, 1024 * d + 256 * jj:1024 * d + 256 * jj + 256] = f(oc["hC"])
        ozt = f(np.ascontiguousarray(oc["oz"].T))
        if d == 0:
            o[:, 256 * jj:256 * jj + 256] = ozt
        else:
            z[:, 256 * jj:256 * jj + 256] = ozt
        yD[:, 128 * core:128 * core + 128] = f(np.ascontiguousarray(oc["yD"].T))
        ag[:, 128 * core:128 * core + 128] = f(np.ascontiguousarray(oc["agT"].T))
    vecs = np.zeros((1, 5120), np.float32)
    vecs[0, 0:1024] = inp['ml_norm_w'][j]
    vecs[0, 4096:5120] = inp['ada_b'][l][2048:3072]
    cv = np.stack([fm8(inp['c'][0]), fm8(inp['c_ctx'])], -1).reshape(128, 16)
    return {"hC": hC, "o": o, "z": z, "yD": yD, "ag": ag, "xres": np.ascontiguousarray(np.concatenate([ctx, xs], 0)),
            "wout": np.ascontiguousarray(inp['cd_w_out'][j]), "adawg": np.ascontiguousarray(inp['ada_w'][l][:, 2048:3072]),
            "vecs": vecs, "cv": np.ascontiguousarray(cv), "cm": consts_cm()}


T_SEQ = 16384
CTX = 256
TT_ALL = T_SEQ + CTX


def _groups_blocks(TT):
    groups = [(0, 2, 1)]
    t = CTX
    while t < TT:
        n = min(4, (TT - t) // 128)
        groups.append((t, n, 0))
        t += n * 128
    blocks = [(0, 2, 0, CTX)]
    t = CTX
    while t < TT:
        n = min(8, (TT - t) // 128)
        blocks.append((t, n, CTX, TT))
        t += n * 128
    return groups, blocks


def build_mix_even():
    b = Bld()
    TT = TT_ALL
    xseq = b.dram("xseq", [TT, 1024], kind="ExternalInput")
    pp_d = b.dram("pp", [128, NPP], kind="ExternalInput")
    win_d = b.dram("win", [1024, NCC * 128], kind="ExternalInput")
    adaw_d = b.dram("adaw", [1024, 2048], kind="ExternalInput")
    w2a2_d = b.dram("w2a2", [128, 256], kind="ExternalInput")
    cm_d = b.dram("cm", [128, 642], kind="ExternalInput")
    sel_d = b.dram("sel", [128, 512], kind="ExternalInput")
    U = b.dram("U", [NCC * 128, TT], kind="Internal")
    yA = b.dram("yA", [TT, 128], kind="ExternalOutput")
    bon = b.dram("bon", [2, TT], kind="ExternalOutput")
    vg = b.dram("vg", [256, TT], kind="ExternalOutput")
    zB = b.dram("zB", [256, TT], kind="ExternalOutput")
    yB = b.dram("yB", [TT, 256], kind="ExternalOutput")
    xsB = b.dram("xsB", [TT, 256], kind="ExternalOutput")
    pp = b.sb([128, NPP], name="pp")
    cm = b.sb([128, 642], name="cm")
    b.dma("sp", pp[:], pp_d[:, :], reads=[pp_d], writes=[pp])
    b.dma("sp", cm[:], cm_d[:, :], reads=[cm_d], writes=[cm])
    modT = setup_mod(b, pp, adaw_d, 16)
    groups, blocks = _groups_blocks(TT)
    dests = {cc: (U, cc * 128) for cc in range(NCC)}
    dests[3] = (vg, 128)
    dests[9] = (zB, 0)
    dests[10] = (zB, 128)
    phase_A(b, xseq, win_d, pp, modT, cm, groups, dests, NCC)
    b.P.barrier()
    rwkv_phase(b, U, pp, w2a2_d, cm, blocks, yA, bon, vg, TT)
    mamba_phase(b, U, pp, cm, sel_d, blocks, yB, xsB, TT)
    b.P.emit()
    b.stacks[0].close()
    return b.nc


def build_mix_odd():
    b = Bld()
    TT = TT_ALL
    xseq = b.dram("xseq", [TT, 1024], kind="ExternalInput")
    pp_d = b.dram("pp", [128, NPP], kind="ExternalInput")
    win_d = b.dram("win", [1024, NCC * 128], kind="ExternalInput")
    adaw_d = b.dram("adaw", [1024, 2048], kind="ExternalInput")
    cmo_d = b.dram("cmo", [128, NCMO], kind="ExternalInput")
    tabs = [b.dram(n, [128, TT], kind="ExternalInput") for n in ("cosq", "sinq", "cosk", "sink")]
    U = b.dram("U", [NCC * 128, TT], kind="Internal")
    hC = b.dram("hC", [TT, 256], kind="ExternalOutput")
    oz = b.dram("oz", [256, TT], kind="ExternalOutput")
    yD = b.dram("yD", [128, TT], kind="ExternalOutput")
    agT = b.dram("agT", [128, TT], kind="ExternalOutput")
    pp = b.sb([128, NPP], name="pp")
    cmo = b.sb([128, NCMO], name="cmo")
    b.dma("sp", pp[:], pp_d[:, :], reads=[pp_d], writes=[pp])
    b.dma("sp", cmo[:], cmo_d[:, :], reads=[cmo_d], writes=[cmo])
    modT = setup_mod(b, pp, adaw_d, 16)
    groups, blocks = _groups_blocks(TT)
    dests = {cc: (U, cc * 128) for cc in range(NCC)}
    dests[6] = (oz, 0)
    dests[7] = (oz, 128)
    dests[10] = (agT, 0)
    phase_A(b, xseq, win_d, pp, modT, cmo, groups, dests, NCC)
    b.P.barrier()
    mlstm_phase(b, U, pp, cmo, blocks, hC, TT)
    attn_phase(b, U, pp, cmo, tabs, TT, yD, need_ctx=True)
    b.P.emit()
    b.stacks[0].close()
    return b.nc


def build_out(kind, R, tiles, final):
    b = Bld()
    if kind == "even":
        shapes = {"yA": [R, 1024], "bon": [R, 16], "v": [R, 512], "ga": [R, 512], "yB": [R, 2048], "xsm": [R, 1024],
                  "z": [R, 1024], "xres": [R, 1024], "wout": [1536, 1024], "adawg": [1024, 1024], "vecs": [1, 5120], "cv": [128, 16]}
    else:
        shapes = {"hC": [R, 2048], "o": [R, 1024], "z": [R, 1024], "yD": [R, 1024], "ag": [R, 1024], "xres": [R, 1024],
                  "wout": [2048, 1024], "adawg": [1024, 1024], "vecs": [1, 5120], "cv": [128, 16]}
    d = {k: b.dram(k, s, kind="ExternalInput") for k, s in shapes.items()}
    d["xo"] = b.dram("xo", [R, 1024], kind="ExternalOutput")
    cm_d = b.dram("cm", [128, 642], kind="ExternalInput")
    cm = b.sb([128, 642], name="cm")
    b.dma("sp", cm[:], cm_d[:, :], reads=[cm_d], writes=[cm])
    if kind == "even":
        out_even(b, tiles, d, cm, final=final)
    else:
        out_odd(b, tiles, d, cm, final=final)
    b.P.emit()
    b.stacks[0].close()
    return b.nc


def kernel(**inp):
    inp = {k: np.asarray(v) for k, v in inp.items()}
    xs = np.ascontiguousarray(inp['x'][0])
    ctx = np.ascontiguousarray(inp['ctx'][0])
    sh = T_SEQ // 8
    tiles = [(0, 1), (128, 1)] + [(CTX + 128 * i, 0) for i in range(sh // 128)]
    cores = list(range(8))
    for l in range(4):
        j = l // 2
        even = (l % 2 == 0)
        fwd = np.ascontiguousarray(np.concatenate([ctx, xs], 0))
        bwd = np.ascontiguousarray(np.concatenate([ctx[::-1], xs[::-1]], 0))
        if even:
            maps = [even_core_inputs(c, j, l, inp, fwd if c < 4 else bwd) for c in cores]
            res = run_bass_kernel_spmd(build_mix_even(), maps, core_ids=cores)
            full = assemble_even_out_inputs(inp, res.results, j, l, xs, ctx)
            row_keys = ("yA", "bon", "v", "ga", "yB", "xsm", "z", "xres")
        else:
            maps = [odd_core_inputs(c, j, l, inp, fwd if c < 4 else bwd, T_SEQ) for c in cores]
            res = run_bass_kernel_spmd(build_mix_odd(), maps, core_ids=cores)
            full = assemble_odd_out_inputs(inp, res.results, j, l, xs, ctx)
            row_keys = ("hC", "o", "z", "yD", "ag", "xres")
        del res, maps
        final = (l == 3)
        full["vecs"][0, 3072:4096] = inp['norm_final']
        maps2 = []
        for c in cores:
            rows = np.concatenate([np.arange(CTX), CTX + c * sh + np.arange(sh)])
            m = {k: np.ascontiguousarray(full[k][rows]) for k in row_keys}
            for k in ("wout", "adawg", "vecs", "cv", "cm"):
                m[k] = full[k]
            maps2.append(m)
        del full
        res2 = run_bass_kernel_spmd(build_out("even" if even else "odd", CTX + sh, tiles, final), maps2, core_ids=cores)
        outs = [np.asarray(r['xo'], dtype=np.float32) for r in res2.results]
        xs = np.ascontiguousarray(np.concatenate([o[CTX:] for o in outs], 0))
        ctx = np.ascontiguousarray(outs[0][:CTX])
        del res2, maps2, outs
    return xs[None]
```

```python
import contextlib
import os
import numpy as np
import concourse.bass as bass
import concourse.mybir as mybir
from concourse.bass_utils import run_bass_kernel_spmd

F32 = mybir.dt.float32
BF16 = mybir.dt.bfloat16
I32 = mybir.dt.int32
AF = mybir.ActivationFunctionType
ALU = mybir.AluOpType
AX = mybir.AxisListType

COMPUTE = ("pe", "dve", "act", "pool")
STREAMS = ("pe", "dve", "act", "pool", "sp")
DMA_K = 4


class Res:
    __slots__ = ("name", "w", "rs")

    def __init__(self, name):
        self.name = name
        self.w = None
        self.rs = []


class Ins:
    __slots__ = ("stream", "fn", "is_dma", "seq", "dn", "waits", "need_inc", "know")


class Prog:
    def __init__(self, nc):
        self.nc = nc
        self.ins = {s: [] for s in STREAMS}
        self.ndma = {s: 0 for s in STREAMS}
        self.know = {s: {e: -1 for e in COMPUTE} for s in STREAMS}
        self.kdma = {s: set() for s in STREAMS}
        self.n_res = 0
        self._bar = {}
        self._last = {}

    def res(self, name=None):
        self.n_res += 1
        return Res(name or f"r{self.n_res}")

    def _dep(self, I, D):
        s = I.stream
        if D is None or D is I:
            return
        if D.is_dma:
            key = (D.stream, D.dn)
            if key in self.kdma[s]:
                return
            self.kdma[s].add(key)
            I.waits.append(("dma", D.stream, D.dn % DMA_K, 16 * (D.dn // DMA_K + 1)))
        else:
            e = D.stream
            if self.know[s][e] >= D.seq:
                return
            if e == "pe" and s == "pe":
                return
            D.need_inc = True
            I.waits.append(("eng", e, D.seq))
            kn = self.know[s]
            kn[e] = D.seq
            for e2, v in D.know.items():
                if v > kn[e2]:
                    kn[e2] = v

    def _add(self, stream, fn, reads, writes, is_dma):
        I = Ins()
        I.stream = stream
        I.fn = fn
        I.is_dma = is_dma
        I.waits = []
        I.need_inc = False
        I.seq = None
        I.dn = None
        if is_dma:
            n = self.ndma[stream]
            self.ndma[stream] = n + 1
            I.dn = n
            if n >= DMA_K:
                key = (stream, n - DMA_K)
                if key not in self.kdma[stream]:
                    self.kdma[stream].add(key)
                    I.waits.append(("dma", stream, n % DMA_K, 16 * ((n - DMA_K) // DMA_K + 1)))
        bar = self._bar.pop(stream, None)
        if bar is not None:
            lastI, snap_d = bar
            for e, D in lastI.items():
                if D is not None:
                    self._dep(I, D)
            for s2, n2 in snap_d.items():
                for k in range(DMA_K):
                    cnt = len(range(k, n2, DMA_K))
                    if cnt:
                        I.waits.append(("dma", s2, k, 16 * cnt))
        for r in reads:
            self._dep(I, r.w)
        for r in writes:
            self._dep(I, r.w)
            for rd in r.rs:
                self._dep(I, rd)
        for r in reads:
            r.rs.append(I)
        for r in writes:
            r.w = I
            r.rs = []
        lst = self.ins[stream]
        if not is_dma:
            I.seq = self._nseq(stream)
            I.know = dict(self.know[stream])
        lst.append(I)
        if not is_dma:
            self._last[stream] = I
        return I

    def _nseq(self, stream):
        c = getattr(self, "_cnt", None)
        if c is None:
            c = self._cnt = {s: 0 for s in STREAMS}
        v = c[stream]
        c[stream] = v + 1
        return v

    def barrier(self):
        snap_e = {e: self._cnt_get(e) - 1 for e in COMPUTE}
        snap_d = {s: self.ndma[s] for s in STREAMS}
        lastI = {e: self._last.get(e) for e in COMPUTE}
        self._bar = {s: (lastI, dict(snap_d)) for s in STREAMS}

    def _cnt_get(self, e):
        c = getattr(self, "_cnt", None)
        return c[e] if c else 0

    def op(self, eng, fn, reads=(), writes=()):
        return self._add(eng, fn, reads, writes, False)

    def dma(self, stream, out, in_, reads=(), writes=()):
        return self._add(stream, lambda e: e.dma_start(out=out, in_=in_), reads, writes, True)

    def emit(self):
        nc = self.nc
        import contextlib
        with contextlib.ExitStack() as st:
            esem = {e: st.enter_context(nc.semaphore(f"s_{e}")) for e in COMPUTE}
            dsem = {}
            for s in STREAMS:
                if self.ndma[s] > 0:
                    for k in range(DMA_K):
                        dsem[(s, k)] = st.enter_context(nc.semaphore(f"d_{s}{k}"))
            inc_count = {e: 0 for e in COMPUTE}
            seq2cnt = {e: {} for e in COMPUTE}
            for e in COMPUTE:
                c = 0
                for I in self.ins[e]:
                    if I.is_dma:
                        continue
                    if I.need_inc:
                        c += 1
                        seq2cnt[e][I.seq] = c
            block = st.enter_context(nc.Block())

            def run(stream, eng):
                for I in self.ins[stream]:
                    for w in I.waits:
                        if w[0] == "dma":
                            eng.wait_ge(dsem[(w[1], w[2])], w[3])
                        else:
                            eng.wait_ge(esem[w[1]], seq2cnt[w[1]][w[2]])
                    r = I.fn(eng)
                    if I.is_dma:
                        r.then_inc(dsem[(stream, I.dn % DMA_K)], 16)
                    elif I.need_inc:
                        r.then_inc(esem[stream], 1)
                if stream == "sp":
                    for (s2, k), sem in dsem.items():
                        n = len(range(k, self.ndma[s2], DMA_K))
                        if n:
                            eng.wait_ge(sem, 16 * n)

            @block.tensor
            def _(eng):
                run("pe", eng)

            @block.vector
            def _(eng):
                run("dve", eng)

            @block.scalar
            def _(eng):
                run("act", eng)

            @block.gpsimd
            def _(eng):
                run("pool", eng)

            @block.sync
            def _(eng):
                run("sp", eng)


EPS = 1e-6
NCC = 12
NPP = 75
EM05 = float(np.exp(-0.5))
NO_POOL_OPS = bool(int(os.environ.get("NO_POOL_OPS", "1")))
NO_POOL_DMA = bool(int(os.environ.get("NO_POOL_DMA", "1")))


class Tl:
    __slots__ = ("t", "r")

    def __init__(self, t, r):
        self.t = t
        self.r = r

    def __getitem__(self, k):
        return self.t[k]


class Bld:
    def __init__(self):
        self.nc = bass.Bass("TRN2", target_bir_lowering=False)
        self.P = Prog(self.nc)
        self.stacks = [contextlib.ExitStack()]
        self.n = 0

    def push(self):
        self.stacks.append(contextlib.ExitStack())

    def pop(self):
        self.P.barrier()
        self.stacks.pop().close()

    def sb(self, shape, dt=F32, name=None):
        self.n += 1
        t = self.stacks[-1].enter_context(self.nc.sbuf_tensor(f"{name or 'sb'}_{self.n}", list(shape), dt))
        return Tl(t, self.P.res())

    def ps(self, shape, dt=F32, name=None):
        self.n += 1
        t = self.stacks[-1].enter_context(self.nc.psum_tensor(f"{name or 'ps'}_{self.n}", list(shape), dt))
        return Tl(t, self.P.res())

    def dram(self, name, shape, dt=F32, kind="Internal"):
        t = self.nc.dram_tensor(name, list(shape), dt, kind=kind)
        return Tl(t.ap(), self.P.res())

    def op(self, eng, fn, reads=(), writes=()):
        if eng == "pool" and NO_POOL_OPS:
            eng = "dve"
        return self.P.op(eng, fn, [x.r for x in reads], [x.r for x in writes])

    def dma(self, q, out, in_, reads=(), writes=()):
        if q == "pool" and NO_POOL_DMA:
            q = "sp"
        return self.P.dma(q, out, in_, [x.r for x in reads], [x.r for x in writes])

    def dbg(self, name, tl, ap, shape, dt=F32):
        if not getattr(self, "debug", False):
            return
        d = self.dram("dbg_" + name, shape, dt, kind="ExternalOutput")
        self.dma("sp", d[tuple(slice(None) for _ in shape)], ap, reads=[tl], writes=[d])

    def ts(self, eng, out, in0, s1, s2, op0, op1=None, reads=(), writes=()):
        if op1 is None:
            return self.op(eng, lambda e: e.tensor_scalar(out, in0, s1, None, op0), reads, writes)
        return self.op(eng, lambda e: e.tensor_scalar(out, in0, s1, s2, op0, op1), reads, writes)

    def tt(self, eng, out, in0, in1, op, reads=(), writes=()):
        return self.op(eng, lambda e: e.tensor_tensor(out, in0, in1, op), reads, writes)

    def stt(self, out, in0, sc, in1, op0, op1, reads=(), writes=()):
        return self.op("dve", lambda e: e.scalar_tensor_tensor(out, in0, sc, in1, op0, op1), reads, writes)

    def act(self, out, in_, func, bias=0.0, scale=1.0, reads=(), writes=(), accum=None):
        if accum is None:
            return self.op("act", lambda e: e.activation(out=out, in_=in_, func=func, bias=bias, scale=scale), reads, writes)
        return self.op("act", lambda e: e.activation(out=out, in_=in_, func=func, bias=bias, scale=scale, accum_out=accum), reads, writes)

    def mm(self, out, lhsT, rhs, start=True, stop=True, reads=(), writes=()):
        return self.op("pe", lambda e: e.matmul(out, lhsT=lhsT, rhs=rhs, start=start, stop=stop), reads, writes)

    def tr(self, out, in_, ident, reads=(), writes=()):
        return self.op("pe", lambda e: e.transpose(out, in_, ident), reads, writes)


def setup_mod(b, pp, adaw_d, nblk):
    sc = b.sb([128, 16], name="sc")
    b.act(sc[:], pp[:, 59:75], AF.Silu, reads=[pp], writes=[sc])
    modT = b.sb([128, nblk, 2], name="modT")
    b.push()
    aw = b.sb([128, 8, nblk * 128], name="aw")
    for k in range(8):
        b.dma("sp" if k % 2 == 0 else "pool", aw[:, k, :], adaw_d[k * 128:(k + 1) * 128, :], reads=[adaw_d], writes=[aw])
    pm = b.ps([128, nblk, 2], name="pm")
    for blk in range(nblk):
        for k in range(8):
            b.mm(pm[:, blk, :], aw[:, k, blk * 128:(blk + 1) * 128], sc[:, 2 * k:2 * k + 2], start=(k == 0), stop=(k == 7),
                 reads=[aw, sc], writes=[pm])
    for r in range(2):
        b.tt("dve", modT[:, :, r], pm[:, :, r], pp[:, 43:43 + nblk], ALU.add, reads=[pm, pp], writes=[modT])
    b.dbg("modT", modT, modT[:], [128, nblk, 2])
    b.pop()
    return modT


def phase_A(b, xseq, win_d, pp, modT, cm, groups, dests, ncc):
    nc = b.nc
    ident = cm[:, 0:128]
    b.push()
    b.dbg("modT2", modT, modT[:], [128, 16, 2])
    gm = b.sb([128, 8, 2], name="gm")
    b.ts("dve", gm[:], modT[:, 8:16, :], 1.0, None, ALU.add, reads=[modT], writes=[gm])
    for r in range(2):
        b.tt("dve", gm[:, :, r], gm[:, :, r], pp[:, 35:43], ALU.mult, reads=[gm, pp], writes=[gm])
    W = [b.sb([128, 8, ncc * 128], BF16, name=f"W{r}") for r in range(2)]
    sW = b.sb([128, ncc, 2], name="sW")
    b.push()
    stg = b.sb([128, 8, ncc * 128], name="stg")
    for k in range(8):
        b.dma("sp" if k % 2 == 0 else "pool", stg[:, k, :], win_d[k * 128:(k + 1) * 128, :], reads=[win_d], writes=[stg])
    psw = b.ps([128, ncc, 2], name="psw")
    for cc in range(ncc):
        for k in range(8):
            b.mm(psw[:, cc, :], stg[:, k, cc * 128:(cc + 1) * 128], modT[:, k, :], start=(k == 0), stop=(k == 7),
                 reads=[stg, modT], writes=[psw])
    b.op("act", lambda e: e.copy(out=sW[:], in_=psw[:]), reads=[psw], writes=[sW])
    for r in range(2):
        for k in range(8):
            eng = "dve" if k % 2 == 0 else "pool"
            b.ts(eng, W[r][:, k, :], stg[:, k, :], gm[:, k, r:r + 1], None, ALU.mult, reads=[stg, gm], writes=[W[r]])
    b.dbg("gm", gm, gm[:], [128, 8, 2])
    b.dbg("modT3", modT, modT[:], [128, 16, 2])
    b.dbg("sW", sW, sW[:], [128, ncc, 2])
    b.dbg("W0", W[0], W[0][:], [128, 8, ncc * 128], BF16)
    b.pop()
    NXB = 8
    xt = [b.sb([128, 1024], name=f"xt{i}") for i in range(NXB)]
    junk = b.sb([128, 1024], BF16, name="junk")
    ss = [b.sb([128, 1], name=f"ss{i}") for i in range(3)]
    rstd = [b.sb([128, 1], name=f"rstd{i}") for i in range(3)]
    xT = [b.sb([128, 8, 512], BF16, name=f"xT{i}") for i in range(2)]
    psT = [b.ps([128, 4, 128], name=f"psT{i}") for i in range(2)]
    pso = [b.ps([128, 512], name=f"pso{i}") for i in range(3)]
    ob = [b.sb([128, 512], name=f"ob{i}") for i in range(4)]
    tile_buf = {}
    nload = [0]

    def issue_loads(gi_):
        t0_, ntl_, _r = groups[gi_]
        for j_ in range(ntl_):
            x_ = xt[nload[0] % NXB]
            nload[0] += 1
            tile_buf[(gi_, j_)] = x_
            b.dma("sp", x_[:], xseq[t0_ + j_ * 128:t0_ + (j_ + 1) * 128, :], reads=[xseq], writes=[x_])

    tcnt = [0]

    def stats(gi_):
        _t0, ntl_, _r = groups[gi_]
        for j_ in range(ntl_):
            x = tile_buf[(gi_, j_)]
            s_ = ss[tcnt[0] % 3]
            rs_ = rstd[tcnt[0] % 3]
            tcnt[0] += 1
            b.act(junk[:], x[:], AF.Square, reads=[x], writes=[junk, s_], accum=s_[:])
            b.act(s_[:], s_[:], AF.Sqrt, bias=EPS, scale=1.0 / 1024, reads=[s_], writes=[s_])
            b.op("dve", lambda e, o=rs_, i=s_: e.reciprocal(o[:], i[:]), reads=[s_], writes=[rs_])
            b.ts("dve", x[:], x[:], rs_[:, 0:1], None, ALU.mult, reads=[x, rs_], writes=[x])

    def transposes(gi_):
        _t0, ntl_, _r = groups[gi_]
        xg_ = xT[gi_ % 2]
        for j_ in range(ntl_):
            x = tile_buf.pop((gi_, j_))
            for half in range(2):
                pt = psT[half]
                for q in range(4):
                    k = half * 4 + q
                    b.tr(pt[:, q, :], x[:, k * 128:(k + 1) * 128], ident, reads=[x, cm], writes=[pt])
                if half == 0:
                    b.op("act", lambda e, o=xg_, p=pt, j=j_: e.copy(out=o[:, 0:4, j * 128:(j + 1) * 128], in_=p[:]), reads=[pt], writes=[xg_])
                else:
                    b.op("dve", lambda e, o=xg_, p=pt, j=j_: e.tensor_copy(out=o[:, 4:8, j * 128:(j + 1) * 128], in_=p[:]), reads=[pt], writes=[xg_])

    ng = len(groups)
    issue_loads(0)
    if ng > 1:
        issue_loads(1)
    stats(0)
    transposes(0)
    oi = 0
    for gi, (t0, ntl, r) in enumerate(groups):
        G = ntl * 128
        xg = xT[gi % 2]
        if gi + 2 < ng:
            issue_loads(gi + 2)
        if gi + 1 < ng:
            stats(gi + 1)
        for cc in range(ncc):
            po = pso[oi % 3]
            o = ob[oi % 4]
            for k in range(8):
                b.mm(po[:, 0:G], W[r][:, k, cc * 128:(cc + 1) * 128], xg[:, k, 0:G], start=(k == 0), stop=(k == 7),
                     reads=[W[r], xg], writes=[po])
            if oi % 2 == 0:
                b.act(o[:, 0:G], po[:, 0:G], AF.Identity, bias=sW[:, cc, r:r + 1], reads=[po, sW], writes=[o])
            else:
                b.ts("dve", o[:, 0:G], po[:, 0:G], sW[:, cc, r:r + 1], None, ALU.add, reads=[po, sW], writes=[o])
            dst, roff = dests[cc]
            b.dma("sp", dst[roff:roff + 128, t0:t0 + G], o[:, 0:G], reads=[o], writes=[dst])
            oi += 1
            if cc == ncc // 2 - 1 and gi + 1 < ng:
                transposes(gi + 1)
    b.pop()


class _Stop(Exception):
    pass


def _stage(n):
    if float(os.environ.get("RW_STOP", "99")) <= n:
        raise _Stop()


def rwkv_phase(b, *a, **k):
    try:
        _rwkv_phase(b, *a, **k)
    except _Stop:
        b.pop()


def _rwkv_phase(b, U, pp, w2a2_d, cm, blocks, yA, bon, vg, TT, dbg=None):
    ident = cm[:, 0:128]
    su = cm[:, 128:256]
    sl = cm[:, 256:384]
    ui = cm[:, 384:512]
    bones = cm[:, 512:640]
    hind = cm[:, 640:642]
    b.push()
    w2a2 = b.sb([128, 256], name="w2a2")
    b.dma("sp", w2a2[:], w2a2_d[:, :], reads=[w2a2_d], writes=[w2a2])
    identb = b.sb([128, 128], BF16, name="identb")
    b.op("dve", lambda e: e.tensor_copy(out=identb[:], in_=ident), reads=[cm], writes=[identb])
    hmu = b.sb([128, 4], name="hmu")
    omm = b.sb([128, 4], name="omm")
    omka = b.sb([128, 1], name="omka")
    b.ts("dve", hmu[:], pp[:, 0:4], 0.5, None, ALU.mult, reads=[pp], writes=[hmu])
    b.ts("dve", omm[:], pp[:, 0:4], -1.0, 1.0, ALU.mult, ALU.add, reads=[pp], writes=[omm])
    b.ts("dve", omka[:], pp[:, 7:8], -1.0, 1.0, ALU.mult, ALU.add, reads=[pp], writes=[omka])
    WB = 1024
    m01 = b.sb([128, WB], name="m01")
    b.op("dve", lambda e: e.memset(m01[:], 1.0), writes=[m01])
    b.op("dve", lambda e: e.memset(m01[:].rearrange("p (n l) -> p n l", l=128)[:, :, 0:1], 0.0), writes=[m01])
    M = b.sb([128, 128], name="M")
    b.op("dve", lambda e: e.memset(M[:], 0.0), writes=[M])
    Mt = b.sb([128, 128], name="Mt")

    def fm(name, dt=F32):
        return b.sb([128, WB], dt, name=name)

    ub = [b.sb([128, WB + 2], name=f"ub{g}") for g in range(4)]
    mixed = [fm(f"mx{g}") for g in range(4)]
    tmp = fm("tmp")
    tmp2 = fm("tmp2")
    twl = fm("twl")
    sgw = fm("sgw")
    av = fm("av")
    cum = fm("cum")
    Pe = fm("Pe")
    Qe = fm("Qe")
    Pm = fm("Pm")
    kk = fm("kk")
    kd = fm("kd")
    vmb = fm("vmb")
    KKt = fm("KKt")
    Rt = fm("Rt")
    Kh = fm("Kh")
    Bh = fm("Bh")
    Khz = [fm(f"Khz{h}") for h in range(2)]
    Bhz = [fm(f"Bhz{h}") for h in range(2)]
    KKz = [fm(f"KKz{h}") for h in range(2)]
    wmid = b.sb([128, 8], name="wmid")
    dA = b.sb([128, 8], name="dA")
    bsb = b.sb([2, WB], name="bsb")
    psL = b.ps([128, 512], name="psL0")

    class _V:
        r = psL.r

        def __getitem__(self, k):
            return psL[:, :].rearrange("p (n l) -> p n l", l=128)[k]
    psLv = _V()
    psT = b.ps([128, 4, 128], name="psTr")
    psA = b.ps([128, 4, 128], name="psA")
    psB = b.ps([128, 4, 128], name="psB")
    psI1 = b.ps([128, 4, 128], name="psI1")
    psI2 = b.ps([128, 4, 128], name="psI2")
    psM = b.ps([128, 512], name="psM")
    psUY = b.ps([128, 2, 128], name="psUY")
    tok4 = b.sb([128, 5, 128], name="tok4")
    SA = b.sb([128, 4, 128], name="SA")
    SB_ = b.sb([128, 4, 128], name="SB")
    Xc = [b.sb([128, 2, 128], name=f"Xc{i}") for i in range(2)]
    XTc = [b.sb([128, 2, 128], name=f"XTc{i}") for i in range(2)]
    Gc = [b.sb([128, 2, 128], name=f"Gc{i}") for i in range(2)]
    KT = b.sb([128, 128], name="KT")
    AV = b.sb([128, 128], name="AV")
    X2 = b.sb([128, 128], name="X2")
    Un = b.sb([128, 128], name="Un")
    t1 = b.sb([128, 128], name="t1")
    Ysb = [b.sb([128, 128], name=f"Ysb{i}") for i in range(2)]
    rowoff = [0, 128, 256, 512]
    ci_glob = 0
    for (t0, nch, seg0, seg1) in blocks:
        Wd = nch * 128
        lo = t0 - 1 if t0 > seg0 else t0
        hi = t0 + Wd + 1 if t0 + Wd < seg1 else t0 + Wd
        for g in range(4):
            if lo == t0:
                b.op("dve", lambda e, u=ub[g]: e.memset(u[:, 0:1], 0.0), writes=[ub[g]])
            if hi == t0 + Wd:
                b.op("dve", lambda e, u=ub[g], Wd=Wd: e.memset(u[:, Wd + 1:Wd + 2], 0.0), writes=[ub[g]])
            b.dma("sp", ub[g][:, 1 - (t0 - lo):1 - (t0 - lo) + (hi - lo)],
                  U[rowoff[g]:rowoff[g] + 128, lo:hi], reads=[U], writes=[ub[g]])
            b.tt("dve", tmp[:, 0:Wd], ub[g][:, 0:Wd], ub[g][:, 2:Wd + 2], ALU.add, reads=[ub[g]], writes=[tmp])
            b.ts("dve", tmp2[:, 0:Wd], ub[g][:, 1:Wd + 1], omm[:, g:g + 1], None, ALU.mult, reads=[ub[g], omm], writes=[tmp2])
            b.stt(mixed[g][:, 0:Wd], tmp[:, 0:Wd], hmu[:, g:g + 1], tmp2[:, 0:Wd], ALU.mult, ALU.add, reads=[tmp, tmp2, hmu], writes=[mixed[g]])
        rm, km, vm, lm = mixed
        b.dma("sp", vg[0:128, t0:t0 + Wd], vm[:, 0:Wd], reads=[vm], writes=[vg])
        b.op("act", lambda e, Wd=Wd: e.copy(out=vmb[:, 0:Wd], in_=vm[:, 0:Wd]), reads=[vm], writes=[vmb])
        _stage(1)
        b.act(twl[:, 0:Wd], lm[:, 0:Wd], AF.Tanh, reads=[lm], writes=[twl])
        for pc in range(0, Wd, 512):
            pw = min(512, Wd - pc)
            b.mm(psL[:, 0:pw], w2a2[:, 0:128], twl[:, pc:pc + pw], reads=[w2a2, twl], writes=[psL])
            b.act(sgw[:, pc:pc + pw], psL[:, 0:pw], AF.Sigmoid, bias=pp[:, 4:5], reads=[psL, pp], writes=[sgw])
            b.mm(psL[:, 0:pw], w2a2[:, 128:256], lm[:, pc:pc + pw], reads=[w2a2, lm], writes=[psL])
            b.act(av[:, pc:pc + pw], psL[:, 0:pw], AF.Sigmoid, bias=pp[:, 5:6], reads=[psL, pp], writes=[av])
        b.ts("dve", sgw[:, 0:Wd], sgw[:, 0:Wd], -EM05, None, ALU.mult, reads=[sgw], writes=[sgw])
        b.op("dve", lambda e, Wd=Wd: e.tensor_tensor_scan(cum[:, 0:Wd], m01[:, 0:Wd], sgw[:, 0:Wd], 0.0, ALU.mult, ALU.add),
             reads=[m01, sgw], writes=[cum])
        c3 = cum[:, 0:Wd].rearrange("p (n l) -> p n l", l=128)
        cbar = c3[:, :, 63:64].to_broadcast([128, nch, 128])

        def v3(t, Wd=Wd):
            return t[:, 0:Wd].rearrange("p (n l) -> p n l", l=128)
        b.act(wmid[:, 0:nch], c3[:, :, 63], AF.Exp, reads=[cum], writes=[wmid])
        b.act(dA[:, 0:nch], c3[:, :, 127], AF.Exp, reads=[cum], writes=[dA])
        b.tt("dve", v3(tmp), c3, cbar, ALU.subtract, reads=[cum], writes=[tmp])
        b.tt("dve", tmp2[:, 0:Wd], tmp[:, 0:Wd], sgw[:, 0:Wd], ALU.subtract, reads=[tmp, sgw], writes=[tmp2])
        b.act(Pe[:, 0:Wd], tmp[:, 0:Wd], AF.Exp, reads=[tmp], writes=[Pe])
        b.act(Qe[:, 0:Wd], tmp[:, 0:Wd], AF.Exp, scale=-1.0, reads=[tmp], writes=[Qe])
        b.act(Pm[:, 0:Wd], tmp2[:, 0:Wd], AF.Exp, reads=[tmp2], writes=[Pm])
        _stage(2)
        b.ts("dve", kk[:, 0:Wd], km[:, 0:Wd], pp[:, 6:7], None, ALU.mult, reads=[km, pp], writes=[kk])
        b.tt("dve", tmp[:, 0:Wd], kk[:, 0:Wd], kk[:, 0:Wd], ALU.mult, reads=[kk], writes=[tmp])
        for pc in range(0, Wd, 512):
            pw = min(512, Wd - pc)
            b.mm(psL[:, 0:pw], bones, tmp[:, pc:pc + pw], reads=[cm, tmp], writes=[psL])
            b.act(tmp2[:, pc:pc + pw], psL[:, 0:pw], AF.Sqrt, reads=[psL], writes=[tmp2])
        b.ts("dve", tmp2[:, 0:Wd], tmp2[:, 0:Wd], 1e-12, None, ALU.max, reads=[tmp2], writes=[tmp2])
        b.op("dve", lambda e, Wd=Wd: e.reciprocal(tmp2[:, 0:Wd], tmp2[:, 0:Wd]), reads=[tmp2], writes=[tmp2])
        b.tt("dve", kk[:, 0:Wd], kk[:, 0:Wd], tmp2[:, 0:Wd], ALU.mult, reads=[kk, tmp2], writes=[kk])
        b.ts("dve", tmp[:, 0:Wd], av[:, 0:Wd], pp[:, 7:8], omka[:, 0:1], ALU.mult, ALU.add, reads=[av, pp, omka], writes=[tmp])
        b.tt("dve", kd[:, 0:Wd], km[:, 0:Wd], tmp[:, 0:Wd], ALU.mult, reads=[km, tmp], writes=[kd])
        b.stt(tmp[:, 0:Wd], rm[:, 0:Wd], pp[:, 8:9], kd[:, 0:Wd], ALU.mult, ALU.mult, reads=[rm, kd, pp], writes=[tmp])
        for pc in range(0, Wd, 512):
            pw = min(512, Wd - pc)
            b.mm(psL[0:2, 0:pw], hind, tmp[:, pc:pc + pw], reads=[cm, tmp], writes=[psL])
            b.op("act", lambda e, pc=pc, pw=pw: e.copy(out=bsb[:, pc:pc + pw], in_=psL[0:2, 0:pw]), reads=[psL], writes=[bsb])
        b.dma("sp", bon[:, t0:t0 + Wd], bsb[:, 0:Wd], reads=[bsb], writes=[bon])
        _stage(3)
        b.tt("dve", KKt[:, 0:Wd], kk[:, 0:Wd], Pm[:, 0:Wd], ALU.mult, reads=[kk, Pm], writes=[KKt])
        b.tt("dve", Rt[:, 0:Wd], rm[:, 0:Wd], Pe[:, 0:Wd], ALU.mult, reads=[rm, Pe], writes=[Rt])
        b.tt("dve", Kh[:, 0:Wd], kd[:, 0:Wd], Qe[:, 0:Wd], ALU.mult, reads=[kd, Qe], writes=[Kh])
        b.tt("dve", tmp[:, 0:Wd], kk[:, 0:Wd], av[:, 0:Wd], ALU.mult, reads=[kk, av], writes=[tmp])
        b.tt("dve", Bh[:, 0:Wd], tmp[:, 0:Wd], Qe[:, 0:Wd], ALU.mult, reads=[tmp, Qe], writes=[Bh])
        for h in range(2):
            b.ts("dve", Khz[h][:, 0:Wd], Kh[:, 0:Wd], hind[:, h:h + 1], None, ALU.mult, reads=[Kh, cm], writes=[Khz[h]])
            b.ts("dve", Bhz[h][:, 0:Wd], Bh[:, 0:Wd], hind[:, h:h + 1], None, ALU.mult, reads=[Bh, cm], writes=[Bhz[h]])
            b.ts("dve", KKz[h][:, 0:Wd], KKt[:, 0:Wd], hind[:, h:h + 1], None, ALU.mult, reads=[KKt, cm], writes=[KKz[h]])
        P3 = v3(Pe)
        _stage(4)
        for j in range(nch):
            c0 = j * 128
            cs = slice(c0, c0 + 128)
            for q, src in enumerate((Kh, Bh, vm, KKz[0])):
                b.tr(psT[:, q, :], src[:, cs], ident, reads=[src, cm], writes=[psT])
            b.tr(psM[:, 384:512], KKz[1][:, cs], ident, reads=[KKz[1], cm], writes=[psM])
            b.op("dve", lambda e: e.tensor_copy(out=tok4[:, 0:4, :], in_=psT[:]), reads=[psT], writes=[tok4])
            b.op("dve", lambda e: e.tensor_copy(out=tok4[:, 4, :], in_=psM[:, 384:512]), reads=[psM], writes=[tok4])
            if ci_glob == 0:
                b.dbg("tok4", tok4, tok4[:], [128, 5, 128])
                b.dbg("Kh", Kh, Kh[:, 0:128], [128, 128])
                b.dbg("KKt", KKt, KKt[:, 0:128], [128, 128])
                b.dbg("Bh", Bh, Bh[:, 0:128], [128, 128])
                b.dbg("Rt", Rt, Rt[:, 0:128], [128, 128])
            _stage(5)
            Kh_t, Bh_t, V_t = (tok4[:, q, :] for q in range(3))
            KKz_t = [tok4[:, 3, :], tok4[:, 4, :]]
            for h in range(2):
                b.mm(psA[:, 2 * h, :], Khz[h][:, cs], KKt[:, cs], reads=[Khz[h], KKt], writes=[psA])
                b.mm(psA[:, 2 * h + 1, :], Bhz[h][:, cs], KKt[:, cs], reads=[Bhz[h], KKt], writes=[psA])
                b.mm(psB[:, 2 * h, :], Khz[h][:, cs], Rt[:, cs], reads=[Khz[h], Rt], writes=[psB])
                b.mm(psB[:, 2 * h + 1, :], Bhz[h][:, cs], Rt[:, cs], reads=[Bhz[h], Rt], writes=[psB])
                b.mm(psL[:, 128 * h:128 * h + 128], KKz[h][:, cs], Bh[:, cs], reads=[Bh, KKz[h]], writes=[psL])
            b.tt("dve", SA[:], psA[:], su.unsqueeze(1).to_broadcast([128, 4, 128]), ALU.mult, reads=[psA, cm], writes=[SA])
            b.tt("dve", SB_[:], psB[:], ui.unsqueeze(1).to_broadcast([128, 4, 128]), ALU.mult, reads=[psB, cm], writes=[SB_])
            if ci_glob == 0:
                b.dbg("SA", SA, SA[:], [128, 4, 128])
                b.dbg("SB", SB_, SB_[:], [128, 4, 128])
            _stage(6)
            Xa, XTa, Ga = Xc[0], XTc[0], Gc[0]
            b.tt("dve", XTa[:], psL[:, 0:256].rearrange("p (n l) -> p n l", l=128), sl.unsqueeze(1).to_broadcast([128, 2, 128]), ALU.mult, reads=[psL, cm], writes=[XTa])
            b.op("dve", lambda e, Xa=Xa: e.tensor_copy(out=Xa[:], in_=SA[:, 1:4:2, :]), reads=[SA], writes=[Xa])
            b.tt("dve", Ga[:], ident.unsqueeze(1).to_broadcast([128, 2, 128]), SA[:, 1:4:2, :], ALU.subtract, reads=[SA, cm], writes=[Ga])
            _stage(6.1)
            cur = 0
            for lvl in range(1, 7):
                Xa, XTa, Ga = Xc[cur], XTc[cur], Gc[cur]
                Xn, XTn, Gn = Xc[1 - cur], XTc[1 - cur], Gc[1 - cur]
                pI = psA if lvl % 2 == 1 else psB
                for h in range(2):
                    if lvl < 6:
                        b.mm(pI[:, h, :], XTa[:, h, :], Xa[:, h, :], reads=[XTa, Xa], writes=[pI])
                    b.mm(pI[:, 2 + h, :], Xa[:, h, :], XTa[:, h, :], reads=[XTa, Xa], writes=[pI])
                _stage(6.2)
                if lvl < 6:
                    b.op("dve", lambda e, Xn=Xn, pI=pI: e.tensor_copy(out=Xn[:], in_=pI[:, 0:2, :]), reads=[pI], writes=[Xn])
                b.op("dve", lambda e, XTn=XTn, pI=pI: e.tensor_copy(out=XTn[:], in_=pI[:, 2:4, :]), reads=[pI], writes=[XTn])
                _stage(6.3)
                for h in range(2):
                    b.mm(psI2[:, h, :], XTn[:, h, :], Ga[:, h, :], reads=[XTn, Ga], writes=[psI2])
                b.tt("dve", Gn[:], Ga[:], psI2[:, 0:2, :], ALU.add, reads=[Ga, psI2], writes=[Gn])
                cur = 1 - cur
                _stage(6.4 + 0.01 * lvl)
            G = Gc[cur]
            if ci_glob == 0:
                b.dbg("G", G, G[:], [128, 2, 128])
            _stage(7)
            for h in range(2):
                hs = slice(64 * h, 64 * h + 64)
                b.mm(psI2[:, 2, :], KKz_t[h], G[:, h, :], start=(h == 0), stop=(h == 1), reads=[tok4, G], writes=[psI2])
                b.mm(psM[:, hs], SA[:, 2 * h, :], V_t[:, hs], reads=[SA, tok4], writes=[psM])
            b.op("dve", lambda e: e.tensor_copy(out=KT[:], in_=psI2[:, 2, :]), reads=[psI2], writes=[KT])
            b.op("dve", lambda e: e.tensor_copy(out=AV[:], in_=psM[:, 0:128]), reads=[psM], writes=[AV])
            for h in range(2):
                hs = slice(64 * h, 64 * h + 64)
                b.mm(psM[:, 128 + 64 * h:128 + 64 * h + 64], G[:, h, :], AV[:, hs], reads=[G, AV], writes=[psM])
            b.op("dve", lambda e: e.tensor_copy(out=X2[:], in_=psM[:, 128:256]), reads=[psM], writes=[X2])
            _stage(8)
            b.ts("dve", Mt[:], M[:], wmid[:, j:j + 1], None, ALU.mult, reads=[M, wmid], writes=[Mt])
            b.mm(psUY[:, 0, :], KT[:], Mt[:], reads=[KT, Mt], writes=[psUY])
            b.stt(Un[:], psUY[:, 0, :], -1.0, X2[:], ALU.mult, ALU.subtract, reads=[psUY, X2], writes=[Un])
            b.mm(psM[:, 256:384], Kh_t, V_t, start=True, stop=False, reads=[tok4], writes=[psM])
            b.mm(psM[:, 256:384], Bh_t, Un[:], start=False, stop=True, reads=[tok4, Un], writes=[psM])
            b.stt(t1[:], psM[:, 256:384], P3[:, j, 127:128], bones, ALU.mult, ALU.mult, reads=[psM, Pe, cm], writes=[t1])
            b.mm(psUY[:, 1, :], Rt[:, cs], Mt[:], start=True, stop=False, reads=[Rt, Mt], writes=[psUY])
            for h in range(2):
                hs = slice(64 * h, 64 * h + 64)
                b.mm(psUY[:, 1, hs], SB_[:, 2 * h, :], V_t[:, hs], start=False, stop=False, reads=[SB_, tok4], writes=[psUY])
                b.mm(psUY[:, 1, hs], SB_[:, 2 * h + 1, :], Un[:, hs], start=False, stop=(h == 1), reads=[SB_, Un], writes=[psUY])
            b.stt(M[:], M[:], dA[:, j:j + 1], t1[:], ALU.mult, ALU.add, reads=[M, dA, t1], writes=[M])
            Y = Ysb[ci_glob % 2]
            b.op("dve", lambda e, Y=Y: e.tensor_copy(out=Y[:], in_=psUY[:, 1, :]), reads=[psUY], writes=[Y])
            b.dma("sp", yA[t0 + c0:t0 + c0 + 128, :], Y[:], reads=[Y], writes=[yA])
            ci_glob += 1
    b.pop()


def mamba_phase(b, U, pp, cm, sel_d, blocks, yB, xsB, TT):
    ident = cm[:, 0:128]
    ui = cm[:, 384:512]
    b.push()
    sel = b.sb([128, 4, 128], name="sel")
    b.dma("sp", sel[:], sel_d[:, :].rearrange("p (h l) -> p h l", l=128), reads=[sel_d], writes=[sel])
    WB = 1024
    m01 = b.sb([128, WB], name="m01")
    b.op("dve", lambda e: e.memset(m01[:], 1.0), writes=[m01])
    b.op("dve", lambda e: e.memset(m01[:].rearrange("p (n l) -> p n l", l=128)[:, :, 0:1], 0.0), writes=[m01])
    Aneg = b.sb([4, 1], name="Aneg")
    b.act(Aneg[:], pp[0:4, 34:35], AF.Exp, reads=[pp], writes=[Aneg])
    b.ts("dve", Aneg[:], Aneg[:], -1.0, None, ALU.mult, reads=[Aneg], writes=[Aneg])
    ST = b.sb([128, 256], name="ST")
    b.op("dve", lambda e: e.memset(ST[:], 0.0), writes=[ST])
    ub = [b.sb([128, WB + 4], name=f"mub{q}") for q in range(4)]
    cv = [b.sb([128, WB], name=f"cv{q}") for q in range(4)]
    acc = b.sb([128, WB], name="acc")
    dtb = b.sb([128, WB], name="dtb")
    dta = b.sb([128, WB], name="dta")
    acs = b.sb([128, WB], name="acs")
    for t_ in (dtb, dta, acs):
        b.op("dve", lambda e, t_=t_: e.memset(t_[:], 0.0), writes=[t_])
    psT2 = b.ps([128, 2, 128], name="mpsT2")
    psT = b.ps([128, 4, 128], name="mpsT")
    psBC = b.ps([128, 4, 128], name="mpsBC")
    psCB = b.ps([128, 128], name="mpsCB")
    psY = b.ps([128, 256], name="mpsY")
    psO = b.ps([128, 256], name="mpsO")
    psS = b.ps([128, 256], name="mpsS")
    tok = b.sb([128, 3, 128], name="mtok")
    sm = b.sb([128, 8], name="msm")
    last = b.sb([128, 4], name="mlast")
    dd = b.sb([128, 4], name="mdd")
    te = b.sb([128, 4], name="mte")
    dec = b.sb([128, 4], name="mdec")
    eA = b.sb([128, 4], name="meA")
    seg = b.sb([128, 4, 128], name="mseg")
    Ee = b.sb([128, 4, 128], name="mE")
    CBm = b.sb([128, 128], name="mCBm")
    Wm = b.sb([128, 4, 128], name="mWm")
    xdt = b.sb([128, 256], name="mxdt")
    xw = b.sb([128, 256], name="mxw")
    ysb = b.sb([128, 256], name="mysb")
    yo = [b.sb([128, 256], name=f"myo{i}") for i in range(2)]
    xo = [b.sb([128, 256], name=f"mxo{i}") for i in range(2)]
    rowoff = [640, 768, 896, 1024]
    ci = 0
    for (t0, nch, seg0, seg1) in blocks:
        Wd = nch * 128
        lo = max(t0 - 2, seg0)
        hi = min(t0 + Wd + 2, seg1)
        for q in range(4):
            u = ub[q]
            if lo > t0 - 2:
                b.op("dve", lambda e, u=u: e.memset(u[:, 0:2], 0.0), writes=[u])
            if hi < t0 + Wd + 2:
                b.op("dve", lambda e, u=u, Wd=Wd: e.memset(u[:, Wd + 2:Wd + 4], 0.0), writes=[u])
            c_lo = lo - (t0 - 2)
            b.dma("sp", u[:, c_lo:c_lo + (hi - lo)], U[rowoff[q]:rowoff[q] + 128, lo:hi], reads=[U], writes=[u])
            wc = 9 + 5 * q
            b.ts("dve", acc[:, 0:Wd], u[:, 0:Wd], pp[:, wc:wc + 1], None, ALU.mult, reads=[u, pp], writes=[acc])
            for k in range(1, 5):
                b.stt(acc[:, 0:Wd], u[:, k:k + Wd], pp[:, wc + k:wc + k + 1], acc[:, 0:Wd], ALU.mult, ALU.add,
                      reads=[u, pp, acc], writes=[acc])
            b.act(cv[q][:, 0:Wd], acc[:, 0:Wd], AF.Silu, bias=pp[:, 29 + q:30 + q], reads=[acc, pp], writes=[cv[q]])
        b.dma("sp", dtb[0:4, 0:Wd], U[1408:1412, t0:t0 + Wd], reads=[U], writes=[dtb])
        b.act(dtb[0:4, 0:Wd], dtb[0:4, 0:Wd], AF.Exp, bias=pp[0:4, 33:34], reads=[dtb, pp], writes=[dtb])
        b.act(dtb[0:4, 0:Wd], dtb[0:4, 0:Wd], AF.Ln, bias=1.0, reads=[dtb], writes=[dtb])
        b.ts("dve", dta[0:4, 0:Wd], dtb[0:4, 0:Wd], Aneg[:, 0:1], None, ALU.mult, reads=[dtb, Aneg], writes=[dta])
        b.op("dve", lambda e, Wd=Wd: e.tensor_tensor_scan(acs[0:4, 0:Wd], m01[0:4, 0:Wd], dta[0:4, 0:Wd], 0.0, ALU.mult, ALU.add),
             reads=[m01, dta], writes=[acs])
        xs0, xs1, Bc, Cc = cv
        for j in range(nch):
            c0 = j * 128
            cs = slice(c0, c0 + 128)
            b.tr(psT[:, 0, :], xs0[:, cs], ident, reads=[xs0, cm], writes=[psT])
            b.tr(psT[:, 1, :], xs1[:, cs], ident, reads=[xs1, cm], writes=[psT])
            b.tr(psT[:, 2, :], Bc[:, cs], ident, reads=[Bc, cm], writes=[psT])
            b.tr(psT2[:, 0, :], acs[:, cs], ident, reads=[acs, cm], writes=[psT2])
            b.tr(psT2[:, 1, :], dtb[:, cs], ident, reads=[dtb, cm], writes=[psT2])
            b.op("dve", lambda e: e.tensor_copy(out=tok[:], in_=psT[:, 0:3, :]), reads=[psT], writes=[tok])
            b.op("dve", lambda e: e.tensor_copy(out=sm[:].rearrange("p (a q) -> p a q", q=4), in_=psT2[:, :, 0:4]), reads=[psT2], writes=[sm])
            x_tok = tok[:, 0:2, :]
            B_tok = tok[:, 2, :]
            for h in range(4):
                b.mm(psBC[:, h, :], sel[:, h, :], acs[:, cs], reads=[sel, acs], writes=[psBC])
            for h in range(4):
                b.ts("dve", seg[:, h, :], psBC[:, h, :], sm[:, h:h + 1], 0.0, ALU.subtract, ALU.min, reads=[psBC, sm], writes=[seg])
            b.op("dve", lambda e: e.tensor_copy(out=last[:], in_=psBC[:, :, 127]), reads=[psBC], writes=[last])
            b.act(Ee[:], seg[:], AF.Exp, reads=[seg], writes=[Ee])
            b.mm(psCB[:], Bc[:, cs], Cc[:, cs], reads=[Bc, Cc], writes=[psCB])
            b.tt("dve", CBm[:], psCB[:], ui, ALU.mult, reads=[psCB, cm], writes=[CBm])
            b.tt("dve", Wm[:], Ee[:], CBm[:].unsqueeze(1).to_broadcast([128, 4, 128]), ALU.mult, reads=[Ee, CBm], writes=[Wm])
            xt3 = tok[:, 0:2, :].rearrange("p a (h2 q) -> p (a h2) q", q=64)
            b.tt("dve", xdt[:].rearrange("p (h q) -> p h q", q=64), xt3, sm[:, 4:8].unsqueeze(2).to_broadcast([128, 4, 64]),
                 ALU.mult, reads=[tok, sm], writes=[xdt])
            for h in range(4):
                b.mm(psY[:, 64 * h:64 * h + 64], Wm[:, h, :], xdt[:, 64 * h:64 * h + 64], reads=[Wm, xdt], writes=[psY])
            b.mm(psO[:], Cc[:, cs], ST[:], reads=[Cc, ST], writes=[psO])
            b.act(eA[:], sm[:, 0:4], AF.Exp, reads=[sm], writes=[eA])
            b.op("dve", lambda e: e.tensor_copy(out=ysb[:], in_=psY[:]), reads=[psY], writes=[ysb])
            y = yo[ci % 2]
            b.tt("dve", y[:].rearrange("p (h q) -> p h q", q=64), psO[:].rearrange("p (h q) -> p h q", q=64),
                 eA[:].unsqueeze(2).to_broadcast([128, 4, 64]), ALU.mult, reads=[psO, eA], writes=[y])
            b.tt("dve", y[:], y[:], ysb[:], ALU.add, reads=[y, ysb], writes=[y])
            b.dma("sp", yB[t0 + c0:t0 + c0 + 128, :], y[:], reads=[y], writes=[yB])
            xout = xo[ci % 2]
            b.op("dve", lambda e, xout=xout: e.tensor_copy(out=xout[:].rearrange("p (a l) -> p a l", l=128), in_=tok[:, 0:2, :]),
                 reads=[tok], writes=[xout])
            b.dma("sp", xsB[t0 + c0:t0 + c0 + 128, :], xout[:], reads=[xout], writes=[xsB])
            b.tt("dve", dd[:], last[:], sm[:, 0:4], ALU.subtract, reads=[last, sm], writes=[dd])
            b.act(te[:], dd[:], AF.Exp, reads=[dd], writes=[te])
            b.act(dec[:], last[:], AF.Exp, reads=[last], writes=[dec])
            b.tt("dve", xw[:].rearrange("p (h q) -> p h q", q=64), xdt[:].rearrange("p (h q) -> p h q", q=64),
                 te[:].unsqueeze(2).to_broadcast([128, 4, 64]), ALU.mult, reads=[xdt, te], writes=[xw])
            b.mm(psS[:], B_tok, xw[:], reads=[tok, xw], writes=[psS])
            b.tt("dve", ST[:].rearrange("p (h q) -> p h q", q=64), ST[:].rearrange("p (h q) -> p h q", q=64),
                 dec[:].unsqueeze(2).to_broadcast([128, 4, 64]), ALU.mult, reads=[ST, dec], writes=[ST])
            b.tt("dve", ST[:], ST[:], psS[:], ALU.add, reads=[ST, psS], writes=[ST])
            ci += 1
    b.pop()


GN_EPS = 64e-5


def out_even(b, tiles, d, cm, final=False):
    ident = cm[:, 0:128]
    b.push()
    vb = b.sb([128, 5120], name="vb")
    for i in range(5):
        b.dma("sp", vb[:, i * 1024:(i + 1) * 1024], d["vecs"][0, i * 1024:(i + 1) * 1024].partition_broadcast(128),
              reads=[d["vecs"]], writes=[vb])
    lnw, lnb = vb[:, 0:512], vb[:, 512:1024]
    dvec, nw, adab = vb[:, 1024:2048], vb[:, 2048:3072], vb[:, 4096:5120]
    cvt = b.sb([128, 16], name="cvt")
    b.dma("sp", cvt[:], d["cv"][:, :], reads=[d["cv"]], writes=[cvt])
    sc = b.sb([128, 16], name="osc")
    b.act(sc[:], cvt[:], AF.Silu, reads=[cvt], writes=[sc])
    gate_b = [b.sb([128, 1024], name=f"gate{r}") for r in range(2)]
    Wg = [b.sb([128, 12, 1024], BF16, name=f"Wg{r}") for r in range(2)]
    b.push()
    aw = b.sb([128, 8, 1024], name="oaw")
    for k in range(8):
        b.dma("sp", aw[:, k, :], d["adawg"][k * 128:(k + 1) * 128, :], reads=[d["adawg"]], writes=[aw])
    scb = b.sb([128, 8, 128], name="scb")
    pg = b.ps([128, 512], name="pg")
    for r in range(2):
        b.op("dve", lambda e, r=r: e.tensor_copy(out=scb[:], in_=sc[:, r:16:2].unsqueeze(2).to_broadcast([128, 8, 128])),
             reads=[sc], writes=[scb])
        for half in range(2):
            hsl = slice(half * 512, half * 512 + 512)
            for k in range(8):
                b.mm(pg[:], scb[:, k, :], aw[:, k, hsl], start=(k == 0), stop=(k == 7), reads=[scb, aw], writes=[pg])
            b.tt("dve", gate_b[r][:, hsl], pg[:], adab[:, hsl], ALU.add, reads=[pg, vb], writes=[gate_b[r]])
    b.pop()
    b.push()
    stg = b.sb([128, 12, 1024], name="ostg")
    for k in range(12):
        b.dma("sp", stg[:, k, :], d["wout"][k * 128:(k + 1) * 128, :], reads=[d["wout"]], writes=[stg])
    for r in range(2):
        for k in range(12):
            b.tt("dve", Wg[r][:, k, :], stg[:, k, :], gate_b[r][:], ALU.mult, reads=[stg, gate_b[r]], writes=[Wg[r]])
    b.pop()
    names = ("yA", "bon", "v", "ga", "yB", "xsm", "z", "xres")
    widths = (1024, 16, 512, 512, 2048, 1024, 1024, 1024)
    sets = [[b.sb([128, w], name=f"oin{s_}_{n}") for n, w in zip(names, widths)] for s_ in range(2)]

    def load(i_):
        r0_, _r = tiles[i_]
        for tl, nm in zip(sets[i_ % 2], names):
            b.dma("sp", tl[:], d[nm][r0_:r0_ + 128, :], reads=[d[nm]], writes=[tl])

    ycat = b.sb([128, 1536], name="ycat")
    t5 = b.sb([128, 512], name="ot5")
    t10 = b.sb([128, 1024], name="ot10")
    st = b.sb([128, 64], name="ost")
    yT = b.sb([128, 12, 128], BF16, name="oyT")
    xo = b.sb([128, 1024], name="oxo")
    psT = [b.ps([128, 4, 128], name=f"opsT{i}") for i in range(3)]
    pso = [b.ps([128, 512], name=f"opso{i}") for i in range(2)]
    load(0)
    for ti_, (r0, r) in enumerate(tiles):
        rs = slice(r0, r0 + 128)
        if ti_ + 1 < len(tiles):
            load(ti_ + 1)
        yA, bon, v, ga, yB, xsm, zz, xres = sets[ti_ % 2]
        y = ycat[:, 0:512]
        y3 = y.rearrange("p (h q) -> p h q", q=64)
        b.tt("dve", y, yA[:, 0:512], yA[:, 512:1024], ALU.add, reads=[yA], writes=[ycat])
        b.op("dve", lambda e: e.tensor_reduce(out=st[:, 0:8], in_=y3, axis=AX.X, op=ALU.add), reads=[ycat], writes=[st])
        b.tt("dve", t5[:], y, y, ALU.mult, reads=[ycat], writes=[t5])
        b.op("dve", lambda e: e.tensor_reduce(out=st[:, 8:16], in_=t5[:].rearrange("p (h q) -> p h q", q=64), axis=AX.X, op=ALU.add),
             reads=[t5], writes=[st])
        b.ts("dve", st[:, 0:8], st[:, 0:8], 1.0 / 64, None, ALU.mult, reads=[st], writes=[st])
        b.tt("dve", st[:, 16:24], st[:, 0:8], st[:, 0:8], ALU.mult, reads=[st], writes=[st])
        b.stt(st[:, 8:16], st[:, 8:16], 1.0 / 64, st[:, 16:24], ALU.mult, ALU.subtract, reads=[st], writes=[st])
        b.act(st[:, 8:16], st[:, 8:16], AF.Sqrt, bias=GN_EPS, reads=[st], writes=[st])
        b.op("dve", lambda e: e.reciprocal(st[:, 8:16], st[:, 8:16]), reads=[st], writes=[st])
        b.tt("dve", y3, y3, st[:, 0:8].unsqueeze(2).to_broadcast([128, 8, 64]), ALU.subtract, reads=[ycat, st], writes=[ycat])
        b.tt("dve", y3, y3, st[:, 8:16].unsqueeze(2).to_broadcast([128, 8, 64]), ALU.mult, reads=[ycat, st], writes=[ycat])
        b.tt("dve", y, y, lnw, ALU.mult, reads=[ycat, vb], writes=[ycat])
        b.tt("dve", y, y, lnb, ALU.add, reads=[ycat, vb], writes=[ycat])
        b.tt("dve", st[:, 24:32], bon[:, 0:8], bon[:, 8:16], ALU.add, reads=[bon], writes=[st])
        b.tt("dve", t5[:].rearrange("p (h q) -> p h q", q=64), v[:].rearrange("p (h q) -> p h q", q=64),
             st[:, 24:32].unsqueeze(2).to_broadcast([128, 8, 64]), ALU.mult, reads=[v, st], writes=[t5])
        b.tt("dve", y, y, t5[:], ALU.add, reads=[ycat, t5], writes=[ycat])
        b.act(ga[:], ga[:], AF.Silu, reads=[ga], writes=[ga])
        b.tt("dve", y, y, ga[:], ALU.mult, reads=[ycat, ga], writes=[ycat])
        yb = ycat[:, 512:1536]
        b.tt("dve", yb, yB[:, 0:1024], yB[:, 1024:2048], ALU.add, reads=[yB], writes=[ycat])
        b.tt("dve", t10[:], xsm[:], dvec, ALU.mult, reads=[xsm, vb], writes=[t10])
        b.tt("dve", yb, yb, t10[:], ALU.add, reads=[ycat, t10], writes=[ycat])
        b.act(zz[:], zz[:], AF.Silu, reads=[zz], writes=[zz])
        b.tt("dve", yb, yb, zz[:], ALU.mult, reads=[ycat, zz], writes=[ycat])
        b.tt("dve", t10[:], yb, yb, ALU.mult, reads=[ycat], writes=[t10])
        b.op("dve", lambda e: e.tensor_reduce(out=st[:, 32:34], in_=t10[:].rearrange("p (g q) -> p g q", q=512), axis=AX.X, op=ALU.add),
             reads=[t10], writes=[st])
        b.act(st[:, 32:34], st[:, 32:34], AF.Sqrt, bias=EPS, scale=1.0 / 512, reads=[st], writes=[st])
        b.op("dve", lambda e: e.reciprocal(st[:, 32:34], st[:, 32:34]), reads=[st], writes=[st])
        b.tt("dve", yb.rearrange("p (g q) -> p g q", q=512), yb.rearrange("p (g q) -> p g q", q=512),
             st[:, 32:34].unsqueeze(2).to_broadcast([128, 2, 512]), ALU.mult, reads=[ycat, st], writes=[ycat])
        b.tt("dve", yb, yb, nw, ALU.mult, reads=[ycat, vb], writes=[ycat])
        for k in range(12):
            pt = psT[k // 4]
            b.tr(pt[:, k % 4, :], ycat[:, k * 128:(k + 1) * 128], ident, reads=[ycat, cm], writes=[pt])
        for i3 in range(3):
            b.op("dve", lambda e, i3=i3: e.tensor_copy(out=yT[:, 4 * i3:4 * i3 + 4, :], in_=psT[i3][:]), reads=[psT[i3]], writes=[yT])
        for half in range(2):
            hsl = slice(half * 512, half * 512 + 512)
            po = pso[half]
            for k in range(12):
                b.mm(po[:], yT[:, k, :], Wg[r][:, k, hsl], start=(k == 0), stop=(k == 11), reads=[yT, Wg[r]], writes=[po])
            b.tt("dve", xo[:, hsl], po[:], xres[:, hsl], ALU.add, reads=[po, xres], writes=[xo])
        if final:
            b.act(t10[:], xo[:], AF.Square, reads=[xo], writes=[t10, st], accum=st[:, 40:41])
            b.act(st[:, 40:41], st[:, 40:41], AF.Sqrt, bias=EPS, scale=1.0 / 1024, reads=[st], writes=[st])
            b.op("dve", lambda e: e.reciprocal(st[:, 40:41], st[:, 40:41]), reads=[st], writes=[st])
            b.stt(xo[:], xo[:], st[:, 40:41], vb[:, 3072:4096], ALU.mult, ALU.mult, reads=[xo, st, vb], writes=[xo])
        b.dma("sp", d["xo"][rs, :], xo[:], reads=[xo], writes=[d["xo"]])
    b.pop()


C_ID, C_UI, C_BONES, C_SEL32, C_RQ, C_RK, C_E0, C_E1, C_SU, C_E64 = 0, 128, 256, 384, 512, 640, 768, 896, 1024, 1152


def mlstm_phase(b, U, pp, cmo, blocks, hC, TT):
    ident = cmo[:, C_ID:C_ID + 128]
    ui = cmo[:, C_UI:C_UI + 128]
    sel32 = cmo[:, C_SEL32:C_SEL32 + 128]
    b.push()
    WB = 1024
    m01 = b.sb([128, WB], name="lm01")
    b.op("dve", lambda e: e.memset(m01[:], 1.0), writes=[m01])
    b.op("dve", lambda e: e.memset(m01[:].rearrange("p (n l) -> p n l", l=128)[:, :, 0:1], 0.0), writes=[m01])
    CT1 = b.sb([128, 2, 257], name="CT1")
    b.op("dve", lambda e: e.memset(CT1[:], 0.0), writes=[CT1])
    ub = [b.sb([128, WB + 4], name=f"lub{q}") for q in range(4)]
    cv = [b.sb([128, WB], name=f"lcv{q}") for q in range(4)]
    vv = [b.sb([128, WB], name=f"lvv{q}") for q in range(2)]
    acc = b.sb([128, WB], name="lacc")
    gt = b.sb([128, WB], name="lgt")
    b.op("dve", lambda e: e.memset(gt[:], 0.0), writes=[gt])
    gt2 = b.sb([128, WB], name="lgt2")
    b.op("dve", lambda e: e.memset(gt2[:], 0.0), writes=[gt2])
    nfb = b.sb([128, 1], name="nfb")
    b.ts("dve", nfb[:], pp[:, 24:25], -1.0, None, ALU.mult, reads=[pp], writes=[nfb])
    psT = b.ps([128, 4, 128], name="lpsT")
    psG = b.ps([128, 3, 128], name="lpsG")
    psS = b.ps([128, 128], name="lpsS")
    psN = b.ps([128, 257], name="lpsN")
    psI = b.ps([128, 257], name="lpsI")
    psC = [b.ps([128, 257], name=f"lpsC{a}") for a in range(2)]
    ktok = b.sb([128, 256], name="lktok")
    v1 = b.sb([128, 257], name="lv1")
    b.op("dve", lambda e: e.memset(v1[:, 256:257], 1.0), writes=[v1])
    gtok = b.sb([128, 128], name="lgtok")
    sm = b.sb([128, 8], name="lsm")
    lw = b.sb([128, 128], name="llw")
    Ee = b.sb([128, 128], name="lE")
    WT = b.sb([128, 128], name="lWT")
    nsb = b.sb([128, 257], name="lnsb")
    tot = b.sb([128, 257], name="ltot")
    kw = b.sb([128, 256], name="lkw")
    ho = [b.sb([128, 256], name=f"lho{i}") for i in range(2)]
    rowoff = [0, 128, 256, 384]
    ci = 0
    for (t0, nch, seg0, seg1) in blocks:
        Wd = nch * 128
        lo = max(t0 - 2, seg0)
        hi = min(t0 + Wd + 2, seg1)
        for q in range(4):
            u = ub[q]
            if lo > t0 - 2:
                b.op("dve", lambda e, u=u: e.memset(u[:, 0:2], 0.0), writes=[u])
            if hi < t0 + Wd + 2:
                b.op("dve", lambda e, u=u, Wd=Wd: e.memset(u[:, Wd + 2:Wd + 4], 0.0), writes=[u])
            c_lo = lo - (t0 - 2)
            b.dma("sp", u[:, c_lo:c_lo + (hi - lo)], U[rowoff[q]:rowoff[q] + 128, lo:hi], reads=[U], writes=[u])
            wc = 5 * q
            b.ts("dve", acc[:, 0:Wd], u[:, 0:Wd], pp[:, wc:wc + 1], None, ALU.mult, reads=[u, pp], writes=[acc])
            for k in range(1, 5):
                b.stt(acc[:, 0:Wd], u[:, k:k + Wd], pp[:, wc + k:wc + k + 1], acc[:, 0:Wd], ALU.mult, ALU.add,
                      reads=[u, pp, acc], writes=[acc])
            b.act(cv[q][:, 0:Wd], acc[:, 0:Wd], AF.Silu, bias=pp[:, 20 + q:21 + q], reads=[acc, pp], writes=[cv[q]])
            if q < 2:
                b.ts("dve", cv[q][:, 0:Wd], cv[q][:, 0:Wd], 0.0625, None, ALU.mult, reads=[cv[q]], writes=[cv[q]])
        for a in range(2):
            b.dma("sp", vv[a][:, 0:Wd], U[512 + 128 * a:640 + 128 * a, t0:t0 + Wd], reads=[U], writes=[vv[a]])
        b.dma("sp", gt[0:1, 0:Wd], U[1408:1409, t0:t0 + Wd], reads=[U], writes=[gt])
        b.dma("sp", gt[32:33, 0:Wd], U[1440:1441, t0:t0 + Wd], reads=[U], writes=[gt])
        b.ts("dve", gt[0:1, 0:Wd], gt[0:1, 0:Wd], pp[0:1, 24:25], None, ALU.add, reads=[gt, pp], writes=[gt])
        b.act(gt[32:33, 0:Wd], gt[32:33, 0:Wd], AF.Exp, bias=nfb[32:33, 0:1], scale=-1.0, reads=[gt, nfb], writes=[gt])
        b.act(gt[32:33, 0:Wd], gt[32:33, 0:Wd], AF.Ln, bias=1.0, reads=[gt], writes=[gt])
        b.ts("dve", gt[32:33, 0:Wd], gt[32:33, 0:Wd], -1.0, None, ALU.mult, reads=[gt], writes=[gt])
        b.op("dve", lambda e, Wd=Wd: e.tensor_tensor_scan(gt2[32:33, 0:Wd], m01[32:33, 0:Wd], gt[32:33, 0:Wd], 0.0, ALU.mult, ALU.add),
             reads=[m01, gt], writes=[gt2])
        q0, q1, k0, k1 = cv
        qc = (q0, q1)
        kc = (k0, k1)
        for j in range(nch):
            c0 = j * 128
            cs = slice(c0, c0 + 128)
            b.tr(psT[:, 0, :], k0[:, cs], ident, reads=[k0, cmo], writes=[psT])
            b.tr(psT[:, 1, :], k1[:, cs], ident, reads=[k1, cmo], writes=[psT])
            b.tr(psT[:, 2, :], vv[0][:, cs], ident, reads=[vv[0], cmo], writes=[psT])
            b.tr(psT[:, 3, :], vv[1][:, cs], ident, reads=[vv[1], cmo], writes=[psT])
            b.tr(psG[:, 0, :], gt[:, cs], ident, reads=[gt, cmo], writes=[psG])
            b.tr(psG[:, 2, :], gt2[:, cs], ident, reads=[gt2, cmo], writes=[psG])
            b.mm(psG[:, 1, :], sel32, gt2[:, cs], reads=[cmo, gt2], writes=[psG])
            b.op("dve", lambda e: e.tensor_copy(out=ktok[:].rearrange("p (a l) -> p a l", l=128), in_=psT[:, 0:2, :]), reads=[psT], writes=[ktok])
            b.op("dve", lambda e: e.tensor_copy(out=v1[:, 0:256].rearrange("p (a l) -> p a l", l=128), in_=psT[:, 2:4, :]), reads=[psT], writes=[v1])
            b.op("dve", lambda e: e.tensor_copy(out=gtok[:, 0:1], in_=psG[:, 0, 0:1]), reads=[psG], writes=[gtok])
            b.op("dve", lambda e: e.tensor_copy(out=gtok[:, 32:33], in_=psG[:, 2, 32:33]), reads=[psG], writes=[gtok])
            b.tt("dve", sm[:, 0:1], gtok[:, 32:33], gtok[:, 0:1], ALU.subtract, reads=[gtok], writes=[sm])
            b.act(sm[:, 1:2], gtok[:, 32:33], AF.Exp, reads=[gtok], writes=[sm])
            b.op("dve", lambda e: e.tensor_copy(out=sm[:, 2:3], in_=psG[:, 1, 127:128]), reads=[psG], writes=[sm])
            b.ts("dve", lw[:], psG[:, 1, :], sm[:, 0:1], None, ALU.subtract, reads=[psG, sm], writes=[lw])
            b.act(Ee[:], lw[:], AF.Exp, reads=[lw], writes=[Ee])
            b.tt("dve", sm[:, 3:4], sm[:, 2:3], sm[:, 0:1], ALU.subtract, reads=[sm], writes=[sm])
            b.act(sm[:, 3:4], sm[:, 3:4], AF.Exp, reads=[sm], writes=[sm])
            b.act(sm[:, 4:5], sm[:, 2:3], AF.Exp, reads=[sm], writes=[sm])
            for a in range(2):
                b.mm(psS[:], kc[a][:, cs], qc[a][:, cs], start=(a == 0), stop=(a == 1), reads=[kc[a], qc[a]], writes=[psS])
            b.tt("dve", WT[:], psS[:], ui, ALU.mult, reads=[psS, cmo], writes=[WT])
            b.tt("dve", WT[:], WT[:], Ee[:], ALU.mult, reads=[WT, Ee], writes=[WT])
            b.mm(psN[:], WT[:], v1[:], reads=[WT, v1], writes=[psN])
            for a in range(2):
                b.mm(psI[:], qc[a][:, cs], CT1[:, a, :], start=(a == 0), stop=(a == 1), reads=[qc[a], CT1], writes=[psI])
            b.op("dve", lambda e: e.tensor_copy(out=nsb[:], in_=psN[:]), reads=[psN], writes=[nsb])
            b.stt(tot[:], psI[:], sm[:, 1:2], nsb[:], ALU.mult, ALU.add, reads=[psI, sm, nsb], writes=[tot])
            b.act(sm[:, 5:6], tot[:, 256:257], AF.Abs, reads=[tot], writes=[sm])
            b.ts("dve", sm[:, 5:6], sm[:, 5:6], 1.0, None, ALU.max, reads=[sm], writes=[sm])
            b.op("dve", lambda e: e.reciprocal(sm[:, 6:7], sm[:, 5:6]), reads=[sm], writes=[sm])
            h = ho[ci % 2]
            b.ts("dve", h[:], tot[:, 0:256], sm[:, 6:7], None, ALU.mult, reads=[tot, sm], writes=[h])
            b.dma("sp", hC[t0 + c0:t0 + c0 + 128, :], h[:], reads=[h], writes=[hC])
            b.ts("dve", kw[:], ktok[:], sm[:, 3:4], None, ALU.mult, reads=[ktok, sm], writes=[kw])
            for a in range(2):
                b.mm(psC[a][:], kw[:, 128 * a:128 * a + 128], v1[:], reads=[kw, v1], writes=[psC[a]])
                b.stt(CT1[:, a, :], CT1[:, a, :], sm[:, 4:5], psC[a][:], ALU.mult, ALU.add, reads=[CT1, sm, psC[a]], writes=[CT1])
            ci += 1
    b.pop()


def attn_phase(b, U, pp, cmo, tabs, TT, yD, need_ctx=True):
    ident = cmo[:, C_ID:C_ID + 128]
    bones = cmo[:, C_BONES:C_BONES + 128]
    Rq = cmo[:, C_RQ:C_RQ + 128]
    Rk = cmo[:, C_RK:C_RK + 128]
    E = [cmo[:, C_E0:C_E0 + 128], cmo[:, C_E1:C_E1 + 128]]
    e64 = cmo[:, C_E64:C_E64 + 64]
    cosq, sinq, cosk, sink = tabs
    NKT = TT // 128
    b.push()
    QT = b.sb([128, TT], BF16, name="QT")
    KTz = [b.sb([128, TT], BF16, name=f"KTz{h}") for h in range(2)]
    V1 = b.sb([128, NKT, 65], BF16, name="V1")
    b.op("dve", lambda e: e.memset(V1[:, :, 64:65], 1.0), writes=[V1])
    b.push()
    xq = b.sb([128, 512], name="axq")
    xk = b.sb([128, 512], name="axk")
    t1 = b.sb([128, 512], name="at1")
    t2 = b.sb([128, 512], name="at2")
    tc_ = b.sb([128, 512], name="atc")
    ts_ = b.sb([128, 512], name="ats")
    ps1 = b.ps([128, 512], name="aps1")
    ps2 = b.ps([128, 512], name="aps2")
    psV = b.ps([128, 4, 128], name="apsV")
    for p0 in range(0, TT, 512):
        pw = min(512, TT - p0)
        for which in range(2):
            x = xq if which == 0 else xk
            r0 = 1024 if which == 0 else 1152
            gcol = 25 + which
            ct, st_ = (cosq, sinq) if which == 0 else (cosk, sink)
            Rm = Rq if which == 0 else Rk
            b.dma("sp", x[:, 0:pw], U[r0:r0 + 128, p0:p0 + pw], reads=[U], writes=[x])
            b.dma("sp", tc_[:, 0:pw], ct[:, p0:p0 + pw], reads=[ct], writes=[tc_])
            b.dma("sp", ts_[:, 0:pw], st_[:, p0:p0 + pw], reads=[st_], writes=[ts_])
            b.tt("dve", t1[:, 0:pw], x[:, 0:pw], x[:, 0:pw], ALU.mult, reads=[x], writes=[t1])
            b.mm(ps1[:, 0:pw], bones, t1[:, 0:pw], reads=[cmo, t1], writes=[ps1])
            b.act(t1[:, 0:pw], ps1[:, 0:pw], AF.Sqrt, bias=EPS, scale=1.0 / 64, reads=[ps1], writes=[t1])
            b.op("dve", lambda e, pw=pw: e.reciprocal(t1[:, 0:pw], t1[:, 0:pw]), reads=[t1], writes=[t1])
            if which == 1:
                b.op("dve", lambda e, pw=pw: e.memset(t1[64:128, 0:pw], 1.0), writes=[t1])
            b.stt(t2[:, 0:pw], x[:, 0:pw], pp[:, gcol:gcol + 1], t1[:, 0:pw], ALU.mult, ALU.mult, reads=[x, pp, t1], writes=[t2])
            b.mm(ps2[:, 0:pw], Rm, t2[:, 0:pw], reads=[cmo, t2], writes=[ps2])
            b.tt("dve", t1[:, 0:pw], ps2[:, 0:pw], ts_[:, 0:pw], ALU.mult, reads=[ps2, ts_], writes=[t1])
            b.tt("dve", t2[:, 0:pw], t2[:, 0:pw], tc_[:, 0:pw], ALU.mult, reads=[t2, tc_], writes=[t2])
            if which == 0:
                b.stt(QT[:, p0:p0 + pw], t2[:, 0:pw], 1.0, t1[:, 0:pw], ALU.mult, ALU.add, reads=[t2, t1], writes=[QT])
                b.ts("dve", QT[:, p0:p0 + pw], QT[:, p0:p0 + pw], 0.125, None, ALU.mult, reads=[QT], writes=[QT])
            else:
                b.tt("dve", t2[:, 0:pw], t2[:, 0:pw], t1[:, 0:pw], ALU.add, reads=[t2, t1], writes=[t2])
                for h in range(2):
                    b.mm(ps1[:, 0:pw], E[h], t2[:, 0:pw], reads=[cmo, t2], writes=[ps1])
                    b.op("dve", lambda e, h=h, p0=p0, pw=pw: e.tensor_copy(out=KTz[h][:, p0:p0 + pw], in_=ps1[:, 0:pw]),
                         reads=[ps1], writes=[KTz[h]])
                nt = pw // 128
                for i in range(nt):
                    b.tr(psV[:, i, :], t2[:, i * 128:(i + 1) * 128], ident, reads=[t2, cmo], writes=[psV])
                b.op("dve", lambda e, p0=p0, nt=nt: e.tensor_copy(out=V1[:, p0 // 128:p0 // 128 + nt, 0:64], in_=psV[:, 0:nt, 64:128]),
                     reads=[psV], writes=[V1])
    b.pop()
    psS = [b.ps([128, 512], name=f"apsS{i}") for i in range(3)]
    psO = [b.ps([128, 512], name=f"apsO{h}") for h in range(2)]
    psD = b.ps([128, 512], name="apsD")
    PT = [b.sb([128, 512], BF16, name=f"aPT{i}") for i in range(3)]
    OT = b.sb([128, 512], name="aOT")
    b.op("dve", lambda e: e.memset(OT[:], 0.0), writes=[OT])
    rd = b.sb([64, 512], name="ard")
    yo = [b.sb([64, 512], name=f"ayo{i}") for i in range(2)]
    qblocks = []
    if need_ctx:
        qblocks.append((0, 256, 0, 2))
    t = 256
    while t < TT:
        w = min(512, TT - t)
        qblocks.append((t, w, 0, NKT))
        t += w
    steps = []
    for (q0, qw, k_lo, k_hi) in qblocks:
        for kt in range(k_lo, k_hi):
            for h in range(2):
                steps.append((q0, qw, kt, h, kt == k_lo, kt == k_hi - 1))
    oi = 0

    def issue_S(i):
        q0, qw, kt, h, first, last = steps[i]
        b.mm(psS[i % 3][:, 0:qw], KTz[h][:, kt * 128:(kt + 1) * 128], QT[:, q0:q0 + qw], reads=[KTz[h], QT], writes=[psS[i % 3]])

    LOOK = 2
    for i in range(min(LOOK, len(steps))):
        issue_S(i)
    for i, (q0, qw, kt, h, first, last) in enumerate(steps):
        ps = psS[i % 3]
        pt = PT[i % 3]
        b.act(pt[:, 0:qw], ps[:, 0:qw], AF.Exp, reads=[ps], writes=[pt])
        if i + LOOK < len(steps):
            issue_S(i + LOOK)
        b.mm(psO[h][0:65, 0:qw], V1[:, kt, :], pt[:, 0:qw], start=first, stop=last, reads=[V1, pt], writes=[psO[h]])
        if last:
            b.op("dve", lambda e, h=h, qw=qw: e.tensor_copy(out=OT[0:65, 0:qw], in_=psO[h][0:65, 0:qw]), reads=[psO[h]], writes=[OT])
            b.mm(psD[0:64, 0:qw], e64, OT[:, 0:qw], reads=[cmo, OT], writes=[psD])
            b.op("dve", lambda e, qw=qw: e.reciprocal(rd[:, 0:qw], psD[0:64, 0:qw]), reads=[psD], writes=[rd])
            y = yo[oi % 2]
            oi += 1
            b.tt("dve", y[:, 0:qw], OT[0:64, 0:qw], rd[:, 0:qw], ALU.mult, reads=[OT, rd], writes=[y])
            b.dma("sp", yD[64 * h:64 * h + 64, q0:q0 + qw], y[:, 0:qw], reads=[y], writes=[yD])
    b.pop()


def out_odd(b, tiles, d, cm, final=False):
    ident = cm[:, 0:128]
    b.push()
    vb = b.sb([128, 5120], name="vb")
    for i in range(5):
        b.dma("sp", vb[:, i * 1024:(i + 1) * 1024], d["vecs"][0, i * 1024:(i + 1) * 1024].partition_broadcast(128),
              reads=[d["vecs"]], writes=[vb])
    nw, adab = vb[:, 0:1024], vb[:, 4096:5120]
    cvt = b.sb([128, 16], name="cvt")
    b.dma("sp", cvt[:], d["cv"][:, :], reads=[d["cv"]], writes=[cvt])
    sc = b.sb([128, 16], name="osc")
    b.act(sc[:], cvt[:], AF.Silu, reads=[cvt], writes=[sc])
    gate_b = [b.sb([128, 1024], name=f"gate{r}") for r in range(2)]
    Wg = [b.sb([128, 16, 1024], BF16, name=f"Wg{r}") for r in range(2)]
    b.push()
    aw = b.sb([128, 8, 1024], name="oaw")
    for k in range(8):
        b.dma("sp", aw[:, k, :], d["adawg"][k * 128:(k + 1) * 128, :], reads=[d["adawg"]], writes=[aw])
    scb = b.sb([128, 8, 128], name="scb")
    pg = b.ps([128, 512], name="pg")
    for r in range(2):
        b.op("dve", lambda e, r=r: e.tensor_copy(out=scb[:], in_=sc[:, r:16:2].unsqueeze(2).to_broadcast([128, 8, 128])),
             reads=[sc], writes=[scb])
        for half in range(2):
            hsl = slice(half * 512, half * 512 + 512)
            for k in range(8):
                b.mm(pg[:], scb[:, k, :], aw[:, k, hsl], start=(k == 0), stop=(k == 7), reads=[scb, aw], writes=[pg])
            b.tt("dve", gate_b[r][:, hsl], pg[:], adab[:, hsl], ALU.add, reads=[pg, vb], writes=[gate_b[r]])
    b.pop()
    for part in range(2):
        b.push()
        stg = b.sb([128, 8, 1024], name="ostg")
        for k in range(8):
            kk = part * 8 + k
            b.dma("sp", stg[:, k, :], d["wout"][kk * 128:(kk + 1) * 128, :], reads=[d["wout"]], writes=[stg])
        for r in range(2):
            for k in range(8):
                b.tt("dve", Wg[r][:, part * 8 + k, :], stg[:, k, :], gate_b[r][:], ALU.mult, reads=[stg, gate_b[r]], writes=[Wg[r]])
        b.pop()
    names = ("hC", "o", "z", "yD", "ag", "xres")
    widths = (2048, 1024, 1024, 1024, 1024, 1024)
    sets = [[b.sb([128, w], name=f"oin{s_}_{n}") for n, w in zip(names, widths)] for s_ in range(2)]

    def load(i_):
        r0_, _r = tiles[i_]
        for tl, nm in zip(sets[i_ % 2], names):
            b.dma("sp", tl[:], d[nm][r0_:r0_ + 128, :], reads=[d[nm]], writes=[tl])

    ycat = b.sb([128, 2048], name="ycat")
    t10 = b.sb([128, 1024], name="ot10")
    st = b.sb([128, 64], name="ost")
    yT = b.sb([128, 16, 128], BF16, name="oyT")
    xo = b.sb([128, 1024], name="oxo")
    psT = [b.ps([128, 4, 128], name=f"opsT{i}") for i in range(4)]
    pso = [b.ps([128, 512], name=f"opso{i}") for i in range(2)]
    load(0)
    for ti_, (r0, r) in enumerate(tiles):
        rs = slice(r0, r0 + 128)
        if ti_ + 1 < len(tiles):
            load(ti_ + 1)
        hC, oo, zz, yD, ag, xres = sets[ti_ % 2]
        h = ycat[:, 0:1024]
        h3 = h.rearrange("p (g q) -> p g q", q=256)
        b.tt("dve", h, hC[:, 0:1024], hC[:, 1024:2048], ALU.add, reads=[hC], writes=[ycat])
        b.tt("dve", t10[:], h, h, ALU.mult, reads=[ycat], writes=[t10])
        b.op("dve", lambda e: e.tensor_reduce(out=st[:, 0:4], in_=t10[:].rearrange("p (g q) -> p g q", q=256), axis=AX.X, op=ALU.add),
             reads=[t10], writes=[st])
        b.act(st[:, 0:4], st[:, 0:4], AF.Sqrt, bias=EPS, scale=1.0 / 256, reads=[st], writes=[st])
        b.op("dve", lambda e: e.reciprocal(st[:, 0:4], st[:, 0:4]), reads=[st], writes=[st])
        b.tt("dve", h3, h3, st[:, 0:4].unsqueeze(2).to_broadcast([128, 4, 256]), ALU.mult, reads=[ycat, st], writes=[ycat])
        b.tt("dve", h, h, nw, ALU.mult, reads=[ycat, vb], writes=[ycat])
        b.act(oo[:], oo[:], AF.Sigmoid, reads=[oo], writes=[oo])
        b.act(zz[:], zz[:], AF.Silu, reads=[zz], writes=[zz])
        b.tt("dve", h, h, oo[:], ALU.mult, reads=[ycat, oo], writes=[ycat])
        b.tt("dve", h, h, zz[:], ALU.mult, reads=[ycat, zz], writes=[ycat])
        b.act(ag[:], ag[:], AF.Silu, reads=[ag], writes=[ag])
        b.tt("dve", ycat[:, 1024:2048], yD[:], ag[:], ALU.mult, reads=[yD, ag], writes=[ycat])
        for k in range(16):
            pt = psT[k // 4]
            b.tr(pt[:, k % 4, :], ycat[:, k * 128:(k + 1) * 128], ident, reads=[ycat, cm], writes=[pt])
        for i4 in range(4):
            b.op("dve", lambda e, i4=i4: e.tensor_copy(out=yT[:, 4 * i4:4 * i4 + 4, :], in_=psT[i4][:]), reads=[psT[i4]], writes=[yT])
        for half in range(2):
            hsl = slice(half * 512, half * 512 + 512)
            po = pso[half]
            for k in range(16):
                b.mm(po[:], yT[:, k, :], Wg[r][:, k, hsl], start=(k == 0), stop=(k == 15), reads=[yT, Wg[r]], writes=[po])
            b.tt("dve", xo[:, hsl], po[:], xres[:, hsl], ALU.add, reads=[po, xres], writes=[xo])
        if final:
            b.act(t10[:], xo[:], AF.Square, reads=[xo], writes=[t10, st], accum=st[:, 40:41])
            b.act(st[:, 40:41], st[:, 40:41], AF.Sqrt, bias=EPS, scale=1.0 / 1024, reads=[st], writes=[st])
            b.op("dve", lambda e: e.reciprocal(st[:, 40:41], st[:, 40:41]), reads=[st], writes=[st])
            b.stt(xo[:], xo[:], st[:, 40:41], vb[:, 3072:4096], ALU.mult, ALU.mult, reads=[xo, st, vb], writes=[xo])
        b.dma("sp", d["xo"][rs, :], xo[:], reads=[xo], writes=[d["xo"]])
    b.pop()


def consts_cm():
    cm = np.zeros((128, 642), np.float32)
    i = np.arange(128)
    cm[:, 0:128] = np.eye(128)
    cm[:, 128:256] = (i[:, None] < i[None, :])
    cm[:, 256:384] = (i[:, None] > i[None, :])
    cm[:, 384:512] = (i[:, None] <= i[None, :])
    cm[:, 512:640] = ((i[:, None] // 64) == (i[None, :] // 64))
    cm[:, 640] = (i < 64)
    cm[:, 641] = (i >= 64)
    return cm


def fm8(v):
    return np.ascontiguousarray(v.reshape(8, 128).T)


def even_core_inputs(core, j_layer, l, inp, xseq):
    d, jj = core // 4, core % 4
    W = inp['ab_w_in'][j_layer]
    o_r, o_k, o_v = 0, 512, 1024
    o_wl, o_al = 1536, 1664
    o_ga = 1792
    o_xbc = 2304
    o_dt = o_xbc + 1536
    o_z = o_dt + 32
    hc = slice(128 * jj, 128 * jj + 128)
    g = jj // 2
    cols = []
    cols.append(np.arange(o_r, o_r + 512)[hc])
    cols.append(np.arange(o_k, o_k + 512)[hc])
    cols.append(np.arange(o_v, o_v + 512)[hc])
    cols.append(np.arange(o_ga, o_ga + 512)[hc])
    cols.append(np.concatenate([np.arange(o_wl + 64 * d, o_wl + 64 * d + 64), np.arange(o_al + 64 * d, o_al + 64 * d + 64)]))
    xs_cols = np.arange(o_xbc, o_xbc + 1024)[256 * jj:256 * jj + 256]
    cols.append(xs_cols[:128])
    cols.append(xs_cols[128:])
    cols.append(np.arange(o_xbc + 1024 + 128 * g, o_xbc + 1024 + 128 * g + 128))
    cols.append(np.arange(o_xbc + 1280 + 128 * g, o_xbc + 1280 + 128 * g + 128))
    z_cols = np.arange(o_z, o_z + 1024)[256 * jj:256 * jj + 256]
    cols.append(z_cols[:128])
    cols.append(z_cols[128:])
    win = np.zeros((1024, NCC * 128), np.float32)
    for cc, c in enumerate(cols):
        win[:, cc * 128:cc * 128 + len(c)] = W[:, c]
    dt_cols = np.arange(o_dt + 16 * d + 4 * jj, o_dt + 16 * d + 4 * jj + 4)
    win[:, 11 * 128:11 * 128 + 4] = W[:, dt_cols]
    pp = np.zeros((128, NPP), np.float32)
    mu = inp['rk_mu'][j_layer]
    pp[:, 0] = mu[o_r:o_r + 512][hc]
    pp[:, 1] = mu[o_k:o_k + 512][hc]
    pp[:, 2] = mu[o_v:o_v + 512][hc]
    pp[0:64, 3] = mu[o_wl + 64 * d:o_wl + 64 * d + 64]
    pp[64:128, 3] = mu[o_al + 64 * d:o_al + 64 * d + 64]
    pp[:, 4] = inp['rk_w0'][j_layer, d][hc]
    pp[:, 5] = inp['rk_a0'][j_layer, d][hc]
    pp[:, 6] = inp['rk_k_k'][j_layer][hc]
    pp[:, 7] = inp['rk_k_a'][j_layer][hc]
    pp[:, 8] = inp['rk_r_k'][j_layer].reshape(512)[hc]
    cw = inp['mb_conv_w'][j_layer]
    cb = inp['mb_conv_b'][j_layer]
    if d == 1:
        cw = cw[::-1]
    xbc_rel = [xs_cols[:128] - o_xbc, xs_cols[128:] - o_xbc, cols[7] - o_xbc, cols[8] - o_xbc]
    for q, rel in enumerate(xbc_rel):
        pp[:, 9 + 5 * q:14 + 5 * q] = cw[:, rel].T
        pp[:, 29 + q] = cb[rel]
    pp[0:4, 33] = inp['mb_dt_bias'][j_layer, d, 4 * jj:4 * jj + 4]
    pp[0:4, 34] = inp['mb_a_log'][j_layer, d, 4 * jj:4 * jj + 4]
    pp[:, 35:43] = fm8(inp['norm_g'][l])
    pp[:, 43:51] = fm8(inp['ada_b'][l][0:1024])
    pp[:, 51:59] = fm8(inp['ada_b'][l][1024:2048])
    cv = np.stack([fm8(inp['c'][0]), fm8(inp['c_ctx'])], -1)
    pp[:, 59:75] = cv.reshape(128, 16)
    w2a2 = np.zeros((128, 256), np.float32)
    w2a2[0:64, 0:128] = inp['rk_w2'][j_layer, d][:, hc]
    w2a2[64:128, 128:256] = inp['rk_a2'][j_layer, d][:, hc]
    return {
        'xseq': xseq, 'pp': pp, 'win': win,
        'adaw': np.ascontiguousarray(inp['ada_w'][l][:, 0:2048]),
        'w2a2': w2a2, 'cm': consts_cm(), 'sel': np.concatenate([np.kron(np.eye(4, dtype=np.float32), np.ones((1, 128), np.float32)), np.zeros((124, 512), np.float32)], 0),
    }


NCMO = 128 * 9 + 64


def consts_cmo():
    i = np.arange(128)
    c = np.zeros((128, NCMO), np.float32)
    c[:, 0:128] = np.eye(128)
    c[:, 128:256] = (i[:, None] <= i[None, :])
    c[:, 256:384] = ((i[:, None] // 64) == (i[None, :] // 64))
    c[32, 384:512] = 1.0
    R = np.zeros((128, 128), np.float32)
    for p in range(128):
        q = p % 32
        if q < 16:
            R[p + 16, p] = -1.0
        else:
            R[p - 16, p] = 1.0
    c[:, 512:640] = R
    Rk = R.copy()
    Rk[64:, :] = 0
    Rk[:, 64:] = 0
    c[:, 640:768] = Rk
    E0 = np.zeros((128, 128), np.float32)
    E1 = np.zeros((128, 128), np.float32)
    for dd in range(64):
        E0[dd, dd] = 1.0
        E1[dd, 64 + dd] = 1.0
    c[:, 768:896] = E0
    c[:, 896:1024] = E1
    c[:, 1024:1152] = (i[:, None] < i[None, :])
    c[64, 1152:1216] = 1.0
    return c


def rope_tables(TT, T, d, grid_w=64, theta=10000.0):
    idx = np.arange(T)
    t = idx if d == 0 else T - 1 - idx
    row = (t // grid_w).astype(np.float64)
    col = (t % grid_w).astype(np.float64)
    inv = theta ** (-np.arange(16, dtype=np.float64) / 16)
    cos = np.ones((128, TT), np.float64)
    sin = np.zeros((128, TT), np.float64)
    for p in range(128):
        pp_ = p % 64
        pos = row if pp_ < 32 else col
        ang = pos * inv[pp_ % 16]
        cos[p, TT - T:] = np.cos(ang)
        sin[p, TT - T:] = np.sin(ang)
    cosk, sink = cos.copy(), sin.copy()
    cosk[64:] = 1.0
    sink[64:] = 0.0
    return cos.astype(np.float32), sin.astype(np.float32), cosk.astype(np.float32), sink.astype(np.float32)


def odd_core_inputs(core, j, l, inp, xseq, T):
    d, jj = core // 4, core % 4
    c = core
    W = inp['cd_w_in'][j]
    TT = xseq.shape[0]
    cols = [np.arange(256 * jj, 256 * jj + 128), np.arange(256 * jj + 128, 256 * jj + 256),
            np.arange(1024 + 256 * jj, 1024 + 256 * jj + 128), np.arange(1024 + 256 * jj + 128, 1024 + 256 * jj + 256),
            np.arange(2048 + 256 * jj, 2048 + 256 * jj + 128), np.arange(2048 + 256 * jj + 128, 2048 + 256 * jj + 256)]
    oz = 3072 if d == 0 else 4112
    cols += [np.arange(oz + 256 * jj, oz + 256 * jj + 128), np.arange(oz + 256 * jj + 128, oz + 256 * jj + 256)]
    cols.append(np.arange(5136 + 128 * c, 5136 + 128 * c + 128))
    cols.append(np.concatenate([np.arange(6160 + 64 * (c // 2), 6160 + 64 * (c // 2) + 64),
                                np.arange(6416 + 64 * (c // 2), 6416 + 64 * (c // 2) + 64)]))
    cols.append(np.arange(6672 + 128 * c, 6672 + 128 * c + 128))
    win = np.zeros((1024, NCC * 128), np.float32)
    for cc, cl in enumerate(cols):
        win[:, cc * 128:cc * 128 + len(cl)] = W[:, cl]
    win[:, 11 * 128 + 0] = W[:, 4096 + 4 * d + jj]
    win[:, 11 * 128 + 32] = W[:, 4104 + 4 * d + jj]
    pp = np.zeros((128, NPP), np.float32)
    cw = inp['ml_conv_w'][j]
    cb = inp['ml_conv_b'][j]
    if d == 1:
        cw = cw[::-1]
    for q in range(4):
        pp[:, 5 * q:5 * q + 5] = cw[:, cols[q]].T
        pp[:, 20 + q] = cb[cols[q]]
    pp[0, 24] = inp['ml_i_bias'][j, d, jj]
    pp[32, 24] = inp['ml_f_bias'][j, d, jj]
    pp[:, 25] = np.tile(inp['at_q_norm'][j], 2)
    pp[0:64, 26] = inp['at_k_norm'][j]
    pp[64:, 26] = 1.0
    pp[:, 35:43] = fm8(inp['norm_g'][l])
    pp[:, 43:51] = fm8(inp['ada_b'][l][0:1024])
    pp[:, 51:59] = fm8(inp['ada_b'][l][1024:2048])
    cv = np.stack([fm8(inp['c'][0]), fm8(inp['c_ctx'])], -1)
    pp[:, 59:75] = cv.reshape(128, 16)
    cq, sq, ck, sk = rope_tables(TT, T, d)
    return {'xseq': xseq, 'pp': pp, 'win': win, 'adaw': np.ascontiguousarray(inp['ada_w'][l][:, 0:2048]),
            'cmo': consts_cmo(), 'cosq': cq, 'sinq': sq, 'cosk': ck, 'sink': sk}


def assemble_even_out_inputs(inp, outs, j, l, xs, ctx):
    def unflip(a):
        return np.concatenate([a[:256][::-1], a[256:][::-1]], 0)
    R = ctx.shape[0] + xs.shape[0]
    yA = np.zeros((R, 1024), np.float32); bon = np.zeros((R, 16), np.float32)
    v = np.zeros((R, 512), np.float32); ga = np.zeros((R, 512), np.float32)
    yB = np.zeros((R, 2048), np.float32); xsm = np.zeros((R, 1024), np.float32); z = np.zeros((R, 1024), np.float32)
    for core in range(8):
        d, jj = core // 4, core % 4
        o = outs[core]
        f = (lambda a: a) if d == 0 else unflip
        yA[:, 512 * d + 128 * jj:512 * d + 128 * jj + 128] = f(o["yA"])
        bon[:, 8 * d + 2 * jj:8 * d + 2 * jj + 2] = f(np.ascontiguousarray(o["bon"].T))
        yB[:, 1024 * d + 256 * jj:1024 * d + 256 * jj + 256] = f(o["yB"])
        if d == 0:
            v[:, 128 * jj:128 * jj + 128] = o["vg"][0:128].T
            ga[:, 128 * jj:128 * jj + 128] = o["vg"][128:256].T
            xsm[:, 256 * jj:256 * jj + 256] = o["xsB"]
            z[:, 256 * jj:256 * jj + 256] = o["zB"].T
    vecs = np.zeros((1, 5120), np.float32)
    vecs[0, 0:512] = inp['rk_ln_w'][j]; vecs[0, 512:1024] = inp['rk_ln_b'][j]
    vecs[0, 1024:2048] = np.repeat(inp['mb_d'][j], 64); vecs[0, 2048:3072] = inp['mb_norm_w'][j]
    vecs[0, 4096:5120] = inp['ada_b'][l][2048:3072]
    cv = np.stack([fm8(inp['c'][0]), fm8(inp['c_ctx'])], -1).reshape(128, 16)
    return {"yA": yA, "bon": bon, "v": v, "ga": ga, "yB": yB, "xsm": xsm, "z": z,
            "xres": np.ascontiguousarray(np.concatenate([ctx, xs], 0)),
            "wout": np.ascontiguousarray(inp['ab_w_out'][j]), "adawg": np.ascontiguousarray(inp['ada_w'][l][:, 2048:3072]),
            "vecs": vecs, "cv": np.ascontiguousarray(cv), "cm": consts_cm()}


def unflip(a):
    return np.concatenate([a[:256][::-1], a[256:][::-1]], 0)

def assemble_odd_out_inputs(inp, outs, j, l, xs, ctx):
    R = ctx.shape[0] + xs.shape[0]
    hC = np.zeros((R, 2048), np.float32); o = np.zeros((R, 1024), np.float32); z = np.zeros((R, 1024), np.float32)
    yD = np.zeros((R, 1024), np.float32); ag = np.zeros((R, 1024), np.float32)
    for core in range(8):
        d, jj = core // 4, core % 4
        oc = outs[core]
        f = (lambda a: a) if d == 0 else unflip
        hC[:, 1024 * d + 256 * jj:1024 * d + 256 * jj + 256] = f(oc["hC"])
        ozt = f(np.ascontiguousarray(oc["oz"].T))
        if d == 0:
            o[:, 256 * jj:256 * jj + 256] = ozt
        else:
            z[:, 256 * jj:256 * jj + 256] = ozt
        yD[:, 128 * core:128 * core + 128] = f(np.ascontiguousarray(oc["yD"].T))
        ag[:, 128 * core:128 * core + 128] = f(np.ascontiguousarray(oc["agT"].T))
    vecs = np.zeros((1, 5120), np.float32)
    vecs[0, 0:1024] = inp['ml_norm_w'][j]
    vecs[0, 4096:5120] = inp['ada_b'][l][2048:3072]
    cv = np.stack([fm8(inp['c'][0]), fm8(inp['c_ctx'])], -1).reshape(128, 16)
    return {"hC": hC, "o": o, "z": z, "yD": yD, "ag": ag, "xres": np.ascontiguousarray(np.concatenate([ctx, xs], 0)),
            "wout": np.ascontiguousarray(inp['cd_w_out'][j]), "adawg": np.ascontiguousarray(inp['ada_w'][l][:, 2048:3072]),
            "vecs": vecs, "cv": np.ascontiguousarray(cv), "cm": consts_cm()}


T_SEQ = 16384
CTX = 256
TT_ALL = T_SEQ + CTX


def _groups_blocks(TT):
    groups = [(0, 2, 1)]
    t = CTX
    while t < TT:
        n = min(4, (TT - t) // 128)
        groups.append((t, n, 0))
        t += n * 128
    blocks = [(0, 2, 0, CTX)]
    t = CTX
    while t < TT:
        n = min(8, (TT - t) // 128)
        blocks.append((t, n, CTX, TT))
        t += n * 128
    return groups, blocks


def build_mix_even():
    b = Bld()
    TT = TT_ALL
    xseq = b.dram("xseq", [TT, 1024], kind="ExternalInput")
    pp_d = b.dram("pp", [128, NPP], kind="ExternalInput")
    win_d = b.dram("win", [1024, NCC * 128], kind="ExternalInput")
    adaw_d = b.dram("adaw", [1024, 2048], kind="ExternalInput")
    w2a2_d = b.dram("w2a2", [128, 256], kind="ExternalInput")
    cm_d = b.dram("cm", [128, 642], kind="ExternalInput")
    sel_d = b.dram("sel", [128, 512], kind="ExternalInput")
    U = b.dram("U", [NCC * 128, TT], kind="Internal")
    yA = b.dram("yA", [TT, 128], kind="ExternalOutput")
    bon = b.dram("bon", [2, TT], kind="ExternalOutput")
    vg = b.dram("vg", [256, TT], kind="ExternalOutput")
    zB = b.dram("zB", [256, TT], kind="ExternalOutput")
    yB = b.dram("yB", [TT, 256], kind="ExternalOutput")
    xsB = b.dram("xsB", [TT, 256], kind="ExternalOutput")
    pp = b.sb([128, NPP], name="pp")
    cm = b.sb([128, 642], name="cm")
    b.dma("sp", pp[:], pp_d[:, :], reads=[pp_d], writes=[pp])
    b.dma("sp", cm[:], cm_d[:, :], reads=[cm_d], writes=[cm])
    modT = setup_mod(b, pp, adaw_d, 16)
    groups, blocks = _groups_blocks(TT)
    dests = {cc: (U, cc * 128) for cc in range(NCC)}
    dests[3] = (vg, 128)
    dests[9] = (zB, 0)
    dests[10] = (zB, 128)
    phase_A(b, xseq, win_d, pp, modT, cm, groups, dests, NCC)
    b.P.barrier()
    rwkv_phase(b, U, pp, w2a2_d, cm, blocks, yA, bon, vg, TT)
    mamba_phase(b, U, pp, cm, sel_d, blocks, yB, xsB, TT)
    b.P.emit()
    b.stacks[0].close()
    return b.nc


def build_mix_odd():
    b = Bld()
    TT = TT_ALL
    xseq = b.dram("xseq", [TT, 1024], kind="ExternalInput")
    pp_d = b.dram("pp", [128, NPP], kind="ExternalInput")
    win_d = b.dram("win", [1024, NCC * 128], kind="ExternalInput")
    adaw_d = b.dram("adaw", [1024, 2048], kind="ExternalInput")
    cmo_d = b.dram("cmo", [128, NCMO], kind="ExternalInput")
    tabs = [b.dram(n, [128, TT], kind="ExternalInput") for n in ("cosq", "sinq", "cosk", "sink")]
    U = b.dram("U", [NCC * 128, TT], kind="Internal")
    hC = b.dram("hC", [TT, 256], kind="ExternalOutput")
    oz = b.dram("oz", [256, TT], kind="ExternalOutput")
    yD = b.dram("yD", [128, TT], kind="ExternalOutput")
    agT = b.dram("agT", [128, TT], kind="ExternalOutput")
    pp = b.sb([128, NPP], name="pp")
    cmo = b.sb([128, NCMO], name="cmo")
    b.dma("sp", pp[:], pp_d[:, :], reads=[pp_d], writes=[pp])
    b.dma("sp", cmo[:], cmo_d[:, :], reads=[cmo_d], writes=[cmo])
    modT = setup_mod(b, pp, adaw_d, 16)
    groups, blocks = _groups_blocks(TT)
    dests = {cc: (U, cc * 128) for cc in range(NCC)}
    dests[6] = (oz, 0)
    dests[7] = (oz, 128)
    dests[10] = (agT, 0)
    phase_A(b, xseq, win_d, pp, modT, cmo, groups, dests, NCC)
    b.P.barrier()
    mlstm_phase(b, U, pp, cmo, blocks, hC, TT)
    attn_phase(b, U, pp, cmo, tabs, TT, yD, need_ctx=True)
    b.P.emit()
    b.stacks[0].close()
    return b.nc


def build_out(kind, R, tiles, final):
    b = Bld()
    if kind == "even":
        shapes = {"yA": [R, 1024], "bon": [R, 16], "v": [R, 512], "ga": [R, 512], "yB": [R, 2048], "xsm": [R, 1024],
                  "z": [R, 1024], "xres": [R, 1024], "wout": [1536, 1024], "adawg": [1024, 1024], "vecs": [1, 5120], "cv": [128, 16]}
    else:
        shapes = {"hC": [R, 2048], "o": [R, 1024], "z": [R, 1024], "yD": [R, 1024], "ag": [R, 1024], "xres": [R, 1024],
                  "wout": [2048, 1024], "adawg": [1024, 1024], "vecs": [1, 5120], "cv": [128, 16]}
    d = {k: b.dram(k, s, kind="ExternalInput") for k, s in shapes.items()}
    d["xo"] = b.dram("xo", [R, 1024], kind="ExternalOutput")
    cm_d = b.dram("cm", [128, 642], kind="ExternalInput")
    cm = b.sb([128, 642], name="cm")
    b.dma("sp", cm[:], cm_d[:, :], reads=[cm_d], writes=[cm])
    if kind == "even":
        out_even(b, tiles, d, cm, final=final)
    else:
        out_odd(b, tiles, d, cm, final=final)
    b.P.emit()
    b.stacks[0].close()
    return b.nc


def kernel(**inp):
    inp = {k: np.asarray(v) for k, v in inp.items()}
    xs = np.ascontiguousarray(inp['x'][0])
    ctx = np.ascontiguousarray(inp['ctx'][0])
    sh = T_SEQ // 8
    tiles = [(0, 1), (128, 1)] + [(CTX + 128 * i, 0) for i in range(sh // 128)]
    cores = list(range(8))
    for l in range(4):
        j = l // 2
        even = (l % 2 == 0)
        fwd = np.ascontiguousarray(np.concatenate([ctx, xs], 0))
        bwd = np.ascontiguousarray(np.concatenate([ctx[::-1], xs[::-1]], 0))
        if even:
            maps = [even_core_inputs(c, j, l, inp, fwd if c < 4 else bwd) for c in cores]
            res = run_bass_kernel_spmd(build_mix_even(), maps, core_ids=cores)
            full = assemble_even_out_inputs(inp, res.results, j, l, xs, ctx)
            row_keys = ("yA", "bon", "v", "ga", "yB", "xsm", "z", "xres")
        else:
            maps = [odd_core_inputs(c, j, l, inp, fwd if c < 4 else bwd, T_SEQ) for c in cores]
            res = run_bass_kernel_spmd(build_mix_odd(), maps, core_ids=cores)
            full = assemble_odd_out_inputs(inp, res.results, j, l, xs, ctx)
            row_keys = ("hC", "o", "z", "yD", "ag", "xres")
        del res, maps
        final = (l == 3)
        full["vecs"][0, 3072:4096] = inp['norm_final']
        maps2 = []
        for c in cores:
            rows = np.concatenate([np.arange(CTX), CTX + c * sh + np.arange(sh)])
            m = {k: np.ascontiguousarray(full[k][rows]) for k in row_keys}
            for k in ("wout", "adawg", "vecs", "cv", "cm"):
                m[k] = full[k]
            maps2.append(m)
        del full
        res2 = run_bass_kernel_spmd(build_out("even" if even else "odd", CTX + sh, tiles, final), maps2, core_ids=cores)
        outs = [np.asarray(r['xo'], dtype=np.float32) for r in res2.results]
        xs = np.ascontiguousarray(np.concatenate([o[CTX:] for o in outs], 0))
        ctx = np.ascontiguousarray(outs[0][:CTX])
        del res2, maps2, outs
    return xs[None]
```
